# Optimizing a Trainium2 kernel written in Bass

```python
import jax, jax.numpy as jnp
from jax import lax
import numpy as np

D_MODEL = 1024
BATCH = 4
SEQ = 4096
DEPTH = 2
DEC_BATCH = 128
DEC_SEQ = 1
PAST_LEN = 16384
PAGE_SIZE = 128

N_A_LAYERS = DEPTH // 2
N_B_LAYERS = DEPTH - N_A_LAYERS
EXPAND = 2
MIX_WIDTH = EXPAND * D_MODEL
HGRN_HEAD_DIM = 128
HGRN_HEADS = MIX_WIDTH // HGRN_HEAD_DIM
CHUNK = 16
HEAD_DIM = 64
N_Q_HEADS = MIX_WIDTH // HEAD_DIM
KV_HEADS = N_Q_HEADS // 8
Q_PER_KV = N_Q_HEADS // KV_HEADS
WINDOW = 128
ROT_DIM = HEAD_DIM // 4
ROPE_THETA = 500000.0
PLE_DIM = 256
EPS = 1e-6
ATTN_SCALE = HEAD_DIM ** -0.5

kernel_name = "yoco_hgrn2_swa_sink_step"


def rmsnorm(x, g):
    xf = x.astype(jnp.float32)
    y = xf * lax.rsqrt(jnp.mean(xf * xf, axis=-1, keepdims=True) + EPS)
    return (y * g.astype(jnp.float32)).astype(x.dtype)


def rope_partial(x, pos):
    inv = ROPE_THETA ** (-jnp.arange(0, ROT_DIM, 2, dtype=jnp.float32) / ROT_DIM)
    ang = pos.astype(jnp.float32)[:, None] * inv[None, :]
    cos = jnp.cos(ang)[None, :, None, :]
    sin = jnp.sin(ang)[None, :, None, :]
    xf = x.astype(jnp.float32)
    x1 = xf[..., :ROT_DIM // 2]
    x2 = xf[..., ROT_DIM // 2:ROT_DIM]
    out = jnp.concatenate([x1 * cos - x2 * sin, x2 * cos + x1 * sin, xf[..., ROT_DIM:]], axis=-1)
    return out.astype(x.dtype)


def hgrn2_chunked(q, g, k, v, S0):
    B, L, H, D = q.shape
    c = min(CHUNK, L)
    nc = -(-L // c)
    pad = nc * c - L

    def blocks(t):
        t = jnp.pad(t, ((0, 0), (0, pad), (0, 0), (0, 0)))
        return t.reshape(B, nc, c, H, D).transpose(1, 0, 3, 2, 4)

    causal = jnp.tril(jnp.ones((c, c), dtype=bool))

    def step(S, xs):
        qc, gc, kc, vc = xs
        b = jnp.cumsum(gc, axis=2)
        o_inter = jnp.einsum('bhtk,bhkv->bhtv', qc * jnp.exp(b), S)
        diff = b[:, :, :, None, :] - b[:, :, None, :, :]
        decay = jnp.exp(jnp.where(causal[:, :, None], diff, -jnp.inf))
        A = jnp.einsum('bhtk,bhsk,bhtsk->bhts', qc, kc, decay)
        o = o_inter + jnp.einsum('bhts,bhsv->bhtv', A, vc)
        b_last = b[:, :, -1]
        S_new = jnp.exp(b_last)[..., None] * S + jnp.einsum(
            'bhsk,bhsv->bhkv', kc * jnp.exp(b_last[:, :, None, :] - b), vc)
        return S_new, o

    S, o = lax.scan(step, S0, (blocks(q), blocks(g), blocks(k), blocks(v)))
    o = o.transpose(1, 0, 3, 2, 4).reshape(B, nc * c, H, D)[:, :L]
    return o, S


def hgrn2_mixer(xn, w_in, lb, onorm_g, w_out, S0):
    B, L, _ = xn.shape
    u = xn @ w_in
    q, fpre, i, z = jnp.split(u, 4, axis=-1)
    q = jax.nn.silu(q.astype(jnp.float32))
    f = lb + (1.0 - lb) * jax.nn.sigmoid(fpre.astype(jnp.float32))
    g = jnp.log(f)
    k = 1.0 - f
    heads = lambda t: t.reshape(B, L, HGRN_HEADS, HGRN_HEAD_DIM)
    o, S = hgrn2_chunked(heads(q), heads(g), heads(k), heads(i.astype(jnp.float32)), S0)
    o = rmsnorm(o, onorm_g.reshape(HGRN_HEADS, HGRN_HEAD_DIM))
    o = o.reshape(B, L, MIX_WIDTH).astype(xn.dtype) * jax.nn.silu(z)
    return o @ w_out, S


def sink_softmax(s, sinks):
    sink = sinks.astype(jnp.float32).reshape(KV_HEADS, Q_PER_KV)[:, :, None, None]
    sink = jnp.broadcast_to(sink, s.shape[:-1] + (1,))
    pr = jax.nn.softmax(jnp.concatenate([s, sink], axis=-1), axis=-1)
    return pr[..., :-1]


def window_attn_prompt(q, k, v, sinks):
    B, L = q.shape[0], q.shape[1]
    nb = L // WINDOW
    qb = q.reshape(B, nb, WINDOW, KV_HEADS, Q_PER_KV, HEAD_DIM).transpose(1, 0, 2, 3, 4, 5)

    def band(t):
        tp = jnp.pad(t, ((0, 0), (WINDOW, 0), (0, 0), (0, 0)))
        prev = tp[:, :L].reshape(B, nb, WINDOW, KV_HEADS, HEAD_DIM)
        cur = t.reshape(B, nb, WINDOW, KV_HEADS, HEAD_DIM)
        return jnp.concatenate([prev, cur], axis=2).transpose(1, 0, 2, 3, 4)

    kb, vb = band(k), band(v)
    a = jnp.arange(WINDOW)[:, None]
    cidx = jnp.arange(2 * WINDOW)[None, :]
    rel = a + WINDOW - cidx
    in_band = (rel >= 0) & (rel < WINDOW)

    def one_block(args):
        qn, kn, vn, n = args
        mask = in_band & ((cidx >= WINDOW) | (n > 0))
        s = jnp.einsum('bqhgd,bkhd->bhgqk', qn.astype(jnp.float32), kn.astype(jnp.float32)) * ATTN_SCALE
        s = jnp.where(mask, s, -jnp.inf)
        pr = sink_softmax(s, sinks)
        return jnp.einsum('bhgqk,bkhd->bqhgd', pr, vn.astype(jnp.float32))

    o = lax.map(one_block, (qb, kb, vb, jnp.arange(nb)))
    return o.transpose(1, 0, 2, 3, 4, 5).reshape(B, L, N_Q_HEADS, HEAD_DIM).astype(q.dtype)


def window_attn_sample(q, k_all, v_all, qpos, kpos, sinks):
    B, T = q.shape[0], q.shape[1]
    qg = q.reshape(B, T, KV_HEADS, Q_PER_KV, HEAD_DIM).astype(jnp.float32)
    rel = qpos[:, None] - kpos[None, :]
    mask = (rel >= 0) & (rel < WINDOW)
    s = jnp.einsum('bqhgd,bkhd->bhgqk', qg, k_all.astype(jnp.float32)) * ATTN_SCALE
    s = jnp.where(mask, s, -jnp.inf)
    pr = sink_softmax(s, sinks)
    o = jnp.einsum('bhgqk,bkhd->bqhgd', pr, v_all.astype(jnp.float32))
    return o.reshape(B, T, N_Q_HEADS, HEAD_DIM).astype(q.dtype)


def trunk(x, p, pos, hgrn_state, k_past, v_past,
          pre_norm_g, post_norm_g, w_in_a, lb_logits, onorm_a, w_out_a,
          kv_norm_g, w_kv, w_in_b, sinks, w_out_b, w_pe, w_pg):
    B, L, _ = x.shape
    lb_all = jnp.cumsum(jax.nn.softmax(lb_logits.astype(jnp.float32), axis=0), axis=0)
    h = x
    states = []
    k_sh = v_sh = None
    for l in range(DEPTH):
        xn = rmsnorm(h, pre_norm_g[l])
        if l < N_A_LAYERS:
            if hgrn_state is None:
                S0 = jnp.zeros((B, HGRN_HEADS, HGRN_HEAD_DIM, HGRN_HEAD_DIM), jnp.float32)
            else:
                S0 = hgrn_state[l].astype(jnp.float32)
            mix, S = hgrn2_mixer(xn, w_in_a[l], lb_all[l], onorm_a[l], w_out_a[l], S0)
            states.append(S)
        else:
            if l == N_A_LAYERS:
                kv = rmsnorm(h, kv_norm_g) @ w_kv
                k_sh, v_sh = jnp.split(kv, 2, axis=-1)
                k_sh = rope_partial(k_sh.reshape(B, L, KV_HEADS, HEAD_DIM), pos)
                v_sh = v_sh.reshape(B, L, KV_HEADS, HEAD_DIM)
            j = l - N_A_LAYERS
            u = xn @ w_in_b[j]
            q, z = jnp.split(u, 2, axis=-1)
            q = rope_partial(q.reshape(B, L, N_Q_HEADS, HEAD_DIM), pos)
            if k_past is None:
                o = window_attn_prompt(q, k_sh, v_sh, sinks[j])
            else:
                wb = k_past.shape[1]
                kpos = jnp.concatenate([PAST_LEN - wb + jnp.arange(wb), pos])
                k_all = jnp.concatenate([k_past.astype(k_sh.dtype), k_sh], axis=1)
                v_all = jnp.concatenate([v_past.astype(v_sh.dtype), v_sh], axis=1)
                o = window_attn_sample(q, k_all, v_all, pos, kpos, sinks[j])
            mix = (o.reshape(B, L, MIX_WIDTH) * jax.nn.silu(z)) @ w_out_b[j]
        h = h + rmsnorm(mix, post_norm_g[l])
        h = h + (p[l] @ w_pe[l]) * jax.nn.sigmoid(h @ w_pg[l])
    return h, jnp.stack(states), k_sh, v_sh


def setup_inputs(seed: int = 0) -> dict:
    key = jax.random.key(seed)
    ks = jax.random.split(key, 20)
    f32 = jnp.float32
    w_buf = min(WINDOW, PAST_LEN)

    def nrm(k, shape, scale=1.0):
        return jax.random.normal(k, shape, f32) * scale

    return {
        "x_prompt": nrm(ks[0], (BATCH, SEQ, D_MODEL)),
        "x_sample": nrm(ks[1], (DEC_BATCH, DEC_SEQ, D_MODEL)),
        "p_prompt": nrm(ks[2], (DEPTH, BATCH, SEQ, PLE_DIM)),
        "p_sample": nrm(ks[3], (DEPTH, DEC_BATCH, DEC_SEQ, PLE_DIM)),
        "state_hgrn": nrm(ks[4], (N_A_LAYERS, DEC_BATCH, HGRN_HEADS, HGRN_HEAD_DIM, HGRN_HEAD_DIM), 0.5),
        "cache_k": nrm(ks[5], (DEC_BATCH, w_buf, KV_HEADS, HEAD_DIM)),
        "cache_v": nrm(ks[6], (DEC_BATCH, w_buf, KV_HEADS, HEAD_DIM)),
        "pre_norm_g": 1.0 + nrm(ks[7], (DEPTH, D_MODEL), 0.02),
        "post_norm_g": 1.0 + nrm(ks[8], (DEPTH, D_MODEL), 0.02),
        "w_in_a": nrm(ks[9], (N_A_LAYERS, D_MODEL, 4 * MIX_WIDTH), D_MODEL ** -0.5),
        "lb_logits": nrm(ks[10], (N_A_LAYERS + 1, MIX_WIDTH)),
        "onorm_a": 1.0 + nrm(ks[11], (N_A_LAYERS, MIX_WIDTH), 0.02),
        "w_out_a": nrm(ks[12], (N_A_LAYERS, MIX_WIDTH, D_MODEL), MIX_WIDTH ** -0.5),
        "kv_norm_g": 1.0 + nrm(ks[13], (D_MODEL,), 0.02),
        "w_kv": nrm(ks[14], (D_MODEL, 2 * KV_HEADS * HEAD_DIM), D_MODEL ** -0.5),
        "w_in_b": nrm(ks[15], (N_B_LAYERS, D_MODEL, 2 * MIX_WIDTH), D_MODEL ** -0.5),
        "sinks": nrm(ks[16], (N_B_LAYERS, N_Q_HEADS), 0.5),
        "w_out_b": nrm(ks[17], (N_B_LAYERS, MIX_WIDTH, D_MODEL), MIX_WIDTH ** -0.5),
        "w_pe": nrm(ks[18], (DEPTH, PLE_DIM, D_MODEL), PLE_DIM ** -0.5),
        "w_pg": nrm(ks[19], (DEPTH, D_MODEL, D_MODEL), D_MODEL ** -0.5),
    }


def reference(x_prompt, x_sample, p_prompt, p_sample, state_hgrn, cache_k, cache_v,
              pre_norm_g, post_norm_g, w_in_a, lb_logits, onorm_a, w_out_a,
              kv_norm_g, w_kv, w_in_b, sinks, w_out_b, w_pe, w_pg):
    L_p = x_prompt.shape[1]
    T_s = x_sample.shape[1]
    pos_p = jnp.arange(L_p)
    pos_s = PAST_LEN + jnp.arange(T_s)
    y_prompt, st_p, k_p, v_p = trunk(
        x_prompt, p_prompt, pos_p, None, None, None,
        pre_norm_g, post_norm_g, w_in_a, lb_logits, onorm_a, w_out_a,
        kv_norm_g, w_kv, w_in_b, sinks, w_out_b, w_pe, w_pg)
    y_sample, st_s, k_s, v_s = trunk(
        x_sample, p_sample, pos_s, state_hgrn, cache_k, cache_v,
        pre_norm_g, post_norm_g, w_in_a, lb_logits, onorm_a, w_out_a,
        kv_norm_g, w_kv, w_in_b, sinks, w_out_b, w_pe, w_pg)
    keep = min(WINDOW, L_p)
    k_prompt_rows = k_p[:, L_p - keep:]
    v_prompt_rows = v_p[:, L_p - keep:]
    return (y_prompt, y_sample, st_p, st_s, k_prompt_rows, v_prompt_rows, k_s, v_s)
```

```python
import numpy as np
from contextlib import ExitStack
import concourse.bass as bass
import concourse.mybir as mybir
from concourse.bass_utils import run_bass_kernel_spmd

F32 = mybir.dt.float32
BF16 = mybir.dt.bfloat16
U32 = mybir.dt.uint32
ALU = mybir.AluOpType
AF = mybir.ActivationFunctionType

COMPUTE = ("pe", "act", "dve")
QUEUES = ("sp", "pool")
ALLENG = COMPUTE + QUEUES


class Buf:
    __slots__ = ("name", "last_w", "readers", "sem", "keep", "excl")

    def __init__(self, name="", keep=False, excl=False):
        self.excl = excl
        self.name = name
        self.last_w = None
        self.readers = {}
        self.sem = None
        self.keep = keep


class Carrier:
    __slots__ = ("cnt", "handle", "q")

    def __init__(self):
        self.cnt = 0
        self.handle = None
        self.q = None


class Op:
    __slots__ = ("eng", "fn", "deps", "signal", "sigval", "is_dma", "carrier", "dval")

    def __init__(self, eng, fn, is_dma=False):
        self.eng = eng
        self.fn = fn
        self.deps = []
        self.signal = False
        self.sigval = 0
        self.is_dma = is_dma
        self.carrier = None
        self.dval = 0


class Prog:
    def __init__(self, nc):
        self.nc = nc
        self.ops = []
        self.es = ExitStack()
        self.carriers = []
        self.free_carriers = {"sp": [], "pool": []}
        self.active_bufs = []
        self.out_ops = []
        self.last = {e: None for e in ALLENG}
        self.bar = None
        self.bar_pending = set()
        self.dma_since_bar = []
        self.nalloc = 0
        self.nuniq = 0

    def sbuf(self, name, shape, dtype, es=None):
        self.nalloc += 1
        return (es or self.es).enter_context(self.nc.sbuf_tensor("s%d_%s" % (self.nalloc, name), list(shape), dtype))

    def psum(self, name, shape, dtype=F32):
        return self.es.enter_context(self.nc.psum_tensor("p_" + name, list(shape), dtype))

    def _adddep(self, op, d):
        if d is op or d is None:
            return
        for x in op.deps:
            if x is d:
                return
        op.deps.append(d)
        d.signal = True

    def _deps(self, op, reads, writes):
        ex = [b for b in reads if b.excl and b not in writes]
        if ex:
            reads = [b for b in reads if not b.excl]
            writes = list(writes) + ex
        for b in reads:
            d = b.last_w
            if d is not None:
                if (not d.is_dma) and (not op.is_dma) and d.eng == op.eng and op.eng == "pe":
                    pass
                else:
                    self._adddep(op, d)
        for b in writes:
            cands = [b.last_w] + list(b.readers.values())
            for d in cands:
                if d is None:
                    continue
                if (not d.is_dma) and (not op.is_dma) and d.eng == op.eng:
                    continue
                self._adddep(op, d)
        if op.eng in self.bar_pending:
            self.bar_pending.discard(op.eng)
            for d in self.bar:
                if d.is_dma or d.eng != op.eng:
                    self._adddep(op, d)
        for b in reads:
            if op.is_dma:
                self.nuniq += 1
                b.readers["dma%d" % self.nuniq] = op
            else:
                b.readers[op.eng] = op
        for b in writes:
            b.last_w = op
            b.readers = {}
        self.last[op.eng] = op

    def op(self, eng, fn, reads=(), writes=()):
        o = Op(eng, fn)
        self._deps(o, reads, writes)
        self.ops.append(o)
        return o

    def dma(self, q, pairs, reads=(), writes=(), dbuf=None, is_out=False):
        if dbuf.sem is None:
            if self.free_carriers[q]:
                dbuf.sem = self.free_carriers[q].pop()
            else:
                dbuf.sem = Carrier()
                dbuf.sem.q = q
                self.carriers.append(dbuf.sem)
            self.active_bufs.append(dbuf)
        c = dbuf.sem
        assert c.q == q, "a DMA buffer must stay on one queue type"
        o = Op(q, pairs, is_dma=True)
        c.cnt += 16 * len(pairs)
        o.carrier = c
        o.dval = c.cnt
        o.signal = True
        self._deps(o, reads, writes)
        self.ops.append(o)
        self.dma_since_bar.append(o)
        if is_out:
            self.out_ops.append(o)
        return o

    def barrier(self):
        ops = [o for o in self.last.values() if o is not None and not o.is_dma]
        latest = {}
        for o in self.dma_since_bar:
            latest[id(o.carrier)] = o
        ops += list(latest.values())
        self.dma_since_bar = []
        self.bar = ops
        self.bar_pending = set(ALLENG)
        keep = []
        for b in self.active_bufs:
            if b.keep:
                keep.append(b)
            else:
                self.free_carriers[b.sem.q].append(b.sem)
                b.sem = None
        self.active_bufs = keep

    def emit(self):
        nc = self.nc
        es = self.es
        sems = {}
        for e in COMPUTE:
            sems[e] = es.enter_context(nc.semaphore("sem_" + e))
        for i, c in enumerate(self.carriers):
            c.handle = es.enter_context(nc.semaphore("dsem%d" % i))
        cnt = {e: 0 for e in COMPUTE}
        for o in self.ops:
            if o.is_dma:
                continue
            if o.signal:
                cnt[o.eng] += 1
                o.sigval = cnt[o.eng]
        by_eng = {e: [] for e in ALLENG}
        for o in self.ops:
            by_eng[o.eng].append(o)
        self.stats = {e: len(v) for e, v in by_eng.items()}
        self.stats["sig"] = dict(cnt)
        self.stats["dma_sems"] = len(self.carriers)
        final_waits = {}
        for o in self.out_ops:
            k = id(o.carrier)
            if k not in final_waits or final_waits[k][1] < o.dval:
                final_waits[k] = (o.carrier.handle, o.dval)

        def stream(ename, e):
            known = {}
            nwait = 0
            for o in by_eng[ename]:
                need = {}
                for d in o.deps:
                    if d.is_dma:
                        s, v = d.carrier.handle, d.dval
                    else:
                        s, v = sems[d.eng], d.sigval
                    k = id(s)
                    if k not in need or need[k][1] < v:
                        need[k] = (s, v)
                for k, (s, v) in need.items():
                    if known.get(k, 0) >= v:
                        continue
                    known[k] = v
                    e.wait_ge(s, v)
                    nwait += 1
                if o.is_dma:
                    for (out_ap, in_ap) in o.fn:
                        e.dma_start(out=out_ap, in_=in_ap).then_inc(o.carrier.handle, 16)
                else:
                    ins = o.fn(e)
                    if o.signal:
                        ins.then_inc(sems[ename], 1)
            if ename == "sp":
                for s, v in final_waits.values():
                    e.wait_ge(s, v)
            self.stats[ename + "_waits"] = nwait

        with nc.Block() as block:
            @block.tensor
            def _(e):
                stream("pe", e)

            @block.scalar
            def _(e):
                stream("act", e)

            @block.vector
            def _(e):
                stream("dve", e)

            @block.gpsimd
            def _(e):
                stream("pool", e)

            @block.sync
            def _(e):
                stream("sp", e)
        es.close()


D = 1024
NMAIN = 2176
NPRE = 1920
NS = 16
TTOT = NMAIN + NS
SBW = 1152
EPS = 1e-6
PAST_LEN = 16384
ROPE_THETA = 500000.0

V_PRE0, V_PRE1, V_POST0, V_POST1, V_KV, V_ON, V_LB0, V_LB1 = 0, 8, 16, 24, 32, 40, 56, 72
NVEC = 88

SBS = [
    dict(kind="P", t0=0, nblk=8, ns=0),
    dict(kind="P", t0=1024, nblk=7, ns=0),
    dict(kind="M", t0=0, nblk=9, ns=0),
    dict(kind="M", t0=1152, nblk=8, ns=NS),
]


def sb_tiles(sb):
    tiles = []
    nb = sb["nblk"]
    j = 0
    while j < nb:
        k = min(4, nb - j)
        tiles.append(dict(c0=j * 128, n=k * 128, blks=list(range(j, j + k)), samp=False))
        j += k
    if sb["ns"]:
        tiles.append(dict(c0=nb * 128, n=sb["ns"], blks=[], samp=True))
    return tiles


def build_program(stop_after=None):
    nc = bass.Bass("TRN2", target_bir_lowering=False)
    P = Prog(nc)

    def din(name, shape, dt=F32):
        return nc.dram_tensor(name, list(shape), dt, kind="ExternalInput").ap()

    def dout(name, shape, dt=F32):
        return nc.dram_tensor(name, list(shape), dt, kind="ExternalOutput").ap()

    xm = din("xm", [NMAIN, D])
    xp = din("xp", [NPRE, D])
    xs = din("xs", [NS, D])
    pm = din("pm", [2, NMAIN, 256])
    psm = din("psm", [2, NS, 256])
    st_in = din("st_in", [NS, 16, 128, 128])
    ck = din("ck", [NS, 128, 256])
    cv = din("cv", [NS, 128, 256])
    w_in_a = din("w_in_a", [D, 8192])
    w_out_a = din("w_out_a", [2048, D])
    w_kv = din("w_kv", [D, 512])
    w_in_b = din("w_in_b", [D, 4096])
    w_out_b = din("w_out_b", [2048, D])
    w_pe = din("w_pe", [2, 256, D])
    w_pg = din("w_pg", [2, D, D])
    vecs_d = din("vecs", [128, NVEC])
    sinks_d = din("sinks", [1, 32])
    ident_d = din("ident", [128, 128])
    mtri_d = din("mtri", [128, 128], U32)
    maskc_d = din("maskc", [128, 128])
    maskp_d = din("maskp", [128, 128])
    maskf_d = din("maskf", [128, 128])
    prot_d = din("prot", [128, 128])
    ropec_d = din("ropec", [128, TTOT])
    ropes_d = din("ropes", [128, TTOT])

    y_d = dout("y", [2048 + NS, D])
    stp_d = dout("st_p", [16, 128, 128])
    sts_d = dout("st_s", [NS, 16, 128, 128])
    kp_d = dout("kp", [128, 256])
    vp_d = dout("vp", [128, 256])
    ks_d = dout("ks", [NS, 256])
    vs_d = dout("vs", [NS, 256])
    dbg_d = dout("dbg", [128, 8, TTOT]) if stop_after else None

    hT = P.sbuf("hT", [128, 8, SBW], F32)
    xnT = P.sbuf("xnT", [128, 8, SBW], BF16)
    ogT = P.sbuf("ogT", [128, 16, SBW], BF16)
    pT = P.sbuf("pT", [128, 2, 2, SBW], BF16)
    Sst = P.sbuf("Sst", [128, 16, 128], F32)
    ident_f = P.sbuf("ident_f", [128, 128], F32)
    ident_b = P.sbuf("ident_b", [128, 128], BF16)
    ones_b = P.sbuf("ones_b", [128, 128], BF16)
    ones_f = P.sbuf("ones_f", [128, 128], F32)
    epsc = P.sbuf("epsc", [128, 1], F32)
    mtri = P.sbuf("mtri", [128, 128], U32)
    maskc = P.sbuf("maskc", [128, 128], BF16)
    maskp = P.sbuf("maskp", [128, 128], BF16)
    maskf = P.sbuf("maskf", [128, 128], BF16)
    prot = P.sbuf("prot", [128, 128], BF16)
    maskc4 = P.sbuf("maskc4", [128, 512], BF16)
    maskp4 = P.sbuf("maskp4", [128, 512], BF16)
    maskf4 = P.sbuf("maskf4", [128, 512], BF16)
    vecs = P.sbuf("vecs", [128, NVEC], F32)
    lbv = P.sbuf("lbv", [128, 16], F32)
    omlv = P.sbuf("omlv", [128, 16], F32)
    nomlv = P.sbuf("nomlv", [128, 16], F32)
    lnomlv = P.sbuf("lnomlv", [128, 16], F32)
    onec = P.sbuf("onec", [128, 1], F32)
    sinkx = P.sbuf("sinkx", [1, 32], F32)
    sinkrow = P.sbuf("sinkrow", [1, 4, 2, 4, 128], BF16)
    kT_halo = P.sbuf("kT_halo", [128, 4, 128], BF16)
    V_halo = P.sbuf("V_halo", [128, 256], BF16)

    MAXT = 3
    hT_b = [Buf("hT%d" % i, keep=True) for i in range(MAXT)]
    xnT_b = [Buf("xnT%d" % i) for i in range(MAXT)]
    pT_b = [Buf("pT%d" % i) for i in range(MAXT)]
    og_b = [[Buf("og%d_%d" % (h, i)) for i in range(MAXT)] for h in range(16)]
    S_b = [Buf("S%d" % h) for h in range(16)]
    const_b = Buf("const", keep=True)
    halo_b = Buf("halo")
    stp_cb = Buf("stp_carrier", keep=True)

    banks = [P.psum("bank%d" % i, [128, 512]) for i in range(8)]
    bankb = [Buf("bank%d" % i, excl=True) for i in range(8)]
    bq = [[bankb[i]] * 4 for i in range(8)]

    def bqs(i, q0=0, q1=4):
        return [bankb[i]]


    def MM(out, lhsT, rhs, start, stop, reads, writes, **kw):
        P.op("pe", lambda e: e.matmul(out, lhsT, rhs, start=start, stop=stop, **kw), reads, writes)

    def ACT(out, in_, func, reads, writes, bias=None, scale=None):
        kw = {}
        if bias is not None:
            kw["bias"] = bias
        if scale is not None:
            kw["scale"] = scale
        P.op("act", lambda e: e.activation(out, in_, func, **kw), reads, writes)

    def ACOPY(out, in_, reads, writes):
        P.op("act", lambda e: e.copy(out, in_), reads, writes)

    def TCOPY(out, in_, reads, writes):
        P.op("dve", lambda e: e.tensor_copy(out, in_), reads, writes)

    def TT(out, in0, in1, op, reads, writes):
        P.op("dve", lambda e: e.tensor_tensor(out, in0, in1, op), reads, writes)

    def TS(out, in0, s1, s2, op0, op1, reads, writes):
        if s2 is None:
            P.op("dve", lambda e: e.tensor_scalar(out, in0, s1, None, op0), reads, writes)
        else:
            P.op("dve", lambda e: e.tensor_scalar(out, in0, s1, s2, op0, op1), reads, writes)

    def STT(out, in0, sc, in1, op0, op1, reads, writes):
        P.op("dve", lambda e: e.scalar_tensor_tensor(out, in0, sc, in1, op0, op1), reads, writes)

    def SCAN(out, d0, d1, reads, writes):
        P.op("dve", lambda e: e.tensor_tensor_scan(out, d0, d1, 0.0, ALU.mult, ALU.add), reads, writes)

    def CPRED(out, mask, data, reads, writes):
        P.op("dve", lambda e: e.copy_predicated(out, mask, data), reads, writes)

    def MEMSET(ap, val, writes):
        P.op("dve", lambda e: e.memset(ap, val), (), writes)

    print("sbuf remaining after persistent:", nc.sbuf_bytes_remaining)

    P.dma("sp", [(ident_f[:], ident_d)], writes=[const_b], dbuf=const_b)
    P.dma("sp", [(mtri[:], mtri_d)], writes=[const_b], dbuf=const_b)
    P.dma("sp", [(vecs[:], vecs_d)], writes=[const_b], dbuf=const_b)
    P.dma("sp", [(sinkx[:], sinks_d)], writes=[const_b], dbuf=const_b)
    cb2 = Buf("const2", keep=True)
    P.dma("pool", [(ident_b[:], ident_d), (maskc[:], maskc_d), (maskp[:], maskp_d), (maskf[:], maskf_d),
                   (prot[:], prot_d)], writes=[cb2], dbuf=cb2)
    cb3 = Buf("const3")
    MEMSET(ones_b[:], 1.0, [cb3])
    MEMSET(ones_f[:], 1.0, [cb3])
    MEMSET(epsc[:], EPS, [cb3])
    MEMSET(Sst[:], 0.0, S_b)
    MEMSET(ogT[:], 0.0, [b for hb in og_b for b in hb])
    MEMSET(kT_halo[:], 0.0, [halo_b])
    MEMSET(V_halo[:], 0.0, [halo_b])
    TT(lbv[:], vecs[:, V_LB0:V_LB0 + 16], vecs[:, V_LB1:V_LB1 + 16], ALU.subtract, [const_b], [cb3])
    ACT(lbv[:], lbv[:], AF.Sigmoid, [cb3], [cb3])
    TS(omlv[:], lbv[:], -1.0, 1.0, ALU.mult, ALU.add, [cb3], [cb3])
    TS(nomlv[:], lbv[:], 1.0, -1.0, ALU.mult, ALU.add, [cb3], [cb3])
    MEMSET(onec[:], 1.0, [cb3])
    ACT(lnomlv[:], omlv[:], AF.Ln, [cb3], [cb3])
    ACT(sinkx[:], sinkx[:], AF.Exp, [const_b], [cb3])
    sx4 = sinkx[:].rearrange("p (kh i par o) -> p kh par i o", kh=4, i=4, par=2, o=1)
    TCOPY(sinkrow[:], sx4.to_broadcast([1, 4, 2, 4, 128]), [cb3], [cb3])
    for m4, m1 in ((maskc4, maskc), (maskp4, maskp), (maskf4, maskf)):
        for i in range(4):
            TCOPY(m4[:, i * 128:(i + 1) * 128], m1[:], [cb2], [cb3])
    CONST = [const_b, cb2, cb3]

    def gcol(base, c):
        return vecs[:, base + c:base + c + 1]

    def stage_load(sb, ph):
        kind = sb["kind"]
        src = xp if kind == "P" else xm
        tiles = sb_tiles(sb)
        NX = 6
        xin = [P.sbuf("xin%d" % i, [128, D], F32, ph) for i in range(NX)]
        xin_b = [Buf("xin%d" % i) for i in range(NX)]
        pin = [P.sbuf("pin%d" % i, [128, 2, 256], F32, ph) for i in range(NX)] if kind == "M" else None
        pin_b = [Buf("pin%d" % i) for i in range(NX)]
        cnt = 0
        ev = 0
        for ti, t in enumerate(tiles):
            c0, n = t["c0"], t["n"]
            if not t["samp"]:
                slots = []
                for j in t["blks"]:
                    s = cnt % NX
                    cnt += 1
                    r0 = sb["t0"] + j * 128
                    P.dma("sp", [(xin[s][:], src[r0:r0 + 128, :])], writes=[xin_b[s]], dbuf=xin_b[s])
                    if kind == "M":
                        P.dma("sp", [(pin[s][:], pm[:, r0:r0 + 128, :].rearrange("l t f -> t l f"))],
                              writes=[pin_b[s]], dbuf=pin_b[s])
                    slots.append(s)
                nb = len(slots)
                for c in range(8):
                    bk = c % 2
                    for jj, s in enumerate(slots):
                        MM(banks[bk][:, jj * 128:(jj + 1) * 128], xin[s][:, c * 128:(c + 1) * 128], ident_f[:], True, True,
                           [xin_b[s]] + CONST, [bq[bk][jj]])
                    ev += 1
                    if ev % 2 == 0:
                        ACOPY(hT[:, c, c0:c0 + n], banks[bk][:, 0:n], bqs(bk, 0, nb), [hT_b[ti]])
                    else:
                        TCOPY(hT[:, c, c0:c0 + n], banks[bk][:, 0:n], bqs(bk, 0, nb), [hT_b[ti]])
                if kind == "M":
                    for l in range(2):
                        for pc in range(2):
                            bk = 2 + (l * 2 + pc) % 2
                            for jj, s in enumerate(slots):
                                MM(banks[bk][:, jj * 128:(jj + 1) * 128], pin[s][:, l, pc * 128:(pc + 1) * 128], ident_f[:], True, True,
                                   [pin_b[s]] + CONST, [bq[bk][jj]])
                            ACOPY(pT[:, l, pc, c0:c0 + n], banks[bk][:, 0:n], bqs(bk, 0, nb), [pT_b[ti]])
            else:
                s = cnt % NX
                cnt += 1
                P.dma("sp", [(xin[s][0:NS, :], xs)], writes=[xin_b[s]], dbuf=xin_b[s])
                P.dma("sp", [(pin[s][0:NS, :, :], psm.rearrange("l t f -> t l f"))], writes=[pin_b[s]], dbuf=pin_b[s])
                for c in range(8):
                    MM(banks[0][:, c * NS:(c + 1) * NS], xin[s][0:NS, c * 128:(c + 1) * 128], ident_f[0:NS, 0:NS], True, True,
                       [xin_b[s]] + CONST, [bq[0][0]])
                ACOPY(hT[:, :, c0:c0 + NS], banks[0][:, 0:8 * NS].rearrange("p (c n) -> p c n", c=8), [bq[0][0]], [hT_b[ti]])
                for l in range(2):
                    for pc in range(2):
                        q = l * 2 + pc
                        MM(banks[1][:, q * NS:(q + 1) * NS], pin[s][0:NS, l, pc * 128:(pc + 1) * 128], ident_f[0:NS, 0:NS], True, True,
                           [pin_b[s]] + CONST, [bq[1][0]])
                ACOPY(pT[:, :, :, c0:c0 + NS], banks[1][:, 0:4 * NS].rearrange("p (l c n) -> p l c n", l=2, c=2), [bq[1][0]], [pT_b[ti]])

    def rms_stats(srcs, n, rt, reads, invd):
        sq, sq_b, lnv, rstd, r_b = rt
        ssb = 7
        for c, s_ap in enumerate(srcs):
            k = c % 2
            ACT(sq[k][:, 0:n], s_ap, AF.Square, reads, [sq_b[k]])
            MM(banks[ssb][:, 0:n], ones_b[:], sq[k][:, 0:n], c == 0, c == len(srcs) - 1, [sq_b[k]] + CONST, bqs(ssb))
        ACT(lnv[:, 0:n], banks[ssb][:, 0:n], AF.Ln, bqs(ssb) + CONST, [r_b], bias=epsc[:, 0:1], scale=invd)
        ACT(rstd[:, 0:n], lnv[:, 0:n], AF.Exp, [r_b], [r_b], scale=-0.5)
        return rstd

    def alloc_rms(ph, tag):
        sq = [P.sbuf("sq%s%d" % (tag, i), [128, 512], BF16, ph) for i in range(2)]
        sq_b = [Buf("sq") for _ in range(2)]
        lnv = P.sbuf("lnv" + tag, [128, 512], F32, ph)
        rstd = P.sbuf("rstd" + tag, [128, 512], F32, ph)
        return (sq, sq_b, lnv, rstd, Buf("rstd"))

    def stage_prenorm(sb, ph, gbase, dst, dst_b):
        rt = alloc_rms(ph, "pn%d" % gbase)
        for ti, t in enumerate(sb_tiles(sb)):
            c0, n = t["c0"], t["n"]
            rstd = rms_stats([hT[:, c, c0:c0 + n] for c in range(8)], n, rt, [hT_b[ti]], 1.0 / D)
            for c in range(8):
                STT(dst[:, c, c0:c0 + n], hT[:, c, c0:c0 + n], gcol(gbase, c), rstd[:, 0:n], ALU.mult, ALU.mult,
                    [hT_b[ti], rt[4]] + CONST, [dst_b[ti]])

    def stage_l0(sb, ph):
        kind = sb["kind"]
        full = kind == "M"
        tiles = sb_tiles(sb)
        nblk = sb["nblk"]
        has_s = sb["ns"] > 0
        wsl = [[P.sbuf("w0_%d_%d" % (s, k), [128, 8, 128], BF16, ph) for k in range(4)] for s in range(2)]
        wsl_b = [[Buf("w0") for k in range(4)] for s in range(2)]
        qd = [P.sbuf("qd%d" % s, [128, SBW], BF16, ph) for s in range(2)] if full else None
        kd = [P.sbuf("kd%d" % s, [128, SBW], BF16, ph) for s in range(2)]
        zs = [P.sbuf("zs%d" % s, [128, SBW], BF16, ph) for s in range(2)] if full else None
        Vtm = [P.sbuf("Vtm%d" % s, [128, 9, 128], BF16, ph) for s in range(2)]
        kdtm = [P.sbuf("kdtm%d" % s, [128, 9, 128], BF16, ph) for s in range(2)]
        hd = [P.sbuf("hd%d" % s, [128, 16], F32, ph) for s in range(2)]
        dec = [P.sbuf("dec%d" % s, [128, 16], F32, ph) for s in range(2)]
        negr = [P.sbuf("negr%d" % s, [128, 16], F32, ph) for s in range(2)]
        rr = [P.sbuf("rr%d" % s, [128, 16], F32, ph) for s in range(2)]
        hp_b = [[Buf("hp%d_%d" % (s, i)) for i in range(MAXT)] for s in range(2)]
        sc_b = [[Buf("sc%d_%d" % (s, i)) for i in range(MAXT)] for s in range(2)]
        tq = [P.sbuf("tq%d" % s, [128, 512], F32, ph) for s in range(2)] if full else None
        tsg = [P.sbuf("tsg%d" % s, [128, 512], F32, ph) for s in range(2)]
        tk = [P.sbuf("tk%d" % s, [128, 512], F32, ph) for s in range(2)]
        tb = [P.sbuf("tb%d" % s, [128, 512], F32, ph) for s in range(2)]
        tE1 = [P.sbuf("tE1%d" % s, [128, 512], BF16, ph) for s in range(2)] if full else None
        tE2 = [P.sbuf("tE2%d" % s, [128, 512], BF16, ph) for s in range(2)]
        tq_b = [Buf("tq") for _ in range(2)]
        tsg_b = [Buf("tsg") for _ in range(2)]
        tk_b = [Buf("tk") for _ in range(2)]
        tb_b = [Buf("tb") for _ in range(2)]
        tE1_b = [Buf("tE1") for _ in range(2)]
        tE2_b = [Buf("tE2") for _ in range(2)]
        ATs_b = [Buf("ATs") for _ in range(2)]
        Sp_b = [Buf("Sp") for _ in range(2)]
        Sd = [P.sbuf("Sd%d" % s, [128, 128], F32, ph) for s in range(2)]
        Sd_b = [Buf("Sd") for _ in range(2)]
        if full:
            ATs = [P.sbuf("ATs%d" % s, [128, 128], BF16, ph) for s in range(2)]
            Sp = [P.sbuf("Sp%d" % s, [128, 128], BF16, ph) for s in range(2)]
            osq = [P.sbuf("osq0", [128, 512], BF16, ph)] * 2
            osq_b = [Buf("osq")] * 2
            lnv0 = P.sbuf("lnv0", [128, 512], F32, ph)
            rstd0 = P.sbuf("rstd0", [128, 512], F32, ph)
            r0_b = Buf("r0")
            t1 = [P.sbuf("t1_0", [128, 512], F32, ph)] * 2
            t1_b = [Buf("t1")] * 2
            for s in range(2):
                MEMSET(ATs[s][:], 0.0, [ATs_b[s]])
        if has_s:
            qss = [P.sbuf("qss%d" % s, [128, NS], F32, ph) for s in range(2)]
            fss = [P.sbuf("fss%d" % s, [128, NS], F32, ph) for s in range(2)]
            kstm = [P.sbuf("kstm%d" % s, [NS, 128], BF16, ph) for s in range(2)]
            vstm = [P.sbuf("vstm%d" % s, [NS, 128], F32, ph) for s in range(2)]
            vblk = [P.sbuf("vblk%d" % s, [NS, NS, 128], BF16, ph) for s in range(2)]
            ss_b = [Buf("ss") for _ in range(2)]
            NSIN = 3
            sin = [P.sbuf("sin%d" % s, [128, 4, 128], F32, ph) for s in range(NSIN)]
            sin_b = [Buf("sin") for _ in range(NSIN)]
            sin_ctr = [0]
        print("  l0 scratch remaining:", nc.sbuf_bytes_remaining)

        def load_w(h):
            s = h % 2
            cols = [h * 128, 2048 + h * 128, 4096 + h * 128, 6144 + h * 128]
            for k in range(4):
                if not full and k in (0, 3):
                    continue
                P.dma("pool", [(wsl[s][k][:], w_in_a[:, cols[k]:cols[k] + 128].rearrange("(kc p) n -> p kc n", p=128))],
                      writes=[wsl_b[s][k]], dbuf=wsl_b[s][k])

        def head_proj(h):
            s = h % 2
            wq, wf, wi, wz = wsl[s]
            wq_b, wf_b, wi_b, wz_b = wsl_b[s]
            lnomc = lnomlv[:, h:h + 1]

            def stageA(ti):
                t = tiles[ti]
                c0, n = t["c0"], t["n"]
                p2 = ti % 2
                xr = [xnT_b[ti]]
                hpb = hp_b[s][ti]
                nb = len(t["blks"])
                for kc in range(8):
                    MM(banks[1][:, 0:n], wf[:, kc, :], xnT[:, kc, c0:c0 + n], kc == 0, kc == 7, [wf_b] + xr, bqs(1))
                yield
                ACT(tsg[p2][:, 0:n], banks[1][:, 0:n], AF.Exp, bqs(1), [tsg_b[p2]])
                ACT(tsg[p2][:, 0:n], tsg[p2][:, 0:n], AF.Ln, [tsg_b[p2]] + CONST, [tsg_b[p2]], bias=onec[:, 0:1], scale=1.0)
                ACT(tk[p2][:, 0:n], tsg[p2][:, 0:n], AF.Exp, [tsg_b[p2]] + CONST, [tk_b[p2]], bias=lnomc, scale=-1.0)
                if not t["samp"]:
                    ACT(tsg[p2][:, 0:n], tk[p2][:, 0:n], AF.Ln, [tk_b[p2]] + CONST, [tsg_b[p2]], bias=onec[:, 0:1], scale=-1.0)
                    j0 = t["blks"][0]
                    for jj, j in enumerate(t["blks"]):
                        for kc in range(8):
                            MM(banks[3][:, jj * 128:(jj + 1) * 128], xnT[:, kc, j * 128:(j + 1) * 128], wi[:, kc, :], kc == 0, kc == 7,
                               [wi_b] + xr, [bq[3][jj]])
                        if jj % 2 == 1 and jj + 1 < nb:
                            yield
                    TCOPY(Vtm[s][:, j0:j0 + nb, :], banks[3][:, 0:n].rearrange("p (j t) -> p j t", t=128), bqs(3, 0, nb), [hpb])
                else:
                    for kc in range(8):
                        MM(banks[3][0:NS, 0:128], xnT[:, kc, c0:c0 + NS], wi[:, kc, :], kc == 0, kc == 7, [wi_b] + xr, [bq[3][0]])
                yield
                if full:
                    for kc in range(8):
                        MM(banks[0][:, 0:n], wq[:, kc, :], xnT[:, kc, c0:c0 + n], kc == 0, kc == 7, [wq_b] + xr, bqs(0))
                    for kc in range(8):
                        MM(banks[2][:, 0:n], wz[:, kc, :], xnT[:, kc, c0:c0 + n], kc == 0, kc == 7, [wz_b] + xr, bqs(2))
                    yield
                    if not t["samp"]:
                        ACT(tq[p2][:, 0:n], banks[0][:, 0:n], AF.Silu, bqs(0), [tq_b[p2]])
                    else:
                        ACT(qss[s][:, 0:n], banks[0][:, 0:n], AF.Silu, bqs(0), [ss_b[s]])
                    ACT(zs[s][:, c0:c0 + n], banks[2][:, 0:n], AF.Silu, bqs(2), [hpb])
                if t["samp"]:
                    TS(fss[s][:, 0:n], tk[p2][:, 0:n], -1.0, 1.0, ALU.mult, ALU.add, [tk_b[p2]], [ss_b[s]])
                    TCOPY(kd[s][:, c0:c0 + n], tk[p2][:, 0:n], [tk_b[p2]], [hpb])
                    MM(banks[4][0:NS, 0:128], kd[s][:, c0:c0 + n], ident_b[:], True, True, [hpb] + CONST, [bq[4][0]])
                    ACOPY(kstm[s][:], banks[4][0:NS, 0:128], [bq[4][0]], [ss_b[s]])
                    ACOPY(vstm[s][:], banks[3][0:NS, 0:128], [bq[3][0]], [ss_b[s]])
                    for sp_ in range(NS):
                        TS(vblk[s][:, sp_, :], vstm[s][:], ident_f[0:NS, sp_:sp_ + 1], None, ALU.mult, None, [ss_b[s]] + CONST, [ss_b[s]])
                yield

            def stageB(ti):
                t = tiles[ti]
                c0, n = t["c0"], t["n"]
                p2 = ti % 2
                hpb = hp_b[s][ti]
                scb = sc_b[s][ti]
                nb = len(t["blks"])
                j0 = t["blks"][0]
                for jj in range(nb):
                    sl = slice(jj * 128, (jj + 1) * 128)
                    SCAN(tb[p2][:, sl], ones_f[:], tsg[p2][:, sl], [tsg_b[p2]] + CONST, [tb_b[p2]])
                tb3 = tb[p2][:, 0:n].rearrange("p (j t) -> p j t", t=128)
                blast = tb3[:, :, 127:128].rearrange("p j o -> p (j o)")
                TS(rr[s][:, j0:j0 + nb], blast, 0.5, None, ALU.mult, None, [tb_b[p2]], [scb])
                TT(tb3, tb3, rr[s][:, j0:j0 + nb].rearrange("p (j o) -> p j o", o=1).to_broadcast([128, nb, 128]), ALU.subtract,
                   [tb_b[p2], scb], [tb_b[p2]])
                yield
                ACT(hd[s][:, j0:j0 + nb], rr[s][:, j0:j0 + nb], AF.Exp, [scb], [scb])
                ACT(dec[s][:, j0:j0 + nb], rr[s][:, j0:j0 + nb], AF.Exp, [scb], [scb], scale=2.0)
                if full:
                    ACT(tE1[p2][:, 0:n], tb[p2][:, 0:n], AF.Exp, [tb_b[p2]], [tE1_b[p2]])
                ACT(tE2[p2][:, 0:n], tb[p2][:, 0:n], AF.Exp, [tb_b[p2]], [tE2_b[p2]], scale=-1.0)
                yield
                TT(kd[s][:, c0:c0 + n], tk[p2][:, 0:n], tE2[p2][:, 0:n], ALU.mult, [tk_b[p2], tE2_b[p2]], [hpb])
                if full:
                    TT(qd[s][:, c0:c0 + n], tq[p2][:, 0:n], tE1[p2][:, 0:n], ALU.mult, [tq_b[p2], tE1_b[p2]], [hpb])
                for jj, j in enumerate(t["blks"]):
                    MM(banks[4][:, jj * 128:(jj + 1) * 128], kd[s][:, j * 128:(j + 1) * 128], ident_b[:], True, True, [hpb] + CONST, [bq[4][jj]])
                TCOPY(kdtm[s][:, j0:j0 + nb, :], banks[4][:, 0:n].rearrange("p (j t) -> p j t", t=128), bqs(4, 0, nb), [hpb])
                yield

            nt = len(tiles)
            for r in range(nt + 1):
                gens = []
                if r >= 1 and not tiles[r - 1]["samp"]:
                    gens.append(stageB(r - 1))
                if r < nt:
                    gens.append(stageA(r))
                while gens:
                    for g in list(gens):
                        try:
                            next(g)
                            yield
                        except StopIteration:
                            gens.remove(g)

        def dphase(h, s, ti, c0, n):
            k = ti % 2
            oc = t1[0]
            ACT(oc[:, 0:n], banks[6][:, 0:n], AF.Identity, bqs(6), [t1_b[0]])
            ACT(osq[k][:, 0:n], oc[:, 0:n], AF.Square, [t1_b[0]], [osq_b[k]])
            MM(banks[6][:, 0:n], ones_b[:], osq[k][:, 0:n], True, True, [osq_b[k]] + CONST, bqs(6))
            ACT(lnv0[:, 0:n], banks[6][:, 0:n], AF.Ln, bqs(6) + CONST, [r0_b], bias=epsc[:, 0:1], scale=1.0 / 128)
            ACT(rstd0[:, 0:n], lnv0[:, 0:n], AF.Exp, [r0_b], [r0_b], scale=-0.5)
            TT(oc[:, 0:n], oc[:, 0:n], rstd0[:, 0:n], ALU.mult, [t1_b[0], r0_b], [t1_b[0]])
            STT(ogT[:, h, c0:c0 + n], oc[:, 0:n], gcol(V_ON, h), zs[s][:, c0:c0 + n], ALU.mult, ALU.mult,
                [t1_b[0], hp_b[s][ti]] + CONST, [og_b[h][ti]])

        def head_scan(h):
            s = h % 2
            Sh = Sst[:, h, :]
            pending = []

            def flush():
                for fn in pending:
                    fn()
                del pending[:]

            blocks = [(ti, jj, j) for ti, t in enumerate(tiles) if not t["samp"] for jj, j in enumerate(t["blks"])]
            slots_ = {}
            NPF = 2

            def load_sin(g4):
                g = sin_ctr[0] % NSIN
                sin_ctr[0] += 1
                slots_[g4] = g
                P.dma("sp", [(sin[g][:], st_in[g4 * 4:g4 * 4 + 4, h].rearrange("s k v -> k s v"))], writes=[sin_b[g]], dbuf=sin_b[g])

            if has_s:
                for g4 in range(NPF):
                    load_sin(g4)

            def pe_front(bi):
                ti, jj, j = blocks[bi]
                a = j % 2
                blk = slice(j * 128, (j + 1) * 128)
                MM(banks[7][:, a * 128:(a + 1) * 128], kdtm[s][:, j, :], Vtm[s][:, j, :], True, True, [hp_b[s][ti]], [bq[7][a]])
                if full:
                    MM(banks[5][:, a * 128:(a + 1) * 128], kd[s][:, blk], qd[s][:, blk], True, True, [hp_b[s][ti]], [bq[5][a]])

            pe_front(0)
            for bi, (ti, jj, j) in enumerate(blocks):
                t = tiles[ti]
                c0, n = t["c0"], t["n"]
                nb = len(t["blks"])
                hpr = [hp_b[s][ti]]
                scr = [sc_b[s][ti]]
                a = j % 2
                blk = slice(j * 128, (j + 1) * 128)
                Uq = banks[7][:, a * 128:(a + 1) * 128]
                Aq = banks[5][:, a * 128:(a + 1) * 128]
                if bi + 1 < len(blocks):
                    pe_front(bi + 1)
                if full:
                    ACT(Sp[a][:], Sh, AF.Identity, [S_b[h]] + scr, [Sp_b[a]], scale=hd[s][:, j:j + 1])
                ACT(Sd[a][:], Uq, AF.Identity, [bq[7][a]] + scr, [Sd_b[a]], scale=hd[s][:, j:j + 1])
                if full:
                    CPRED(ATs[a][:], mtri[:], Aq, [bq[5][a], ATs_b[a]] + CONST, [ATs_b[a]])
                STT(Sh, Sh, dec[s][:, j:j + 1], Sd[a][:], ALU.mult, ALU.add, [S_b[h], Sd_b[a]] + scr, [S_b[h]])
                flush()
                if full:
                    def cons(a=a, j=j, jj=jj, blk=blk, hpr=hpr):
                        oq = banks[6][:, jj * 128:(jj + 1) * 128]
                        MM(oq, Sp[a][:], qd[s][:, blk], True, False, [Sp_b[a]] + hpr, [bq[6][jj]])
                        MM(oq, Vtm[s][:, j, :], ATs[a][:], False, True, [ATs_b[a]] + hpr, [bq[6][jj]])
                    pending.append(cons)
                    if jj == nb - 1:
                        pending.append(lambda ti=ti, c0=c0, n=n: dphase(h, s, ti, c0, n))
                yield
            flush()
            yield
            if has_s:
                P.dma("sp", [(stp_d[h], Sh)], reads=[S_b[h]], dbuf=stp_cb, is_out=True)
                ti = len(tiles) - 1
                c0 = tiles[ti]["c0"]
                for g4 in range(4):
                    s0 = g4 * 4
                    bk = 4 + g4 % 2
                    g = slots_[g4]
                    MM(banks[bk][:, :], kstm[s][:], vblk[s][:, s0:s0 + 4, :], True, True, [ss_b[s]], bqs(bk))
                    for si in range(4):
                        sidx = s0 + si
                        STT(sin[g][:, si, :], sin[g][:, si, :], fss[s][:, sidx:sidx + 1], banks[bk][:, si * 128:(si + 1) * 128], ALU.mult, ALU.add,
                            [sin_b[g], ss_b[s], bq[bk][si]], [sin_b[g]])
                        MM(banks[6][:, sidx:sidx + 1], sin[g][:, si, :], qss[s][:, sidx:sidx + 1], True, True, [sin_b[g], ss_b[s]], [bq[6][0]])
                    P.dma("sp", [(sts_d[s0:s0 + 4, h].rearrange("s k v -> k s v"), sin[g][:])], reads=[sin_b[g]], dbuf=sin_b[g], is_out=True)
                    if g4 + NPF < 4:
                        load_sin(g4 + NPF)
                    yield
                dphase(h, s, ti, c0, NS)
                yield

        def drive(gens):
            RATIO = int(os.environ.get("KB_RATIO", "2"))
            gens = [[g, (RATIO if i == 0 else 1)] for i, g in enumerate(gens) if g is not None]
            while gens:
                for it in list(gens):
                    for _ in range(it[1]):
                        try:
                            next(it[0])
                        except StopIteration:
                            gens.remove(it)
                            break

        import os
        NH = int(os.environ.get("KB_NH", "16"))
        load_w(0)
        prev_scan = None
        for h in range(NH):
            if h + 1 < NH:
                load_w(h + 1)
            drive([head_proj(h), prev_scan])
            prev_scan = head_scan(h)
        drive([prev_scan])

    def stage_l0u(sb, ph):
        kind = sb["kind"]
        full = kind == "M"
        tiles = sb_tiles(sb)
        has_s = sb["ns"] > 0
        NU = 3
        wsl = [[P.sbuf("w0_%d_%d" % (s, k), [128, 8, 128], BF16, ph) for k in range(4)] for s in range(NU)]
        wsl_b = [[Buf("w0") for k in range(4)] for s in range(NU)]
        qd = [P.sbuf("qd%d" % s, [128, 512], BF16, ph) for s in range(NU)] if full else None
        kd = [P.sbuf("kd%d" % s, [128, 512], BF16, ph) for s in range(NU)]
        zs = [P.sbuf("zs%d" % s, [128, 512], BF16, ph) for s in range(NU)] if full else None
        Vtm = [P.sbuf("Vtm%d" % s, [128, 4, 128], BF16, ph) for s in range(NU)]
        kdtm = [P.sbuf("kdtm%d" % s, [128, 4, 128], BF16, ph) for s in range(NU)]
        hd = [P.sbuf("hd%d" % s, [128, 4], F32, ph) for s in range(NU)]
        dec = [P.sbuf("dec%d" % s, [128, 4], F32, ph) for s in range(NU)]
        rr = [P.sbuf("rr%d" % s, [128, 4], F32, ph) for s in range(NU)]
        hp_b = [Buf("hp%d" % s) for s in range(NU)]
        sc_b = [Buf("sc%d" % s) for s in range(NU)]
        tq = [P.sbuf("tq%d" % s, [128, 512], F32, ph) for s in range(2)] if full else None
        tsg = [P.sbuf("tsg%d" % s, [128, 512], F32, ph) for s in range(2)]
        tk = [P.sbuf("tk%d" % s, [128, 512], F32, ph) for s in range(2)]
        tb = [P.sbuf("tb%d" % s, [128, 512], F32, ph) for s in range(2)]
        tE1 = [P.sbuf("tE1%d" % s, [128, 512], BF16, ph) for s in range(2)] if full else None
        tE2 = [P.sbuf("tE2%d" % s, [128, 512], BF16, ph) for s in range(2)]
        tq_b = [Buf("tq") for _ in range(2)]
        tsg_b = [Buf("tsg") for _ in range(2)]
        tk_b = [Buf("tk") for _ in range(2)]
        tb_b = [Buf("tb") for _ in range(2)]
        tE1_b = [Buf("tE1") for _ in range(2)]
        tE2_b = [Buf("tE2") for _ in range(2)]
        ATs_b = [Buf("ATs") for _ in range(2)]
        Sp_b = [Buf("Sp") for _ in range(2)]
        Sd = [P.sbuf("Sd%d" % s, [128, 128], F32, ph) for s in range(2)]
        Sd_b = [Buf("Sd") for _ in range(2)]
        if full:
            ATs = [P.sbuf("ATs%d" % s, [128, 128], BF16, ph) for s in range(2)]
            Sp = [P.sbuf("Sp%d" % s, [128, 128], BF16, ph) for s in range(2)]
            osq = P.sbuf("osq0", [128, 512], BF16, ph)
            osq_b = Buf("osq")
            lnv0 = P.sbuf("lnv0", [128, 512], F32, ph)
            rstd0 = P.sbuf("rstd0", [128, 512], F32, ph)
            r0_b = Buf("r0")
            oc = P.sbuf("oc", [128, 512], F32, ph)
            oc_b = Buf("oc")
            for s in range(2):
                MEMSET(ATs[s][:], 0.0, [ATs_b[s]])
        if has_s:
            qss = [P.sbuf("qss%d" % s, [128, NS], F32, ph) for s in range(NU)]
            fss = [P.sbuf("fss%d" % s, [128, NS], F32, ph) for s in range(NU)]
            kstm = [P.sbuf("kstm%d" % s, [NS, 128], BF16, ph) for s in range(NU)]
            vstm = [P.sbuf("vstm%d" % s, [NS, 128], F32, ph) for s in range(NU)]
            ss_b = [Buf("ss") for _ in range(NU)]
            vblk = P.sbuf("vblk", [NS, NS, 128], BF16, ph)
            vblk_b = Buf("vblk")
            NSIN = 3
            sin = [P.sbuf("sin%d" % s, [128, 4, 128], F32, ph) for s in range(NSIN)]
            sin_b = [Buf("sin") for _ in range(NSIN)]
            sin_ctr = [0]
        print("  l0u scratch remaining:", nc.sbuf_bytes_remaining)

        units = [(ti, h) for ti in range(len(tiles)) for h in range(16)]
        NUN = len(units)

        def load_w(ui):
            ti, h = units[ui]
            s = ui % NU
            cols = [h * 128, 2048 + h * 128, 4096 + h * 128, 6144 + h * 128]
            for k in range(4):
                if not full and k in (0, 3):
                    continue
                P.dma("pool", [(wsl[s][k][:], w_in_a[:, cols[k]:cols[k] + 128].rearrange("(kc p) n -> p kc n", p=128))],
                      writes=[wsl_b[s][k]], dbuf=wsl_b[s][k])

        sin_slots = {}

        def load_sin(h, g4):
            g = sin_ctr[0] % NSIN
            sin_ctr[0] += 1
            sin_slots[(h, g4)] = g
            P.dma("sp", [(sin[g][:], st_in[g4 * 4:g4 * 4 + 4, h].rearrange("s k v -> k s v"))], writes=[sin_b[g]], dbuf=sin_b[g])

        def stageA(ui):
            ti, h = units[ui]
            s = ui % NU
            p2 = ui % 2
            t = tiles[ti]
            c0, n = t["c0"], t["n"]
            nb = len(t["blks"])
            wq, wf, wi, wz = wsl[s]
            wq_b, wf_b, wi_b, wz_b = wsl_b[s]
            lnomc = lnomlv[:, h:h + 1]
            xr = [xnT_b[ti]]
            hpb = hp_b[s]
            for kc in range(8):
                MM(banks[1][:, 0:n], wf[:, kc, :], xnT[:, kc, c0:c0 + n], kc == 0, kc == 7, [wf_b] + xr, bqs(1))
            yield
            ACT(tsg[p2][:, 0:n], banks[1][:, 0:n], AF.Exp, bqs(1), [tsg_b[p2]])
            ACT(tsg[p2][:, 0:n], tsg[p2][:, 0:n], AF.Ln, [tsg_b[p2]] + CONST, [tsg_b[p2]], bias=onec[:, 0:1], scale=1.0)
            ACT(tk[p2][:, 0:n], tsg[p2][:, 0:n], AF.Exp, [tsg_b[p2]] + CONST, [tk_b[p2]], bias=lnomc, scale=-1.0)
            if not t["samp"]:
                ACT(tsg[p2][:, 0:n], tk[p2][:, 0:n], AF.Ln, [tk_b[p2]] + CONST, [tsg_b[p2]], bias=onec[:, 0:1], scale=-1.0)
                for jj, j in enumerate(t["blks"]):
                    for kc in range(8):
                        MM(banks[3][:, jj * 128:(jj + 1) * 128], xnT[:, kc, j * 128:(j + 1) * 128], wi[:, kc, :], kc == 0, kc == 7,
                           [wi_b] + xr, [bq[3][jj]])
                    if jj % 2 == 1 and jj + 1 < nb:
                        yield
                TCOPY(Vtm[s][:, 0:nb, :], banks[3][:, 0:n].rearrange("p (j t) -> p j t", t=128), bqs(3, 0, nb), [hpb])
            else:
                for kc in range(8):
                    MM(banks[3][0:NS, 0:128], xnT[:, kc, c0:c0 + NS], wi[:, kc, :], kc == 0, kc == 7, [wi_b] + xr, [bq[3][0]])
            yield
            if full:
                for kc in range(8):
                    MM(banks[0][:, 0:n], wq[:, kc, :], xnT[:, kc, c0:c0 + n], kc == 0, kc == 7, [wq_b] + xr, bqs(0))
                for kc in range(8):
                    MM(banks[2][:, 0:n], wz[:, kc, :], xnT[:, kc, c0:c0 + n], kc == 0, kc == 7, [wz_b] + xr, bqs(2))
                yield
                if not t["samp"]:
                    ACT(tq[p2][:, 0:n], banks[0][:, 0:n], AF.Silu, bqs(0), [tq_b[p2]])
                else:
                    ACT(qss[s][:, 0:n], banks[0][:, 0:n], AF.Silu, bqs(0), [ss_b[s]])
                ACT(zs[s][:, 0:n], banks[2][:, 0:n], AF.Silu, bqs(2), [hpb])
            if t["samp"]:
                TS(fss[s][:, 0:n], tk[p2][:, 0:n], -1.0, 1.0, ALU.mult, ALU.add, [tk_b[p2]], [ss_b[s]])
                TCOPY(kd[s][:, 0:n], tk[p2][:, 0:n], [tk_b[p2]], [hpb])
                MM(banks[4][0:NS, 0:128], kd[s][:, 0:n], ident_b[:], True, True, [hpb] + CONST, [bq[4][0]])
                ACOPY(kstm[s][:], banks[4][0:NS, 0:128], [bq[4][0]], [ss_b[s]])
                ACOPY(vstm[s][:], banks[3][0:NS, 0:128], [bq[3][0]], [ss_b[s]])
            yield

        def stageB(ui):
            ti, h = units[ui]
            s = ui % NU
            p2 = ui % 2
            t = tiles[ti]
            if t["samp"]:
                return
            n = t["n"]
            nb = len(t["blks"])
            hpb = hp_b[s]
            scb = sc_b[s]
            for jj in range(nb):
                sl = slice(jj * 128, (jj + 1) * 128)
                SCAN(tb[p2][:, sl], ones_f[:], tsg[p2][:, sl], [tsg_b[p2]] + CONST, [tb_b[p2]])
            tb3 = tb[p2][:, 0:n].rearrange("p (j t) -> p j t", t=128)
            blast = tb3[:, :, 127:128].rearrange("p j o -> p (j o)")
            TS(rr[s][:, 0:nb], blast, 0.5, None, ALU.mult, None, [tb_b[p2]], [scb])
            TT(tb3, tb3, rr[s][:, 0:nb].rearrange("p (j o) -> p j o", o=1).to_broadcast([128, nb, 128]), ALU.subtract,
               [tb_b[p2], scb], [tb_b[p2]])
            yield
            ACT(hd[s][:, 0:nb], rr[s][:, 0:nb], AF.Exp, [scb], [scb])
            ACT(dec[s][:, 0:nb], rr[s][:, 0:nb], AF.Exp, [scb], [scb], scale=2.0)
            if full:
                ACT(tE1[p2][:, 0:n], tb[p2][:, 0:n], AF.Exp, [tb_b[p2]], [tE1_b[p2]])
            ACT(tE2[p2][:, 0:n], tb[p2][:, 0:n], AF.Exp, [tb_b[p2]], [tE2_b[p2]], scale=-1.0)
            yield
            TT(kd[s][:, 0:n], tk[p2][:, 0:n], tE2[p2][:, 0:n], ALU.mult, [tk_b[p2], tE2_b[p2]], [hpb])
            if full:
                TT(qd[s][:, 0:n], tq[p2][:, 0:n], tE1[p2][:, 0:n], ALU.mult, [tq_b[p2], tE1_b[p2]], [hpb])
            for jj in range(nb):
                MM(banks[4][:, jj * 128:(jj + 1) * 128], kd[s][:, jj * 128:(jj + 1) * 128], ident_b[:], True, True, [hpb] + CONST, [bq[4][jj]])
            TCOPY(kdtm[s][:, 0:nb, :], banks[4][:, 0:n].rearrange("p (j t) -> p j t", t=128), bqs(4, 0, nb), [hpb])
            yield

        def dphase(h, s, c0, n):
            ACT(oc[:, 0:n], banks[6][:, 0:n], AF.Identity, bqs(6), [oc_b])
            ACT(osq[:, 0:n], oc[:, 0:n], AF.Square, [oc_b], [osq_b])
            MM(banks[6][:, 0:n], ones_b[:], osq[:, 0:n], True, True, [osq_b] + CONST, bqs(6))
            ACT(lnv0[:, 0:n], banks[6][:, 0:n], AF.Ln, bqs(6) + CONST, [r0_b], bias=epsc[:, 0:1], scale=1.0 / 128)
            ACT(rstd0[:, 0:n], lnv0[:, 0:n], AF.Exp, [r0_b], [r0_b], scale=-0.5)
            TT(oc[:, 0:n], oc[:, 0:n], rstd0[:, 0:n], ALU.mult, [oc_b, r0_b], [oc_b])
            STT(ogT[:, h, c0:c0 + n], oc[:, 0:n], gcol(V_ON, h), zs[s][:, 0:n], ALU.mult, ALU.mult,
                [oc_b, hp_b[s]] + CONST, [og_b[h][0], og_b[h][1], og_b[h][2]])

        def scan(ui):
            ti, h = units[ui]
            s = ui % NU
            t = tiles[ti]
            c0, n = t["c0"], t["n"]
            nb = len(t["blks"])
            Sh = Sst[:, h, :]
            hpr = [hp_b[s]]
            scr = [sc_b[s]]
            if not t["samp"]:
                def pe_front(jj):
                    a = jj % 2
                    blk = slice(jj * 128, (jj + 1) * 128)
                    MM(banks[7][:, a * 128:(a + 1) * 128], kdtm[s][:, jj, :], Vtm[s][:, jj, :], True, True, hpr, [bq[7][a]])
                    if full:
                        MM(banks[5][:, a * 128:(a + 1) * 128], kd[s][:, blk], qd[s][:, blk], True, True, hpr, [bq[5][a]])
                pend = []
                pe_front(0)
                for jj in range(nb):
                    a = jj % 2
                    blk = slice(jj * 128, (jj + 1) * 128)
                    Uq = banks[7][:, a * 128:(a + 1) * 128]
                    Aq = banks[5][:, a * 128:(a + 1) * 128]
                    if jj + 1 < nb:
                        pe_front(jj + 1)
                    if full:
                        ACT(Sp[a][:], Sh, AF.Identity, [S_b[h]] + scr, [Sp_b[a]], scale=hd[s][:, jj:jj + 1])
                    ACT(Sd[a][:], Uq, AF.Identity, [bq[7][a]] + scr, [Sd_b[a]], scale=hd[s][:, jj:jj + 1])
                    if full:
                        CPRED(ATs[a][:], mtri[:], Aq, [bq[5][a], ATs_b[a]] + CONST, [ATs_b[a]])
                    STT(Sh, Sh, dec[s][:, jj:jj + 1], Sd[a][:], ALU.mult, ALU.add, [S_b[h], Sd_b[a]] + scr, [S_b[h]])
                    for fn in pend:
                        fn()
                    del pend[:]
                    if full:
                        def cons(a=a, jj=jj, blk=blk):
                            oq = banks[6][:, jj * 128:(jj + 1) * 128]
                            MM(oq, Sp[a][:], qd[s][:, blk], True, False, [Sp_b[a]] + hpr, [bq[6][jj]])
                            MM(oq, Vtm[s][:, jj, :], ATs[a][:], False, True, [ATs_b[a]] + hpr, [bq[6][jj]])
                        pend.append(cons)
                    yield
                for fn in pend:
                    fn()
                if full:
                    dphase(h, s, c0, n)
                yield
                return
            P.dma("sp", [(stp_d[h], Sh)], reads=[S_b[h]], dbuf=stp_cb, is_out=True)
            for g4 in range(2):
                load_sin(h, g4)
            for sp_ in range(NS):
                TS(vblk[:, sp_, :], vstm[s][:], ident_f[0:NS, sp_:sp_ + 1], None, ALU.mult, None, [ss_b[s]] + CONST, [vblk_b])
            yield
            for g4 in range(4):
                s0 = g4 * 4
                bk = 5 if g4 % 2 == 0 else 7
                g = sin_slots[(h, g4)]
                MM(banks[bk][:, :], kstm[s][:], vblk[:, s0:s0 + 4, :], True, True, [ss_b[s], vblk_b], bqs(bk))
                for si in range(4):
                    sidx = s0 + si
                    STT(sin[g][:, si, :], sin[g][:, si, :], fss[s][:, sidx:sidx + 1], banks[bk][:, si * 128:(si + 1) * 128], ALU.mult, ALU.add,
                        [sin_b[g], ss_b[s], bq[bk][si]], [sin_b[g]])
                    MM(banks[6][:, sidx:sidx + 1], sin[g][:, si, :], qss[s][:, sidx:sidx + 1], True, True, [sin_b[g], ss_b[s]], [bq[6][0]])
                P.dma("sp", [(sts_d[s0:s0 + 4, h].rearrange("s k v -> k s v"), sin[g][:])], reads=[sin_b[g]], dbuf=sin_b[g], is_out=True)
                if g4 + 2 < 4:
                    load_sin(h, g4 + 2)
                yield
            dphase(h, s, c0, NS)
            yield

        load_w(0)
        if NUN > 1:
            load_w(1)
        for idx in range(NUN + 2):
            if idx + 2 < NUN:
                load_w(idx + 2)
            gens = []
            if 0 <= idx - 1 < NUN:
                gens.append(stageB(idx - 1))
            if idx < NUN:
                gens.append(stageA(idx))
            if 0 <= idx - 2 < NUN:
                gens.append(scan(idx - 2))
            while gens:
                for g in list(gens):
                    try:
                        next(g)
                    except StopIteration:
                        gens.remove(g)

    def stage_out(sb, ph, l, w_out):
        tiles = sb_tiles(sb)
        mix = P.sbuf("mix", [128, 8, SBW], F32, ph)
        mix_b = [[Buf("mix") for _ in range(MAXT)] for _ in range(8)]
        wo = [P.sbuf("wo%d" % i, [128, 16, 128], BF16, ph) for i in range(2)]
        wo_b = [Buf("wo") for _ in range(2)]
        rt = alloc_rms(ph, "po")
        tmp = [P.sbuf("tmpo%d" % i, [128, 512], F32, ph) for i in range(2)]
        tmp_b = [Buf("tmpo") for _ in range(2)]
        wg = [P.sbuf("wg%d" % i, [128, 8, 128], BF16, ph) for i in range(2)]
        wg_b = [Buf("wg") for _ in range(2)]
        wp = [P.sbuf("wp%d" % i, [128, 2, 128], BF16, ph) for i in range(2)]
        sgt = [P.sbuf("sgt%d" % i, [128, 512], F32, ph) for i in range(2)]
        sgt_b = [Buf("sgt") for _ in range(2)]
        print("  out scratch remaining:", nc.sbuf_bytes_remaining)

        def ldo(dc):
            P.dma("pool", [(wo[dc % 2][:], w_out[:, dc * 128:(dc + 1) * 128].rearrange("(cc p) n -> p cc n", p=128))],
                  writes=[wo_b[dc % 2]], dbuf=wo_b[dc % 2])
        ldo(0)
        k = 0
        for dc in range(8):
            if dc + 1 < 8:
                ldo(dc + 1)
            for ti, t in enumerate(tiles):
                c0, n = t["c0"], t["n"]
                bk = k % 3
                k += 1
                for cc in range(16):
                    MM(banks[bk][:, 0:n], wo[dc % 2][:, cc, :], ogT[:, cc, c0:c0 + n], cc == 0, cc == 15, [wo_b[dc % 2], og_b[cc][ti]], bqs(bk))
                ACOPY(mix[:, dc, c0:c0 + n], banks[bk][:, 0:n], bqs(bk), [mix_b[dc][ti]])
        gb = V_POST0 if l == 0 else V_POST1
        for ti, t in enumerate(tiles):
            c0, n = t["c0"], t["n"]
            rstd = rms_stats([mix[:, c, c0:c0 + n] for c in range(8)], n, rt, [mix_b[c][ti] for c in range(8)], 1.0 / D)
            for c in range(8):
                a = c % 2
                STT(tmp[a][:, 0:n], mix[:, c, c0:c0 + n], gcol(gb, c), rstd[:, 0:n], ALU.mult, ALU.mult, [mix_b[c][ti], rt[4]] + CONST, [tmp_b[a]])
                TT(hT[:, c, c0:c0 + n], hT[:, c, c0:c0 + n], tmp[a][:, 0:n], ALU.add, [tmp_b[a], hT_b[ti]], [hT_b[ti]])
                ACOPY(xnT[:, c, c0:c0 + n], hT[:, c, c0:c0 + n], [hT_b[ti]], [xnT_b[ti]])

        def ldg(dc):
            P.dma("pool", [(wg[dc % 2][:], w_pg[l][:, dc * 128:(dc + 1) * 128].rearrange("(kc p) n -> p kc n", p=128)),
                           (wp[dc % 2][:], w_pe[l][:, dc * 128:(dc + 1) * 128].rearrange("(kc p) n -> p kc n", p=128))],
                  writes=[wg_b[dc % 2]], dbuf=wg_b[dc % 2])
        ldg(0)
        k = 0
        for dc in range(8):
            if dc + 1 < 8:
                ldg(dc + 1)
            for ti, t in enumerate(tiles):
                c0, n = t["c0"], t["n"]
                a = k % 2
                bg = a
                bp = 2 + a
                k += 1
                for kc in range(8):
                    MM(banks[bg][:, 0:n], wg[dc % 2][:, kc, :], xnT[:, kc, c0:c0 + n], kc == 0, kc == 7, [wg_b[dc % 2], xnT_b[ti]], bqs(bg))
                for pc in range(2):
                    MM(banks[bp][:, 0:n], wp[dc % 2][:, pc, :], pT[:, l, pc, c0:c0 + n], pc == 0, pc == 1, [wg_b[dc % 2], pT_b[ti]], bqs(bp))
                ACT(sgt[a][:, 0:n], banks[bg][:, 0:n], AF.Sigmoid, bqs(bg), [sgt_b[a]])
                TT(tmp[a][:, 0:n], banks[bp][:, 0:n], sgt[a][:, 0:n], ALU.mult, bqs(bp) + [sgt_b[a]], [tmp_b[a]])
                TT(hT[:, dc, c0:c0 + n], hT[:, dc, c0:c0 + n], tmp[a][:, 0:n], ALU.add, [tmp_b[a], hT_b[ti]], [hT_b[ti]])

    def stage_l1(sb, sbi, ph):
        tiles = sb_tiles(sb)
        nblk = sb["nblk"]
        has_s = sb["ns"] > 0
        first = sb["t0"] == 0
        NT = nblk * 128 + sb["ns"]
        kT = P.sbuf("kT", [128, 4, 128 + SBW], BF16, ph)
        Vt = P.sbuf("Vt", [128, 10, 256], BF16, ph)
        rC = P.sbuf("rC", [128, SBW], BF16, ph)
        rS = P.sbuf("rS", [128, SBW], BF16, ph)
        kT_b = [Buf("kT%d" % i) for i in range(MAXT + 1)]
        Vt_b = [Buf("Vt%d" % i) for i in range(10)]
        rope_b = Buf("rope")
        rf = P.sbuf("rf", [128, 512], F32, ph)
        rb = P.sbuf("rb", [128, 512], BF16, ph)
        rt1 = P.sbuf("rt1", [128, 512], F32, ph)
        rt2 = P.sbuf("rt2", [128, 512], F32, ph)
        rf_b, rb_b, rt1_b, rt2_b = Buf("rf"), Buf("rb"), Buf("rt1"), Buf("rt2")
        if has_s:
            kfl = P.sbuf("kfl", [128, 4, 128], F32, ph)
            ksf = P.sbuf("ksf", [128, 4, NS], F32, ph)
            kfl_b = Buf("kfl")
            qsa = P.sbuf("qsa", [128, 16, NS], BF16, ph)
            zsa = P.sbuf("zsa", [128, 16, NS], BF16, ph)
            qsa_b = Buf("qsa")
            vsb = P.sbuf("vsb", [NS, 256], BF16, ph)
            ostg = P.sbuf("ostg", [128, 768], F32, ph)
            ostg_b = Buf("ostg")
        g0 = sb["t0"]
        pairs = [(rC[:, 0:nblk * 128], ropec_d[:, g0:g0 + nblk * 128]), (rS[:, 0:nblk * 128], ropes_d[:, g0:g0 + nblk * 128])]
        if has_s:
            pairs += [(rC[:, nblk * 128:NT], ropec_d[:, NMAIN:NMAIN + NS]), (rS[:, nblk * 128:NT], ropes_d[:, NMAIN:NMAIN + NS])]
        P.dma("pool", pairs, writes=[rope_b], dbuf=rope_b)
        ACOPY(kT[:, :, 0:128], kT_halo[:], [halo_b], [kT_b[0]])
        ACOPY(Vt[:, 0, :], V_halo[:], [halo_b], [Vt_b[0]])

        def rope(src_ps, src_bank, dst, c0, n, reads_extra, writes, f32copy=None):
            ACOPY(rf[:, 0:n], src_ps, [bankb[src_bank]], [rf_b])
            ACOPY(rb[:, 0:n], src_ps, [bankb[src_bank]], [rb_b])
            MM(banks[3][:, 0:n], prot[:], rb[:, 0:n], True, True, [rb_b] + CONST, [bankb[3]])
            TT(rt1[:, 0:n], rf[:, 0:n], rC[:, c0:c0 + n], ALU.mult, [rf_b, rope_b], [rt1_b])
            TT(rt2[:, 0:n], banks[3][:, 0:n], rS[:, c0:c0 + n], ALU.mult, [bankb[3], rope_b], [rt2_b])
            TT(dst, rt1[:, 0:n], rt2[:, 0:n], ALU.add, [rt1_b, rt2_b] + reads_extra, writes)
            if f32copy is not None:
                o_ap, lo, hi, wr = f32copy
                TT(o_ap, rt1[:, lo:hi], rt2[:, lo:hi], ALU.add, [rt1_b, rt2_b], wr)

        pa = ExitStack()
        kvn = P.sbuf("kvn", [128, 8, 512], BF16, pa)
        kvn_b = Buf("kvn")
        wk = P.sbuf("wk", [128, 8, 4, 128], BF16, pa)
        wv = P.sbuf("wv", [128, 8, 256], BF16, pa)
        wkv_b = Buf("wkv")
        rt = alloc_rms(pa, "l1")
        vlast = P.sbuf("vlast", [128, 256], F32, pa)
        vlast_b = Buf("vlast")
        print("  l1a scratch remaining:", nc.sbuf_bytes_remaining)
        wkpairs = []
        for kh_ in range(4):
            src_ = w_kv[:, kh_ * 64:(kh_ + 1) * 64].rearrange("(kc p) d -> p kc d", p=128)
            wkpairs += [(wk[:, :, kh_, 0:64], src_), (wk[:, :, kh_, 64:128], src_)]
        wkpairs.append((wv[:], w_kv[:, 256:512].rearrange("(kc p) n -> p kc n", p=128)))
        P.dma("pool", wkpairs, writes=[wkv_b], dbuf=wkv_b)
        for ti, t in enumerate(tiles):
            c0, n = t["c0"], t["n"]
            rstd = rms_stats([hT[:, c, c0:c0 + n] for c in range(8)], n, rt, [hT_b[ti]], 1.0 / D)
            for c in range(8):
                STT(xnT[:, c, c0:c0 + n], hT[:, c, c0:c0 + n], gcol(V_PRE1, c), rstd[:, 0:n], ALU.mult, ALU.mult,
                    [hT_b[ti], rt[4]] + CONST, [xnT_b[ti]])
                STT(kvn[:, c, 0:n], hT[:, c, c0:c0 + n], gcol(V_KV, c), rstd[:, 0:n], ALU.mult, ALU.mult,
                    [hT_b[ti], rt[4]] + CONST, [kvn_b])
            for kh in range(4):
                bk = kh % 2
                for kc in range(8):
                    MM(banks[bk][:, 0:n], wk[:, kc, kh, :], kvn[:, kc, 0:n], kc == 0, kc == 7, [wkv_b, kvn_b], [bankb[bk]])
                f32c = None
                if has_s and t["samp"]:
                    f32c = (ksf[:, kh, :], 0, NS, [kfl_b])
                elif has_s and (nblk - 1) in t["blks"]:
                    lo = (nblk - 1) * 128 - c0
                    f32c = (kfl[:, kh, :], lo, lo + 128, [kfl_b])
                rope(banks[bk][:, 0:n], bk, kT[:, kh, 128 + c0:128 + c0 + n], c0, n, [], [kT_b[1 + ti]], f32c)
            if not t["samp"]:
                for jj, j in enumerate(t["blks"]):
                    bk = 4 + jj % 2
                    for kc in range(8):
                        MM(banks[bk][:, 0:256], kvn[:, kc, jj * 128:(jj + 1) * 128], wv[:, kc, :], kc == 0, kc == 7, [wkv_b, kvn_b], [bankb[bk]])
                    ACOPY(Vt[:, 1 + j, :], banks[bk][:, 0:256], [bankb[bk]], [Vt_b[1 + j]])
                    if has_s and j == nblk - 1:
                        ACOPY(vlast[:], banks[bk][:, 0:256], [bankb[bk]], [vlast_b])
                        P.dma("sp", [(vp_d, vlast[:])], reads=[vlast_b], dbuf=vlast_b, is_out=True)
            else:
                for kc in range(8):
                    MM(banks[4][0:NS, 0:256], kvn[:, kc, 0:NS], wv[:, kc, :], kc == 0, kc == 7, [wkv_b, kvn_b], [bankb[4]])
                ACOPY(ostg[0:NS, 512:768], banks[4][0:NS, 0:256], [bankb[4]], [ostg_b])
        if has_s:
            for kh in range(4):
                MM(banks[5][:, kh * 128:(kh + 1) * 128], kfl[:, kh, :], ident_f[:], True, True, [kfl_b] + CONST, [bankb[5]])
            kp_st = P.sbuf("kp_st", [128, 256], F32, pa)
            kp_b = Buf("kp_st")
            ACOPY(kp_st[:].rearrange("p (kh d) -> p kh d", kh=4), banks[5][:, :].rearrange("p (kh e) -> p kh e", kh=4)[:, :, 0:64], [bankb[5]], [kp_b])
            P.dma("sp", [(kp_d, kp_st[:])], reads=[kp_b], dbuf=kp_b, is_out=True)
            for kh in range(4):
                MM(banks[6][0:NS, kh * 128:(kh + 1) * 128], ksf[:, kh, :], ident_f[:], True, True, [kfl_b] + CONST, [bankb[6]])
            ACOPY(ostg[0:NS, 0:256].rearrange("p (kh d) -> p kh d", kh=4), banks[6][0:NS, :].rearrange("p (kh e) -> p kh e", kh=4)[:, :, 0:64], [bankb[6]], [ostg_b])
            ksd_b = Buf("ksd")
            P.dma("sp", [(ks_d, ostg[0:NS, 0:256]), (vs_d, ostg[0:NS, 512:768])], reads=[ostg_b], writes=[ksd_b], dbuf=ostg_b, is_out=True)
        P.barrier()
        pa.close()
        if os.environ.get("KB_L1", "") == "a":
            return

        wq = P.sbuf("wq", [128, 8, 512], BF16, ph)
        wz = P.sbuf("wz", [128, 8, 512], BF16, ph)
        wq_b, wz_b = Buf("wq"), Buf("wz")
        qg = P.sbuf("qg", [128, 4, SBW], BF16, ph)
        zg = P.sbuf("zg", [128, 4, SBW], BF16, ph)
        qg_b = [Buf("qg%d" % i) for i in range(MAXT)]
        PT = [P.sbuf("PT%d" % i, [128, 2, 512], BF16, ph) for i in range(4)]
        PT_b = [Buf("PT") for _ in range(4)]
        rden = P.sbuf("rden", [128, 512], F32, ph)
        rden_b = Buf("rden")
        at = P.sbuf("at", [128, 512], F32, ph)
        at_b = Buf("at")
        if has_s:
            ckd = [P.sbuf("ckd%d" % i, [128, 4, 2, 64], BF16, ph) for i in range(2)]
            cvt = [P.sbuf("cvt%d" % i, [128, 256], BF16, ph) for i in range(2)]
            cc_b = [Buf("cc") for _ in range(2)]
            KcT = P.sbuf("KcT", [128, 512], BF16, ph)
            KcT_b = Buf("KcT")
            PTs = P.sbuf("PTs", [128, 32], BF16, ph)
            PTs_b = Buf("PTs")
        print("  l1b scratch remaining:", nc.sbuf_bytes_remaining)

        def ldq(kh):
            P.dma("pool", [(wq[:], w_in_b[:, kh * 512:(kh + 1) * 512].rearrange("(kc p) n -> p kc n", p=128))], writes=[wq_b], dbuf=wq_b)
            P.dma("pool", [(wz[:], w_in_b[:, 2048 + kh * 512:2048 + (kh + 1) * 512].rearrange("(kc p) n -> p kc n", p=128))], writes=[wz_b], dbuf=wz_b)

        ldq(0)
        blkcount = 0
        for kh in range(4):
            for ti, t in enumerate(tiles):
                c0, n = t["c0"], t["n"]
                for i in range(4):
                    bk = i % 2
                    for kc in range(8):
                        MM(banks[bk][:, 0:n], wq[:, kc, i * 128:(i + 1) * 128], xnT[:, kc, c0:c0 + n], kc == 0, kc == 7, [wq_b, xnT_b[ti]], [bankb[bk]])
                    f32c = None
                    rope(banks[bk][:, 0:n], bk, qg[:, i, c0:c0 + n], c0, n, [], [qg_b[ti]], None)
                    bz = 4 + i % 2
                    for kc in range(8):
                        MM(banks[bz][:, 0:n], wz[:, kc, i * 128:(i + 1) * 128], xnT[:, kc, c0:c0 + n], kc == 0, kc == 7, [wz_b, xnT_b[ti]], [bankb[bz]])
                    ACT(zg[:, i, c0:c0 + n], banks[bz][:, 0:n], AF.Silu, [bankb[bz]], [qg_b[ti]])
                if t["samp"]:
                    ACOPY(qsa[:, kh * 4:(kh + 1) * 4, :], qg[:, :, c0:c0 + NS], [qg_b[ti]], [qsa_b])
                    ACOPY(zsa[:, kh * 4:(kh + 1) * 4, :], zg[:, :, c0:c0 + NS], [qg_b[ti]], [qsa_b])
            if kh + 1 < 4:
                ldq(kh + 1)
            steps = [(ti, j) for ti, t in enumerate(tiles) for j in t["blks"] if not (first and j == 0)]

            def emit_scores(idx):
                ti, j = steps[idx]
                st_ = idx % 2
                pm_ = maskf4 if (first and j == 1) else maskp4
                prev_cols = slice(j * 128, (j + 1) * 128)
                cur_cols = slice((j + 1) * 128, (j + 2) * 128)
                blk = slice(j * 128, (j + 1) * 128)
                kprev_b = kT_b[0] if j == 0 else kT_b[1 + (j - 1) // 4]
                kcur_b = kT_b[1 + j // 4]
                for par in range(2):
                    pr = slice(par * 64, (par + 1) * 64)
                    b0, b1 = 2 * par, 2 * par + 1
                    pt = PT[st_ * 2 + par]
                    ptb = PT_b[st_ * 2 + par]
                    MM(banks[b0][:, :], kT[pr, kh, prev_cols], qg[pr, :, blk], True, True, [kprev_b, qg_b[ti]], [bankb[b0]])
                    MM(banks[b1][:, :], kT[pr, kh, cur_cols], qg[pr, :, blk], True, True, [kcur_b, qg_b[ti]], [bankb[b1]])
                    ACT(pt[:, 0, :], banks[b0][:, :], AF.Exp, [bankb[b0]], [ptb], scale=0.125)
                    ACT(pt[:, 1, :], banks[b1][:, :], AF.Exp, [bankb[b1]], [ptb], scale=0.125)
                    TT(pt[:, 0, :], pt[:, 0, :], pm_[:], ALU.mult, [ptb] + CONST, [ptb])
                    TT(pt[:, 1, :], pt[:, 1, :], maskc4[:], ALU.mult, [ptb] + CONST, [ptb])

            def emit_pv(idx):
                ti, j = steps[idx]
                st_ = idx % 2
                blk = slice(j * 128, (j + 1) * 128)
                bo = 4 + 2 * (idx % 2)
                bd = bo + 1
                for par in range(2):
                    pr = slice(par * 64, (par + 1) * 64)
                    pt = PT[st_ * 2 + par]
                    ptb = PT_b[st_ * 2 + par]
                    tp = (0, par * 64)
                    MM(banks[bo][pr, :], Vt[:, j, kh * 64:(kh + 1) * 64], pt[:, 0, :], True, False, [Vt_b[j], ptb], [bankb[bo]], tile_position=tp)
                    MM(banks[bo][pr, :], Vt[:, 1 + j, kh * 64:(kh + 1) * 64], pt[:, 1, :], False, True, [Vt_b[1 + j], ptb], [bankb[bo]], tile_position=tp)
                    MM(banks[bd][pr, :], ones_b[:, 0:64], pt[:, 0, :], True, False, [ptb] + CONST, [bankb[bd]], tile_position=tp)
                    MM(banks[bd][pr, :], ones_b[:, 0:64], pt[:, 1, :], False, False, [ptb] + CONST, [bankb[bd]], tile_position=tp)
                    MM(banks[bd][pr, :], ones_b[0:1, 0:64], sinkrow[0:1, kh, par, :, :].rearrange("p i q -> p (i q)"), False, True, CONST, [bankb[bd]], tile_position=tp)
                ACT(rden[:], banks[bd][:, :], AF.Ln, [bankb[bd]], [rden_b])
                ACT(rden[:], rden[:], AF.Exp, [rden_b], [rden_b], scale=-1.0)
                TT(at[:], banks[bo][:, :], rden[:], ALU.mult, [bankb[bo], rden_b], [at_b])
                TT(ogT[:, kh * 4:(kh + 1) * 4, blk], at[:].rearrange("p (i q) -> p i q", i=4), zg[:, :, blk], ALU.mult,
                   [at_b, qg_b[ti]], [og_b[kh * 4 + i][ti] for i in range(4)])

            for idx in range(len(steps) + 1):
                if idx < len(steps):
                    emit_scores(idx)
                if idx >= 1:
                    emit_pv(idx - 1)
        lastj = nblk - 1
        ACOPY(kT_halo[:], kT[:, :, 128 + lastj * 128:128 + (lastj + 1) * 128], [kT_b[1 + lastj // 4]], [halo_b])
        ACOPY(V_halo[:], Vt[:, 1 + lastj, :], [Vt_b[1 + lastj]], [halo_b])
        if has_s and os.environ.get("KB_L1", "") != "b":
            ti = len(tiles) - 1
            c0 = tiles[ti]["c0"]
            for s_ in range(NS):
                a = s_ % 2
                ksrc = ck[s_].rearrange("j (kh d) -> j kh d", kh=4)
                P.dma("pool", [(ckd[a][:, :, 0, :], ksrc), (ckd[a][:, :, 1, :], ksrc), (cvt[a][:], cv[s_])],
                      writes=[cc_b[a]], dbuf=cc_b[a])
                MM(banks[5][0:1, 0:256], ident_f[0:NS, s_:s_ + 1], ostg[0:NS, 0:256], True, True, [ostg_b] + CONST, [bankb[5]])
                MM(banks[5][0:1, 256:512], ident_f[0:NS, s_:s_ + 1], ostg[0:NS, 512:768], True, True, [ostg_b] + CONST, [bankb[5]])
                ACOPY(ckd[a][0:1, :, 0, :], banks[5][0:1, 0:256].rearrange("p (kh d) -> p kh d", kh=4), [bankb[5]], [cc_b[a]])
                ACOPY(ckd[a][0:1, :, 1, :], banks[5][0:1, 0:256].rearrange("p (kh d) -> p kh d", kh=4), [bankb[5]], [cc_b[a]])
                ACOPY(cvt[a][0:1, :], banks[5][0:1, 256:512], [bankb[5]], [cc_b[a]])
                for kh in range(4):
                    MM(banks[0][:, kh * 128:(kh + 1) * 128], ckd[a][:, kh, :, :].rearrange("p c d -> p (c d)"), ident_b[:], True, True, [cc_b[a]] + CONST, [bankb[0]])
                ACOPY(KcT[:], banks[0][:, :], [bankb[0]], [KcT_b])
                for par in range(2):
                    pr = slice(par * 64, (par + 1) * 64)
                    qb_ = 1 if par == 0 else 4
                    for kh in range(4):
                        MM(banks[qb_][:, kh * 4:(kh + 1) * 4], KcT[pr, kh * 128:(kh + 1) * 128], qsa[pr, kh * 4:(kh + 1) * 4, s_], True, True, [KcT_b, qsa_b], [bankb[qb_]])
                    ACT(PTs[:, par * 16:(par + 1) * 16], banks[qb_][:, 0:16], AF.Exp, [bankb[qb_]], [PTs_b], scale=0.125)
                for kh in range(4):
                    for par in range(2):
                        pr = slice(par * 64, (par + 1) * 64)
                        col = par * 16 + kh * 4
                        tp = (0, par * 64)
                        MM(banks[2][pr, kh * 4:(kh + 1) * 4], cvt[a][:, kh * 64:(kh + 1) * 64], PTs[:, col:col + 4], True, True, [cc_b[a], PTs_b], [bankb[2]], tile_position=tp)
                        MM(banks[3][pr, kh * 4:(kh + 1) * 4], ones_b[:, 0:64], PTs[:, col:col + 4], True, False, [PTs_b] + CONST, [bankb[3]], tile_position=tp)
                        MM(banks[3][pr, kh * 4:(kh + 1) * 4], ones_b[0:1, 0:64], sinkrow[0:1, kh, par, :, 0:1].rearrange("p i q -> p (i q)"), False, True, CONST, [bankb[3]], tile_position=tp)
                P.op("dve", (lambda o, i_: (lambda e: e.reciprocal(o, i_)))(rden[:, 0:16], banks[3][:, 0:16]), [bankb[3]], [rden_b])
                TT(at[:, 0:16], banks[2][:, 0:16], rden[:, 0:16], ALU.mult, [bankb[2], rden_b], [at_b])
                TT(ogT[:, :, c0 + s_], at[:, 0:16], zsa[:, :, s_], ALU.mult, [at_b, qsa_b], [og_b[h_][ti] for h_ in range(16)])

    def stage_y(sb, ph):
        tiles = sb_tiles(sb)
        first = sb["t0"] == 0
        yst = [P.sbuf("yst%d" % i, [128, D], F32, ph) for i in range(3)]
        yst_b = [Buf("yst") for _ in range(3)]
        k = 0
        for ti, t in enumerate(tiles):
            c0 = t["c0"]
            if t["samp"]:
                a = k % 3
                k += 1
                for half in range(2):
                    bk = half
                    for cc in range(4):
                        c = half * 4 + cc
                        MM(banks[bk][0:NS, cc * 128:(cc + 1) * 128], hT[:, c, c0:c0 + NS], ident_f[:], True, True, [hT_b[ti]] + CONST, [bankb[bk]])
                    ACOPY(yst[a][0:NS, half * 512:(half + 1) * 512], banks[bk][0:NS, :], [bankb[bk]], [yst_b[a]])
                P.dma("sp", [(y_d[2048:2048 + NS, :], yst[a][0:NS, :])], reads=[yst_b[a]], dbuf=yst_b[a], is_out=True)
                continue
            for j in t["blks"]:
                if first and j == 0:
                    continue
                a = k % 3
                k += 1
                for half in range(2):
                    bk = (2 * k + half) % 4
                    for cc in range(4):
                        c = half * 4 + cc
                        MM(banks[bk][:, cc * 128:(cc + 1) * 128], hT[:, c, j * 128:(j + 1) * 128], ident_f[:], True, True, [hT_b[ti]] + CONST, [bankb[bk]])
                    if half == 0:
                        ACOPY(yst[a][:, 0:512], banks[bk][:, :], [bankb[bk]], [yst_b[a]])
                    else:
                        TCOPY(yst[a][:, 512:1024], banks[bk][:, :], [bankb[bk]], [yst_b[a]])
                row = (j - 1) * 128 if first else (8 + j) * 128
                P.dma("sp", [(y_d[row:row + 128, :], yst[a][:])], reads=[yst_b[a]], dbuf=yst_b[a], is_out=True)

    def dump_h(sb):
        col0 = sb["t0"]
        for ti, t in enumerate(sb_tiles(sb)):
            c0, n = t["c0"], t["n"]
            g0 = (NMAIN if t["samp"] else col0 + c0)
            P.dma("sp", [(dbg_d[:, :, g0:g0 + n], hT[:, :, c0:c0 + n])], reads=[hT_b[ti]], dbuf=hT_b[ti], is_out=True)

    import os
    sel = os.environ.get("KB_SBS")
    for sbi, sb in enumerate(SBS):
        if sel is not None and str(sbi) not in sel.split(","):
            continue
        if sb["kind"] == "P" and stop_after in ("L0nopre", "LOAD"):
            continue
        ph = ExitStack()
        stage_load(sb, ph)
        stage_prenorm(sb, ph, V_PRE0, xnT, xnT_b)
        P.barrier()
        ph.close()
        if stop_after == "LOAD":
            dump_h(sb)
            P.barrier()
            continue
        ph = ExitStack()
        if os.environ.get("KB_L0U", "1") == "1":
            stage_l0u(sb, ph)
        else:
            stage_l0(sb, ph)
        P.barrier()
        ph.close()
        if sb["kind"] == "P":
            continue
        if os.environ.get("KB_OUT", "1") == "1":
            ph = ExitStack()
            stage_out(sb, ph, 0, w_out_a)
            P.barrier()
            ph.close()
        if stop_after in ("L0", "L0nopre"):
            dump_h(sb)
            P.barrier()
            continue
        ph = ExitStack()
        stage_l1(sb, sbi, ph)
        P.barrier()
        ph.close()
        if os.environ.get("KB_OUT1", "1") == "1":
            ph = ExitStack()
            stage_out(sb, ph, 1, w_out_b)
            P.barrier()
            ph.close()
        if stop_after == "L1":
            dump_h(sb)
            P.barrier()
        if os.environ.get("KB_Y", "1") == "1":
            ph = ExitStack()
            stage_y(sb, ph)
            P.barrier()
            ph.close()

    P.emit()
    print("stats:", P.stats)
    return nc


def _consts(half):
    ident = np.eye(128, dtype=np.float32)
    s = np.arange(128)[:, None]
    t = np.arange(128)[None, :]
    mtri = (t >= s).astype(np.uint32)
    maskc = (s <= t).astype(np.float32)
    maskp = (s > t).astype(np.float32)
    maskf = maskp.copy() if half == 1 else np.zeros((128, 128), np.float32)
    prot = np.zeros((128, 128), np.float32)
    for base in (0, 64):
        for i in range(8):
            prot[base + i + 8, base + i] = 1.0
            prot[base + i, base + i + 8] = 1.0
    pos = np.concatenate([half * 2048 - 128 + np.arange(NMAIN), np.full(NS, PAST_LEN)]).astype(np.float32)
    inv = (ROPE_THETA ** (-np.arange(0, 16, 2, dtype=np.float32) / 16)).astype(np.float32)
    ang = pos[None, :] * inv[:, None]
    cos = np.cos(ang).astype(np.float32)
    sin = np.sin(ang).astype(np.float32)
    ropec = np.ones((128, TTOT), np.float32)
    ropes = np.zeros((128, TTOT), np.float32)
    for base in (0, 64):
        ropec[base:base + 8] = cos
        ropec[base + 8:base + 16] = cos
        ropes[base:base + 8] = -sin
        ropes[base + 8:base + 16] = sin
    return dict(ident=ident, mtri=mtri, maskc=maskc, maskp=maskp, maskf=maskf, prot=prot, ropec=ropec, ropes=ropes)


def _col(v):
    v = np.asarray(v, np.float32).reshape(-1, 128)
    return np.ascontiguousarray(v.T)


def make_in_maps(inp):
    f = lambda a: np.ascontiguousarray(np.asarray(a, dtype=np.float32))
    xpr, xsm = f(inp["x_prompt"]), f(inp["x_sample"])
    ppr, psa = f(inp["p_prompt"]), f(inp["p_sample"])
    st, ck, cv = f(inp["state_hgrn"]), f(inp["cache_k"]), f(inp["cache_v"])
    vecs = np.concatenate([
        _col(inp["pre_norm_g"][0]), _col(inp["pre_norm_g"][1]), _col(inp["post_norm_g"][0]), _col(inp["post_norm_g"][1]),
        _col(inp["kv_norm_g"]), _col(inp["onorm_a"][0]), _col(inp["lb_logits"][0]), _col(inp["lb_logits"][1])], axis=1)
    assert vecs.shape == (128, NVEC)
    shared = dict(
        w_in_a=f(inp["w_in_a"][0]), w_out_a=f(inp["w_out_a"][0]), w_kv=f(inp["w_kv"]), w_in_b=f(inp["w_in_b"][0]),
        w_out_b=f(inp["w_out_b"][0]), w_pe=f(inp["w_pe"]), w_pg=f(inp["w_pg"]), vecs=np.ascontiguousarray(vecs),
        sinks=f(inp["sinks"]).reshape(1, 32))
    maps = []
    for c in range(8):
        b, half = c // 2, c % 2
        xm = np.zeros((NMAIN, D), np.float32)
        xp = np.zeros((NPRE, D), np.float32)
        pm = np.zeros((2, NMAIN, 256), np.float32)
        if half == 0:
            xm[128:] = xpr[b, 0:2048]
            pm[:, 128:] = ppr[:, b, 0:2048]
        else:
            xm[:] = xpr[b, 1920:4096]
            pm[:] = ppr[:, b, 1920:4096]
            xp[:] = xpr[b, 0:1920]
        m = dict(shared)
        m.update(_consts(half))
        m.update(xm=xm, xp=xp, xs=np.ascontiguousarray(xsm[c * NS:(c + 1) * NS, 0]), pm=pm,
                 psm=np.ascontiguousarray(psa[:, c * NS:(c + 1) * NS, 0]),
                 st_in=np.ascontiguousarray(st[0, c * NS:(c + 1) * NS]),
                 ck=np.ascontiguousarray(ck[c * NS:(c + 1) * NS].reshape(NS, 128, 256)),
                 cv=np.ascontiguousarray(cv[c * NS:(c + 1) * NS].reshape(NS, 128, 256)))
        maps.append(m)
    return maps


def assemble(results):
    y_p = np.zeros((4, 4096, D), np.float32)
    y_s = np.zeros((128, 1, D), np.float32)
    st_p = np.zeros((1, 4, 16, 128, 128), np.float32)
    st_s = np.zeros((1, 128, 16, 128, 128), np.float32)
    k_p = np.zeros((4, 128, 4, 64), np.float32)
    v_p = np.zeros((4, 128, 4, 64), np.float32)
    k_s = np.zeros((128, 1, 4, 64), np.float32)
    v_s = np.zeros((128, 1, 4, 64), np.float32)
    for c in range(8):
        r = results[c]
        b, half = c // 2, c % 2
        y_p[b, half * 2048:(half + 1) * 2048] = r["y"][0:2048]
        y_s[c * NS:(c + 1) * NS, 0] = r["y"][2048:2048 + NS]
        st_s[0, c * NS:(c + 1) * NS] = r["st_s"]
        k_s[c * NS:(c + 1) * NS, 0] = r["ks"].reshape(NS, 4, 64)
        v_s[c * NS:(c + 1) * NS, 0] = r["vs"].reshape(NS, 4, 64)
        if half == 1:
            st_p[0, b] = r["st_p"]
            k_p[b] = r["kp"].reshape(128, 4, 64)
            v_p[b] = r["vp"].reshape(128, 4, 64)
    return (y_p, y_s, st_p, st_s, k_p, v_p, k_s, v_s)


def kernel(**inputs):
    nc = build_program()
    in_maps = make_in_maps(inputs)
    res = run_bass_kernel_spmd(nc, in_maps, core_ids=list(range(8)))
    return assemble(res.results)
```

```python
import numpy as np
from contextlib import ExitStack
import concourse.bass as bass
import concourse.mybir as mybir
from concourse.bass_utils import run_bass_kernel_spmd

F32 = mybir.dt.float32
BF16 = mybir.dt.bfloat16
U32 = mybir.dt.uint32
ALU = mybir.AluOpType
AF = mybir.ActivationFunctionType

COMPUTE = ("pe", "act", "dve")
QUEUES = ("sp", "pool")
ALLENG = COMPUTE + QUEUES


class Buf:
    __slots__ = ("name", "last_w", "readers", "sem", "keep", "excl")

    def __init__(self, name="", keep=False, excl=False):
        self.excl = excl
        self.name = name
        self.last_w = None
        self.readers = {}
        self.sem = None
        self.keep = keep


class Carrier:
    __slots__ = ("cnt", "handle", "q")

    def __init__(self):
        self.cnt = 0
        self.handle = None
        self.q = None


class Op:
    __slots__ = ("eng", "fn", "deps", "signal", "sigval", "is_dma", "carrier", "dval")

    def __init__(self, eng, fn, is_dma=False):
        self.eng = eng
        self.fn = fn
        self.deps = []
        self.signal = False
        self.sigval = 0
        self.is_dma = is_dma
        self.carrier = None
        self.dval = 0


class Prog:
    def __init__(self, nc):
        self.nc = nc
        self.ops = []
        self.es = ExitStack()
        self.carriers = []
        self.free_carriers = {"sp": [], "pool": []}
        self.active_bufs = []
        self.out_ops = []
        self.last = {e: None for e in ALLENG}
        self.bar = None
        self.bar_pending = set()
        self.dma_since_bar = []
        self.nalloc = 0
        self.nuniq = 0

    def sbuf(self, name, shape, dtype, es=None):
        self.nalloc += 1
        return (es or self.es).enter_context(self.nc.sbuf_tensor("s%d_%s" % (self.nalloc, name), list(shape), dtype))

    def psum(self, name, shape, dtype=F32):
        return self.es.enter_context(self.nc.psum_tensor("p_" + name, list(shape), dtype))

    def _adddep(self, op, d):
        if d is op or d is None:
            return
        for x in op.deps:
            if x is d:
                return
        op.deps.append(d)
        d.signal = True

    def _deps(self, op, reads, writes):
        ex = [b for b in reads if b.excl and b not in writes]
        if ex:
            reads = [b for b in reads if not b.excl]
            writes = list(writes) + ex
        for b in reads:
            d = b.last_w
            if d is not None:
                if (not d.is_dma) and (not op.is_dma) and d.eng == op.eng and op.eng == "pe":
                    pass
                else:
                    self._adddep(op, d)
        for b in writes:
            cands = [b.last_w] + list(b.readers.values())
            for d in cands:
                if d is None:
                    continue
                if (not d.is_dma) and (not op.is_dma) and d.eng == op.eng:
                    continue
                self._adddep(op, d)
        if op.eng in self.bar_pending:
            self.bar_pending.discard(op.eng)
            for d in self.bar:
                if d.is_dma or d.eng != op.eng:
                    self._adddep(op, d)
        for b in reads:
            if op.is_dma:
                self.nuniq += 1
                b.readers["dma%d" % self.nuniq] = op
            else:
                b.readers[op.eng] = op
        for b in writes:
            b.last_w = op
            b.readers = {}
        self.last[op.eng] = op

    def op(self, eng, fn, reads=(), writes=()):
        o = Op(eng, fn)
        self._deps(o, reads, writes)
        self.ops.append(o)
        return o

    def dma(self, q, pairs, reads=(), writes=(), dbuf=None, is_out=False):
        if dbuf.sem is None:
            if self.free_carriers[q]:
                dbuf.sem = self.free_carriers[q].pop()
            else:
                dbuf.sem = Carrier()
                dbuf.sem.q = q
                self.carriers.append(dbuf.sem)
            self.active_bufs.append(dbuf)
        c = dbuf.sem
        assert c.q == q, "a DMA buffer must stay on one queue type"
        o = Op(q, pairs, is_dma=True)
        c.cnt += 16 * len(pairs)
        o.carrier = c
        o.dval = c.cnt
        o.signal = True
        self._deps(o, reads, writes)
        self.ops.append(o)
        self.dma_since_bar.append(o)
        if is_out:
            self.out_ops.append(o)
        return o

    def barrier(self):
        ops = [o for o in self.last.values() if o is not None and not o.is_dma]
        latest = {}
        for o in self.dma_since_bar:
            latest[id(o.carrier)] = o
        ops += list(latest.values())
        self.dma_since_bar = []
        self.bar = ops
        self.bar_pending = set(ALLENG)
        keep = []
        for b in self.active_bufs:
            if b.keep:
                keep.append(b)
            else:
                self.free_carriers[b.sem.q].append(b.sem)
                b.sem = None
        self.active_bufs = keep

    def emit(self):
        nc = self.nc
        es = self.es
        sems = {}
        for e in COMPUTE:
            sems[e] = es.enter_context(nc.semaphore("sem_" + e))
        for i, c in enumerate(self.carriers):
            c.handle = es.enter_context(nc.semaphore("dsem%d" % i))
        cnt = {e: 0 for e in COMPUTE}
        for o in self.ops:
            if o.is_dma:
                continue
            if o.signal:
                cnt[o.eng] += 1
                o.sigval = cnt[o.eng]
        by_eng = {e: [] for e in ALLENG}
        for o in self.ops:
            by_eng[o.eng].append(o)
        self.stats = {e: len(v) for e, v in by_eng.items()}
        self.stats["sig"] = dict(cnt)
        self.stats["dma_sems"] = len(self.carriers)
        final_waits = {}
        for o in self.out_ops:
            k = id(o.carrier)
            if k not in final_waits or final_waits[k][1] < o.dval:
                final_waits[k] = (o.carrier.handle, o.dval)

        def stream(ename, e):
            known = {}
            nwait = 0
            for o in by_eng[ename]:
                need = {}
                for d in o.deps:
                    if d.is_dma:
                        s, v = d.carrier.handle, d.dval
                    else:
                        s, v = sems[d.eng], d.sigval
                    k = id(s)
                    if k not in need or need[k][1] < v:
                        need[k] = (s, v)
                for k, (s, v) in need.items():
                    if known.get(k, 0) >= v:
                        continue
                    known[k] = v
                    e.wait_ge(s, v)
                    nwait += 1
                if o.is_dma:
                    for (out_ap, in_ap) in o.fn:
                        e.dma_start(out=out_ap, in_=in_ap).then_inc(o.carrier.handle, 16)
                else:
                    ins = o.fn(e)
                    if o.signal:
                        ins.then_inc(sems[ename], 1)
            if ename == "sp":
                for s, v in final_waits.values():
                    e.wait_ge(s, v)
            self.stats[ename + "_waits"] = nwait

        with nc.Block() as block:
            @block.tensor
            def _(e):
                stream("pe", e)

            @block.scalar
            def _(e):
                stream("act", e)

            @block.vector
            def _(e):
                stream("dve", e)

            @block.gpsimd
            def _(e):
                stream("pool", e)

            @block.sync
            def _(e):
                stream("sp", e)
        es.close()


D = 1024
NMAIN = 2176
NPRE = 1920
NS = 16
TTOT = NMAIN + NS
SBW = 1152
EPS = 1e-6
PAST_LEN = 16384
ROPE_THETA = 500000.0

V_PRE0, V_PRE1, V_POST0, V_POST1, V_KV, V_ON, V_LB0, V_LB1 = 0, 8, 16, 24, 32, 40, 56, 72
NVEC = 88

SBS = [
    dict(kind="P", t0=0, nblk=8, ns=0),
    dict(kind="P", t0=1024, nblk=7, ns=0),
    dict(kind="M", t0=0, nblk=9, ns=0),
    dict(kind="M", t0=1152, nblk=8, ns=NS),
]


def sb_tiles(sb):
    tiles = []
    nb = sb["nblk"]
    j = 0
    while j < nb:
        k = min(4, nb - j)
        tiles.append(dict(c0=j * 128, n=k * 128, blks=list(range(j, j + k)), samp=False))
        j += k
    if sb["ns"]:
        tiles.append(dict(c0=nb * 128, n=sb["ns"], blks=[], samp=True))
    return tiles


def build_program(stop_after=None):
    nc = bass.Bass("TRN2", target_bir_lowering=False)
    P = Prog(nc)

    def din(name, shape, dt=F32):
        return nc.dram_tensor(name, list(shape), dt, kind="ExternalInput").ap()

    def dout(name, shape, dt=F32):
        return nc.dram_tensor(name, list(shape), dt, kind="ExternalOutput").ap()

    xm = din("xm", [NMAIN, D])
    xp = din("xp", [NPRE, D])
    xs = din("xs", [NS, D])
    pm = din("pm", [2, NMAIN, 256])
    psm = din("psm", [2, NS, 256])
    st_in = din("st_in", [NS, 16, 128, 128])
    ck = din("ck", [NS, 128, 256])
    cv = din("cv", [NS, 128, 256])
    w_in_a = din("w_in_a", [D, 8192])
    w_out_a = din("w_out_a", [2048, D])
    w_kv = din("w_kv", [D, 512])
    w_in_b = din("w_in_b", [D, 4096])
    w_out_b = din("w_out_b", [2048, D])
    w_pe = din("w_pe", [2, 256, D])
    w_pg = din("w_pg", [2, D, D])
    vecs_d = din("vecs", [128, NVEC])
    sinks_d = din("sinks", [1, 32])
    ident_d = din("ident", [128, 128])
    mtri_d = din("mtri", [128, 128], U32)
    maskc_d = din("maskc", [128, 128])
    maskp_d = din("maskp", [128, 128])
    maskf_d = din("maskf", [128, 128])
    prot_d = din("prot", [128, 128])
    ropec_d = din("ropec", [128, TTOT])
    ropes_d = din("ropes", [128, TTOT])

    y_d = dout("y", [2048 + NS, D])
    stp_d = dout("st_p", [16, 128, 128])
    sts_d = dout("st_s", [NS, 16, 128, 128])
    kp_d = dout("kp", [128, 256])
    vp_d = dout("vp", [128, 256])
    ks_d = dout("ks", [NS, 256])
    vs_d = dout("vs", [NS, 256])
    dbg_d = dout("dbg", [128, 8, TTOT]) if stop_after else None

    hT = P.sbuf("hT", [128, 8, SBW], F32)
    xnT = P.sbuf("xnT", [128, 8, SBW], BF16)
    ogT = P.sbuf("ogT", [128, 16, SBW], BF16)
    pT = P.sbuf("pT", [128, 2, 2, SBW], BF16)
    Sst = P.sbuf("Sst", [128, 16, 128], F32)
    ident_f = P.sbuf("ident_f", [128, 128], F32)
    ident_b = P.sbuf("ident_b", [128, 128], BF16)
    ones_b = P.sbuf("ones_b", [128, 128], BF16)
    ones_f = P.sbuf("ones_f", [128, 128], F32)
    epsc = P.sbuf("epsc", [128, 1], F32)
    mtri = P.sbuf("mtri", [128, 128], U32)
    maskc = P.sbuf("maskc", [128, 128], BF16)
    maskp = P.sbuf("maskp", [128, 128], BF16)
    maskf = P.sbuf("maskf", [128, 128], BF16)
    prot = P.sbuf("prot", [128, 128], BF16)
    maskc4 = P.sbuf("maskc4", [128, 512], BF16)
    maskp4 = P.sbuf("maskp4", [128, 512], BF16)
    maskf4 = P.sbuf("maskf4", [128, 512], BF16)
    vecs = P.sbuf("vecs", [128, NVEC], F32)
    lbv = P.sbuf("lbv", [128, 16], F32)
    omlv = P.sbuf("omlv", [128, 16], F32)
    nomlv = P.sbuf("nomlv", [128, 16], F32)
    lnomlv = P.sbuf("lnomlv", [128, 16], F32)
    onec = P.sbuf("onec", [128, 1], F32)
    sinkx = P.sbuf("sinkx", [1, 32], F32)
    sinkrow = P.sbuf("sinkrow", [1, 4, 2, 4, 128], BF16)
    kT_halo = P.sbuf("kT_halo", [128, 4, 128], BF16)
    V_halo = P.sbuf("V_halo", [128, 256], BF16)

    MAXT = 3
    hT_b = [Buf("hT%d" % i, keep=True) for i in range(MAXT)]
    xnT_b = [Buf("xnT%d" % i) for i in range(MAXT)]
    pT_b = [Buf("pT%d" % i) for i in range(MAXT)]
    og_b = [[Buf("og%d_%d" % (h, i)) for i in range(MAXT)] for h in range(16)]
    S_b = [Buf("S%d" % h) for h in range(16)]
    const_b = Buf("const", keep=True)
    halo_b = Buf("halo")
    stp_cb = Buf("stp_carrier", keep=True)

    banks = [P.psum("bank%d" % i, [128, 512]) for i in range(8)]
    bankb = [Buf("bank%d" % i, excl=True) for i in range(8)]
    bq = [[bankb[i]] * 4 for i in range(8)]

    def bqs(i, q0=0, q1=4):
        return [bankb[i]]


    def MM(out, lhsT, rhs, start, stop, reads, writes, **kw):
        P.op("pe", lambda e: e.matmul(out, lhsT, rhs, start=start, stop=stop, **kw), reads, writes)

    def ACT(out, in_, func, reads, writes, bias=None, scale=None):
        kw = {}
        if bias is not None:
            kw["bias"] = bias
        if scale is not None:
            kw["scale"] = scale
        P.op("act", lambda e: e.activation(out, in_, func, **kw), reads, writes)

    def ACOPY(out, in_, reads, writes):
        P.op("act", lambda e: e.copy(out, in_), reads, writes)

    def TCOPY(out, in_, reads, writes):
        P.op("dve", lambda e: e.tensor_copy(out, in_), reads, writes)

    def TT(out, in0, in1, op, reads, writes):
        P.op("dve", lambda e: e.tensor_tensor(out, in0, in1, op), reads, writes)

    def TS(out, in0, s1, s2, op0, op1, reads, writes):
        if s2 is None:
            P.op("dve", lambda e: e.tensor_scalar(out, in0, s1, None, op0), reads, writes)
        else:
            P.op("dve", lambda e: e.tensor_scalar(out, in0, s1, s2, op0, op1), reads, writes)

    def STT(out, in0, sc, in1, op0, op1, reads, writes):
        P.op("dve", lambda e: e.scalar_tensor_tensor(out, in0, sc, in1, op0, op1), reads, writes)

    def SCAN(out, d0, d1, reads, writes):
        P.op("dve", lambda e: e.tensor_tensor_scan(out, d0, d1, 0.0, ALU.mult, ALU.add), reads, writes)

    def CPRED(out, mask, data, reads, writes):
        P.op("dve", lambda e: e.copy_predicated(out, mask, data), reads, writes)

    def MEMSET(ap, val, writes):
        P.op("dve", lambda e: e.memset(ap, val), (), writes)

    print("sbuf remaining after persistent:", nc.sbuf_bytes_remaining)

    P.dma("sp", [(ident_f[:], ident_d)], writes=[const_b], dbuf=const_b)
    P.dma("sp", [(mtri[:], mtri_d)], writes=[const_b], dbuf=const_b)
    P.dma("sp", [(vecs[:], vecs_d)], writes=[const_b], dbuf=const_b)
    P.dma("sp", [(sinkx[:], sinks_d)], writes=[const_b], dbuf=const_b)
    cb2 = Buf("const2", keep=True)
    P.dma("pool", [(ident_b[:], ident_d), (maskc[:], maskc_d), (maskp[:], maskp_d), (maskf[:], maskf_d),
                   (prot[:], prot_d)], writes=[cb2], dbuf=cb2)
    cb3 = Buf("const3")
    MEMSET(ones_b[:], 1.0, [cb3])
    MEMSET(ones_f[:], 1.0, [cb3])
    MEMSET(epsc[:], EPS, [cb3])
    MEMSET(Sst[:], 0.0, S_b)
    MEMSET(ogT[:], 0.0, [b for hb in og_b for b in hb])
    MEMSET(kT_halo[:], 0.0, [halo_b])
    MEMSET(V_halo[:], 0.0, [halo_b])
    TT(lbv[:], vecs[:, V_LB0:V_LB0 + 16], vecs[:, V_LB1:V_LB1 + 16], ALU.subtract, [const_b], [cb3])
    ACT(lbv[:], lbv[:], AF.Sigmoid, [cb3], [cb3])
    TS(omlv[:], lbv[:], -1.0, 1.0, ALU.mult, ALU.add, [cb3], [cb3])
    TS(nomlv[:], lbv[:], 1.0, -1.0, ALU.mult, ALU.add, [cb3], [cb3])
    MEMSET(onec[:], 1.0, [cb3])
    ACT(lnomlv[:], omlv[:], AF.Ln, [cb3], [cb3])
    ACT(sinkx[:], sinkx[:], AF.Exp, [const_b], [cb3])
    sx4 = sinkx[:].rearrange("p (kh i par o) -> p kh par i o", kh=4, i=4, par=2, o=1)
    TCOPY(sinkrow[:], sx4.to_broadcast([1, 4, 2, 4, 128]), [cb3], [cb3])
    for m4, m1 in ((maskc4, maskc), (maskp4, maskp), (maskf4, maskf)):
        for i in range(4):
            TCOPY(m4[:, i * 128:(i + 1) * 128], m1[:], [cb2], [cb3])
    CONST = [const_b, cb2, cb3]

    def gcol(base, c):
        return vecs[:, base + c:base + c + 1]

    def stage_load(sb, ph):
        kind = sb["kind"]
        src = xp if kind == "P" else xm
        tiles = sb_tiles(sb)
        NX = 6
        xin = [P.sbuf("xin%d" % i, [128, D], F32, ph) for i in range(NX)]
        xin_b = [Buf("xin%d" % i) for i in range(NX)]
        pin = [P.sbuf("pin%d" % i, [128, 2, 256], F32, ph) for i in range(NX)] if kind == "M" else None
        pin_b = [Buf("pin%d" % i) for i in range(NX)]
        cnt = 0
        ev = 0
        for ti, t in enumerate(tiles):
            c0, n = t["c0"], t["n"]
            if not t["samp"]:
                slots = []
                for j in t["blks"]:
                    s = cnt % NX
                    cnt += 1
                    r0 = sb["t0"] + j * 128
                    P.dma("sp", [(xin[s][:], src[r0:r0 + 128, :])], writes=[xin_b[s]], dbuf=xin_b[s])
                    if kind == "M":
                        P.dma("sp", [(pin[s][:], pm[:, r0:r0 + 128, :].rearrange("l t f -> t l f"))],
                              writes=[pin_b[s]], dbuf=pin_b[s])
                    slots.append(s)
                nb = len(slots)
                for c in range(8):
                    bk = c % 2
                    for jj, s in enumerate(slots):
                        MM(banks[bk][:, jj * 128:(jj + 1) * 128], xin[s][:, c * 128:(c + 1) * 128], ident_f[:], True, True,
                           [xin_b[s]] + CONST, [bq[bk][jj]])
                    ev += 1
                    if ev % 2 == 0:
                        ACOPY(hT[:, c, c0:c0 + n], banks[bk][:, 0:n], bqs(bk, 0, nb), [hT_b[ti]])
                    else:
                        TCOPY(hT[:, c, c0:c0 + n], banks[bk][:, 0:n], bqs(bk, 0, nb), [hT_b[ti]])
                if kind == "M":
                    for l in range(2):
                        for pc in range(2):
                            bk = 2 + (l * 2 + pc) % 2
                            for jj, s in enumerate(slots):
                                MM(banks[bk][:, jj * 128:(jj + 1) * 128], pin[s][:, l, pc * 128:(pc + 1) * 128], ident_f[:], True, True,
                                   [pin_b[s]] + CONST, [bq[bk][jj]])
                            ACOPY(pT[:, l, pc, c0:c0 + n], banks[bk][:, 0:n], bqs(bk, 0, nb), [pT_b[ti]])
            else:
                s = cnt % NX
                cnt += 1
                P.dma("sp", [(xin[s][0:NS, :], xs)], writes=[xin_b[s]], dbuf=xin_b[s])
                P.dma("sp", [(pin[s][0:NS, :, :], psm.rearrange("l t f -> t l f"))], writes=[pin_b[s]], dbuf=pin_b[s])
                for c in range(8):
                    MM(banks[0][:, c * NS:(c + 1) * NS], xin[s][0:NS, c * 128:(c + 1) * 128], ident_f[0:NS, 0:NS], True, True,
                       [xin_b[s]] + CONST, [bq[0][0]])
                ACOPY(hT[:, :, c0:c0 + NS], banks[0][:, 0:8 * NS].rearrange("p (c n) -> p c n", c=8), [bq[0][0]], [hT_b[ti]])
                for l in range(2):
                    for pc in range(2):
                        q = l * 2 + pc
                        MM(banks[1][:, q * NS:(q + 1) * NS], pin[s][0:NS, l, pc * 128:(pc + 1) * 128], ident_f[0:NS, 0:NS], True, True,
                           [pin_b[s]] + CONST, [bq[1][0]])
                ACOPY(pT[:, :, :, c0:c0 + NS], banks[1][:, 0:4 * NS].rearrange("p (l c n) -> p l c n", l=2, c=2), [bq[1][0]], [pT_b[ti]])

    def rms_stats(srcs, n, rt, reads, invd):
        sq, sq_b, lnv, rstd, r_b = rt
        ssb = 7
        for c, s_ap in enumerate(srcs):
            k = c % 2
            ACT(sq[k][:, 0:n], s_ap, AF.Square, reads, [sq_b[k]])
            MM(banks[ssb][:, 0:n], ones_b[:], sq[k][:, 0:n], c == 0, c == len(srcs) - 1, [sq_b[k]] + CONST, bqs(ssb))
        ACT(lnv[:, 0:n], banks[ssb][:, 0:n], AF.Ln, bqs(ssb) + CONST, [r_b], bias=epsc[:, 0:1], scale=invd)
        ACT(rstd[:, 0:n], lnv[:, 0:n], AF.Exp, [r_b], [r_b], scale=-0.5)
        return rstd

    def alloc_rms(ph, tag):
        sq = [P.sbuf("sq%s%d" % (tag, i), [128, 512], BF16, ph) for i in range(2)]
        sq_b = [Buf("sq") for _ in range(2)]
        lnv = P.sbuf("lnv" + tag, [128, 512], F32, ph)
        rstd = P.sbuf("rstd" + tag, [128, 512], F32, ph)
        return (sq, sq_b, lnv, rstd, Buf("rstd"))

    def stage_prenorm(sb, ph, gbase, dst, dst_b):
        rt = alloc_rms(ph, "pn%d" % gbase)
        for ti, t in enumerate(sb_tiles(sb)):
            c0, n = t["c0"], t["n"]
            rstd = rms_stats([hT[:, c, c0:c0 + n] for c in range(8)], n, rt, [hT_b[ti]], 1.0 / D)
            for c in range(8):
                STT(dst[:, c, c0:c0 + n], hT[:, c, c0:c0 + n], gcol(gbase, c), rstd[:, 0:n], ALU.mult, ALU.mult,
                    [hT_b[ti], rt[4]] + CONST, [dst_b[ti]])

    def stage_l0(sb, ph):
        kind = sb["kind"]
        full = kind == "M"
        tiles = sb_tiles(sb)
        nblk = sb["nblk"]
        has_s = sb["ns"] > 0
        wsl = [[P.sbuf("w0_%d_%d" % (s, k), [128, 8, 128], BF16, ph) for k in range(4)] for s in range(2)]
        wsl_b = [[Buf("w0") for k in range(4)] for s in range(2)]
        qd = [P.sbuf("qd%d" % s, [128, SBW], BF16, ph) for s in range(2)] if full else None
        kd = [P.sbuf("kd%d" % s, [128, SBW], BF16, ph) for s in range(2)]
        zs = [P.sbuf("zs%d" % s, [128, SBW], BF16, ph) for s in range(2)] if full else None
        Vtm = [P.sbuf("Vtm%d" % s, [128, 9, 128], BF16, ph) for s in range(2)]
        kdtm = [P.sbuf("kdtm%d" % s, [128, 9, 128], BF16, ph) for s in range(2)]
        hd = [P.sbuf("hd%d" % s, [128, 16], F32, ph) for s in range(2)]
        dec = [P.sbuf("dec%d" % s, [128, 16], F32, ph) for s in range(2)]
        negr = [P.sbuf("negr%d" % s, [128, 16], F32, ph) for s in range(2)]
        rr = [P.sbuf("rr%d" % s, [128, 16], F32, ph) for s in range(2)]
        hp_b = [[Buf("hp%d_%d" % (s, i)) for i in range(MAXT)] for s in range(2)]
        sc_b = [[Buf("sc%d_%d" % (s, i)) for i in range(MAXT)] for s in range(2)]
        tq = [P.sbuf("tq%d" % s, [128, 512], F32, ph) for s in range(2)] if full else None
        tsg = [P.sbuf("tsg%d" % s, [128, 512], F32, ph) for s in range(2)]
        tk = [P.sbuf("tk%d" % s, [128, 512], F32, ph) for s in range(2)]
        tb = [P.sbuf("tb%d" % s, [128, 512], F32, ph) for s in range(2)]
        tE1 = [P.sbuf("tE1%d" % s, [128, 512], BF16, ph) for s in range(2)] if full else None
        tE2 = [P.sbuf("tE2%d" % s, [128, 512], BF16, ph) for s in range(2)]
        tq_b = [Buf("tq") for _ in range(2)]
        tsg_b = [Buf("tsg") for _ in range(2)]
        tk_b = [Buf("tk") for _ in range(2)]
        tb_b = [Buf("tb") for _ in range(2)]
        tE1_b = [Buf("tE1") for _ in range(2)]
        tE2_b = [Buf("tE2") for _ in range(2)]
        ATs_b = [Buf("ATs") for _ in range(2)]
        Sp_b = [Buf("Sp") for _ in range(2)]
        Sd = [P.sbuf("Sd%d" % s, [128, 128], F32, ph) for s in range(2)]
        Sd_b = [Buf("Sd") for _ in range(2)]
        if full:
            ATs = [P.sbuf("ATs%d" % s, [128, 128], BF16, ph) for s in range(2)]
            Sp = [P.sbuf("Sp%d" % s, [128, 128], BF16, ph) for s in range(2)]
            osq = [P.sbuf("osq0", [128, 512], BF16, ph)] * 2
            osq_b = [Buf("osq")] * 2
            lnv0 = P.sbuf("lnv0", [128, 512], F32, ph)
            rstd0 = P.sbuf("rstd0", [128, 512], F32, ph)
            r0_b = Buf("r0")
            t1 = [P.sbuf("t1_0", [128, 512], F32, ph)] * 2
            t1_b = [Buf("t1")] * 2
            for s in range(2):
                MEMSET(ATs[s][:], 0.0, [ATs_b[s]])
        if has_s:
            qss = [P.sbuf("qss%d" % s, [128, NS], F32, ph) for s in range(2)]
            fss = [P.sbuf("fss%d" % s, [128, NS], F32, ph) for s in range(2)]
            kstm = [P.sbuf("kstm%d" % s, [NS, 128], BF16, ph) for s in range(2)]
            vstm = [P.sbuf("vstm%d" % s, [NS, 128], F32, ph) for s in range(2)]
            vblk = [P.sbuf("vblk%d" % s, [NS, NS, 128], BF16, ph) for s in range(2)]
            ss_b = [Buf("ss") for _ in range(2)]
            NSIN = 3
            sin = [P.sbuf("sin%d" % s, [128, 4, 128], F32, ph) for s in range(NSIN)]
            sin_b = [Buf("sin") for _ in range(NSIN)]
            sin_ctr = [0]
        print("  l0 scratch remaining:", nc.sbuf_bytes_remaining)

        def load_w(h):
            s = h % 2
            cols = [h * 128, 2048 + h * 128, 4096 + h * 128, 6144 + h * 128]
            for k in range(4):
                if not full and k in (0, 3):
                    continue
                P.dma("pool", [(wsl[s][k][:], w_in_a[:, cols[k]:cols[k] + 128].rearrange("(kc p) n -> p kc n", p=128))],
                      writes=[wsl_b[s][k]], dbuf=wsl_b[s][k])

        def head_proj(h):
            s = h % 2
            wq, wf, wi, wz = wsl[s]
            wq_b, wf_b, wi_b, wz_b = wsl_b[s]
            lnomc = lnomlv[:, h:h + 1]

            def stageA(ti):
                t = tiles[ti]
                c0, n = t["c0"], t["n"]
                p2 = ti % 2
                xr = [xnT_b[ti]]
                hpb = hp_b[s][ti]
                nb = len(t["blks"])
                for kc in range(8):
                    MM(banks[1][:, 0:n], wf[:, kc, :], xnT[:, kc, c0:c0 + n], kc == 0, kc == 7, [wf_b] + xr, bqs(1))
                yield
                ACT(tsg[p2][:, 0:n], banks[1][:, 0:n], AF.Exp, bqs(1), [tsg_b[p2]])
                ACT(tsg[p2][:, 0:n], tsg[p2][:, 0:n], AF.Ln, [tsg_b[p2]] + CONST, [tsg_b[p2]], bias=onec[:, 0:1], scale=1.0)
                ACT(tk[p2][:, 0:n], tsg[p2][:, 0:n], AF.Exp, [tsg_b[p2]] + CONST, [tk_b[p2]], bias=lnomc, scale=-1.0)
                if not t["samp"]:
                    ACT(tsg[p2][:, 0:n], tk[p2][:, 0:n], AF.Ln, [tk_b[p2]] + CONST, [tsg_b[p2]], bias=onec[:, 0:1], scale=-1.0)
                    j0 = t["blks"][0]
                    for jj, j in enumerate(t["blks"]):
                        for kc in range(8):
                            MM(banks[3][:, jj * 128:(jj + 1) * 128], xnT[:, kc, j * 128:(j + 1) * 128], wi[:, kc, :], kc == 0, kc == 7,
                               [wi_b] + xr, [bq[3][jj]])
                        if jj % 2 == 1 and jj + 1 < nb:
                            yield
                    TCOPY(Vtm[s][:, j0:j0 + nb, :], banks[3][:, 0:n].rearrange("p (j t) -> p j t", t=128), bqs(3, 0, nb), [hpb])
                else:
                    for kc in range(8):
                        MM(banks[3][0:NS, 0:128], xnT[:, kc, c0:c0 + NS], wi[:, kc, :], kc == 0, kc == 7, [wi_b] + xr, [bq[3][0]])
                yield
                if full:
                    for kc in range(8):
                        MM(banks[0][:, 0:n], wq[:, kc, :], xnT[:, kc, c0:c0 + n], kc == 0, kc == 7, [wq_b] + xr, bqs(0))
                    for kc in range(8):
                        MM(banks[2][:, 0:n], wz[:, kc, :], xnT[:, kc, c0:c0 + n], kc == 0, kc == 7, [wz_b] + xr, bqs(2))
                    yield
                    if not t["samp"]:
                        ACT(tq[p2][:, 0:n], banks[0][:, 0:n], AF.Silu, bqs(0), [tq_b[p2]])
                    else:
                        ACT(qss[s][:, 0:n], banks[0][:, 0:n], AF.Silu, bqs(0), [ss_b[s]])
                    ACT(zs[s][:, c0:c0 + n], banks[2][:, 0:n], AF.Silu, bqs(2), [hpb])
                if t["samp"]:
                    TS(fss[s][:, 0:n], tk[p2][:, 0:n], -1.0, 1.0, ALU.mult, ALU.add, [tk_b[p2]], [ss_b[s]])
                    TCOPY(kd[s][:, c0:c0 + n], tk[p2][:, 0:n], [tk_b[p2]], [hpb])
                    MM(banks[4][0:NS, 0:128], kd[s][:, c0:c0 + n], ident_b[:], True, True, [hpb] + CONST, [bq[4][0]])
                    ACOPY(kstm[s][:], banks[4][0:NS, 0:128], [bq[4][0]], [ss_b[s]])
                    ACOPY(vstm[s][:], banks[3][0:NS, 0:128], [bq[3][0]], [ss_b[s]])
                    for sp_ in range(NS):
                        TS(vblk[s][:, sp_, :], vstm[s][:], ident_f[0:NS, sp_:sp_ + 1], None, ALU.mult, None, [ss_b[s]] + CONST, [ss_b[s]])
                yield

            def stageB(ti):
                t = tiles[ti]
                c0, n = t["c0"], t["n"]
                p2 = ti % 2
                hpb = hp_b[s][ti]
                scb = sc_b[s][ti]
                nb = len(t["blks"])
                j0 = t["blks"][0]
                for jj in range(nb):
                    sl = slice(jj * 128, (jj + 1) * 128)
                    SCAN(tb[p2][:, sl], ones_f[:], tsg[p2][:, sl], [tsg_b[p2]] + CONST, [tb_b[p2]])
                tb3 = tb[p2][:, 0:n].rearrange("p (j t) -> p j t", t=128)
                blast = tb3[:, :, 127:128].rearrange("p j o -> p (j o)")
                TS(rr[s][:, j0:j0 + nb], blast, 0.5, None, ALU.mult, None, [tb_b[p2]], [scb])
                TT(tb3, tb3, rr[s][:, j0:j0 + nb].rearrange("p (j o) -> p j o", o=1).to_broadcast([128, nb, 128]), ALU.subtract,
                   [tb_b[p2], scb], [tb_b[p2]])
                yield
                ACT(hd[s][:, j0:j0 + nb], rr[s][:, j0:j0 + nb], AF.Exp, [scb], [scb])
                ACT(dec[s][:, j0:j0 + nb], rr[s][:, j0:j0 + nb], AF.Exp, [scb], [scb], scale=2.0)
                if full:
                    ACT(tE1[p2][:, 0:n], tb[p2][:, 0:n], AF.Exp, [tb_b[p2]], [tE1_b[p2]])
                ACT(tE2[p2][:, 0:n], tb[p2][:, 0:n], AF.Exp, [tb_b[p2]], [tE2_b[p2]], scale=-1.0)
                yield
                TT(kd[s][:, c0:c0 + n], tk[p2][:, 0:n], tE2[p2][:, 0:n], ALU.mult, [tk_b[p2], tE2_b[p2]], [hpb])
                if full:
                    TT(qd[s][:, c0:c0 + n], tq[p2][:, 0:n], tE1[p2][:, 0:n], ALU.mult, [tq_b[p2], tE1_b[p2]], [hpb])
                for jj, j in enumerate(t["blks"]):
                    MM(banks[4][:, jj * 128:(jj + 1) * 128], kd[s][:, j * 128:(j + 1) * 128], ident_b[:], True, True, [hpb] + CONST, [bq[4][jj]])
                TCOPY(kdtm[s][:, j0:j0 + nb, :], banks[4][:, 0:n].rearrange("p (j t) -> p j t", t=128), bqs(4, 0, nb), [hpb])
                yield

            nt = len(tiles)
            for r in range(nt + 1):
                gens = []
                if r >= 1 and not tiles[r - 1]["samp"]:
                    gens.append(stageB(r - 1))
                if r < nt:
                    gens.append(stageA(r))
                while gens:
                    for g in list(gens):
                        try:
                            next(g)
                            yield
                        except StopIteration:
                            gens.remove(g)

        def dphase(h, s, ti, c0, n):
            k = ti % 2
            oc = t1[0]
            ACT(oc[:, 0:n], banks[6][:, 0:n], AF.Identity, bqs(6), [t1_b[0]])
            ACT(osq[k][:, 0:n], oc[:, 0:n], AF.Square, [t1_b[0]], [osq_b[k]])
            MM(banks[6][:, 0:n], ones_b[:], osq[k][:, 0:n], True, True, [osq_b[k]] + CONST, bqs(6))
            ACT(lnv0[:, 0:n], banks[6][:, 0:n], AF.Ln, bqs(6) + CONST, [r0_b], bias=epsc[:, 0:1], scale=1.0 / 128)
            ACT(rstd0[:, 0:n], lnv0[:, 0:n], AF.Exp, [r0_b], [r0_b], scale=-0.5)
            TT(oc[:, 0:n], oc[:, 0:n], rstd0[:, 0:n], ALU.mult, [t1_b[0], r0_b], [t1_b[0]])
            STT(ogT[:, h, c0:c0 + n], oc[:, 0:n], gcol(V_ON, h), zs[s][:, c0:c0 + n], ALU.mult, ALU.mult,
                [t1_b[0], hp_b[s][ti]] + CONST, [og_b[h][ti]])

        def head_scan(h):
            s = h % 2
            Sh = Sst[:, h, :]
            pending = []

            def flush():
                for fn in pending:
                    fn()
                del pending[:]

            blocks = [(ti, jj, j) for ti, t in enumerate(tiles) if not t["samp"] for jj, j in enumerate(t["blks"])]
            slots_ = {}
            NPF = 2

            def load_sin(g4):
                g = sin_ctr[0] % NSIN
                sin_ctr[0] += 1
                slots_[g4] = g
                P.dma("sp", [(sin[g][:], st_in[g4 * 4:g4 * 4 + 4, h].rearrange("s k v -> k s v"))], writes=[sin_b[g]], dbuf=sin_b[g])

            if has_s:
                for g4 in range(NPF):
                    load_sin(g4)

            def pe_front(bi):
                ti, jj, j = blocks[bi]
                a = j % 2
                blk = slice(j * 128, (j + 1) * 128)
                MM(banks[7][:, a * 128:(a + 1) * 128], kdtm[s][:, j, :], Vtm[s][:, j, :], True, True, [hp_b[s][ti]], [bq[7][a]])
                if full:
                    MM(banks[5][:, a * 128:(a + 1) * 128], kd[s][:, blk], qd[s][:, blk], True, True, [hp_b[s][ti]], [bq[5][a]])

            pe_front(0)
            for bi, (ti, jj, j) in enumerate(blocks):
                t = tiles[ti]
                c0, n = t["c0"], t["n"]
                nb = len(t["blks"])
                hpr = [hp_b[s][ti]]
                scr = [sc_b[s][ti]]
                a = j % 2
                blk = slice(j * 128, (j + 1) * 128)
                Uq = banks[7][:, a * 128:(a + 1) * 128]
                Aq = banks[5][:, a * 128:(a + 1) * 128]
                if bi + 1 < len(blocks):
                    pe_front(bi + 1)
                if full:
                    ACT(Sp[a][:], Sh, AF.Identity, [S_b[h]] + scr, [Sp_b[a]], scale=hd[s][:, j:j + 1])
                ACT(Sd[a][:], Uq, AF.Identity, [bq[7][a]] + scr, [Sd_b[a]], scale=hd[s][:, j:j + 1])
                if full:
                    CPRED(ATs[a][:], mtri[:], Aq, [bq[5][a], ATs_b[a]] + CONST, [ATs_b[a]])
                STT(Sh, Sh, dec[s][:, j:j + 1], Sd[a][:], ALU.mult, ALU.add, [S_b[h], Sd_b[a]] + scr, [S_b[h]])
                flush()
                if full:
                    def cons(a=a, j=j, jj=jj, blk=blk, hpr=hpr):
                        oq = banks[6][:, jj * 128:(jj + 1) * 128]
                        MM(oq, Sp[a][:], qd[s][:, blk], True, False, [Sp_b[a]] + hpr, [bq[6][jj]])
                        MM(oq, Vtm[s][:, j, :], ATs[a][:], False, True, [ATs_b[a]] + hpr, [bq[6][jj]])
                    pending.append(cons)
                    if jj == nb - 1:
                        pending.append(lambda ti=ti, c0=c0, n=n: dphase(h, s, ti, c0, n))
                yield
            flush()
            yield
            if has_s:
                P.dma("sp", [(stp_d[h], Sh)], reads=[S_b[h]], dbuf=stp_cb, is_out=True)
                ti = len(tiles) - 1
                c0 = tiles[ti]["c0"]
                for g4 in range(4):
                    s0 = g4 * 4
                    bk = 4 + g4 % 2
                    g = slots_[g4]
                    MM(banks[bk][:, :], kstm[s][:], vblk[s][:, s0:s0 + 4, :], True, True, [ss_b[s]], bqs(bk))
                    for si in range(4):
                        sidx = s0 + si
                        STT(sin[g][:, si, :], sin[g][:, si, :], fss[s][:, sidx:sidx + 1], banks[bk][:, si * 128:(si + 1) * 128], ALU.mult, ALU.add,
                            [sin_b[g], ss_b[s], bq[bk][si]], [sin_b[g]])
                        MM(banks[6][:, sidx:sidx + 1], sin[g][:, si, :], qss[s][:, sidx:sidx + 1], True, True, [sin_b[g], ss_b[s]], [bq[6][0]])
                    P.dma("sp", [(sts_d[s0:s0 + 4, h].rearrange("s k v -> k s v"), sin[g][:])], reads=[sin_b[g]], dbuf=sin_b[g], is_out=True)
                    if g4 + NPF < 4:
                        load_sin(g4 + NPF)
                    yield
                dphase(h, s, ti, c0, NS)
                yield

        def drive(gens):
            RATIO = int(os.environ.get("KB_RATIO", "2"))
            gens = [[g, (RATIO if i == 0 else 1)] for i, g in enumerate(gens) if g is not None]
            while gens:
                for it in list(gens):
                    for _ in range(it[1]):
                        try:
                            next(it[0])
                        except StopIteration:
                            gens.remove(it)
                            break

        import os
        NH = int(os.environ.get("KB_NH", "16"))
        load_w(0)
        prev_scan = None
        for h in range(NH):
            if h + 1 < NH:
                load_w(h + 1)
            drive([head_proj(h), prev_scan])
            prev_scan = head_scan(h)
        drive([prev_scan])

    def stage_l0u(sb, ph):
        kind = sb["kind"]
        full = kind == "M"
        tiles = sb_tiles(sb)
        has_s = sb["ns"] > 0
        NU = 3
        wsl = [[P.sbuf("w0_%d_%d" % (s, k), [128, 8, 128], BF16, ph) for k in range(4)] for s in range(NU)]
        wsl_b = [[Buf("w0") for k in range(4)] for s in range(NU)]
        qd = [P.sbuf("qd%d" % s, [128, 512], BF16, ph) for s in range(NU)] if full else None
        kd = [P.sbuf("kd%d" % s, [128, 512], BF16, ph) for s in range(NU)]
        zs = [P.sbuf("zs%d" % s, [128, 512], BF16, ph) for s in range(NU)] if full else None
        Vtm = [P.sbuf("Vtm%d" % s, [128, 4, 128], BF16, ph) for s in range(NU)]
        kdtm = [P.sbuf("kdtm%d" % s, [128, 4, 128], BF16, ph) for s in range(NU)]
        hd = [P.sbuf("hd%d" % s, [128, 4], F32, ph) for s in range(NU)]
        dec = [P.sbuf("dec%d" % s, [128, 4], F32, ph) for s in range(NU)]
        rr = [P.sbuf("rr%d" % s, [128, 4], F32, ph) for s in range(NU)]
        hp_b = [Buf("hp%d" % s) for s in range(NU)]
        sc_b = [Buf("sc%d" % s) for s in range(NU)]
        tq = [P.sbuf("tq%d" % s, [128, 512], F32, ph) for s in range(2)] if full else None
        tsg = [P.sbuf("tsg%d" % s, [128, 512], F32, ph) for s in range(2)]
        tk = [P.sbuf("tk%d" % s, [128, 512], F32, ph) for s in range(2)]
        tb = [P.sbuf("tb%d" % s, [128, 512], F32, ph) for s in range(2)]
        tE1 = [P.sbuf("tE1%d" % s, [128, 512], BF16, ph) for s in range(2)] if full else None
        tE2 = [P.sbuf("tE2%d" % s, [128, 512], BF16, ph) for s in range(2)]
        tq_b = [Buf("tq") for _ in range(2)]
        tsg_b = [Buf("tsg") for _ in range(2)]
        tk_b = [Buf("tk") for _ in range(2)]
        tb_b = [Buf("tb") for _ in range(2)]
        tE1_b = [Buf("tE1") for _ in range(2)]
        tE2_b = [Buf("tE2") for _ in range(2)]
        ATs_b = [Buf("ATs") for _ in range(2)]
        Sp_b = [Buf("Sp") for _ in range(2)]
        Sd = [P.sbuf("Sd%d" % s, [128, 128], F32, ph) for s in range(2)]
        Sd_b = [Buf("Sd") for _ in range(2)]
        if full:
            ATs = [P.sbuf("ATs%d" % s, [128, 128], BF16, ph) for s in range(2)]
            Sp = [P.sbuf("Sp%d" % s, [128, 128], BF16, ph) for s in range(2)]
            osq = P.sbuf("osq0", [128, 512], BF16, ph)
            osq_b = Buf("osq")
            lnv0 = P.sbuf("lnv0", [128, 512], F32, ph)
            rstd0 = P.sbuf("rstd0", [128, 512], F32, ph)
            r0_b = Buf("r0")
            oc = P.sbuf("oc", [128, 512], F32, ph)
            oc_b = Buf("oc")
            for s in range(2):
                MEMSET(ATs[s][:], 0.0, [ATs_b[s]])
        if has_s:
            qss = [P.sbuf("qss%d" % s, [128, NS], F32, ph) for s in range(NU)]
            fss = [P.sbuf("fss%d" % s, [128, NS], F32, ph) for s in range(NU)]
            kstm = [P.sbuf("kstm%d" % s, [NS, 128], BF16, ph) for s in range(NU)]
            vstm = [P.sbuf("vstm%d" % s, [NS, 128], F32, ph) for s in range(NU)]
            ss_b = [Buf("ss") for _ in range(NU)]
            vblk = P.sbuf("vblk", [NS, NS, 128], BF16, ph)
            vblk_b = Buf("vblk")
            NSIN = 3
            sin = [P.sbuf("sin%d" % s, [128, 4, 128], F32, ph) for s in range(NSIN)]
            sin_b = [Buf("sin") for _ in range(NSIN)]
            sin_ctr = [0]
        print("  l0u scratch remaining:", nc.sbuf_bytes_remaining)

        units = [(ti, h) for ti in range(len(tiles)) for h in range(16)]
        NUN = len(units)

        def load_w(ui):
            ti, h = units[ui]
            s = ui % NU
            cols = [h * 128, 2048 + h * 128, 4096 + h * 128, 6144 + h * 128]
            for k in range(4):
                if not full and k in (0, 3):
                    continue
                P.dma("pool", [(wsl[s][k][:], w_in_a[:, cols[k]:cols[k] + 128].rearrange("(kc p) n -> p kc n", p=128))],
                      writes=[wsl_b[s][k]], dbuf=wsl_b[s][k])

        sin_slots = {}

        def load_sin(h, g4):
            g = sin_ctr[0] % NSIN
            sin_ctr[0] += 1
            sin_slots[(h, g4)] = g
            P.dma("sp", [(sin[g][:], st_in[g4 * 4:g4 * 4 + 4, h].rearrange("s k v -> k s v"))], writes=[sin_b[g]], dbuf=sin_b[g])

        def stageA(ui):
            ti, h = units[ui]
            s = ui % NU
            p2 = ui % 2
            t = tiles[ti]
            c0, n = t["c0"], t["n"]
            nb = len(t["blks"])
            wq, wf, wi, wz = wsl[s]
            wq_b, wf_b, wi_b, wz_b = wsl_b[s]
            lnomc = lnomlv[:, h:h + 1]
            xr = [xnT_b[ti]]
            hpb = hp_b[s]
            for kc in range(8):
                MM(banks[1][:, 0:n], wf[:, kc, :], xnT[:, kc, c0:c0 + n], kc == 0, kc == 7, [wf_b] + xr, bqs(1))
            yield
            ACT(tsg[p2][:, 0:n], banks[1][:, 0:n], AF.Exp, bqs(1), [tsg_b[p2]])
            ACT(tsg[p2][:, 0:n], tsg[p2][:, 0:n], AF.Ln, [tsg_b[p2]] + CONST, [tsg_b[p2]], bias=onec[:, 0:1], scale=1.0)
            ACT(tk[p2][:, 0:n], tsg[p2][:, 0:n], AF.Exp, [tsg_b[p2]] + CONST, [tk_b[p2]], bias=lnomc, scale=-1.0)
            if not t["samp"]:
                ACT(tsg[p2][:, 0:n], tk[p2][:, 0:n], AF.Ln, [tk_b[p2]] + CONST, [tsg_b[p2]], bias=onec[:, 0:1], scale=-1.0)
                for jj, j in enumerate(t["blks"]):
                    for kc in range(8):
                        MM(banks[3][:, jj * 128:(jj + 1) * 128], xnT[:, kc, j * 128:(j + 1) * 128], wi[:, kc, :], kc == 0, kc == 7,
                           [wi_b] + xr, [bq[3][jj]])
                    if jj % 2 == 1 and jj + 1 < nb:
                        yield
                TCOPY(Vtm[s][:, 0:nb, :], banks[3][:, 0:n].rearrange("p (j t) -> p j t", t=128), bqs(3, 0, nb), [hpb])
            else:
                for kc in range(8):
                    MM(banks[3][0:NS, 0:128], xnT[:, kc, c0:c0 + NS], wi[:, kc, :], kc == 0, kc == 7, [wi_b] + xr, [bq[3][0]])
            yield
            if full:
                for kc in range(8):
                    MM(banks[0][:, 0:n], wq[:, kc, :], xnT[:, kc, c0:c0 + n], kc == 0, kc == 7, [wq_b] + xr, bqs(0))
                for kc in range(8):
                    MM(banks[2][:, 0:n], wz[:, kc, :], xnT[:, kc, c0:c0 + n], kc == 0, kc == 7, [wz_b] + xr, bqs(2))
                yield
                if not t["samp"]:
                    ACT(tq[p2][:, 0:n], banks[0][:, 0:n], AF.Silu, bqs(0), [tq_b[p2]])
                else:
                    ACT(qss[s][:, 0:n], banks[0][:, 0:n], AF.Silu, bqs(0), [ss_b[s]])
                ACT(zs[s][:, 0:n], banks[2][:, 0:n], AF.Silu, bqs(2), [hpb])
            if t["samp"]:
                TS(fss[s][:, 0:n], tk[p2][:, 0:n], -1.0, 1.0, ALU.mult, ALU.add, [tk_b[p2]], [ss_b[s]])
                TCOPY(kd[s][:, 0:n], tk[p2][:, 0:n], [tk_b[p2]], [hpb])
                MM(banks[4][0:NS, 0:128], kd[s][:, 0:n], ident_b[:], True, True, [hpb] + CONST, [bq[4][0]])
                ACOPY(kstm[s][:], banks[4][0:NS, 0:128], [bq[4][0]], [ss_b[s]])
                ACOPY(vstm[s][:], banks[3][0:NS, 0:128], [bq[3][0]], [ss_b[s]])
            yield

        def stageB(ui):
            ti, h = units[ui]
            s = ui % NU
            p2 = ui % 2
            t = tiles[ti]
            if t["samp"]:
                return
            n = t["n"]
            nb = len(t["blks"])
            hpb = hp_b[s]
            scb = sc_b[s]
            for jj in range(nb):
                sl = slice(jj * 128, (jj + 1) * 128)
                SCAN(tb[p2][:, sl], ones_f[:], tsg[p2][:, sl], [tsg_b[p2]] + CONST, [tb_b[p2]])
            tb3 = tb[p2][:, 0:n].rearrange("p (j t) -> p j t", t=128)
            blast = tb3[:, :, 127:128].rearrange("p j o -> p (j o)")
            TS(rr[s][:, 0:nb], blast, 0.5, None, ALU.mult, None, [tb_b[p2]], [scb])
            TT(tb3, tb3, rr[s][:, 0:nb].rearrange("p (j o) -> p j o", o=1).to_broadcast([128, nb, 128]), ALU.subtract,
               [tb_b[p2], scb], [tb_b[p2]])
            yield
            ACT(hd[s][:, 0:nb], rr[s][:, 0:nb], AF.Exp, [scb], [scb])
            ACT(dec[s][:, 0:nb], rr[s][:, 0:nb], AF.Exp, [scb], [scb], scale=2.0)
            if full:
                ACT(tE1[p2][:, 0:n], tb[p2][:, 0:n], AF.Exp, [tb_b[p2]], [tE1_b[p2]])
            ACT(tE2[p2][:, 0:n], tb[p2][:, 0:n], AF.Exp, [tb_b[p2]], [tE2_b[p2]], scale=-1.0)
            yield
            TT(kd[s][:, 0:n], tk[p2][:, 0:n], tE2[p2][:, 0:n], ALU.mult, [tk_b[p2], tE2_b[p2]], [hpb])
            if full:
                TT(qd[s][:, 0:n], tq[p2][:, 0:n], tE1[p2][:, 0:n], ALU.mult, [tq_b[p2], tE1_b[p2]], [hpb])
            for jj in range(nb):
                MM(banks[4][:, jj * 128:(jj + 1) * 128], kd[s][:, jj * 128:(jj + 1) * 128], ident_b[:], True, True, [hpb] + CONST, [bq[4][jj]])
            TCOPY(kdtm[s][:, 0:nb, :], banks[4][:, 0:n].rearrange("p (j t) -> p j t", t=128), bqs(4, 0, nb), [hpb])
            yield

        def dphase(h, s, c0, n):
            ACT(oc[:, 0:n], banks[6][:, 0:n], AF.Identity, bqs(6), [oc_b])
            ACT(osq[:, 0:n], oc[:, 0:n], AF.Square, [oc_b], [osq_b])
            MM(banks[6][:, 0:n], ones_b[:], osq[:, 0:n], True, True, [osq_b] + CONST, bqs(6))
            ACT(lnv0[:, 0:n], banks[6][:, 0:n], AF.Ln, bqs(6) + CONST, [r0_b], bias=epsc[:, 0:1], scale=1.0 / 128)
            ACT(rstd0[:, 0:n], lnv0[:, 0:n], AF.Exp, [r0_b], [r0_b], scale=-0.5)
            TT(oc[:, 0:n], oc[:, 0:n], rstd0[:, 0:n], ALU.mult, [oc_b, r0_b], [oc_b])
            STT(ogT[:, h, c0:c0 + n], oc[:, 0:n], gcol(V_ON, h), zs[s][:, 0:n], ALU.mult, ALU.mult,
                [oc_b, hp_b[s]] + CONST, [og_b[h][0], og_b[h][1], og_b[h][2]])

        def scan(ui):
            ti, h = units[ui]
            s = ui % NU
            t = tiles[ti]
            c0, n = t["c0"], t["n"]
            nb = len(t["blks"])
            Sh = Sst[:, h, :]
            hpr = [hp_b[s]]
            scr = [sc_b[s]]
            if not t["samp"]:
                def pe_front(jj):
                    a = jj % 2
                    blk = slice(jj * 128, (jj + 1) * 128)
                    MM(banks[7][:, a * 128:(a + 1) * 128], kdtm[s][:, jj, :], Vtm[s][:, jj, :], True, True, hpr, [bq[7][a]])
                    if full:
                        MM(banks[5][:, a * 128:(a + 1) * 128], kd[s][:, blk], qd[s][:, blk], True, True, hpr, [bq[5][a]])
                pend = []
                pe_front(0)
                for jj in range(nb):
                    a = jj % 2
                    blk = slice(jj * 128, (jj + 1) * 128)
                    Uq = banks[7][:, a * 128:(a + 1) * 128]
                    Aq = banks[5][:, a * 128:(a + 1) * 128]
                    if jj + 1 < nb:
                        pe_front(jj + 1)
                    if full:
                        ACT(Sp[a][:], Sh, AF.Identity, [S_b[h]] + scr, [Sp_b[a]], scale=hd[s][:, jj:jj + 1])
                    ACT(Sd[a][:], Uq, AF.Identity, [bq[7][a]] + scr, [Sd_b[a]], scale=hd[s][:, jj:jj + 1])
                    if full:
                        CPRED(ATs[a][:], mtri[:], Aq, [bq[5][a], ATs_b[a]] + CONST, [ATs_b[a]])
                    STT(Sh, Sh, dec[s][:, jj:jj + 1], Sd[a][:], ALU.mult, ALU.add, [S_b[h], Sd_b[a]] + scr, [S_b[h]])
                    for fn in pend:
                        fn()
                    del pend[:]
                    if full:
                        def cons(a=a, jj=jj, blk=blk):
                            oq = banks[6][:, jj * 128:(jj + 1) * 128]
                            MM(oq, Sp[a][:], qd[s][:, blk], True, False, [Sp_b[a]] + hpr, [bq[6][jj]])
                            MM(oq, Vtm[s][:, jj, :], ATs[a][:], False, True, [ATs_b[a]] + hpr, [bq[6][jj]])
                        pend.append(cons)
                    yield
                for fn in pend:
                    fn()
                if full:
                    dphase(h, s, c0, n)
                yield
                return
            P.dma("sp", [(stp_d[h], Sh)], reads=[S_b[h]], dbuf=stp_cb, is_out=True)
            for g4 in range(2):
                load_sin(h, g4)
            for sp_ in range(NS):
                TS(vblk[:, sp_, :], vstm[s][:], ident_f[0:NS, sp_:sp_ + 1], None, ALU.mult, None, [ss_b[s]] + CONST, [vblk_b])
            yield
            for g4 in range(4):
                s0 = g4 * 4
                bk = 5 if g4 % 2 == 0 else 7
                g = sin_slots[(h, g4)]
                MM(banks[bk][:, :], kstm[s][:], vblk[:, s0:s0 + 4, :], True, True, [ss_b[s], vblk_b], bqs(bk))
                for si in range(4):
                    sidx = s0 + si
                    STT(sin[g][:, si, :], sin[g][:, si, :], fss[s][:, sidx:sidx + 1], banks[bk][:, si * 128:(si + 1) * 128], ALU.mult, ALU.add,
                        [sin_b[g], ss_b[s], bq[bk][si]], [sin_b[g]])
                    MM(banks[6][:, sidx:sidx + 1], sin[g][:, si, :], qss[s][:, sidx:sidx + 1], True, True, [sin_b[g], ss_b[s]], [bq[6][0]])
                P.dma("sp", [(sts_d[s0:s0 + 4, h].rearrange("s k v -> k s v"), sin[g][:])], reads=[sin_b[g]], dbuf=sin_b[g], is_out=True)
                if g4 + 2 < 4:
                    load_sin(h, g4 + 2)
                yield
            dphase(h, s, c0, NS)
            yield

        load_w(0)
        if NUN > 1:
            load_w(1)
        for idx in range(NUN + 2):
            if idx + 2 < NUN:
                load_w(idx + 2)
            gens = []
            if 0 <= idx - 1 < NUN:
                gens.append(stageB(idx - 1))
            if idx < NUN:
                gens.append(stageA(idx))
            if 0 <= idx - 2 < NUN:
                gens.append(scan(idx - 2))
            while gens:
                for g in list(gens):
                    try:
                        next(g)
                    except StopIteration:
                        gens.remove(g)

    def stage_out(sb, ph, l, w_out):
        tiles = sb_tiles(sb)
        mix = P.sbuf("mix", [128, 8, SBW], F32, ph)
        mix_b = [[Buf("mix") for _ in range(MAXT)] for _ in range(8)]
        wo = [P.sbuf("wo%d" % i, [128, 16, 128], BF16, ph) for i in range(2)]
        wo_b = [Buf("wo") for _ in range(2)]
        rt = alloc_rms(ph, "po")
        tmp = [P.sbuf("tmpo%d" % i, [128, 512], F32, ph) for i in range(2)]
        tmp_b = [Buf("tmpo") for _ in range(2)]
        wg = [P.sbuf("wg%d" % i, [128, 8, 128], BF16, ph) for i in range(2)]
        wg_b = [Buf("wg") for _ in range(2)]
        wp = [P.sbuf("wp%d" % i, [128, 2, 128], BF16, ph) for i in range(2)]
        sgt = [P.sbuf("sgt%d" % i, [128, 512], F32, ph) for i in range(2)]
        sgt_b = [Buf("sgt") for _ in range(2)]
        print("  out scratch remaining:", nc.sbuf_bytes_remaining)

        def ldo(dc):
            P.dma("pool", [(wo[dc % 2][:], w_out[:, dc * 128:(dc + 1) * 128].rearrange("(cc p) n -> p cc n", p=128))],
                  writes=[wo_b[dc % 2]], dbuf=wo_b[dc % 2])
        ldo(0)
        k = 0
        for dc in range(8):
            if dc + 1 < 8:
                ldo(dc + 1)
            for ti, t in enumerate(tiles):
                c0, n = t["c0"], t["n"]
                bk = k % 3
                k += 1
                for cc in range(16):
                    MM(banks[bk][:, 0:n], wo[dc % 2][:, cc, :], ogT[:, cc, c0:c0 + n], cc == 0, cc == 15, [wo_b[dc % 2], og_b[cc][ti]], bqs(bk))
                ACOPY(mix[:, dc, c0:c0 + n], banks[bk][:, 0:n], bqs(bk), [mix_b[dc][ti]])
        gb = V_POST0 if l == 0 else V_POST1
        for ti, t in enumerate(tiles):
            c0, n = t["c0"], t["n"]
            rstd = rms_stats([mix[:, c, c0:c0 + n] for c in range(8)], n, rt, [mix_b[c][ti] for c in range(8)], 1.0 / D)
            for c in range(8):
                a = c % 2
                STT(tmp[a][:, 0:n], mix[:, c, c0:c0 + n], gcol(gb, c), rstd[:, 0:n], ALU.mult, ALU.mult, [mix_b[c][ti], rt[4]] + CONST, [tmp_b[a]])
                TT(hT[:, c, c0:c0 + n], hT[:, c, c0:c0 + n], tmp[a][:, 0:n], ALU.add, [tmp_b[a], hT_b[ti]], [hT_b[ti]])
                ACOPY(xnT[:, c, c0:c0 + n], hT[:, c, c0:c0 + n], [hT_b[ti]], [xnT_b[ti]])

        def ldg(dc):
            P.dma("pool", [(wg[dc % 2][:], w_pg[l][:, dc * 128:(dc + 1) * 128].rearrange("(kc p) n -> p kc n", p=128)),
                           (wp[dc % 2][:], w_pe[l][:, dc * 128:(dc + 1) * 128].rearrange("(kc p) n -> p kc n", p=128))],
                  writes=[wg_b[dc % 2]], dbuf=wg_b[dc % 2])
        ldg(0)
        k = 0
        for dc in range(8):
            if dc + 1 < 8:
                ldg(dc + 1)
            for ti, t in enumerate(tiles):
                c0, n = t["c0"], t["n"]
                a = k % 2
                bg = a
                bp = 2 + a
                k += 1
                for kc in range(8):
                    MM(banks[bg][:, 0:n], wg[dc % 2][:, kc, :], xnT[:, kc, c0:c0 + n], kc == 0, kc == 7, [wg_b[dc % 2], xnT_b[ti]], bqs(bg))
                for pc in range(2):
                    MM(banks[bp][:, 0:n], wp[dc % 2][:, pc, :], pT[:, l, pc, c0:c0 + n], pc == 0, pc == 1, [wg_b[dc % 2], pT_b[ti]], bqs(bp))
                ACT(sgt[a][:, 0:n], banks[bg][:, 0:n], AF.Sigmoid, bqs(bg), [sgt_b[a]])
                TT(tmp[a][:, 0:n], banks[bp][:, 0:n], sgt[a][:, 0:n], ALU.mult, bqs(bp) + [sgt_b[a]], [tmp_b[a]])
                TT(hT[:, dc, c0:c0 + n], hT[:, dc, c0:c0 + n], tmp[a][:, 0:n], ALU.add, [tmp_b[a], hT_b[ti]], [hT_b[ti]])

    def stage_l1(sb, sbi, ph):
        tiles = sb_tiles(sb)
        nblk = sb["nblk"]
        has_s = sb["ns"] > 0
        first = sb["t0"] == 0
        NT = nblk * 128 + sb["ns"]
        kT = P.sbuf("kT", [128, 4, 128 + SBW], BF16, ph)
        Vt = P.sbuf("Vt", [128, 10, 256], BF16, ph)
        rC = P.sbuf("rC", [128, SBW], BF16, ph)
        rS = P.sbuf("rS", [128, SBW], BF16, ph)
        kT_b = [Buf("kT%d" % i) for i in range(MAXT + 1)]
        Vt_b = [Buf("Vt%d" % i) for i in range(10)]
        rope_b = Buf("rope")
        rf = P.sbuf("rf", [128, 512], F32, ph)
        rb = P.sbuf("rb", [128, 512], BF16, ph)
        rt1 = P.sbuf("rt1", [128, 512], F32, ph)
        rt2 = P.sbuf("rt2", [128, 512], F32, ph)
        rf_b, rb_b, rt1_b, rt2_b = Buf("rf"), Buf("rb"), Buf("rt1"), Buf("rt2")
        if has_s:
            kfl = P.sbuf("kfl", [128, 4, 128], F32, ph)
            ksf = P.sbuf("ksf", [128, 4, NS], F32, ph)
            kfl_b = Buf("kfl")
            qsa = P.sbuf("qsa", [128, 16, NS], BF16, ph)
            zsa = P.sbuf("zsa", [128, 16, NS], BF16, ph)
            qsa_b = Buf("qsa")
            vsb = P.sbuf("vsb", [NS, 256], BF16, ph)
            ostg = P.sbuf("ostg", [128, 768], F32, ph)
            ostg_b = Buf("ostg")
        g0 = sb["t0"]
        pairs = [(rC[:, 0:nblk * 128], ropec_d[:, g0:g0 + nblk * 128]), (rS[:, 0:nblk * 128], ropes_d[:, g0:g0 + nblk * 128])]
        if has_s:
            pairs += [(rC[:, nblk * 128:NT], ropec_d[:, NMAIN:NMAIN + NS]), (rS[:, nblk * 128:NT], ropes_d[:, NMAIN:NMAIN + NS])]
        P.dma("pool", pairs, writes=[rope_b], dbuf=rope_b)
        ACOPY(kT[:, :, 0:128], kT_halo[:], [halo_b], [kT_b[0]])
        ACOPY(Vt[:, 0, :], V_halo[:], [halo_b], [Vt_b[0]])

        def rope(src_ps, src_bank, dst, c0, n, reads_extra, writes, f32copy=None):
            ACOPY(rf[:, 0:n], src_ps, [bankb[src_bank]], [rf_b])
            ACOPY(rb[:, 0:n], src_ps, [bankb[src_bank]], [rb_b])
            MM(banks[3][:, 0:n], prot[:], rb[:, 0:n], True, True, [rb_b] + CONST, [bankb[3]])
            TT(rt1[:, 0:n], rf[:, 0:n], rC[:, c0:c0 + n], ALU.mult, [rf_b, rope_b], [rt1_b])
            TT(rt2[:, 0:n], banks[3][:, 0:n], rS[:, c0:c0 + n], ALU.mult, [bankb[3], rope_b], [rt2_b])
            TT(dst, rt1[:, 0:n], rt2[:, 0:n], ALU.add, [rt1_b, rt2_b] + reads_extra, writes)
            if f32copy is not None:
                o_ap, lo, hi, wr = f32copy
                TT(o_ap, rt1[:, lo:hi], rt2[:, lo:hi], ALU.add, [rt1_b, rt2_b], wr)

        pa = ExitStack()
        kvn = P.sbuf("kvn", [128, 8, 512], BF16, pa)
        kvn_b = Buf("kvn")
        wk = P.sbuf("wk", [128, 8, 4, 128], BF16, pa)
        wv = P.sbuf("wv", [128, 8, 256], BF16, pa)
        wkv_b = Buf("wkv")
        rt = alloc_rms(pa, "l1")
        vlast = P.sbuf("vlast", [128, 256], F32, pa)
        vlast_b = Buf("vlast")
        print("  l1a scratch remaining:", nc.sbuf_bytes_remaining)
        wkpairs = []
        for kh_ in range(4):
            src_ = w_kv[:, kh_ * 64:(kh_ + 1) * 64].rearrange("(kc p) d -> p kc d", p=128)
            wkpairs += [(wk[:, :, kh_, 0:64], src_), (wk[:, :, kh_, 64:128], src_)]
        wkpairs.append((wv[:], w_kv[:, 256:512].rearrange("(kc p) n -> p kc n", p=128)))
        P.dma("pool", wkpairs, writes=[wkv_b], dbuf=wkv_b)
        for ti, t in enumerate(tiles):
            c0, n = t["c0"], t["n"]
            rstd = rms_stats([hT[:, c, c0:c0 + n] for c in range(8)], n, rt, [hT_b[ti]], 1.0 / D)
            for c in range(8):
                STT(xnT[:, c, c0:c0 + n], hT[:, c, c0:c0 + n], gcol(V_PRE1, c), rstd[:, 0:n], ALU.mult, ALU.mult,
                    [hT_b[ti], rt[4]] + CONST, [xnT_b[ti]])
                STT(kvn[:, c, 0:n], hT[:, c, c0:c0 + n], gcol(V_KV, c), rstd[:, 0:n], ALU.mult, ALU.mult,
                    [hT_b[ti], rt[4]] + CONST, [kvn_b])
            for kh in range(4):
                bk = kh % 2
                for kc in range(8):
                    MM(banks[bk][:, 0:n], wk[:, kc, kh, :], kvn[:, kc, 0:n], kc == 0, kc == 7, [wkv_b, kvn_b], [bankb[bk]])
                f32c = None
                if has_s and t["samp"]:
                    f32c = (ksf[:, kh, :], 0, NS, [kfl_b])
                elif has_s and (nblk - 1) in t["blks"]:
                    lo = (nblk - 1) * 128 - c0
                    f32c = (kfl[:, kh, :], lo, lo + 128, [kfl_b])
                rope(banks[bk][:, 0:n], bk, kT[:, kh, 128 + c0:128 + c0 + n], c0, n, [], [kT_b[1 + ti]], f32c)
            if not t["samp"]:
                for jj, j in enumerate(t["blks"]):
                    bk = 4 + jj % 2
                    for kc in range(8):
                        MM(banks[bk][:, 0:256], kvn[:, kc, jj * 128:(jj + 1) * 128], wv[:, kc, :], kc == 0, kc == 7, [wkv_b, kvn_b], [bankb[bk]])
                    ACOPY(Vt[:, 1 + j, :], banks[bk][:, 0:256], [bankb[bk]], [Vt_b[1 + j]])
                    if has_s and j == nblk - 1:
                        ACOPY(vlast[:], banks[bk][:, 0:256], [bankb[bk]], [vlast_b])
                        P.dma("sp", [(vp_d, vlast[:])], reads=[vlast_b], dbuf=vlast_b, is_out=True)
            else:
                for kc in range(8):
                    MM(banks[4][0:NS, 0:256], kvn[:, kc, 0:NS], wv[:, kc, :], kc == 0, kc == 7, [wkv_b, kvn_b], [bankb[4]])
                ACOPY(ostg[0:NS, 512:768], banks[4][0:NS, 0:256], [bankb[4]], [ostg_b])
        if has_s:
            for kh in range(4):
                MM(banks[5][:, kh * 128:(kh + 1) * 128], kfl[:, kh, :], ident_f[:], True, True, [kfl_b] + CONST, [bankb[5]])
            kp_st = P.sbuf("kp_st", [128, 256], F32, pa)
            kp_b = Buf("kp_st")
            ACOPY(kp_st[:].rearrange("p (kh d) -> p kh d", kh=4), banks[5][:, :].rearrange("p (kh e) -> p kh e", kh=4)[:, :, 0:64], [bankb[5]], [kp_b])
            P.dma("sp", [(kp_d, kp_st[:])], reads=[kp_b], dbuf=kp_b, is_out=True)
            for kh in range(4):
                MM(banks[6][0:NS, kh * 128:(kh + 1) * 128], ksf[:, kh, :], ident_f[:], True, True, [kfl_b] + CONST, [bankb[6]])
            ACOPY(ostg[0:NS, 0:256].rearrange("p (kh d) -> p kh d", kh=4), banks[6][0:NS, :].rearrange("p (kh e) -> p kh e", kh=4)[:, :, 0:64], [bankb[6]], [ostg_b])
            ksd_b = Buf("ksd")
            P.dma("sp", [(ks_d, ostg[0:NS, 0:256]), (vs_d, ostg[0:NS, 512:768])], reads=[ostg_b], writes=[ksd_b], dbuf=ostg_b, is_out=True)
        P.barrier()
        pa.close()
        if os.environ.get("KB_L1", "") == "a":
            return

        wq = P.sbuf("wq", [128, 8, 512], BF16, ph)
        wz = P.sbuf("wz", [128, 8, 512], BF16, ph)
        wq_b, wz_b = Buf("wq"), Buf("wz")
        qg = P.sbuf("qg", [128, 4, SBW], BF16, ph)
        zg = P.sbuf("zg", [128, 4, SBW], BF16, ph)
        qg_b = [Buf("qg%d" % i) for i in range(MAXT)]
        PT = [P.sbuf("PT%d" % i, [128, 2, 512], BF16, ph) for i in range(4)]
        PT_b = [Buf("PT") for _ in range(4)]
        rden = P.sbuf("rden", [128, 512], F32, ph)
        rden_b = Buf("rden")
        at = P.sbuf("at", [128, 512], F32, ph)
        at_b = Buf("at")
        if has_s:
            ckd = [P.sbuf("ckd%d" % i, [128, 4, 2, 64], BF16, ph) for i in range(2)]
            cvt = [P.sbuf("cvt%d" % i, [128, 256], BF16, ph) for i in range(2)]
            cc_b = [Buf("cc") for _ in range(2)]
            KcT = P.sbuf("KcT", [128, 512], BF16, ph)
            KcT_b = Buf("KcT")
            PTs = P.sbuf("PTs", [128, 32], BF16, ph)
            PTs_b = Buf("PTs")
        print("  l1b scratch remaining:", nc.sbuf_bytes_remaining)

        def ldq(kh):
            P.dma("pool", [(wq[:], w_in_b[:, kh * 512:(kh + 1) * 512].rearrange("(kc p) n -> p kc n", p=128))], writes=[wq_b], dbuf=wq_b)
            P.dma("pool", [(wz[:], w_in_b[:, 2048 + kh * 512:2048 + (kh + 1) * 512].rearrange("(kc p) n -> p kc n", p=128))], writes=[wz_b], dbuf=wz_b)

        ldq(0)
        blkcount = 0
        for kh in range(4):
            for ti, t in enumerate(tiles):
                c0, n = t["c0"], t["n"]
                for i in range(4):
                    bk = i % 2
                    for kc in range(8):
                        MM(banks[bk][:, 0:n], wq[:, kc, i * 128:(i + 1) * 128], xnT[:, kc, c0:c0 + n], kc == 0, kc == 7, [wq_b, xnT_b[ti]], [bankb[bk]])
                    f32c = None
                    rope(banks[bk][:, 0:n], bk, qg[:, i, c0:c0 + n], c0, n, [], [qg_b[ti]], None)
                    bz = 4 + i % 2
                    for kc in range(8):
                        MM(banks[bz][:, 0:n], wz[:, kc, i * 128:(i + 1) * 128], xnT[:, kc, c0:c0 + n], kc == 0, kc == 7, [wz_b, xnT_b[ti]], [bankb[bz]])
                    ACT(zg[:, i, c0:c0 + n], banks[bz][:, 0:n], AF.Silu, [bankb[bz]], [qg_b[ti]])
                if t["samp"]:
                    ACOPY(qsa[:, kh * 4:(kh + 1) * 4, :], qg[:, :, c0:c0 + NS], [qg_b[ti]], [qsa_b])
                    ACOPY(zsa[:, kh * 4:(kh + 1) * 4, :], zg[:, :, c0:c0 + NS], [qg_b[ti]], [qsa_b])
            if kh + 1 < 4:
                ldq(kh + 1)
            steps = [(ti, j) for ti, t in enumerate(tiles) for j in t["blks"] if not (first and j == 0)]

            def emit_scores(idx):
                ti, j = steps[idx]
                st_ = idx % 2
                pm_ = maskf4 if (first and j == 1) else maskp4
                prev_cols = slice(j * 128, (j + 1) * 128)
                cur_cols = slice((j + 1) * 128, (j + 2) * 128)
                blk = slice(j * 128, (j + 1) * 128)
                kprev_b = kT_b[0] if j == 0 else kT_b[1 + (j - 1) // 4]
                kcur_b = kT_b[1 + j // 4]
                for par in range(2):
                    pr = slice(par * 64, (par + 1) * 64)
                    b0, b1 = 2 * par, 2 * par + 1
                    pt = PT[st_ * 2 + par]
                    ptb = PT_b[st_ * 2 + par]
                    MM(banks[b0][:, :], kT[pr, kh, prev_cols], qg[pr, :, blk], True, True, [kprev_b, qg_b[ti]], [bankb[b0]])
                    MM(banks[b1][:, :], kT[pr, kh, cur_cols], qg[pr, :, blk], True, True, [kcur_b, qg_b[ti]], [bankb[b1]])
                    ACT(pt[:, 0, :], banks[b0][:, :], AF.Exp, [bankb[b0]], [ptb], scale=0.125)
                    ACT(pt[:, 1, :], banks[b1][:, :], AF.Exp, [bankb[b1]], [ptb], scale=0.125)
                    TT(pt[:, 0, :], pt[:, 0, :], pm_[:], ALU.mult, [ptb] + CONST, [ptb])
                    TT(pt[:, 1, :], pt[:, 1, :], maskc4[:], ALU.mult, [ptb] + CONST, [ptb])

            def emit_pv(idx):
                ti, j = steps[idx]
                st_ = idx % 2
                blk = slice(j * 128, (j + 1) * 128)
                bo = 4 + 2 * (idx % 2)
                bd = bo + 1
                for par in range(2):
                    pr = slice(par * 64, (par + 1) * 64)
                    pt = PT[st_ * 2 + par]
                    ptb = PT_b[st_ * 2 + par]
                    tp = (0, par * 64)
                    MM(banks[bo][pr, :], Vt[:, j, kh * 64:(kh + 1) * 64], pt[:, 0, :], True, False, [Vt_b[j], ptb], [bankb[bo]], tile_position=tp)
                    MM(banks[bo][pr, :], Vt[:, 1 + j, kh * 64:(kh + 1) * 64], pt[:, 1, :], False, True, [Vt_b[1 + j], ptb], [bankb[bo]], tile_position=tp)
                    MM(banks[bd][pr, :], ones_b[:, 0:64], pt[:, 0, :], True, False, [ptb] + CONST, [bankb[bd]], tile_position=tp)
                    MM(banks[bd][pr, :], ones_b[:, 0:64], pt[:, 1, :], False, False, [ptb] + CONST, [bankb[bd]], tile_position=tp)
                    MM(banks[bd][pr, :], ones_b[0:1, 0:64], sinkrow[0:1, kh, par, :, :].rearrange("p i q -> p (i q)"), False, True, CONST, [bankb[bd]], tile_position=tp)
                ACT(rden[:], banks[bd][:, :], AF.Ln, [bankb[bd]], [rden_b])
                ACT(rden[:], rden[:], AF.Exp, [rden_b], [rden_b], scale=-1.0)
                TT(at[:], banks[bo][:, :], rden[:], ALU.mult, [bankb[bo], rden_b], [at_b])
                TT(ogT[:, kh * 4:(kh + 1) * 4, blk], at[:].rearrange("p (i q) -> p i q", i=4), zg[:, :, blk], ALU.mult,
                   [at_b, qg_b[ti]], [og_b[kh * 4 + i][ti] for i in range(4)])

            for idx in range(len(steps) + 1):
                if idx < len(steps):
                    emit_scores(idx)
                if idx >= 1:
                    emit_pv(idx - 1)
        lastj = nblk - 1
        ACOPY(kT_halo[:], kT[:, :, 128 + lastj * 128:128 + (lastj + 1) * 128], [kT_b[1 + lastj // 4]], [halo_b])
        ACOPY(V_halo[:], Vt[:, 1 + lastj, :], [Vt_b[1 + lastj]], [halo_b])
        if has_s and os.environ.get("KB_L1", "") != "b":
            ti = len(tiles) - 1
            c0 = tiles[ti]["c0"]
            for s_ in range(NS):
                a = s_ % 2
                ksrc = ck[s_].rearrange("j (kh d) -> j kh d", kh=4)
                P.dma("pool", [(ckd[a][:, :, 0, :], ksrc), (ckd[a][:, :, 1, :], ksrc), (cvt[a][:], cv[s_])],
                      writes=[cc_b[a]], dbuf=cc_b[a])
                MM(banks[5][0:1, 0:256], ident_f[0:NS, s_:s_ + 1], ostg[0:NS, 0:256], True, True, [ostg_b] + CONST, [bankb[5]])
                MM(banks[5][0:1, 256:512], ident_f[0:NS, s_:s_ + 1], ostg[0:NS, 512:768], True, True, [ostg_b] + CONST, [bankb[5]])
                ACOPY(ckd[a][0:1, :, 0, :], banks[5][0:1, 0:256].rearrange("p (kh d) -> p kh d", kh=4), [bankb[5]], [cc_b[a]])
                ACOPY(ckd[a][0:1, :, 1, :], banks[5][0:1, 0:256].rearrange("p (kh d) -> p kh d", kh=4), [bankb[5]], [cc_b[a]])
                ACOPY(cvt[a][0:1, :], banks[5][0:1, 256:512], [bankb[5]], [cc_b[a]])
                for kh in range(4):
                    MM(banks[0][:, kh * 128:(kh + 1) * 128], ckd[a][:, kh, :, :].rearrange("p c d -> p (c d)"), ident_b[:], True, True, [cc_b[a]] + CONST, [bankb[0]])
                ACOPY(KcT[:], banks[0][:, :], [bankb[0]], [KcT_b])
                for par in range(2):
                    pr = slice(par * 64, (par + 1) * 64)
                    qb_ = 1 if par == 0 else 4
                    for kh in range(4):
                        MM(banks[qb_][:, kh * 4:(kh + 1) * 4], KcT[pr, kh * 128:(kh + 1) * 128], qsa[pr, kh * 4:(kh + 1) * 4, s_], True, True, [KcT_b, qsa_b], [bankb[qb_]])
                    ACT(PTs[:, par * 16:(par + 1) * 16], banks[qb_][:, 0:16], AF.Exp, [bankb[qb_]], [PTs_b], scale=0.125)
                for kh in range(4):
                    for par in range(2):
                        pr = slice(par * 64, (par + 1) * 64)
                        col = par * 16 + kh * 4
                        tp = (0, par * 64)
                        MM(banks[2][pr, kh * 4:(kh + 1) * 4], cvt[a][:, kh * 64:(kh + 1) * 64], PTs[:, col:col + 4], True, True, [cc_b[a], PTs_b], [bankb[2]], tile_position=tp)
                        MM(banks[3][pr, kh * 4:(kh + 1) * 4], ones_b[:, 0:64], PTs[:, col:col + 4], True, False, [PTs_b] + CONST, [bankb[3]], tile_position=tp)
                        MM(banks[3][pr, kh * 4:(kh + 1) * 4], ones_b[0:1, 0:64], sinkrow[0:1, kh, par, :, 0:1].rearrange("p i q -> p (i q)"), False, True, CONST, [bankb[3]], tile_position=tp)
                P.op("dve", (lambda o, i_: (lambda e: e.reciprocal(o, i_)))(rden[:, 0:16], banks[3][:, 0:16]), [bankb[3]], [rden_b])
                TT(at[:, 0:16], banks[2][:, 0:16], rden[:, 0:16], ALU.mult, [bankb[2], rden_b], [at_b])
                TT(ogT[:, :, c0 + s_], at[:, 0:16], zsa[:, :, s_], ALU.mult, [at_b, qsa_b], [og_b[h_][ti] for h_ in range(16)])

    def stage_y(sb, ph):
        tiles = sb_tiles(sb)
        first = sb["t0"] == 0
        yst = [P.sbuf("yst%d" % i, [128, D], F32, ph) for i in range(3)]
        yst_b = [Buf("yst") for _ in range(3)]
        k = 0
        for ti, t in enumerate(tiles):
            c0 = t["c0"]
            if t["samp"]:
                a = k % 3
                k += 1
                for half in range(2):
                    bk = half
                    for cc in range(4):
                        c = half * 4 + cc
                        MM(banks[bk][0:NS, cc * 128:(cc + 1) * 128], hT[:, c, c0:c0 + NS], ident_f[:], True, True, [hT_b[ti]] + CONST, [bankb[bk]])
                    ACOPY(yst[a][0:NS, half * 512:(half + 1) * 512], banks[bk][0:NS, :], [bankb[bk]], [yst_b[a]])
                P.dma("sp", [(y_d[2048:2048 + NS, :], yst[a][0:NS, :])], reads=[yst_b[a]], dbuf=yst_b[a], is_out=True)
                continue
            for j in t["blks"]:
                if first and j == 0:
                    continue
                a = k % 3
                k += 1
                for half in range(2):
                    bk = (2 * k + half) % 4
                    for cc in range(4):
                        c = half * 4 + cc
                        MM(banks[bk][:, cc * 128:(cc + 1) * 128], hT[:, c, j * 128:(j + 1) * 128], ident_f[:], True, True, [hT_b[ti]] + CONST, [bankb[bk]])
                    if half == 0:
                        ACOPY(yst[a][:, 0:512], banks[bk][:, :], [bankb[bk]], [yst_b[a]])
                    else:
                        TCOPY(yst[a][:, 512:1024], banks[bk][:, :], [bankb[bk]], [yst_b[a]])
                row = (j - 1) * 128 if first else (8 + j) * 128
                P.dma("sp", [(y_d[row:row + 128, :], yst[a][:])], reads=[yst_b[a]], dbuf=yst_b[a], is_out=True)

    def dump_h(sb):
        col0 = sb["t0"]
        for ti, t in enumerate(sb_tiles(sb)):
            c0, n = t["c0"], t["n"]
            g0 = (NMAIN if t["samp"] else col0 + c0)
            P.dma("sp", [(dbg_d[:, :, g0:g0 + n], hT[:, :, c0:c0 + n])], reads=[hT_b[ti]], dbuf=hT_b[ti], is_out=True)

    import os
    sel = os.environ.get("KB_SBS")
    for sbi, sb in enumerate(SBS):
        if sel is not None and str(sbi) not in sel.split(","):
            continue
        if sb["kind"] == "P" and stop_after in ("L0nopre", "LOAD"):
            continue
        ph = ExitStack()
        stage_load(sb, ph)
        stage_prenorm(sb, ph, V_PRE0, xnT, xnT_b)
        P.barrier()
        ph.close()
        if stop_after == "LOAD":
            dump_h(sb)
            P.barrier()
            continue
        ph = ExitStack()
        if sb["kind"] == "P":
            stage_l0u(sb, ph)
        else:
            stage_l0(sb, ph)
        P.barrier()
        ph.close()
        if sb["kind"] == "P":
            continue
        if os.environ.get("KB_OUT", "1") == "1":
            ph = ExitStack()
            stage_out(sb, ph, 0, w_out_a)
            P.barrier()
            ph.close()
        if stop_after in ("L0", "L0nopre"):
            dump_h(sb)
            P.barrier()
            continue
        ph = ExitStack()
        stage_l1(sb, sbi, ph)
        P.barrier()
        ph.close()
        if os.environ.get("KB_OUT1", "1") == "1":
            ph = ExitStack()
            stage_out(sb, ph, 1, w_out_b)
            P.barrier()
            ph.close()
        if stop_after == "L1":
            dump_h(sb)
            P.barrier()
        if os.environ.get("KB_Y", "1") == "1":
            ph = ExitStack()
            stage_y(sb, ph)
            P.barrier()
            ph.close()

    P.emit()
    print("stats:", P.stats)
    return nc


def _consts(half):
    ident = np.eye(128, dtype=np.float32)
    s = np.arange(128)[:, None]
    t = np.arange(128)[None, :]
    mtri = (t >= s).astype(np.uint32)
    maskc = (s <= t).astype(np.float32)
    maskp = (s > t).astype(np.float32)
    maskf = maskp.copy() if half == 1 else np.zeros((128, 128), np.float32)
    prot = np.zeros((128, 128), np.float32)
    for base in (0, 64):
        for i in range(8):
            prot[base + i + 8, base + i] = 1.0
            prot[base + i, base + i + 8] = 1.0
    pos = np.concatenate([half * 2048 - 128 + np.arange(NMAIN), np.full(NS, PAST_LEN)]).astype(np.float32)
    inv = (ROPE_THETA ** (-np.arange(0, 16, 2, dtype=np.float32) / 16)).astype(np.float32)
    ang = pos[None, :] * inv[:, None]
    cos = np.cos(ang).astype(np.float32)
    sin = np.sin(ang).astype(np.float32)
    ropec = np.ones((128, TTOT), np.float32)
    ropes = np.zeros((128, TTOT), np.float32)
    for base in (0, 64):
        ropec[base:base + 8] = cos
        ropec[base + 8:base + 16] = cos
        ropes[base:base + 8] = -sin
        ropes[base + 8:base + 16] = sin
    return dict(ident=ident, mtri=mtri, maskc=maskc, maskp=maskp, maskf=maskf, prot=prot, ropec=ropec, ropes=ropes)


def _col(v):
    v = np.asarray(v, np.float32).reshape(-1, 128)
    return np.ascontiguousarray(v.T)


def make_in_maps(inp):
    f = lambda a: np.ascontiguousarray(np.asarray(a, dtype=np.float32))
    xpr, xsm = f(inp["x_prompt"]), f(inp["x_sample"])
    ppr, psa = f(inp["p_prompt"]), f(inp["p_sample"])
    st, ck, cv = f(inp["state_hgrn"]), f(inp["cache_k"]), f(inp["cache_v"])
    vecs = np.concatenate([
        _col(inp["pre_norm_g"][0]), _col(inp["pre_norm_g"][1]), _col(inp["post_norm_g"][0]), _col(inp["post_norm_g"][1]),
        _col(inp["kv_norm_g"]), _col(inp["onorm_a"][0]), _col(inp["lb_logits"][0]), _col(inp["lb_logits"][1])], axis=1)
    assert vecs.shape == (128, NVEC)
    shared = dict(
        w_in_a=f(inp["w_in_a"][0]), w_out_a=f(inp["w_out_a"][0]), w_kv=f(inp["w_kv"]), w_in_b=f(inp["w_in_b"][0]),
        w_out_b=f(inp["w_out_b"][0]), w_pe=f(inp["w_pe"]), w_pg=f(inp["w_pg"]), vecs=np.ascontiguousarray(vecs),
        sinks=f(inp["sinks"]).reshape(1, 32))
    maps = []
    for c in range(8):
        b, half = c // 2, c % 2
        xm = np.zeros((NMAIN, D), np.float32)
        xp = np.zeros((NPRE, D), np.float32)
        pm = np.zeros((2, NMAIN, 256), np.float32)
        if half == 0:
            xm[128:] = xpr[b, 0:2048]
            pm[:, 128:] = ppr[:, b, 0:2048]
        else:
            xm[:] = xpr[b, 1920:4096]
            pm[:] = ppr[:, b, 1920:4096]
            xp[:] = xpr[b, 0:1920]
        m = dict(shared)
        m.update(_consts(half))
        m.update(xm=xm, xp=xp, xs=np.ascontiguousarray(xsm[c * NS:(c + 1) * NS, 0]), pm=pm,
                 psm=np.ascontiguousarray(psa[:, c * NS:(c + 1) * NS, 0]),
                 st_in=np.ascontiguousarray(st[0, c * NS:(c + 1) * NS]),
                 ck=np.ascontiguousarray(ck[c * NS:(c + 1) * NS].reshape(NS, 128, 256)),
                 cv=np.ascontiguousarray(cv[c * NS:(c + 1) * NS].reshape(NS, 128, 256)))
        maps.append(m)
    return maps


def assemble(results):
    y_p = np.zeros((4, 4096, D), np.float32)
    y_s = np.zeros((128, 1, D), np.float32)
    st_p = np.zeros((1, 4, 16, 128, 128), np.float32)
    st_s = np.zeros((1, 128, 16, 128, 128), np.float32)
    k_p = np.zeros((4, 128, 4, 64), np.float32)
    v_p = np.zeros((4, 128, 4, 64), np.float32)
    k_s = np.zeros((128, 1, 4, 64), np.float32)
    v_s = np.zeros((128, 1, 4, 64), np.float32)
    for c in range(8):
        r = results[c]
        b, half = c // 2, c % 2
        y_p[b, half * 2048:(half + 1) * 2048] = r["y"][0:2048]
        y_s[c * NS:(c + 1) * NS, 0] = r["y"][2048:2048 + NS]
        st_s[0, c * NS:(c + 1) * NS] = r["st_s"]
        k_s[c * NS:(c + 1) * NS, 0] = r["ks"].reshape(NS, 4, 64)
        v_s[c * NS:(c + 1) * NS, 0] = r["vs"].reshape(NS, 4, 64)
        if half == 1:
            st_p[0, b] = r["st_p"]
            k_p[b] = r["kp"].reshape(128, 4, 64)
            v_p[b] = r["vp"].reshape(128, 4, 64)
    return (y_p, y_s, st_p, st_s, k_p, v_p, k_s, v_s)


def kernel(**inputs):
    nc = build_program()
    in_maps = make_in_maps(inputs)
    res = run_bass_kernel_spmd(nc, in_maps, core_ids=list(range(8)))
    return assemble(res.results)
```

```python
import numpy as np
from contextlib import ExitStack
import concourse.bass as bass
import concourse.mybir as mybir
from concourse.bass_utils import run_bass_kernel_spmd

F32 = mybir.dt.float32
BF16 = mybir.dt.bfloat16
U32 = mybir.dt.uint32
ALU = mybir.AluOpType
AF = mybir.ActivationFunctionType

COMPUTE = ("pe", "act", "dve")
QUEUES = ("sp", "pool")
ALLENG = COMPUTE + QUEUES


class Buf:
    __slots__ = ("name", "last_w", "readers", "sem", "keep", "excl")

    def __init__(self, name="", keep=False, excl=False):
        self.excl = excl
        self.name = name
        self.last_w = None
        self.readers = {}
        self.sem = None
        self.keep = keep


class Carrier:
    __slots__ = ("cnt", "handle", "q")

    def __init__(self):
        self.cnt = 0
        self.handle = None
        self.q = None


class Op:
    __slots__ = ("eng", "fn", "deps", "signal", "sigval", "is_dma", "carrier", "dval")

    def __init__(self, eng, fn, is_dma=False):
        self.eng = eng
        self.fn = fn
        self.deps = []
        self.signal = False
        self.sigval = 0
        self.is_dma = is_dma
        self.carrier = None
        self.dval = 0


class Prog:
    def __init__(self, nc):
        self.nc = nc
        self.ops = []
        self.es = ExitStack()
        self.carriers = []
        self.free_carriers = {"sp": [], "pool": []}
        self.active_bufs = []
        self.out_ops = []
        self.last = {e: None for e in ALLENG}
        self.bar = None
        self.bar_pending = set()
        self.dma_since_bar = []
        self.nalloc = 0
        self.nuniq = 0

    def sbuf(self, name, shape, dtype, es=None):
        self.nalloc += 1
        return (es or self.es).enter_context(self.nc.sbuf_tensor("s%d_%s" % (self.nalloc, name), list(shape), dtype))

    def psum(self, name, shape, dtype=F32):
        return self.es.enter_context(self.nc.psum_tensor("p_" + name, list(shape), dtype))

    def _adddep(self, op, d):
        if d is op or d is None:
            return
        for x in op.deps:
            if x is d:
                return
        op.deps.append(d)
        d.signal = True

    def _deps(self, op, reads, writes):
        ex = [b for b in reads if b.excl and b not in writes]
        if ex:
            reads = [b for b in reads if not b.excl]
            writes = list(writes) + ex
        for b in reads:
            d = b.last_w
            if d is not None:
                if (not d.is_dma) and (not op.is_dma) and d.eng == op.eng and op.eng == "pe":
                    pass
                else:
                    self._adddep(op, d)
        for b in writes:
            cands = [b.last_w] + list(b.readers.values())
            for d in cands:
                if d is None:
                    continue
                if (not d.is_dma) and (not op.is_dma) and d.eng == op.eng:
                    continue
                self._adddep(op, d)
        if op.eng in self.bar_pending:
            self.bar_pending.discard(op.eng)
            for d in self.bar:
                if d.is_dma or d.eng != op.eng:
                    self._adddep(op, d)
        for b in reads:
            if op.is_dma:
                self.nuniq += 1
                b.readers["dma%d" % self.nuniq] = op
            else:
                b.readers[op.eng] = op
        for b in writes:
            b.last_w = op
            b.readers = {}
        self.last[op.eng] = op

    def op(self, eng, fn, reads=(), writes=()):
        o = Op(eng, fn)
        self._deps(o, reads, writes)
        self.ops.append(o)
        return o

    def dma(self, q, pairs, reads=(), writes=(), dbuf=None, is_out=False):
        if dbuf.sem is None:
            if self.free_carriers[q]:
                dbuf.sem = self.free_carriers[q].pop()
            else:
                dbuf.sem = Carrier()
                dbuf.sem.q = q
                self.carriers.append(dbuf.sem)
            self.active_bufs.append(dbuf)
        c = dbuf.sem
        assert c.q == q, "a DMA buffer must stay on one queue type"
        o = Op(q, pairs, is_dma=True)
        c.cnt += 16 * len(pairs)
        o.carrier = c
        o.dval = c.cnt
        o.signal = True
        self._deps(o, reads, writes)
        self.ops.append(o)
        self.dma_since_bar.append(o)
        if is_out:
            self.out_ops.append(o)
        return o

    def barrier(self):
        ops = [o for o in self.last.values() if o is not None and not o.is_dma]
        latest = {}
        for o in self.dma_since_bar:
            latest[id(o.carrier)] = o
        ops += list(latest.values())
        self.dma_since_bar = []
        self.bar = ops
        self.bar_pending = set(ALLENG)
        keep = []
        for b in self.active_bufs:
            if b.keep:
                keep.append(b)
            else:
                self.free_carriers[b.sem.q].append(b.sem)
                b.sem = None
        self.active_bufs = keep

    def emit(self):
        nc = self.nc
        es = self.es
        sems = {}
        for e in COMPUTE:
            sems[e] = es.enter_context(nc.semaphore("sem_" + e))
        for i, c in enumerate(self.carriers):
            c.handle = es.enter_context(nc.semaphore("dsem%d" % i))
        cnt = {e: 0 for e in COMPUTE}
        for o in self.ops:
            if o.is_dma:
                continue
            if o.signal:
                cnt[o.eng] += 1
                o.sigval = cnt[o.eng]
        by_eng = {e: [] for e in ALLENG}
        for o in self.ops:
            by_eng[o.eng].append(o)
        self.stats = {e: len(v) for e, v in by_eng.items()}
        self.stats["sig"] = dict(cnt)
        self.stats["dma_sems"] = len(self.carriers)
        final_waits = {}
        for o in self.out_ops:
            k = id(o.carrier)
            if k not in final_waits or final_waits[k][1] < o.dval:
                final_waits[k] = (o.carrier.handle, o.dval)

        def stream(ename, e):
            known = {}
            nwait = 0
            for o in by_eng[ename]:
                need = {}
                for d in o.deps:
                    if d.is_dma:
                        s, v = d.carrier.handle, d.dval
                    else:
                        s, v = sems[d.eng], d.sigval
                    k = id(s)
                    if k not in need or need[k][1] < v:
                        need[k] = (s, v)
                for k, (s, v) in need.items():
                    if known.get(k, 0) >= v:
                        continue
                    known[k] = v
                    e.wait_ge(s, v)
                    nwait += 1
                if o.is_dma:
                    for (out_ap, in_ap) in o.fn:
                        e.dma_start(out=out_ap, in_=in_ap).then_inc(o.carrier.handle, 16)
                else:
                    ins = o.fn(e)
                    if o.signal:
                        ins.then_inc(sems[ename], 1)
            if ename == "sp":
                for s, v in final_waits.values():
                    e.wait_ge(s, v)
            self.stats[ename + "_waits"] = nwait

        with nc.Block() as block:
            @block.tensor
            def _(e):
                stream("pe", e)

            @block.scalar
            def _(e):
                stream("act", e)

            @block.vector
            def _(e):
                stream("dve", e)

            @block.gpsimd
            def _(e):
                stream("pool", e)

            @block.sync
            def _(e):
                stream("sp", e)
        es.close()


D = 1024
NMAIN = 2176
NPRE = 1920
NS = 16
TTOT = NMAIN + NS
SBW = 1152
EPS = 1e-6
PAST_LEN = 16384
ROPE_THETA = 500000.0

V_PRE0, V_PRE1, V_POST0, V_POST1, V_KV, V_ON, V_LB0, V_LB1 = 0, 8, 16, 24, 32, 40, 56, 72
NVEC = 88

SBS = [
    dict(kind="P", t0=0, nblk=8, ns=0),
    dict(kind="P", t0=1024, nblk=7, ns=0),
    dict(kind="M", t0=0, nblk=9, ns=0),
    dict(kind="M", t0=1152, nblk=8, ns=NS),
]


def sb_tiles(sb):
    tiles = []
    nb = sb["nblk"]
    j = 0
    while j < nb:
        k = min(4, nb - j)
        tiles.append(dict(c0=j * 128, n=k * 128, blks=list(range(j, j + k)), samp=False))
        j += k
    if sb["ns"]:
        tiles.append(dict(c0=nb * 128, n=sb["ns"], blks=[], samp=True))
    return tiles


def build_program(stop_after=None):
    nc = bass.Bass("TRN2", target_bir_lowering=False)
    P = Prog(nc)

    def din(name, shape, dt=F32):
        return nc.dram_tensor(name, list(shape), dt, kind="ExternalInput").ap()

    def dout(name, shape, dt=F32):
        return nc.dram_tensor(name, list(shape), dt, kind="ExternalOutput").ap()

    xm = din("xm", [NMAIN, D])
    xp = din("xp", [NPRE, D])
    xs = din("xs", [NS, D])
    pm = din("pm", [2, NMAIN, 256])
    psm = din("psm", [2, NS, 256])
    st_in = din("st_in", [NS, 16, 128, 128])
    ck = din("ck", [NS, 128, 256])
    cv = din("cv", [NS, 128, 256])
    w_in_a = din("w_in_a", [D, 8192])
    w_out_a = din("w_out_a", [2048, D])
    w_kv = din("w_kv", [D, 512])
    w_in_b = din("w_in_b", [D, 4096])
    w_out_b = din("w_out_b", [2048, D])
    w_pe = din("w_pe", [2, 256, D])
    w_pg = din("w_pg", [2, D, D])
    vecs_d = din("vecs", [128, NVEC])
    sinks_d = din("sinks", [1, 32])
    ident_d = din("ident", [128, 128])
    mtri_d = din("mtri", [128, 128], U32)
    maskc_d = din("maskc", [128, 128])
    maskp_d = din("maskp", [128, 128])
    maskf_d = din("maskf", [128, 128])
    prot_d = din("prot", [128, 128])
    ropec_d = din("ropec", [128, TTOT])
    ropes_d = din("ropes", [128, TTOT])

    y_d = dout("y", [2048 + NS, D])
    stp_d = dout("st_p", [16, 128, 128])
    sts_d = dout("st_s", [NS, 16, 128, 128])
    kp_d = dout("kp", [128, 256])
    vp_d = dout("vp", [128, 256])
    ks_d = dout("ks", [NS, 256])
    vs_d = dout("vs", [NS, 256])
    dbg_d = dout("dbg", [128, 8, TTOT]) if stop_after else None

    hT = P.sbuf("hT", [128, 8, SBW], F32)
    xnT = P.sbuf("xnT", [128, 8, SBW], BF16)
    ogT = P.sbuf("ogT", [128, 16, SBW], BF16)
    pT = P.sbuf("pT", [128, 2, 2, SBW], BF16)
    Sst = P.sbuf("Sst", [128, 16, 128], F32)
    ident_f = P.sbuf("ident_f", [128, 128], F32)
    ident_b = P.sbuf("ident_b", [128, 128], BF16)
    ones_b = P.sbuf("ones_b", [128, 128], BF16)
    ones_f = P.sbuf("ones_f", [128, 128], F32)
    epsc = P.sbuf("epsc", [128, 1], F32)
    mtri = P.sbuf("mtri", [128, 128], U32)
    maskc = P.sbuf("maskc", [128, 128], BF16)
    maskp = P.sbuf("maskp", [128, 128], BF16)
    maskf = P.sbuf("maskf", [128, 128], BF16)
    prot = P.sbuf("prot", [128, 128], BF16)
    maskc4 = P.sbuf("maskc4", [128, 512], BF16)
    maskp4 = P.sbuf("maskp4", [128, 512], BF16)
    maskf4 = P.sbuf("maskf4", [128, 512], BF16)
    vecs = P.sbuf("vecs", [128, NVEC], F32)
    lbv = P.sbuf("lbv", [128, 16], F32)
    omlv = P.sbuf("omlv", [128, 16], F32)
    nomlv = P.sbuf("nomlv", [128, 16], F32)
    lnomlv = P.sbuf("lnomlv", [128, 16], F32)
    onec = P.sbuf("onec", [128, 1], F32)
    sinkx = P.sbuf("sinkx", [1, 32], F32)
    sinkrow = P.sbuf("sinkrow", [1, 4, 2, 4, 128], BF16)
    kT_halo = P.sbuf("kT_halo", [128, 4, 128], BF16)
    V_halo = P.sbuf("V_halo", [128, 256], BF16)

    MAXT = 3
    hT_b = [Buf("hT%d" % i, keep=True) for i in range(MAXT)]
    xnT_b = [Buf("xnT%d" % i) for i in range(MAXT)]
    pT_b = [Buf("pT%d" % i) for i in range(MAXT)]
    og_b = [[Buf("og%d_%d" % (h, i)) for i in range(MAXT)] for h in range(16)]
    S_b = [Buf("S%d" % h) for h in range(16)]
    const_b = Buf("const", keep=True)
    halo_b = Buf("halo")
    stp_cb = Buf("stp_carrier", keep=True)

    banks = [P.psum("bank%d" % i, [128, 512]) for i in range(8)]
    bankb = [Buf("bank%d" % i, excl=True) for i in range(8)]
    bq = [[bankb[i]] * 4 for i in range(8)]

    def bqs(i, q0=0, q1=4):
        return [bankb[i]]


    def MM(out, lhsT, rhs, start, stop, reads, writes, **kw):
        P.op("pe", lambda e: e.matmul(out, lhsT, rhs, start=start, stop=stop, **kw), reads, writes)

    def ACT(out, in_, func, reads, writes, bias=None, scale=None):
        kw = {}
        if bias is not None:
            kw["bias"] = bias
        if scale is not None:
            kw["scale"] = scale
        P.op("act", lambda e: e.activation(out, in_, func, **kw), reads, writes)

    def ACOPY(out, in_, reads, writes):
        P.op("act", lambda e: e.copy(out, in_), reads, writes)

    def TCOPY(out, in_, reads, writes):
        P.op("dve", lambda e: e.tensor_copy(out, in_), reads, writes)

    def TT(out, in0, in1, op, reads, writes):
        P.op("dve", lambda e: e.tensor_tensor(out, in0, in1, op), reads, writes)

    def TS(out, in0, s1, s2, op0, op1, reads, writes):
        if s2 is None:
            P.op("dve", lambda e: e.tensor_scalar(out, in0, s1, None, op0), reads, writes)
        else:
            P.op("dve", lambda e: e.tensor_scalar(out, in0, s1, s2, op0, op1), reads, writes)

    def STT(out, in0, sc, in1, op0, op1, reads, writes):
        P.op("dve", lambda e: e.scalar_tensor_tensor(out, in0, sc, in1, op0, op1), reads, writes)

    def SCAN(out, d0, d1, reads, writes):
        P.op("dve", lambda e: e.tensor_tensor_scan(out, d0, d1, 0.0, ALU.mult, ALU.add), reads, writes)

    def CPRED(out, mask, data, reads, writes):
        P.op("dve", lambda e: e.copy_predicated(out, mask, data), reads, writes)

    def MEMSET(ap, val, writes):
        P.op("dve", lambda e: e.memset(ap, val), (), writes)

    print("sbuf remaining after persistent:", nc.sbuf_bytes_remaining)

    P.dma("sp", [(ident_f[:], ident_d)], writes=[const_b], dbuf=const_b)
    P.dma("sp", [(mtri[:], mtri_d)], writes=[const_b], dbuf=const_b)
    P.dma("sp", [(vecs[:], vecs_d)], writes=[const_b], dbuf=const_b)
    P.dma("sp", [(sinkx[:], sinks_d)], writes=[const_b], dbuf=const_b)
    cb2 = Buf("const2", keep=True)
    P.dma("pool", [(ident_b[:], ident_d), (maskc[:], maskc_d), (maskp[:], maskp_d), (maskf[:], maskf_d),
                   (prot[:], prot_d)], writes=[cb2], dbuf=cb2)
    cb3 = Buf("const3")
    MEMSET(ones_b[:], 1.0, [cb3])
    MEMSET(ones_f[:], 1.0, [cb3])
    MEMSET(epsc[:], EPS, [cb3])
    MEMSET(Sst[:], 0.0, S_b)
    MEMSET(ogT[:], 0.0, [b for hb in og_b for b in hb])
    MEMSET(kT_halo[:], 0.0, [halo_b])
    MEMSET(V_halo[:], 0.0, [halo_b])
    TT(lbv[:], vecs[:, V_LB0:V_LB0 + 16], vecs[:, V_LB1:V_LB1 + 16], ALU.subtract, [const_b], [cb3])
    ACT(lbv[:], lbv[:], AF.Sigmoid, [cb3], [cb3])
    TS(omlv[:], lbv[:], -1.0, 1.0, ALU.mult, ALU.add, [cb3], [cb3])
    TS(nomlv[:], lbv[:], 1.0, -1.0, ALU.mult, ALU.add, [cb3], [cb3])
    MEMSET(onec[:], 1.0, [cb3])
    ACT(lnomlv[:], omlv[:], AF.Ln, [cb3], [cb3])
    ACT(sinkx[:], sinkx[:], AF.Exp, [const_b], [cb3])
    sx4 = sinkx[:].rearrange("p (kh i par o) -> p kh par i o", kh=4, i=4, par=2, o=1)
    TCOPY(sinkrow[:], sx4.to_broadcast([1, 4, 2, 4, 128]), [cb3], [cb3])
    for m4, m1 in ((maskc4, maskc), (maskp4, maskp), (maskf4, maskf)):
        for i in range(4):
            TCOPY(m4[:, i * 128:(i + 1) * 128], m1[:], [cb2], [cb3])
    CONST = [const_b, cb2, cb3]

    def gcol(base, c):
        return vecs[:, base + c:base + c + 1]

    def stage_load(sb, ph):
        kind = sb["kind"]
        src = xp if kind == "P" else xm
        tiles = sb_tiles(sb)
        NX = 6
        xin = [P.sbuf("xin%d" % i, [128, D], F32, ph) for i in range(NX)]
        xin_b = [Buf("xin%d" % i) for i in range(NX)]
        pin = [P.sbuf("pin%d" % i, [128, 2, 256], F32, ph) for i in range(NX)] if kind == "M" else None
        pin_b = [Buf("pin%d" % i) for i in range(NX)]
        cnt = 0
        ev = 0
        for ti, t in enumerate(tiles):
            c0, n = t["c0"], t["n"]
            if not t["samp"]:
                slots = []
                for j in t["blks"]:
                    s = cnt % NX
                    cnt += 1
                    r0 = sb["t0"] + j * 128
                    P.dma("sp", [(xin[s][:], src[r0:r0 + 128, :])], writes=[xin_b[s]], dbuf=xin_b[s])
                    if kind == "M":
                        P.dma("sp", [(pin[s][:], pm[:, r0:r0 + 128, :].rearrange("l t f -> t l f"))],
                              writes=[pin_b[s]], dbuf=pin_b[s])
                    slots.append(s)
                nb = len(slots)
                for c in range(8):
                    bk = c % 2
                    for jj, s in enumerate(slots):
                        MM(banks[bk][:, jj * 128:(jj + 1) * 128], xin[s][:, c * 128:(c + 1) * 128], ident_f[:], True, True,
                           [xin_b[s]] + CONST, [bq[bk][jj]])
                    ev += 1
                    if ev % 2 == 0:
                        ACOPY(hT[:, c, c0:c0 + n], banks[bk][:, 0:n], bqs(bk, 0, nb), [hT_b[ti]])
                    else:
                        TCOPY(hT[:, c, c0:c0 + n], banks[bk][:, 0:n], bqs(bk, 0, nb), [hT_b[ti]])
                if kind == "M":
                    for l in range(2):
                        for pc in range(2):
                            bk = 2 + (l * 2 + pc) % 2
                            for jj, s in enumerate(slots):
                                MM(banks[bk][:, jj * 128:(jj + 1) * 128], pin[s][:, l, pc * 128:(pc + 1) * 128], ident_f[:], True, True,
                                   [pin_b[s]] + CONST, [bq[bk][jj]])
                            ACOPY(pT[:, l, pc, c0:c0 + n], banks[bk][:, 0:n], bqs(bk, 0, nb), [pT_b[ti]])
            else:
                s = cnt % NX
                cnt += 1
                P.dma("sp", [(xin[s][0:NS, :], xs)], writes=[xin_b[s]], dbuf=xin_b[s])
                P.dma("sp", [(pin[s][0:NS, :, :], psm.rearrange("l t f -> t l f"))], writes=[pin_b[s]], dbuf=pin_b[s])
                for c in range(8):
                    MM(banks[0][:, c * NS:(c + 1) * NS], xin[s][0:NS, c * 128:(c + 1) * 128], ident_f[0:NS, 0:NS], True, True,
                       [xin_b[s]] + CONST, [bq[0][0]])
                ACOPY(hT[:, :, c0:c0 + NS], banks[0][:, 0:8 * NS].rearrange("p (c n) -> p c n", c=8), [bq[0][0]], [hT_b[ti]])
                for l in range(2):
                    for pc in range(2):
                        q = l * 2 + pc
                        MM(banks[1][:, q * NS:(q + 1) * NS], pin[s][0:NS, l, pc * 128:(pc + 1) * 128], ident_f[0:NS, 0:NS], True, True,
                           [pin_b[s]] + CONST, [bq[1][0]])
                ACOPY(pT[:, :, :, c0:c0 + NS], banks[1][:, 0:4 * NS].rearrange("p (l c n) -> p l c n", l=2, c=2), [bq[1][0]], [pT_b[ti]])

    def rms_stats(srcs, n, rt, reads, invd):
        sq, sq_b, lnv, rstd, r_b = rt
        ssb = 7
        for c, s_ap in enumerate(srcs):
            k = c % 2
            ACT(sq[k][:, 0:n], s_ap, AF.Square, reads, [sq_b[k]])
            MM(banks[ssb][:, 0:n], ones_b[:], sq[k][:, 0:n], c == 0, c == len(srcs) - 1, [sq_b[k]] + CONST, bqs(ssb))
        ACT(lnv[:, 0:n], banks[ssb][:, 0:n], AF.Ln, bqs(ssb) + CONST, [r_b], bias=epsc[:, 0:1], scale=invd)
        ACT(rstd[:, 0:n], lnv[:, 0:n], AF.Exp, [r_b], [r_b], scale=-0.5)
        return rstd

    def alloc_rms(ph, tag):
        sq = [P.sbuf("sq%s%d" % (tag, i), [128, 512], BF16, ph) for i in range(2)]
        sq_b = [Buf("sq") for _ in range(2)]
        lnv = P.sbuf("lnv" + tag, [128, 512], F32, ph)
        rstd = P.sbuf("rstd" + tag, [128, 512], F32, ph)
        return (sq, sq_b, lnv, rstd, Buf("rstd"))

    def stage_prenorm(sb, ph, gbase, dst, dst_b):
        rt = alloc_rms(ph, "pn%d" % gbase)
        for ti, t in enumerate(sb_tiles(sb)):
            c0, n = t["c0"], t["n"]
            rstd = rms_stats([hT[:, c, c0:c0 + n] for c in range(8)], n, rt, [hT_b[ti]], 1.0 / D)
            for c in range(8):
                STT(dst[:, c, c0:c0 + n], hT[:, c, c0:c0 + n], gcol(gbase, c), rstd[:, 0:n], ALU.mult, ALU.mult,
                    [hT_b[ti], rt[4]] + CONST, [dst_b[ti]])

    def stage_l0(sb, ph):
        kind = sb["kind"]
        full = kind == "M"
        tiles = sb_tiles(sb)
        nblk = sb["nblk"]
        has_s = sb["ns"] > 0
        wsl = [[P.sbuf("w0_%d_%d" % (s, k), [128, 8, 128], BF16, ph) for k in range(4)] for s in range(2)]
        wsl_b = [[Buf("w0") for k in range(4)] for s in range(2)]
        qd = [P.sbuf("qd%d" % s, [128, SBW], BF16, ph) for s in range(2)] if full else None
        kd = [P.sbuf("kd%d" % s, [128, SBW], BF16, ph) for s in range(2)]
        zs = [P.sbuf("zs%d" % s, [128, SBW], BF16, ph) for s in range(2)] if full else None
        Vtm = [P.sbuf("Vtm%d" % s, [128, 9, 128], BF16, ph) for s in range(2)]
        kdtm = [P.sbuf("kdtm%d" % s, [128, 9, 128], BF16, ph) for s in range(2)]
        hd = [P.sbuf("hd%d" % s, [128, 16], F32, ph) for s in range(2)]
        dec = [P.sbuf("dec%d" % s, [128, 16], F32, ph) for s in range(2)]
        negr = [P.sbuf("negr%d" % s, [128, 16], F32, ph) for s in range(2)]
        rr = [P.sbuf("rr%d" % s, [128, 16], F32, ph) for s in range(2)]
        hp_b = [[Buf("hp%d_%d" % (s, i)) for i in range(MAXT)] for s in range(2)]
        sc_b = [[Buf("sc%d_%d" % (s, i)) for i in range(MAXT)] for s in range(2)]
        tq = [P.sbuf("tq%d" % s, [128, 512], F32, ph) for s in range(2)] if full else None
        tsg = [P.sbuf("tsg%d" % s, [128, 512], F32, ph) for s in range(2)]
        tk = [P.sbuf("tk%d" % s, [128, 512], F32, ph) for s in range(2)]
        tb = [P.sbuf("tb%d" % s, [128, 512], F32, ph) for s in range(2)]
        tE1 = [P.sbuf("tE1%d" % s, [128, 512], BF16, ph) for s in range(2)] if full else None
        tE2 = [P.sbuf("tE2%d" % s, [128, 512], BF16, ph) for s in range(2)]
        tq_b = [Buf("tq") for _ in range(2)]
        tsg_b = [Buf("tsg") for _ in range(2)]
        tk_b = [Buf("tk") for _ in range(2)]
        tb_b = [Buf("tb") for _ in range(2)]
        tE1_b = [Buf("tE1") for _ in range(2)]
        tE2_b = [Buf("tE2") for _ in range(2)]
        ATs_b = [Buf("ATs") for _ in range(2)]
        Sp_b = [Buf("Sp") for _ in range(2)]
        Sd = [P.sbuf("Sd%d" % s, [128, 128], F32, ph) for s in range(2)]
        Sd_b = [Buf("Sd") for _ in range(2)]
        if full:
            ATs = [P.sbuf("ATs%d" % s, [128, 128], BF16, ph) for s in range(2)]
            Sp = [P.sbuf("Sp%d" % s, [128, 128], BF16, ph) for s in range(2)]
            osq = [P.sbuf("osq0", [128, 512], BF16, ph)] * 2
            osq_b = [Buf("osq")] * 2
            lnv0 = P.sbuf("lnv0", [128, 512], F32, ph)
            rstd0 = P.sbuf("rstd0", [128, 512], F32, ph)
            r0_b = Buf("r0")
            t1 = [P.sbuf("t1_0", [128, 512], F32, ph)] * 2
            t1_b = [Buf("t1")] * 2
            for s in range(2):
                MEMSET(ATs[s][:], 0.0, [ATs_b[s]])
        if has_s:
            qss = [P.sbuf("qss%d" % s, [128, NS], F32, ph) for s in range(2)]
            fss = [P.sbuf("fss%d" % s, [128, NS], F32, ph) for s in range(2)]
            kstm = [P.sbuf("kstm%d" % s, [NS, 128], BF16, ph) for s in range(2)]
            vstm = [P.sbuf("vstm%d" % s, [NS, 128], F32, ph) for s in range(2)]
            vblk = [P.sbuf("vblk%d" % s, [NS, NS, 128], BF16, ph) for s in range(2)]
            ss_b = [Buf("ss") for _ in range(2)]
            NSIN = 3
            sin = [P.sbuf("sin%d" % s, [128, 4, 128], F32, ph) for s in range(NSIN)]
            sin_b = [Buf("sin") for _ in range(NSIN)]
            sin_ctr = [0]
        print("  l0 scratch remaining:", nc.sbuf_bytes_remaining)

        def load_w(h):
            s = h % 2
            cols = [h * 128, 2048 + h * 128, 4096 + h * 128, 6144 + h * 128]
            for k in range(4):
                if not full and k in (0, 3):
                    continue
                P.dma("pool", [(wsl[s][k][:], w_in_a[:, cols[k]:cols[k] + 128].rearrange("(kc p) n -> p kc n", p=128))],
                      writes=[wsl_b[s][k]], dbuf=wsl_b[s][k])

        def head_proj(h):
            s = h % 2
            wq, wf, wi, wz = wsl[s]
            wq_b, wf_b, wi_b, wz_b = wsl_b[s]
            lnomc = lnomlv[:, h:h + 1]

            def stageA(ti):
                t = tiles[ti]
                c0, n = t["c0"], t["n"]
                p2 = ti % 2
                xr = [xnT_b[ti]]
                hpb = hp_b[s][ti]
                nb = len(t["blks"])
                for kc in range(8):
                    MM(banks[1][:, 0:n], wf[:, kc, :], xnT[:, kc, c0:c0 + n], kc == 0, kc == 7, [wf_b] + xr, bqs(1))
                yield
                ACT(tsg[p2][:, 0:n], banks[1][:, 0:n], AF.Exp, bqs(1), [tsg_b[p2]])
                ACT(tsg[p2][:, 0:n], tsg[p2][:, 0:n], AF.Ln, [tsg_b[p2]] + CONST, [tsg_b[p2]], bias=onec[:, 0:1], scale=1.0)
                ACT(tk[p2][:, 0:n], tsg[p2][:, 0:n], AF.Exp, [tsg_b[p2]] + CONST, [tk_b[p2]], bias=lnomc, scale=-1.0)
                if not t["samp"]:
                    ACT(tsg[p2][:, 0:n], tk[p2][:, 0:n], AF.Ln, [tk_b[p2]] + CONST, [tsg_b[p2]], bias=onec[:, 0:1], scale=-1.0)
                    j0 = t["blks"][0]
                    for jj, j in enumerate(t["blks"]):
                        for kc in range(8):
                            MM(banks[3][:, jj * 128:(jj + 1) * 128], xnT[:, kc, j * 128:(j + 1) * 128], wi[:, kc, :], kc == 0, kc == 7,
                               [wi_b] + xr, [bq[3][jj]])
                        if jj % 2 == 1 and jj + 1 < nb:
                            yield
                    TCOPY(Vtm[s][:, j0:j0 + nb, :], banks[3][:, 0:n].rearrange("p (j t) -> p j t", t=128), bqs(3, 0, nb), [hpb])
                else:
                    for kc in range(8):
                        MM(banks[3][0:NS, 0:128], xnT[:, kc, c0:c0 + NS], wi[:, kc, :], kc == 0, kc == 7, [wi_b] + xr, [bq[3][0]])
                yield
                if full:
                    for kc in range(8):
                        MM(banks[0][:, 0:n], wq[:, kc, :], xnT[:, kc, c0:c0 + n], kc == 0, kc == 7, [wq_b] + xr, bqs(0))
                    for kc in range(8):
                        MM(banks[2][:, 0:n], wz[:, kc, :], xnT[:, kc, c0:c0 + n], kc == 0, kc == 7, [wz_b] + xr, bqs(2))
                    yield
                    if not t["samp"]:
                        ACT(tq[p2][:, 0:n], banks[0][:, 0:n], AF.Silu, bqs(0), [tq_b[p2]])
                    else:
                        ACT(qss[s][:, 0:n], banks[0][:, 0:n], AF.Silu, bqs(0), [ss_b[s]])
                    ACT(zs[s][:, c0:c0 + n], banks[2][:, 0:n], AF.Silu, bqs(2), [hpb])
                if t["samp"]:
                    TS(fss[s][:, 0:n], tk[p2][:, 0:n], -1.0, 1.0, ALU.mult, ALU.add, [tk_b[p2]], [ss_b[s]])
                    TCOPY(kd[s][:, c0:c0 + n], tk[p2][:, 0:n], [tk_b[p2]], [hpb])
                    MM(banks[4][0:NS, 0:128], kd[s][:, c0:c0 + n], ident_b[:], True, True, [hpb] + CONST, [bq[4][0]])
                    ACOPY(kstm[s][:], banks[4][0:NS, 0:128], [bq[4][0]], [ss_b[s]])
                    ACOPY(vstm[s][:], banks[3][0:NS, 0:128], [bq[3][0]], [ss_b[s]])
                    for sp_ in range(NS):
                        TS(vblk[s][:, sp_, :], vstm[s][:], ident_f[0:NS, sp_:sp_ + 1], None, ALU.mult, None, [ss_b[s]] + CONST, [ss_b[s]])
                yield

            def stageB(ti):
                t = tiles[ti]
                c0, n = t["c0"], t["n"]
                p2 = ti % 2
                hpb = hp_b[s][ti]
                scb = sc_b[s][ti]
                nb = len(t["blks"])
                j0 = t["blks"][0]
                for jj in range(nb):
                    sl = slice(jj * 128, (jj + 1) * 128)
                    SCAN(tb[p2][:, sl], ones_f[:], tsg[p2][:, sl], [tsg_b[p2]] + CONST, [tb_b[p2]])
                tb3 = tb[p2][:, 0:n].rearrange("p (j t) -> p j t", t=128)
                blast = tb3[:, :, 127:128].rearrange("p j o -> p (j o)")
                TS(rr[s][:, j0:j0 + nb], blast, 0.5, None, ALU.mult, None, [tb_b[p2]], [scb])
                TT(tb3, tb3, rr[s][:, j0:j0 + nb].rearrange("p (j o) -> p j o", o=1).to_broadcast([128, nb, 128]), ALU.subtract,
                   [tb_b[p2], scb], [tb_b[p2]])
                yield
                ACT(hd[s][:, j0:j0 + nb], rr[s][:, j0:j0 + nb], AF.Exp, [scb], [scb])
                ACT(dec[s][:, j0:j0 + nb], rr[s][:, j0:j0 + nb], AF.Exp, [scb], [scb], scale=2.0)
                if full:
                    ACT(tE1[p2][:, 0:n], tb[p2][:, 0:n], AF.Exp, [tb_b[p2]], [tE1_b[p2]])
                ACT(tE2[p2][:, 0:n], tb[p2][:, 0:n], AF.Exp, [tb_b[p2]], [tE2_b[p2]], scale=-1.0)
                yield
                TT(kd[s][:, c0:c0 + n], tk[p2][:, 0:n], tE2[p2][:, 0:n], ALU.mult, [tk_b[p2], tE2_b[p2]], [hpb])
                if full:
                    TT(qd[s][:, c0:c0 + n], tq[p2][:, 0:n], tE1[p2][:, 0:n], ALU.mult, [tq_b[p2], tE1_b[p2]], [hpb])
                for jj, j in enumerate(t["blks"]):
                    MM(banks[4][:, jj * 128:(jj + 1) * 128], kd[s][:, j * 128:(j + 1) * 128], ident_b[:], True, True, [hpb] + CONST, [bq[4][jj]])
                TCOPY(kdtm[s][:, j0:j0 + nb, :], banks[4][:, 0:n].rearrange("p (j t) -> p j t", t=128), bqs(4, 0, nb), [hpb])
                yield

            nt = len(tiles)
            for r in range(nt + 1):
                gens = []
                if r >= 1 and not tiles[r - 1]["samp"]:
                    gens.append(stageB(r - 1))
                if r < nt:
                    gens.append(stageA(r))
                while gens:
                    for g in list(gens):
                        try:
                            next(g)
                            yield
                        except StopIteration:
                            gens.remove(g)

        def dphase(h, s, ti, c0, n):
            k = ti % 2
            oc = t1[0]
            TCOPY(oc[:, 0:n], banks[6][:, 0:n], bqs(6), [t1_b[0]])
            TT(osq[k][:, 0:n], oc[:, 0:n], oc[:, 0:n], ALU.mult, [t1_b[0]], [osq_b[k]])
            MM(banks[6][:, 0:n], ones_b[:], osq[k][:, 0:n], True, True, [osq_b[k]] + CONST, bqs(6))
            ACT(lnv0[:, 0:n], banks[6][:, 0:n], AF.Ln, bqs(6) + CONST, [r0_b], bias=epsc[:, 0:1], scale=1.0 / 128)
            ACT(rstd0[:, 0:n], lnv0[:, 0:n], AF.Exp, [r0_b], [r0_b], scale=-0.5)
            TT(oc[:, 0:n], oc[:, 0:n], rstd0[:, 0:n], ALU.mult, [t1_b[0], r0_b], [t1_b[0]])
            STT(ogT[:, h, c0:c0 + n], oc[:, 0:n], gcol(V_ON, h), zs[s][:, c0:c0 + n], ALU.mult, ALU.mult,
                [t1_b[0], hp_b[s][ti]] + CONST, [og_b[h][ti]])

        def head_scan(h):
            s = h % 2
            Sh = Sst[:, h, :]
            pending = []

            def flush():
                for fn in pending:
                    fn()
                del pending[:]

            blocks = [(ti, jj, j) for ti, t in enumerate(tiles) if not t["samp"] for jj, j in enumerate(t["blks"])]
            slots_ = {}
            NPF = 2

            def load_sin(g4):
                g = sin_ctr[0] % NSIN
                sin_ctr[0] += 1
                slots_[g4] = g
                P.dma("sp", [(sin[g][:], st_in[g4 * 4:g4 * 4 + 4, h].rearrange("s k v -> k s v"))], writes=[sin_b[g]], dbuf=sin_b[g])

            if has_s:
                for g4 in range(NPF):
                    load_sin(g4)

            def pe_front(bi):
                ti, jj, j = blocks[bi]
                a = j % 2
                blk = slice(j * 128, (j + 1) * 128)
                MM(banks[7][:, a * 128:(a + 1) * 128], kdtm[s][:, j, :], Vtm[s][:, j, :], True, True, [hp_b[s][ti]], [bq[7][a]])
                if full:
                    MM(banks[5][:, a * 128:(a + 1) * 128], kd[s][:, blk], qd[s][:, blk], True, True, [hp_b[s][ti]], [bq[5][a]])

            pe_front(0)
            for bi, (ti, jj, j) in enumerate(blocks):
                t = tiles[ti]
                c0, n = t["c0"], t["n"]
                nb = len(t["blks"])
                hpr = [hp_b[s][ti]]
                scr = [sc_b[s][ti]]
                a = j % 2
                blk = slice(j * 128, (j + 1) * 128)
                Uq = banks[7][:, a * 128:(a + 1) * 128]
                Aq = banks[5][:, a * 128:(a + 1) * 128]
                if bi + 1 < len(blocks):
                    pe_front(bi + 1)
                if full:
                    TS(Sp[a][:], Sh, hd[s][:, j:j + 1], None, ALU.mult, None, [S_b[h]] + scr, [Sp_b[a]])
                TS(Sd[a][:], Uq, hd[s][:, j:j + 1], None, ALU.mult, None, [bq[7][a]] + scr, [Sd_b[a]])
                if full:
                    CPRED(ATs[a][:], mtri[:], Aq, [bq[5][a], ATs_b[a]] + CONST, [ATs_b[a]])
                STT(Sh, Sh, dec[s][:, j:j + 1], Sd[a][:], ALU.mult, ALU.add, [S_b[h], Sd_b[a]] + scr, [S_b[h]])
                flush()
                if full:
                    def cons(a=a, j=j, jj=jj, blk=blk, hpr=hpr):
                        oq = banks[6][:, jj * 128:(jj + 1) * 128]
                        MM(oq, Sp[a][:], qd[s][:, blk], True, False, [Sp_b[a]] + hpr, [bq[6][jj]])
                        MM(oq, Vtm[s][:, j, :], ATs[a][:], False, True, [ATs_b[a]] + hpr, [bq[6][jj]])
                    pending.append(cons)
                    if jj == nb - 1:
                        pending.append(lambda ti=ti, c0=c0, n=n: dphase(h, s, ti, c0, n))
                yield
            flush()
            yield
            if has_s:
                P.dma("sp", [(stp_d[h], Sh)], reads=[S_b[h]], dbuf=stp_cb, is_out=True)
                ti = len(tiles) - 1
                c0 = tiles[ti]["c0"]
                for g4 in range(4):
                    s0 = g4 * 4
                    bk = 4 + g4 % 2
                    g = slots_[g4]
                    MM(banks[bk][:, :], kstm[s][:], vblk[s][:, s0:s0 + 4, :], True, True, [ss_b[s]], bqs(bk))
                    for si in range(4):
                        sidx = s0 + si
                        STT(sin[g][:, si, :], sin[g][:, si, :], fss[s][:, sidx:sidx + 1], banks[bk][:, si * 128:(si + 1) * 128], ALU.mult, ALU.add,
                            [sin_b[g], ss_b[s], bq[bk][si]], [sin_b[g]])
                        MM(banks[6][:, sidx:sidx + 1], sin[g][:, si, :], qss[s][:, sidx:sidx + 1], True, True, [sin_b[g], ss_b[s]], [bq[6][0]])
                    P.dma("sp", [(sts_d[s0:s0 + 4, h].rearrange("s k v -> k s v"), sin[g][:])], reads=[sin_b[g]], dbuf=sin_b[g], is_out=True)
                    if g4 + NPF < 4:
                        load_sin(g4 + NPF)
                    yield
                dphase(h, s, ti, c0, NS)
                yield

        def drive(gens):
            RATIO = int(os.environ.get("KB_RATIO", "2"))
            gens = [[g, (RATIO if i == 0 else 1)] for i, g in enumerate(gens) if g is not None]
            while gens:
                for it in list(gens):
                    for _ in range(it[1]):
                        try:
                            next(it[0])
                        except StopIteration:
                            gens.remove(it)
                            break

        import os
        NH = int(os.environ.get("KB_NH", "16"))
        load_w(0)
        prev_scan = None
        for h in range(NH):
            if h + 1 < NH:
                load_w(h + 1)
            drive([head_proj(h), prev_scan])
            prev_scan = head_scan(h)
        drive([prev_scan])

    def stage_l0u(sb, ph):
        kind = sb["kind"]
        full = kind == "M"
        tiles = sb_tiles(sb)
        has_s = sb["ns"] > 0
        NU = 3
        wsl = [[P.sbuf("w0_%d_%d" % (s, k), [128, 8, 128], BF16, ph) for k in range(4)] for s in range(NU)]
        wsl_b = [[Buf("w0") for k in range(4)] for s in range(NU)]
        qd = [P.sbuf("qd%d" % s, [128, 512], BF16, ph) for s in range(NU)] if full else None
        kd = [P.sbuf("kd%d" % s, [128, 512], BF16, ph) for s in range(NU)]
        zs = [P.sbuf("zs%d" % s, [128, 512], BF16, ph) for s in range(NU)] if full else None
        Vtm = [P.sbuf("Vtm%d" % s, [128, 4, 128], BF16, ph) for s in range(NU)]
        kdtm = [P.sbuf("kdtm%d" % s, [128, 4, 128], BF16, ph) for s in range(NU)]
        hd = [P.sbuf("hd%d" % s, [128, 4], F32, ph) for s in range(NU)]
        dec = [P.sbuf("dec%d" % s, [128, 4], F32, ph) for s in range(NU)]
        rr = [P.sbuf("rr%d" % s, [128, 4], F32, ph) for s in range(NU)]
        hp_b = [Buf("hp%d" % s) for s in range(NU)]
        sc_b = [Buf("sc%d" % s) for s in range(NU)]
        tq = [P.sbuf("tq%d" % s, [128, 512], F32, ph) for s in range(2)] if full else None
        tsg = [P.sbuf("tsg%d" % s, [128, 512], F32, ph) for s in range(2)]
        tk = [P.sbuf("tk%d" % s, [128, 512], F32, ph) for s in range(2)]
        tb = [P.sbuf("tb%d" % s, [128, 512], F32, ph) for s in range(2)]
        tE1 = [P.sbuf("tE1%d" % s, [128, 512], BF16, ph) for s in range(2)] if full else None
        tE2 = [P.sbuf("tE2%d" % s, [128, 512], BF16, ph) for s in range(2)]
        tq_b = [Buf("tq") for _ in range(2)]
        tsg_b = [Buf("tsg") for _ in range(2)]
        tk_b = [Buf("tk") for _ in range(2)]
        tb_b = [Buf("tb") for _ in range(2)]
        tE1_b = [Buf("tE1") for _ in range(2)]
        tE2_b = [Buf("tE2") for _ in range(2)]
        ATs_b = [Buf("ATs") for _ in range(2)]
        Sp_b = [Buf("Sp") for _ in range(2)]
        Sd = [P.sbuf("Sd%d" % s, [128, 128], F32, ph) for s in range(2)]
        Sd_b = [Buf("Sd") for _ in range(2)]
        if full:
            ATs = [P.sbuf("ATs%d" % s, [128, 128], BF16, ph) for s in range(2)]
            Sp = [P.sbuf("Sp%d" % s, [128, 128], BF16, ph) for s in range(2)]
            osq = P.sbuf("osq0", [128, 512], BF16, ph)
            osq_b = Buf("osq")
            lnv0 = P.sbuf("lnv0", [128, 512], F32, ph)
            rstd0 = P.sbuf("rstd0", [128, 512], F32, ph)
            r0_b = Buf("r0")
            oc = P.sbuf("oc", [128, 512], F32, ph)
            oc_b = Buf("oc")
            for s in range(2):
                MEMSET(ATs[s][:], 0.0, [ATs_b[s]])
        if has_s:
            qss = [P.sbuf("qss%d" % s, [128, NS], F32, ph) for s in range(NU)]
            fss = [P.sbuf("fss%d" % s, [128, NS], F32, ph) for s in range(NU)]
            kstm = [P.sbuf("kstm%d" % s, [NS, 128], BF16, ph) for s in range(NU)]
            vstm = [P.sbuf("vstm%d" % s, [NS, 128], F32, ph) for s in range(NU)]
            ss_b = [Buf("ss") for _ in range(NU)]
            vblk = P.sbuf("vblk", [NS, NS, 128], BF16, ph)
            vblk_b = Buf("vblk")
            NSIN = 3
            sin = [P.sbuf("sin%d" % s, [128, 4, 128], F32, ph) for s in range(NSIN)]
            sin_b = [Buf("sin") for _ in range(NSIN)]
            sin_ctr = [0]
        print("  l0u scratch remaining:", nc.sbuf_bytes_remaining)

        units = [(ti, h) for ti in range(len(tiles)) for h in range(16)]
        NUN = len(units)

        def load_w(ui):
            ti, h = units[ui]
            s = ui % NU
            cols = [h * 128, 2048 + h * 128, 4096 + h * 128, 6144 + h * 128]
            for k in range(4):
                if not full and k in (0, 3):
                    continue
                P.dma("pool", [(wsl[s][k][:], w_in_a[:, cols[k]:cols[k] + 128].rearrange("(kc p) n -> p kc n", p=128))],
                      writes=[wsl_b[s][k]], dbuf=wsl_b[s][k])

        sin_slots = {}

        def load_sin(h, g4):
            g = sin_ctr[0] % NSIN
            sin_ctr[0] += 1
            sin_slots[(h, g4)] = g
            P.dma("sp", [(sin[g][:], st_in[g4 * 4:g4 * 4 + 4, h].rearrange("s k v -> k s v"))], writes=[sin_b[g]], dbuf=sin_b[g])

        def stageA(ui):
            ti, h = units[ui]
            s = ui % NU
            p2 = ui % 2
            t = tiles[ti]
            c0, n = t["c0"], t["n"]
            nb = len(t["blks"])
            wq, wf, wi, wz = wsl[s]
            wq_b, wf_b, wi_b, wz_b = wsl_b[s]
            lnomc = lnomlv[:, h:h + 1]
            xr = [xnT_b[ti]]
            hpb = hp_b[s]
            for kc in range(8):
                MM(banks[1][:, 0:n], wf[:, kc, :], xnT[:, kc, c0:c0 + n], kc == 0, kc == 7, [wf_b] + xr, bqs(1))
            yield
            ACT(tsg[p2][:, 0:n], banks[1][:, 0:n], AF.Exp, bqs(1), [tsg_b[p2]])
            ACT(tsg[p2][:, 0:n], tsg[p2][:, 0:n], AF.Ln, [tsg_b[p2]] + CONST, [tsg_b[p2]], bias=onec[:, 0:1], scale=1.0)
            ACT(tk[p2][:, 0:n], tsg[p2][:, 0:n], AF.Exp, [tsg_b[p2]] + CONST, [tk_b[p2]], bias=lnomc, scale=-1.0)
            if not t["samp"]:
                ACT(tsg[p2][:, 0:n], tk[p2][:, 0:n], AF.Ln, [tk_b[p2]] + CONST, [tsg_b[p2]], bias=onec[:, 0:1], scale=-1.0)
                for jj, j in enumerate(t["blks"]):
                    for kc in range(8):
                        MM(banks[3][:, jj * 128:(jj + 1) * 128], xnT[:, kc, j * 128:(j + 1) * 128], wi[:, kc, :], kc == 0, kc == 7,
                           [wi_b] + xr, [bq[3][jj]])
                    if jj % 2 == 1 and jj + 1 < nb:
                        yield
                TCOPY(Vtm[s][:, 0:nb, :], banks[3][:, 0:n].rearrange("p (j t) -> p j t", t=128), bqs(3, 0, nb), [hpb])
            else:
                for kc in range(8):
                    MM(banks[3][0:NS, 0:128], xnT[:, kc, c0:c0 + NS], wi[:, kc, :], kc == 0, kc == 7, [wi_b] + xr, [bq[3][0]])
            yield
            if full:
                for kc in range(8):
                    MM(banks[0][:, 0:n], wq[:, kc, :], xnT[:, kc, c0:c0 + n], kc == 0, kc == 7, [wq_b] + xr, bqs(0))
                for kc in range(8):
                    MM(banks[2][:, 0:n], wz[:, kc, :], xnT[:, kc, c0:c0 + n], kc == 0, kc == 7, [wz_b] + xr, bqs(2))
                yield
                if not t["samp"]:
                    ACT(tq[p2][:, 0:n], banks[0][:, 0:n], AF.Silu, bqs(0), [tq_b[p2]])
                else:
                    ACT(qss[s][:, 0:n], banks[0][:, 0:n], AF.Silu, bqs(0), [ss_b[s]])
                ACT(zs[s][:, 0:n], banks[2][:, 0:n], AF.Silu, bqs(2), [hpb])
            if t["samp"]:
                TS(fss[s][:, 0:n], tk[p2][:, 0:n], -1.0, 1.0, ALU.mult, ALU.add, [tk_b[p2]], [ss_b[s]])
                TCOPY(kd[s][:, 0:n], tk[p2][:, 0:n], [tk_b[p2]], [hpb])
                MM(banks[4][0:NS, 0:128], kd[s][:, 0:n], ident_b[:], True, True, [hpb] + CONST, [bq[4][0]])
                ACOPY(kstm[s][:], banks[4][0:NS, 0:128], [bq[4][0]], [ss_b[s]])
                ACOPY(vstm[s][:], banks[3][0:NS, 0:128], [bq[3][0]], [ss_b[s]])
            yield

        def stageB(ui):
            ti, h = units[ui]
            s = ui % NU
            p2 = ui % 2
            t = tiles[ti]
            if t["samp"]:
                return
            n = t["n"]
            nb = len(t["blks"])
            hpb = hp_b[s]
            scb = sc_b[s]
            for jj in range(nb):
                sl = slice(jj * 128, (jj + 1) * 128)
                SCAN(tb[p2][:, sl], ones_f[:], tsg[p2][:, sl], [tsg_b[p2]] + CONST, [tb_b[p2]])
            tb3 = tb[p2][:, 0:n].rearrange("p (j t) -> p j t", t=128)
            blast = tb3[:, :, 127:128].rearrange("p j o -> p (j o)")
            TS(rr[s][:, 0:nb], blast, 0.5, None, ALU.mult, None, [tb_b[p2]], [scb])
            TT(tb3, tb3, rr[s][:, 0:nb].rearrange("p (j o) -> p j o", o=1).to_broadcast([128, nb, 128]), ALU.subtract,
               [tb_b[p2], scb], [tb_b[p2]])
            yield
            ACT(hd[s][:, 0:nb], rr[s][:, 0:nb], AF.Exp, [scb], [scb])
            ACT(dec[s][:, 0:nb], rr[s][:, 0:nb], AF.Exp, [scb], [scb], scale=2.0)
            if full:
                ACT(tE1[p2][:, 0:n], tb[p2][:, 0:n], AF.Exp, [tb_b[p2]], [tE1_b[p2]])
            ACT(tE2[p2][:, 0:n], tb[p2][:, 0:n], AF.Exp, [tb_b[p2]], [tE2_b[p2]], scale=-1.0)
            yield
            TT(kd[s][:, 0:n], tk[p2][:, 0:n], tE2[p2][:, 0:n], ALU.mult, [tk_b[p2], tE2_b[p2]], [hpb])
            if full:
                TT(qd[s][:, 0:n], tq[p2][:, 0:n], tE1[p2][:, 0:n], ALU.mult, [tq_b[p2], tE1_b[p2]], [hpb])
            for jj in range(nb):
                MM(banks[4][:, jj * 128:(jj + 1) * 128], kd[s][:, jj * 128:(jj + 1) * 128], ident_b[:], True, True, [hpb] + CONST, [bq[4][jj]])
            TCOPY(kdtm[s][:, 0:nb, :], banks[4][:, 0:n].rearrange("p (j t) -> p j t", t=128), bqs(4, 0, nb), [hpb])
            yield

        def dphase(h, s, c0, n):
            ACT(oc[:, 0:n], banks[6][:, 0:n], AF.Identity, bqs(6), [oc_b])
            ACT(osq[:, 0:n], oc[:, 0:n], AF.Square, [oc_b], [osq_b])
            MM(banks[6][:, 0:n], ones_b[:], osq[:, 0:n], True, True, [osq_b] + CONST, bqs(6))
            ACT(lnv0[:, 0:n], banks[6][:, 0:n], AF.Ln, bqs(6) + CONST, [r0_b], bias=epsc[:, 0:1], scale=1.0 / 128)
            ACT(rstd0[:, 0:n], lnv0[:, 0:n], AF.Exp, [r0_b], [r0_b], scale=-0.5)
            TT(oc[:, 0:n], oc[:, 0:n], rstd0[:, 0:n], ALU.mult, [oc_b, r0_b], [oc_b])
            STT(ogT[:, h, c0:c0 + n], oc[:, 0:n], gcol(V_ON, h), zs[s][:, 0:n], ALU.mult, ALU.mult,
                [oc_b, hp_b[s]] + CONST, [og_b[h][0], og_b[h][1], og_b[h][2]])

        def scan(ui):
            ti, h = units[ui]
            s = ui % NU
            t = tiles[ti]
            c0, n = t["c0"], t["n"]
            nb = len(t["blks"])
            Sh = Sst[:, h, :]
            hpr = [hp_b[s]]
            scr = [sc_b[s]]
            if not t["samp"]:
                def pe_front(jj):
                    a = jj % 2
                    blk = slice(jj * 128, (jj + 1) * 128)
                    MM(banks[7][:, a * 128:(a + 1) * 128], kdtm[s][:, jj, :], Vtm[s][:, jj, :], True, True, hpr, [bq[7][a]])
                    if full:
                        MM(banks[5][:, a * 128:(a + 1) * 128], kd[s][:, blk], qd[s][:, blk], True, True, hpr, [bq[5][a]])
                pend = []
                pe_front(0)
                for jj in range(nb):
                    a = jj % 2
                    blk = slice(jj * 128, (jj + 1) * 128)
                    Uq = banks[7][:, a * 128:(a + 1) * 128]
                    Aq = banks[5][:, a * 128:(a + 1) * 128]
                    if jj + 1 < nb:
                        pe_front(jj + 1)
                    if full:
                        ACT(Sp[a][:], Sh, AF.Identity, [S_b[h]] + scr, [Sp_b[a]], scale=hd[s][:, jj:jj + 1])
                    ACT(Sd[a][:], Uq, AF.Identity, [bq[7][a]] + scr, [Sd_b[a]], scale=hd[s][:, jj:jj + 1])
                    if full:
                        CPRED(ATs[a][:], mtri[:], Aq, [bq[5][a], ATs_b[a]] + CONST, [ATs_b[a]])
                    STT(Sh, Sh, dec[s][:, jj:jj + 1], Sd[a][:], ALU.mult, ALU.add, [S_b[h], Sd_b[a]] + scr, [S_b[h]])
                    for fn in pend:
                        fn()
                    del pend[:]
                    if full:
                        def cons(a=a, jj=jj, blk=blk):
                            oq = banks[6][:, jj * 128:(jj + 1) * 128]
                            MM(oq, Sp[a][:], qd[s][:, blk], True, False, [Sp_b[a]] + hpr, [bq[6][jj]])
                            MM(oq, Vtm[s][:, jj, :], ATs[a][:], False, True, [ATs_b[a]] + hpr, [bq[6][jj]])
                        pend.append(cons)
                    yield
                for fn in pend:
                    fn()
                if full:
                    dphase(h, s, c0, n)
                yield
                return
            P.dma("sp", [(stp_d[h], Sh)], reads=[S_b[h]], dbuf=stp_cb, is_out=True)
            for g4 in range(2):
                load_sin(h, g4)
            for sp_ in range(NS):
                TS(vblk[:, sp_, :], vstm[s][:], ident_f[0:NS, sp_:sp_ + 1], None, ALU.mult, None, [ss_b[s]] + CONST, [vblk_b])
            yield
            for g4 in range(4):
                s0 = g4 * 4
                bk = 5 if g4 % 2 == 0 else 7
                g = sin_slots[(h, g4)]
                MM(banks[bk][:, :], kstm[s][:], vblk[:, s0:s0 + 4, :], True, True, [ss_b[s], vblk_b], bqs(bk))
                for si in range(4):
                    sidx = s0 + si
                    STT(sin[g][:, si, :], sin[g][:, si, :], fss[s][:, sidx:sidx + 1], banks[bk][:, si * 128:(si + 1) * 128], ALU.mult, ALU.add,
                        [sin_b[g], ss_b[s], bq[bk][si]], [sin_b[g]])
                    MM(banks[6][:, sidx:sidx + 1], sin[g][:, si, :], qss[s][:, sidx:sidx + 1], True, True, [sin_b[g], ss_b[s]], [bq[6][0]])
                P.dma("sp", [(sts_d[s0:s0 + 4, h].rearrange("s k v -> k s v"), sin[g][:])], reads=[sin_b[g]], dbuf=sin_b[g], is_out=True)
                if g4 + 2 < 4:
                    load_sin(h, g4 + 2)
                yield
            dphase(h, s, c0, NS)
            yield

        load_w(0)
        if NUN > 1:
            load_w(1)
        for idx in range(NUN + 2):
            if idx + 2 < NUN:
                load_w(idx + 2)
            gens = []
            if 0 <= idx - 1 < NUN:
                gens.append(stageB(idx - 1))
            if idx < NUN:
                gens.append(stageA(idx))
            if 0 <= idx - 2 < NUN:
                gens.append(scan(idx - 2))
            while gens:
                for g in list(gens):
                    try:
                        next(g)
                    except StopIteration:
                        gens.remove(g)

    def stage_out(sb, ph, l, w_out):
        tiles = sb_tiles(sb)
        mix = P.sbuf("mix", [128, 8, SBW], F32, ph)
        mix_b = [[Buf("mix") for _ in range(MAXT)] for _ in range(8)]
        wo = [P.sbuf("wo%d" % i, [128, 16, 128], BF16, ph) for i in range(2)]
        wo_b = [Buf("wo") for _ in range(2)]
        rt = alloc_rms(ph, "po")
        tmp = [P.sbuf("tmpo%d" % i, [128, 512], F32, ph) for i in range(2)]
        tmp_b = [Buf("tmpo") for _ in range(2)]
        wg = [P.sbuf("wg%d" % i, [128, 8, 128], BF16, ph) for i in range(2)]
        wg_b = [Buf("wg") for _ in range(2)]
        wp = [P.sbuf("wp%d" % i, [128, 2, 128], BF16, ph) for i in range(2)]
        sgt = [P.sbuf("sgt%d" % i, [128, 512], F32, ph) for i in range(2)]
        sgt_b = [Buf("sgt") for _ in range(2)]
        print("  out scratch remaining:", nc.sbuf_bytes_remaining)

        def ldo(dc):
            P.dma("pool", [(wo[dc % 2][:], w_out[:, dc * 128:(dc + 1) * 128].rearrange("(cc p) n -> p cc n", p=128))],
                  writes=[wo_b[dc % 2]], dbuf=wo_b[dc % 2])
        ldo(0)
        k = 0
        for dc in range(8):
            if dc + 1 < 8:
                ldo(dc + 1)
            for ti, t in enumerate(tiles):
                c0, n = t["c0"], t["n"]
                bk = k % 3
                k += 1
                for cc in range(16):
                    MM(banks[bk][:, 0:n], wo[dc % 2][:, cc, :], ogT[:, cc, c0:c0 + n], cc == 0, cc == 15, [wo_b[dc % 2], og_b[cc][ti]], bqs(bk))
                ACOPY(mix[:, dc, c0:c0 + n], banks[bk][:, 0:n], bqs(bk), [mix_b[dc][ti]])
        gb = V_POST0 if l == 0 else V_POST1
        for ti, t in enumerate(tiles):
            c0, n = t["c0"], t["n"]
            rstd = rms_stats([mix[:, c, c0:c0 + n] for c in range(8)], n, rt, [mix_b[c][ti] for c in range(8)], 1.0 / D)
            for c in range(8):
                a = c % 2
                STT(tmp[a][:, 0:n], mix[:, c, c0:c0 + n], gcol(gb, c), rstd[:, 0:n], ALU.mult, ALU.mult, [mix_b[c][ti], rt[4]] + CONST, [tmp_b[a]])
                TT(hT[:, c, c0:c0 + n], hT[:, c, c0:c0 + n], tmp[a][:, 0:n], ALU.add, [tmp_b[a], hT_b[ti]], [hT_b[ti]])
                ACOPY(xnT[:, c, c0:c0 + n], hT[:, c, c0:c0 + n], [hT_b[ti]], [xnT_b[ti]])

        def ldg(dc):
            P.dma("pool", [(wg[dc % 2][:], w_pg[l][:, dc * 128:(dc + 1) * 128].rearrange("(kc p) n -> p kc n", p=128)),
                           (wp[dc % 2][:], w_pe[l][:, dc * 128:(dc + 1) * 128].rearrange("(kc p) n -> p kc n", p=128))],
                  writes=[wg_b[dc % 2]], dbuf=wg_b[dc % 2])
        ldg(0)
        k = 0
        for dc in range(8):
            if dc + 1 < 8:
                ldg(dc + 1)
            for ti, t in enumerate(tiles):
                c0, n = t["c0"], t["n"]
                a = k % 2
                bg = a
                bp = 2 + a
                k += 1
                for kc in range(8):
                    MM(banks[bg][:, 0:n], wg[dc % 2][:, kc, :], xnT[:, kc, c0:c0 + n], kc == 0, kc == 7, [wg_b[dc % 2], xnT_b[ti]], bqs(bg))
                for pc in range(2):
                    MM(banks[bp][:, 0:n], wp[dc % 2][:, pc, :], pT[:, l, pc, c0:c0 + n], pc == 0, pc == 1, [wg_b[dc % 2], pT_b[ti]], bqs(bp))
                ACT(sgt[a][:, 0:n], banks[bg][:, 0:n], AF.Sigmoid, bqs(bg), [sgt_b[a]])
                TT(tmp[a][:, 0:n], banks[bp][:, 0:n], sgt[a][:, 0:n], ALU.mult, bqs(bp) + [sgt_b[a]], [tmp_b[a]])
                TT(hT[:, dc, c0:c0 + n], hT[:, dc, c0:c0 + n], tmp[a][:, 0:n], ALU.add, [tmp_b[a], hT_b[ti]], [hT_b[ti]])

    def stage_l1(sb, sbi, ph):
        tiles = sb_tiles(sb)
        nblk = sb["nblk"]
        has_s = sb["ns"] > 0
        first = sb["t0"] == 0
        NT = nblk * 128 + sb["ns"]
        kT = P.sbuf("kT", [128, 4, 128 + SBW], BF16, ph)
        Vt = P.sbuf("Vt", [128, 10, 256], BF16, ph)
        rC = P.sbuf("rC", [128, SBW], BF16, ph)
        rS = P.sbuf("rS", [128, SBW], BF16, ph)
        kT_b = [Buf("kT%d" % i) for i in range(MAXT + 1)]
        Vt_b = [Buf("Vt%d" % i) for i in range(10)]
        rope_b = Buf("rope")
        rf = P.sbuf("rf", [128, 512], F32, ph)
        rb = P.sbuf("rb", [128, 512], BF16, ph)
        rt1 = P.sbuf("rt1", [128, 512], F32, ph)
        rt2 = P.sbuf("rt2", [128, 512], F32, ph)
        rf_b, rb_b, rt1_b, rt2_b = Buf("rf"), Buf("rb"), Buf("rt1"), Buf("rt2")
        if has_s:
            kfl = P.sbuf("kfl", [128, 4, 128], F32, ph)
            ksf = P.sbuf("ksf", [128, 4, NS], F32, ph)
            kfl_b = Buf("kfl")
            qsa = P.sbuf("qsa", [128, 16, NS], BF16, ph)
            zsa = P.sbuf("zsa", [128, 16, NS], BF16, ph)
            qsa_b = Buf("qsa")
            vsb = P.sbuf("vsb", [NS, 256], BF16, ph)
            ostg = P.sbuf("ostg", [128, 768], F32, ph)
            ostg_b = Buf("ostg")
        g0 = sb["t0"]
        pairs = [(rC[:, 0:nblk * 128], ropec_d[:, g0:g0 + nblk * 128]), (rS[:, 0:nblk * 128], ropes_d[:, g0:g0 + nblk * 128])]
        if has_s:
            pairs += [(rC[:, nblk * 128:NT], ropec_d[:, NMAIN:NMAIN + NS]), (rS[:, nblk * 128:NT], ropes_d[:, NMAIN:NMAIN + NS])]
        P.dma("pool", pairs, writes=[rope_b], dbuf=rope_b)
        ACOPY(kT[:, :, 0:128], kT_halo[:], [halo_b], [kT_b[0]])
        ACOPY(Vt[:, 0, :], V_halo[:], [halo_b], [Vt_b[0]])

        def rope(src_ps, src_bank, dst, c0, n, reads_extra, writes, f32copy=None):
            ACOPY(rf[:, 0:n], src_ps, [bankb[src_bank]], [rf_b])
            ACOPY(rb[:, 0:n], src_ps, [bankb[src_bank]], [rb_b])
            MM(banks[3][:, 0:n], prot[:], rb[:, 0:n], True, True, [rb_b] + CONST, [bankb[3]])
            TT(rt1[:, 0:n], rf[:, 0:n], rC[:, c0:c0 + n], ALU.mult, [rf_b, rope_b], [rt1_b])
            TT(rt2[:, 0:n], banks[3][:, 0:n], rS[:, c0:c0 + n], ALU.mult, [bankb[3], rope_b], [rt2_b])
            TT(dst, rt1[:, 0:n], rt2[:, 0:n], ALU.add, [rt1_b, rt2_b] + reads_extra, writes)
            if f32copy is not None:
                o_ap, lo, hi, wr = f32copy
                TT(o_ap, rt1[:, lo:hi], rt2[:, lo:hi], ALU.add, [rt1_b, rt2_b], wr)

        pa = ExitStack()
        kvn = P.sbuf("kvn", [128, 8, 512], BF16, pa)
        kvn_b = Buf("kvn")
        wk = P.sbuf("wk", [128, 8, 4, 128], BF16, pa)
        wv = P.sbuf("wv", [128, 8, 256], BF16, pa)
        wkv_b = Buf("wkv")
        rt = alloc_rms(pa, "l1")
        vlast = P.sbuf("vlast", [128, 256], F32, pa)
        vlast_b = Buf("vlast")
        print("  l1a scratch remaining:", nc.sbuf_bytes_remaining)
        wkpairs = []
        for kh_ in range(4):
            src_ = w_kv[:, kh_ * 64:(kh_ + 1) * 64].rearrange("(kc p) d -> p kc d", p=128)
            wkpairs += [(wk[:, :, kh_, 0:64], src_), (wk[:, :, kh_, 64:128], src_)]
        wkpairs.append((wv[:], w_kv[:, 256:512].rearrange("(kc p) n -> p kc n", p=128)))
        P.dma("pool", wkpairs, writes=[wkv_b], dbuf=wkv_b)
        for ti, t in enumerate(tiles):
            c0, n = t["c0"], t["n"]
            rstd = rms_stats([hT[:, c, c0:c0 + n] for c in range(8)], n, rt, [hT_b[ti]], 1.0 / D)
            for c in range(8):
                STT(xnT[:, c, c0:c0 + n], hT[:, c, c0:c0 + n], gcol(V_PRE1, c), rstd[:, 0:n], ALU.mult, ALU.mult,
                    [hT_b[ti], rt[4]] + CONST, [xnT_b[ti]])
                STT(kvn[:, c, 0:n], hT[:, c, c0:c0 + n], gcol(V_KV, c), rstd[:, 0:n], ALU.mult, ALU.mult,
                    [hT_b[ti], rt[4]] + CONST, [kvn_b])
            for kh in range(4):
                bk = kh % 2
                for kc in range(8):
                    MM(banks[bk][:, 0:n], wk[:, kc, kh, :], kvn[:, kc, 0:n], kc == 0, kc == 7, [wkv_b, kvn_b], [bankb[bk]])
                f32c = None
                if has_s and t["samp"]:
                    f32c = (ksf[:, kh, :], 0, NS, [kfl_b])
                elif has_s and (nblk - 1) in t["blks"]:
                    lo = (nblk - 1) * 128 - c0
                    f32c = (kfl[:, kh, :], lo, lo + 128, [kfl_b])
                rope(banks[bk][:, 0:n], bk, kT[:, kh, 128 + c0:128 + c0 + n], c0, n, [], [kT_b[1 + ti]], f32c)
            if not t["samp"]:
                for jj, j in enumerate(t["blks"]):
                    bk = 4 + jj % 2
                    for kc in range(8):
                        MM(banks[bk][:, 0:256], kvn[:, kc, jj * 128:(jj + 1) * 128], wv[:, kc, :], kc == 0, kc == 7, [wkv_b, kvn_b], [bankb[bk]])
                    ACOPY(Vt[:, 1 + j, :], banks[bk][:, 0:256], [bankb[bk]], [Vt_b[1 + j]])
                    if has_s and j == nblk - 1:
                        ACOPY(vlast[:], banks[bk][:, 0:256], [bankb[bk]], [vlast_b])
                        P.dma("sp", [(vp_d, vlast[:])], reads=[vlast_b], dbuf=vlast_b, is_out=True)
            else:
                for kc in range(8):
                    MM(banks[4][0:NS, 0:256], kvn[:, kc, 0:NS], wv[:, kc, :], kc == 0, kc == 7, [wkv_b, kvn_b], [bankb[4]])
                ACOPY(ostg[0:NS, 512:768], banks[4][0:NS, 0:256], [bankb[4]], [ostg_b])
        if has_s:
            for kh in range(4):
                MM(banks[5][:, kh * 128:(kh + 1) * 128], kfl[:, kh, :], ident_f[:], True, True, [kfl_b] + CONST, [bankb[5]])
            kp_st = P.sbuf("kp_st", [128, 256], F32, pa)
            kp_b = Buf("kp_st")
            ACOPY(kp_st[:].rearrange("p (kh d) -> p kh d", kh=4), banks[5][:, :].rearrange("p (kh e) -> p kh e", kh=4)[:, :, 0:64], [bankb[5]], [kp_b])
            P.dma("sp", [(kp_d, kp_st[:])], reads=[kp_b], dbuf=kp_b, is_out=True)
            for kh in range(4):
                MM(banks[6][0:NS, kh * 128:(kh + 1) * 128], ksf[:, kh, :], ident_f[:], True, True, [kfl_b] + CONST, [bankb[6]])
            ACOPY(ostg[0:NS, 0:256].rearrange("p (kh d) -> p kh d", kh=4), banks[6][0:NS, :].rearrange("p (kh e) -> p kh e", kh=4)[:, :, 0:64], [bankb[6]], [ostg_b])
            ksd_b = Buf("ksd")
            P.dma("sp", [(ks_d, ostg[0:NS, 0:256]), (vs_d, ostg[0:NS, 512:768])], reads=[ostg_b], writes=[ksd_b], dbuf=ostg_b, is_out=True)
        P.barrier()
        pa.close()
        if os.environ.get("KB_L1", "") == "a":
            return

        wq = P.sbuf("wq", [128, 8, 512], BF16, ph)
        wz = P.sbuf("wz", [128, 8, 512], BF16, ph)
        wq_b, wz_b = Buf("wq"), Buf("wz")
        qg = P.sbuf("qg", [128, 4, SBW], BF16, ph)
        zg = P.sbuf("zg", [128, 4, SBW], BF16, ph)
        qg_b = [Buf("qg%d" % i) for i in range(MAXT)]
        PT = [P.sbuf("PT%d" % i, [128, 2, 512], BF16, ph) for i in range(4)]
        PT_b = [Buf("PT") for _ in range(4)]
        rden = P.sbuf("rden", [128, 512], F32, ph)
        rden_b = Buf("rden")
        at = P.sbuf("at", [128, 512], F32, ph)
        at_b = Buf("at")
        if has_s:
            ckd = [P.sbuf("ckd%d" % i, [128, 4, 2, 64], BF16, ph) for i in range(2)]
            cvt = [P.sbuf("cvt%d" % i, [128, 256], BF16, ph) for i in range(2)]
            cc_b = [Buf("cc") for _ in range(2)]
            KcT = P.sbuf("KcT", [128, 512], BF16, ph)
            KcT_b = Buf("KcT")
            PTs = P.sbuf("PTs", [128, 32], BF16, ph)
            PTs_b = Buf("PTs")
        print("  l1b scratch remaining:", nc.sbuf_bytes_remaining)

        def ldq(kh):
            P.dma("pool", [(wq[:], w_in_b[:, kh * 512:(kh + 1) * 512].rearrange("(kc p) n -> p kc n", p=128))], writes=[wq_b], dbuf=wq_b)
            P.dma("pool", [(wz[:], w_in_b[:, 2048 + kh * 512:2048 + (kh + 1) * 512].rearrange("(kc p) n -> p kc n", p=128))], writes=[wz_b], dbuf=wz_b)

        ldq(0)
        blkcount = 0
        for kh in range(4):
            for ti, t in enumerate(tiles):
                c0, n = t["c0"], t["n"]
                for i in range(4):
                    bk = i % 2
                    for kc in range(8):
                        MM(banks[bk][:, 0:n], wq[:, kc, i * 128:(i + 1) * 128], xnT[:, kc, c0:c0 + n], kc == 0, kc == 7, [wq_b, xnT_b[ti]], [bankb[bk]])
                    f32c = None
                    rope(banks[bk][:, 0:n], bk, qg[:, i, c0:c0 + n], c0, n, [], [qg_b[ti]], None)
                    bz = 4 + i % 2
                    for kc in range(8):
                        MM(banks[bz][:, 0:n], wz[:, kc, i * 128:(i + 1) * 128], xnT[:, kc, c0:c0 + n], kc == 0, kc == 7, [wz_b, xnT_b[ti]], [bankb[bz]])
                    ACT(zg[:, i, c0:c0 + n], banks[bz][:, 0:n], AF.Silu, [bankb[bz]], [qg_b[ti]])
                if t["samp"]:
                    ACOPY(qsa[:, kh * 4:(kh + 1) * 4, :], qg[:, :, c0:c0 + NS], [qg_b[ti]], [qsa_b])
                    ACOPY(zsa[:, kh * 4:(kh + 1) * 4, :], zg[:, :, c0:c0 + NS], [qg_b[ti]], [qsa_b])
            if kh + 1 < 4:
                ldq(kh + 1)
            steps = [(ti, j) for ti, t in enumerate(tiles) for j in t["blks"] if not (first and j == 0)]

            def emit_scores(idx):
                ti, j = steps[idx]
                st_ = idx % 2
                pm_ = maskf4 if (first and j == 1) else maskp4
                prev_cols = slice(j * 128, (j + 1) * 128)
                cur_cols = slice((j + 1) * 128, (j + 2) * 128)
                blk = slice(j * 128, (j + 1) * 128)
                kprev_b = kT_b[0] if j == 0 else kT_b[1 + (j - 1) // 4]
                kcur_b = kT_b[1 + j // 4]
                for par in range(2):
                    pr = slice(par * 64, (par + 1) * 64)
                    b0, b1 = 2 * par, 2 * par + 1
                    pt = PT[st_ * 2 + par]
                    ptb = PT_b[st_ * 2 + par]
                    MM(banks[b0][:, :], kT[pr, kh, prev_cols], qg[pr, :, blk], True, True, [kprev_b, qg_b[ti]], [bankb[b0]])
                    MM(banks[b1][:, :], kT[pr, kh, cur_cols], qg[pr, :, blk], True, True, [kcur_b, qg_b[ti]], [bankb[b1]])
                    ACT(pt[:, 0, :], banks[b0][:, :], AF.Exp, [bankb[b0]], [ptb], scale=0.125)
                    ACT(pt[:, 1, :], banks[b1][:, :], AF.Exp, [bankb[b1]], [ptb], scale=0.125)
                    TT(pt[:, 0, :], pt[:, 0, :], pm_[:], ALU.mult, [ptb] + CONST, [ptb])
                    TT(pt[:, 1, :], pt[:, 1, :], maskc4[:], ALU.mult, [ptb] + CONST, [ptb])

            def emit_pv(idx):
                ti, j = steps[idx]
                st_ = idx % 2
                blk = slice(j * 128, (j + 1) * 128)
                bo = 4 + 2 * (idx % 2)
                bd = bo + 1
                for par in range(2):
                    pr = slice(par * 64, (par + 1) * 64)
                    pt = PT[st_ * 2 + par]
                    ptb = PT_b[st_ * 2 + par]
                    tp = (0, par * 64)
                    MM(banks[bo][pr, :], Vt[:, j, kh * 64:(kh + 1) * 64], pt[:, 0, :], True, False, [Vt_b[j], ptb], [bankb[bo]], tile_position=tp)
                    MM(banks[bo][pr, :], Vt[:, 1 + j, kh * 64:(kh + 1) * 64], pt[:, 1, :], False, True, [Vt_b[1 + j], ptb], [bankb[bo]], tile_position=tp)
                    MM(banks[bd][pr, :], ones_b[:, 0:64], pt[:, 0, :], True, False, [ptb] + CONST, [bankb[bd]], tile_position=tp)
                    MM(banks[bd][pr, :], ones_b[:, 0:64], pt[:, 1, :], False, False, [ptb] + CONST, [bankb[bd]], tile_position=tp)
                    MM(banks[bd][pr, :], ones_b[0:1, 0:64], sinkrow[0:1, kh, par, :, :].rearrange("p i q -> p (i q)"), False, True, CONST, [bankb[bd]], tile_position=tp)
                ACT(rden[:], banks[bd][:, :], AF.Ln, [bankb[bd]], [rden_b])
                ACT(rden[:], rden[:], AF.Exp, [rden_b], [rden_b], scale=-1.0)
                TT(at[:], banks[bo][:, :], rden[:], ALU.mult, [bankb[bo], rden_b], [at_b])
                TT(ogT[:, kh * 4:(kh + 1) * 4, blk], at[:].rearrange("p (i q) -> p i q", i=4), zg[:, :, blk], ALU.mult,
                   [at_b, qg_b[ti]], [og_b[kh * 4 + i][ti] for i in range(4)])

            for idx in range(len(steps) + 1):
                if idx < len(steps):
                    emit_scores(idx)
                if idx >= 1:
                    emit_pv(idx - 1)
        lastj = nblk - 1
        ACOPY(kT_halo[:], kT[:, :, 128 + lastj * 128:128 + (lastj + 1) * 128], [kT_b[1 + lastj // 4]], [halo_b])
        ACOPY(V_halo[:], Vt[:, 1 + lastj, :], [Vt_b[1 + lastj]], [halo_b])
        if has_s and os.environ.get("KB_L1", "") != "b":
            ti = len(tiles) - 1
            c0 = tiles[ti]["c0"]
            for s_ in range(NS):
                a = s_ % 2
                ksrc = ck[s_].rearrange("j (kh d) -> j kh d", kh=4)
                P.dma("pool", [(ckd[a][:, :, 0, :], ksrc), (ckd[a][:, :, 1, :], ksrc), (cvt[a][:], cv[s_])],
                      writes=[cc_b[a]], dbuf=cc_b[a])
                MM(banks[5][0:1, 0:256], ident_f[0:NS, s_:s_ + 1], ostg[0:NS, 0:256], True, True, [ostg_b] + CONST, [bankb[5]])
                MM(banks[5][0:1, 256:512], ident_f[0:NS, s_:s_ + 1], ostg[0:NS, 512:768], True, True, [ostg_b] + CONST, [bankb[5]])
                ACOPY(ckd[a][0:1, :, 0, :], banks[5][0:1, 0:256].rearrange("p (kh d) -> p kh d", kh=4), [bankb[5]], [cc_b[a]])
                ACOPY(ckd[a][0:1, :, 1, :], banks[5][0:1, 0:256].rearrange("p (kh d) -> p kh d", kh=4), [bankb[5]], [cc_b[a]])
                ACOPY(cvt[a][0:1, :], banks[5][0:1, 256:512], [bankb[5]], [cc_b[a]])
                for kh in range(4):
                    MM(banks[0][:, kh * 128:(kh + 1) * 128], ckd[a][:, kh, :, :].rearrange("p c d -> p (c d)"), ident_b[:], True, True, [cc_b[a]] + CONST, [bankb[0]])
                ACOPY(KcT[:], banks[0][:, :], [bankb[0]], [KcT_b])
                for par in range(2):
                    pr = slice(par * 64, (par + 1) * 64)
                    qb_ = 1 if par == 0 else 4
                    for kh in range(4):
                        MM(banks[qb_][:, kh * 4:(kh + 1) * 4], KcT[pr, kh * 128:(kh + 1) * 128], qsa[pr, kh * 4:(kh + 1) * 4, s_], True, True, [KcT_b, qsa_b], [bankb[qb_]])
                    ACT(PTs[:, par * 16:(par + 1) * 16], banks[qb_][:, 0:16], AF.Exp, [bankb[qb_]], [PTs_b], scale=0.125)
                for kh in range(4):
                    for par in range(2):
                        pr = slice(par * 64, (par + 1) * 64)
                        col = par * 16 + kh * 4
                        tp = (0, par * 64)
                        MM(banks[2][pr, kh * 4:(kh + 1) * 4], cvt[a][:, kh * 64:(kh + 1) * 64], PTs[:, col:col + 4], True, True, [cc_b[a], PTs_b], [bankb[2]], tile_position=tp)
                        MM(banks[3][pr, kh * 4:(kh + 1) * 4], ones_b[:, 0:64], PTs[:, col:col + 4], True, False, [PTs_b] + CONST, [bankb[3]], tile_position=tp)
                        MM(banks[3][pr, kh * 4:(kh + 1) * 4], ones_b[0:1, 0:64], sinkrow[0:1, kh, par, :, 0:1].rearrange("p i q -> p (i q)"), False, True, CONST, [bankb[3]], tile_position=tp)
                P.op("dve", (lambda o, i_: (lambda e: e.reciprocal(o, i_)))(rden[:, 0:16], banks[3][:, 0:16]), [bankb[3]], [rden_b])
                TT(at[:, 0:16], banks[2][:, 0:16], rden[:, 0:16], ALU.mult, [bankb[2], rden_b], [at_b])
                TT(ogT[:, :, c0 + s_], at[:, 0:16], zsa[:, :, s_], ALU.mult, [at_b, qsa_b], [og_b[h_][ti] for h_ in range(16)])

    def stage_y(sb, ph):
        tiles = sb_tiles(sb)
        first = sb["t0"] == 0
        yst = [P.sbuf("yst%d" % i, [128, D], F32, ph) for i in range(3)]
        yst_b = [Buf("yst") for _ in range(3)]
        k = 0
        for ti, t in enumerate(tiles):
            c0 = t["c0"]
            if t["samp"]:
                a = k % 3
                k += 1
                for half in range(2):
                    bk = half
                    for cc in range(4):
                        c = half * 4 + cc
                        MM(banks[bk][0:NS, cc * 128:(cc + 1) * 128], hT[:, c, c0:c0 + NS], ident_f[:], True, True, [hT_b[ti]] + CONST, [bankb[bk]])
                    ACOPY(yst[a][0:NS, half * 512:(half + 1) * 512], banks[bk][0:NS, :], [bankb[bk]], [yst_b[a]])
                P.dma("sp", [(y_d[2048:2048 + NS, :], yst[a][0:NS, :])], reads=[yst_b[a]], dbuf=yst_b[a], is_out=True)
                continue
            for j in t["blks"]:
                if first and j == 0:
                    continue
                a = k % 3
                k += 1
                for half in range(2):
                    bk = (2 * k + half) % 4
                    for cc in range(4):
                        c = half * 4 + cc
                        MM(banks[bk][:, cc * 128:(cc + 1) * 128], hT[:, c, j * 128:(j + 1) * 128], ident_f[:], True, True, [hT_b[ti]] + CONST, [bankb[bk]])
                    if half == 0:
                        ACOPY(yst[a][:, 0:512], banks[bk][:, :], [bankb[bk]], [yst_b[a]])
                    else:
                        TCOPY(yst[a][:, 512:1024], banks[bk][:, :], [bankb[bk]], [yst_b[a]])
                row = (j - 1) * 128 if first else (8 + j) * 128
                P.dma("sp", [(y_d[row:row + 128, :], yst[a][:])], reads=[yst_b[a]], dbuf=yst_b[a], is_out=True)

    def dump_h(sb):
        col0 = sb["t0"]
        for ti, t in enumerate(sb_tiles(sb)):
            c0, n = t["c0"], t["n"]
            g0 = (NMAIN if t["samp"] else col0 + c0)
            P.dma("sp", [(dbg_d[:, :, g0:g0 + n], hT[:, :, c0:c0 + n])], reads=[hT_b[ti]], dbuf=hT_b[ti], is_out=True)

    import os
    sel = os.environ.get("KB_SBS")
    for sbi, sb in enumerate(SBS):
        if sel is not None and str(sbi) not in sel.split(","):
            continue
        if sb["kind"] == "P" and stop_after in ("L0nopre", "LOAD"):
            continue
        ph = ExitStack()
        stage_load(sb, ph)
        stage_prenorm(sb, ph, V_PRE0, xnT, xnT_b)
        P.barrier()
        ph.close()
        if stop_after == "LOAD":
            dump_h(sb)
            P.barrier()
            continue
        ph = ExitStack()
        if sb["kind"] == "P":
            stage_l0u(sb, ph)
        else:
            stage_l0(sb, ph)
        P.barrier()
        ph.close()
        if sb["kind"] == "P":
            continue
        if os.environ.get("KB_OUT", "1") == "1":
            ph = ExitStack()
            stage_out(sb, ph, 0, w_out_a)
            P.barrier()
            ph.close()
        if stop_after in ("L0", "L0nopre"):
            dump_h(sb)
            P.barrier()
            continue
        ph = ExitStack()
        stage_l1(sb, sbi, ph)
        P.barrier()
        ph.close()
        if os.environ.get("KB_OUT1", "1") == "1":
            ph = ExitStack()
            stage_out(sb, ph, 1, w_out_b)
            P.barrier()
            ph.close()
        if stop_after == "L1":
            dump_h(sb)
            P.barrier()
        if os.environ.get("KB_Y", "1") == "1":
            ph = ExitStack()
            stage_y(sb, ph)
            P.barrier()
            ph.close()

    P.emit()
    print("stats:", P.stats)
    return nc


def _consts(half):
    ident = np.eye(128, dtype=np.float32)
    s = np.arange(128)[:, None]
    t = np.arange(128)[None, :]
    mtri = (t >= s).astype(np.uint32)
    maskc = (s <= t).astype(np.float32)
    maskp = (s > t).astype(np.float32)
    maskf = maskp.copy() if half == 1 else np.zeros((128, 128), np.float32)
    prot = np.zeros((128, 128), np.float32)
    for base in (0, 64):
        for i in range(8):
            prot[base + i + 8, base + i] = 1.0
            prot[base + i, base + i + 8] = 1.0
    pos = np.concatenate([half * 2048 - 128 + np.arange(NMAIN), np.full(NS, PAST_LEN)]).astype(np.float32)
    inv = (ROPE_THETA ** (-np.arange(0, 16, 2, dtype=np.float32) / 16)).astype(np.float32)
    ang = pos[None, :] * inv[:, None]
    cos = np.cos(ang).astype(np.float32)
    sin = np.sin(ang).astype(np.float32)
    ropec = np.ones((128, TTOT), np.float32)
    ropes = np.zeros((128, TTOT), np.float32)
    for base in (0, 64):
        ropec[base:base + 8] = cos
        ropec[base + 8:base + 16] = cos
        ropes[base:base + 8] = -sin
        ropes[base + 8:base + 16] = sin
    return dict(ident=ident, mtri=mtri, maskc=maskc, maskp=maskp, maskf=maskf, prot=prot, ropec=ropec, ropes=ropes)


def _col(v):
    v = np.asarray(v, np.float32).reshape(-1, 128)
    return np.ascontiguousarray(v.T)


def make_in_maps(inp):
    f = lambda a: np.ascontiguousarray(np.asarray(a, dtype=np.float32))
    xpr, xsm = f(inp["x_prompt"]), f(inp["x_sample"])
    ppr, psa = f(inp["p_prompt"]), f(inp["p_sample"])
    st, ck, cv = f(inp["state_hgrn"]), f(inp["cache_k"]), f(inp["cache_v"])
    vecs = np.concatenate([
        _col(inp["pre_norm_g"][0]), _col(inp["pre_norm_g"][1]), _col(inp["post_norm_g"][0]), _col(inp["post_norm_g"][1]),
        _col(inp["kv_norm_g"]), _col(inp["onorm_a"][0]), _col(inp["lb_logits"][0]), _col(inp["lb_logits"][1])], axis=1)
    assert vecs.shape == (128, NVEC)
    shared = dict(
        w_in_a=f(inp["w_in_a"][0]), w_out_a=f(inp["w_out_a"][0]), w_kv=f(inp["w_kv"]), w_in_b=f(inp["w_in_b"][0]),
        w_out_b=f(inp["w_out_b"][0]), w_pe=f(inp["w_pe"]), w_pg=f(inp["w_pg"]), vecs=np.ascontiguousarray(vecs),
        sinks=f(inp["sinks"]).reshape(1, 32))
    maps = []
    for c in range(8):
        b, half = c // 2, c % 2
        xm = np.zeros((NMAIN, D), np.float32)
        xp = np.zeros((NPRE, D), np.float32)
        pm = np.zeros((2, NMAIN, 256), np.float32)
        if half == 0:
            xm[128:] = xpr[b, 0:2048]
            pm[:, 128:] = ppr[:, b, 0:2048]
        else:
            xm[:] = xpr[b, 1920:4096]
            pm[:] = ppr[:, b, 1920:4096]
            xp[:] = xpr[b, 0:1920]
        m = dict(shared)
        m.update(_consts(half))
        m.update(xm=xm, xp=xp, xs=np.ascontiguousarray(xsm[c * NS:(c + 1) * NS, 0]), pm=pm,
                 psm=np.ascontiguousarray(psa[:, c * NS:(c + 1) * NS, 0]),
                 st_in=np.ascontiguousarray(st[0, c * NS:(c + 1) * NS]),
                 ck=np.ascontiguousarray(ck[c * NS:(c + 1) * NS].reshape(NS, 128, 256)),
                 cv=np.ascontiguousarray(cv[c * NS:(c + 1) * NS].reshape(NS, 128, 256)))
        maps.append(m)
    return maps


def assemble(results):
    y_p = np.zeros((4, 4096, D), np.float32)
    y_s = np.zeros((128, 1, D), np.float32)
    st_p = np.zeros((1, 4, 16, 128, 128), np.float32)
    st_s = np.zeros((1, 128, 16, 128, 128), np.float32)
    k_p = np.zeros((4, 128, 4, 64), np.float32)
    v_p = np.zeros((4, 128, 4, 64), np.float32)
    k_s = np.zeros((128, 1, 4, 64), np.float32)
    v_s = np.zeros((128, 1, 4, 64), np.float32)
    for c in range(8):
        r = results[c]
        b, half = c // 2, c % 2
        y_p[b, half * 2048:(half + 1) * 2048] = r["y"][0:2048]
        y_s[c * NS:(c + 1) * NS, 0] = r["y"][2048:2048 + NS]
        st_s[0, c * NS:(c + 1) * NS] = r["st_s"]
        k_s[c * NS:(c + 1) * NS, 0] = r["ks"].reshape(NS, 4, 64)
        v_s[c * NS:(c + 1) * NS, 0] = r["vs"].reshape(NS, 4, 64)
        if half == 1:
            st_p[0, b] = r["st_p"]
            k_p[b] = r["kp"].reshape(128, 4, 64)
            v_p[b] = r["vp"].reshape(128, 4, 64)
    return (y_p, y_s, st_p, st_s, k_p, v_p, k_s, v_s)


def kernel(**inputs):
    nc = build_program()
    in_maps = make_in_maps(inputs)
    res = run_bass_kernel_spmd(nc, in_maps, core_ids=list(range(8)))
    return assemble(res.results)
```

```python
import numpy as np
from contextlib import ExitStack
import concourse.bass as bass
import concourse.mybir as mybir
from concourse.bass_utils import run_bass_kernel_spmd

F32 = mybir.dt.float32
BF16 = mybir.dt.bfloat16
U32 = mybir.dt.uint32
ALU = mybir.AluOpType
AF = mybir.ActivationFunctionType

COMPUTE = ("pe", "act", "dve")
QUEUES = ("sp", "pool")
ALLENG = COMPUTE + QUEUES


class Buf:
    __slots__ = ("name", "last_w", "readers", "sem", "keep", "excl")

    def __init__(self, name="", keep=False, excl=False):
        self.excl = excl
        self.name = name
        self.last_w = None
        self.readers = {}
        self.sem = None
        self.keep = keep


class Carrier:
    __slots__ = ("cnt", "handle", "q")

    def __init__(self):
        self.cnt = 0
        self.handle = None
        self.q = None


class Op:
    __slots__ = ("eng", "fn", "deps", "signal", "sigval", "is_dma", "carrier", "dval")

    def __init__(self, eng, fn, is_dma=False):
        self.eng = eng
        self.fn = fn
        self.deps = []
        self.signal = False
        self.sigval = 0
        self.is_dma = is_dma
        self.carrier = None
        self.dval = 0


class Prog:
    def __init__(self, nc):
        self.nc = nc
        self.ops = []
        self.es = ExitStack()
        self.carriers = []
        self.free_carriers = {"sp": [], "pool": []}
        self.active_bufs = []
        self.out_ops = []
        self.last = {e: None for e in ALLENG}
        self.bar = None
        self.bar_pending = set()
        self.dma_since_bar = []
        self.nalloc = 0
        self.nuniq = 0

    def sbuf(self, name, shape, dtype, es=None):
        self.nalloc += 1
        return (es or self.es).enter_context(self.nc.sbuf_tensor("s%d_%s" % (self.nalloc, name), list(shape), dtype))

    def psum(self, name, shape, dtype=F32):
        return self.es.enter_context(self.nc.psum_tensor("p_" + name, list(shape), dtype))

    def _adddep(self, op, d):
        if d is op or d is None:
            return
        for x in op.deps:
            if x is d:
                return
        op.deps.append(d)
        d.signal = True

    def _deps(self, op, reads, writes):
        ex = [b for b in reads if b.excl and b not in writes]
        if ex:
            reads = [b for b in reads if not b.excl]
            writes = list(writes) + ex
        for b in reads:
            d = b.last_w
            if d is not None:
                if (not d.is_dma) and (not op.is_dma) and d.eng == op.eng and op.eng == "pe":
                    pass
                else:
                    self._adddep(op, d)
        for b in writes:
            cands = [b.last_w] + list(b.readers.values())
            for d in cands:
                if d is None:
                    continue
                if (not d.is_dma) and (not op.is_dma) and d.eng == op.eng:
                    continue
                self._adddep(op, d)
        if op.eng in self.bar_pending:
            self.bar_pending.discard(op.eng)
            for d in self.bar:
                if d.is_dma or d.eng != op.eng:
                    self._adddep(op, d)
        for b in reads:
            if op.is_dma:
                self.nuniq += 1
                b.readers["dma%d" % self.nuniq] = op
            else:
                b.readers[op.eng] = op
        for b in writes:
            b.last_w = op
            b.readers = {}
        self.last[op.eng] = op

    def op(self, eng, fn, reads=(), writes=()):
        o = Op(eng, fn)
        self._deps(o, reads, writes)
        self.ops.append(o)
        return o

    def dma(self, q, pairs, reads=(), writes=(), dbuf=None, is_out=False):
        if dbuf.sem is None:
            if self.free_carriers[q]:
                dbuf.sem = self.free_carriers[q].pop()
            else:
                dbuf.sem = Carrier()
                dbuf.sem.q = q
                self.carriers.append(dbuf.sem)
            self.active_bufs.append(dbuf)
        c = dbuf.sem
        assert c.q == q, "a DMA buffer must stay on one queue type"
        o = Op(q, pairs, is_dma=True)
        c.cnt += 16 * len(pairs)
        o.carrier = c
        o.dval = c.cnt
        o.signal = True
        self._deps(o, reads, writes)
        self.ops.append(o)
        self.dma_since_bar.append(o)
        if is_out:
            self.out_ops.append(o)
        return o

    def barrier(self):
        ops = [o for o in self.last.values() if o is not None and not o.is_dma]
        latest = {}
        for o in self.dma_since_bar:
            latest[id(o.carrier)] = o
        ops += list(latest.values())
        self.dma_since_bar = []
        self.bar = ops
        self.bar_pending = set(ALLENG)
        keep = []
        for b in self.active_bufs:
            if b.keep:
                keep.append(b)
            else:
                self.free_carriers[b.sem.q].append(b.sem)
                b.sem = None
        self.active_bufs = keep

    def emit(self):
        nc = self.nc
        es = self.es
        sems = {}
        for e in COMPUTE:
            sems[e] = es.enter_context(nc.semaphore("sem_" + e))
        for i, c in enumerate(self.carriers):
            c.handle = es.enter_context(nc.semaphore("dsem%d" % i))
        cnt = {e: 0 for e in COMPUTE}
        for o in self.ops:
            if o.is_dma:
                continue
            if o.signal:
                cnt[o.eng] += 1
                o.sigval = cnt[o.eng]
        by_eng = {e: [] for e in ALLENG}
        for o in self.ops:
            by_eng[o.eng].append(o)
        self.stats = {e: len(v) for e, v in by_eng.items()}
        self.stats["sig"] = dict(cnt)
        self.stats["dma_sems"] = len(self.carriers)
        final_waits = {}
        for o in self.out_ops:
            k = id(o.carrier)
            if k not in final_waits or final_waits[k][1] < o.dval:
                final_waits[k] = (o.carrier.handle, o.dval)

        def stream(ename, e):
            known = {}
            nwait = 0
            for o in by_eng[ename]:
                need = {}
                for d in o.deps:
                    if d.is_dma:
                        s, v = d.carrier.handle, d.dval
                    else:
                        s, v = sems[d.eng], d.sigval
                    k = id(s)
                    if k not in need or need[k][1] < v:
                        need[k] = (s, v)
                for k, (s, v) in need.items():
                    if known.get(k, 0) >= v:
                        continue
                    known[k] = v
                    e.wait_ge(s, v)
                    nwait += 1
                if o.is_dma:
                    for (out_ap, in_ap) in o.fn:
                        e.dma_start(out=out_ap, in_=in_ap).then_inc(o.carrier.handle, 16)
                else:
                    ins = o.fn(e)
                    if o.signal:
                        ins.then_inc(sems[ename], 1)
            if ename == "sp":
                for s, v in final_waits.values():
                    e.wait_ge(s, v)
            self.stats[ename + "_waits"] = nwait

        with nc.Block() as block:
            @block.tensor
            def _(e):
                stream("pe", e)

            @block.scalar
            def _(e):
                stream("act", e)

            @block.vector
            def _(e):
                stream("dve", e)

            @block.gpsimd
            def _(e):
                stream("pool", e)

            @block.sync
            def _(e):
                stream("sp", e)
        es.close()


D = 1024
NMAIN = 2176
NPRE = 1920
NS = 16
TTOT = NMAIN + NS
SBW = 1152
EPS = 1e-6
PAST_LEN = 16384
ROPE_THETA = 500000.0

V_PRE0, V_PRE1, V_POST0, V_POST1, V_KV, V_ON, V_LB0, V_LB1 = 0, 8, 16, 24, 32, 40, 56, 72
NVEC = 88

SBS = [
    dict(kind="P", t0=0, nblk=8, ns=0),
    dict(kind="P", t0=1024, nblk=7, ns=0),
    dict(kind="M", t0=0, nblk=9, ns=0),
    dict(kind="M", t0=1152, nblk=8, ns=NS),
]


def sb_tiles(sb):
    tiles = []
    nb = sb["nblk"]
    j = 0
    while j < nb:
        k = min(4, nb - j)
        tiles.append(dict(c0=j * 128, n=k * 128, blks=list(range(j, j + k)), samp=False))
        j += k
    if sb["ns"]:
        tiles.append(dict(c0=nb * 128, n=sb["ns"], blks=[], samp=True))
    return tiles


def build_program(stop_after=None):
    nc = bass.Bass("TRN2", target_bir_lowering=False)
    P = Prog(nc)

    def din(name, shape, dt=F32):
        return nc.dram_tensor(name, list(shape), dt, kind="ExternalInput").ap()

    def dout(name, shape, dt=F32):
        return nc.dram_tensor(name, list(shape), dt, kind="ExternalOutput").ap()

    xm = din("xm", [NMAIN, D])
    xp = din("xp", [NPRE, D])
    xs = din("xs", [NS, D])
    pm = din("pm", [2, NMAIN, 256])
    psm = din("psm", [2, NS, 256])
    st_in = din("st_in", [NS, 16, 128, 128])
    ck = din("ck", [NS, 128, 256])
    cv = din("cv", [NS, 128, 256])
    w_in_a = din("w_in_a", [D, 8192])
    w_out_a = din("w_out_a", [2048, D])
    w_kv = din("w_kv", [D, 512])
    w_in_b = din("w_in_b", [D, 4096])
    w_out_b = din("w_out_b", [2048, D])
    w_pe = din("w_pe", [2, 256, D])
    w_pg = din("w_pg", [2, D, D])
    vecs_d = din("vecs", [128, NVEC])
    sinks_d = din("sinks", [1, 32])
    ident_d = din("ident", [128, 128])
    mtri_d = din("mtri", [128, 128], U32)
    maskc_d = din("maskc", [128, 128])
    maskp_d = din("maskp", [128, 128])
    maskf_d = din("maskf", [128, 128])
    prot_d = din("prot", [128, 128])
    ropec_d = din("ropec", [128, TTOT])
    ropes_d = din("ropes", [128, TTOT])

    y_d = dout("y", [2048 + NS, D])
    stp_d = dout("st_p", [16, 128, 128])
    sts_d = dout("st_s", [NS, 16, 128, 128])
    kp_d = dout("kp", [128, 256])
    vp_d = dout("vp", [128, 256])
    ks_d = dout("ks", [NS, 256])
    vs_d = dout("vs", [NS, 256])
    dbg_d = dout("dbg", [128, 8, TTOT]) if stop_after else None

    hT = P.sbuf("hT", [128, 8, SBW], F32)
    xnT = P.sbuf("xnT", [128, 8, SBW], BF16)
    ogT = P.sbuf("ogT", [128, 16, SBW], BF16)
    pT = P.sbuf("pT", [128, 2, 2, SBW], BF16)
    Sst = P.sbuf("Sst", [128, 16, 128], F32)
    ident_f = P.sbuf("ident_f", [128, 128], F32)
    ident_b = P.sbuf("ident_b", [128, 128], BF16)
    ones_b = P.sbuf("ones_b", [128, 128], BF16)
    ones_f = P.sbuf("ones_f", [128, 128], F32)
    epsc = P.sbuf("epsc", [128, 1], F32)
    mtri = P.sbuf("mtri", [128, 128], U32)
    maskc = P.sbuf("maskc", [128, 128], BF16)
    maskp = P.sbuf("maskp", [128, 128], BF16)
    maskf = P.sbuf("maskf", [128, 128], BF16)
    prot = P.sbuf("prot", [128, 128], BF16)
    maskc4 = P.sbuf("maskc4", [128, 512], BF16)
    maskp4 = P.sbuf("maskp4", [128, 512], BF16)
    maskf4 = P.sbuf("maskf4", [128, 512], BF16)
    vecs = P.sbuf("vecs", [128, NVEC], F32)
    lbv = P.sbuf("lbv", [128, 16], F32)
    omlv = P.sbuf("omlv", [128, 16], F32)
    nomlv = P.sbuf("nomlv", [128, 16], F32)
    lnomlv = P.sbuf("lnomlv", [128, 16], F32)
    onec = P.sbuf("onec", [128, 1], F32)
    sinkx = P.sbuf("sinkx", [1, 32], F32)
    sinkrow = P.sbuf("sinkrow", [1, 4, 2, 4, 128], BF16)
    kT_halo = P.sbuf("kT_halo", [128, 4, 128], BF16)
    V_halo = P.sbuf("V_halo", [128, 256], BF16)

    MAXT = 3
    hT_b = [Buf("hT%d" % i, keep=True) for i in range(MAXT)]
    xnT_b = [Buf("xnT%d" % i) for i in range(MAXT)]
    pT_b = [Buf("pT%d" % i) for i in range(MAXT)]
    og_b = [[Buf("og%d_%d" % (h, i)) for i in range(MAXT)] for h in range(16)]
    S_b = [Buf("S%d" % h) for h in range(16)]
    const_b = Buf("const", keep=True)
    halo_b = Buf("halo")
    stp_cb = Buf("stp_carrier", keep=True)

    banks = [P.psum("bank%d" % i, [128, 512]) for i in range(8)]
    bankb = [Buf("bank%d" % i, excl=True) for i in range(8)]
    bq = [[bankb[i]] * 4 for i in range(8)]

    def bqs(i, q0=0, q1=4):
        return [bankb[i]]


    def MM(out, lhsT, rhs, start, stop, reads, writes, **kw):
        P.op("pe", lambda e: e.matmul(out, lhsT, rhs, start=start, stop=stop, **kw), reads, writes)

    def ACT(out, in_, func, reads, writes, bias=None, scale=None):
        kw = {}
        if bias is not None:
            kw["bias"] = bias
        if scale is not None:
            kw["scale"] = scale
        P.op("act", lambda e: e.activation(out, in_, func, **kw), reads, writes)

    def ACOPY(out, in_, reads, writes):
        P.op("act", lambda e: e.copy(out, in_), reads, writes)

    def TCOPY(out, in_, reads, writes):
        P.op("dve", lambda e: e.tensor_copy(out, in_), reads, writes)

    def TT(out, in0, in1, op, reads, writes):
        P.op("dve", lambda e: e.tensor_tensor(out, in0, in1, op), reads, writes)

    def TS(out, in0, s1, s2, op0, op1, reads, writes):
        if s2 is None:
            P.op("dve", lambda e: e.tensor_scalar(out, in0, s1, None, op0), reads, writes)
        else:
            P.op("dve", lambda e: e.tensor_scalar(out, in0, s1, s2, op0, op1), reads, writes)

    def STT(out, in0, sc, in1, op0, op1, reads, writes):
        P.op("dve", lambda e: e.scalar_tensor_tensor(out, in0, sc, in1, op0, op1), reads, writes)

    def SCAN(out, d0, d1, reads, writes):
        P.op("dve", lambda e: e.tensor_tensor_scan(out, d0, d1, 0.0, ALU.mult, ALU.add), reads, writes)

    def CPRED(out, mask, data, reads, writes):
        P.op("dve", lambda e: e.copy_predicated(out, mask, data), reads, writes)

    def MEMSET(ap, val, writes):
        P.op("dve", lambda e: e.memset(ap, val), (), writes)

    print("sbuf remaining after persistent:", nc.sbuf_bytes_remaining)

    P.dma("sp", [(ident_f[:], ident_d)], writes=[const_b], dbuf=const_b)
    P.dma("sp", [(mtri[:], mtri_d)], writes=[const_b], dbuf=const_b)
    P.dma("sp", [(vecs[:], vecs_d)], writes=[const_b], dbuf=const_b)
    P.dma("sp", [(sinkx[:], sinks_d)], writes=[const_b], dbuf=const_b)
    cb2 = Buf("const2", keep=True)
    P.dma("pool", [(ident_b[:], ident_d), (maskc[:], maskc_d), (maskp[:], maskp_d), (maskf[:], maskf_d),
                   (prot[:], prot_d)], writes=[cb2], dbuf=cb2)
    cb3 = Buf("const3")
    MEMSET(ones_b[:], 1.0, [cb3])
    MEMSET(ones_f[:], 1.0, [cb3])
    MEMSET(epsc[:], EPS, [cb3])
    MEMSET(Sst[:], 0.0, S_b)
    MEMSET(ogT[:], 0.0, [b for hb in og_b for b in hb])
    MEMSET(kT_halo[:], 0.0, [halo_b])
    MEMSET(V_halo[:], 0.0, [halo_b])
    TT(lbv[:], vecs[:, V_LB0:V_LB0 + 16], vecs[:, V_LB1:V_LB1 + 16], ALU.subtract, [const_b], [cb3])
    ACT(lbv[:], lbv[:], AF.Sigmoid, [cb3], [cb3])
    TS(omlv[:], lbv[:], -1.0, 1.0, ALU.mult, ALU.add, [cb3], [cb3])
    TS(nomlv[:], lbv[:], 1.0, -1.0, ALU.mult, ALU.add, [cb3], [cb3])
    MEMSET(onec[:], 1.0, [cb3])
    ACT(lnomlv[:], omlv[:], AF.Ln, [cb3], [cb3])
    ACT(sinkx[:], sinkx[:], AF.Exp, [const_b], [cb3])
    sx4 = sinkx[:].rearrange("p (kh i par o) -> p kh par i o", kh=4, i=4, par=2, o=1)
    TCOPY(sinkrow[:], sx4.to_broadcast([1, 4, 2, 4, 128]), [cb3], [cb3])
    for m4, m1 in ((maskc4, maskc), (maskp4, maskp), (maskf4, maskf)):
        for i in range(4):
            TCOPY(m4[:, i * 128:(i + 1) * 128], m1[:], [cb2], [cb3])
    CONST = [const_b, cb2, cb3]

    def gcol(base, c):
        return vecs[:, base + c:base + c + 1]

    def stage_load(sb, ph):
        kind = sb["kind"]
        src = xp if kind == "P" else xm
        tiles = sb_tiles(sb)
        NX = 6
        xin = [P.sbuf("xin%d" % i, [128, D], F32, ph) for i in range(NX)]
        xin_b = [Buf("xin%d" % i) for i in range(NX)]
        pin = [P.sbuf("pin%d" % i, [128, 2, 256], F32, ph) for i in range(NX)] if kind == "M" else None
        pin_b = [Buf("pin%d" % i) for i in range(NX)]
        cnt = 0
        ev = 0
        for ti, t in enumerate(tiles):
            c0, n = t["c0"], t["n"]
            if not t["samp"]:
                slots = []
                for j in t["blks"]:
                    s = cnt % NX
                    cnt += 1
                    r0 = sb["t0"] + j * 128
                    P.dma("sp", [(xin[s][:], src[r0:r0 + 128, :])], writes=[xin_b[s]], dbuf=xin_b[s])
                    if kind == "M":
                        P.dma("sp", [(pin[s][:], pm[:, r0:r0 + 128, :].rearrange("l t f -> t l f"))],
                              writes=[pin_b[s]], dbuf=pin_b[s])
                    slots.append(s)
                nb = len(slots)
                for c in range(8):
                    bk = c % 2
                    for jj, s in enumerate(slots):
                        MM(banks[bk][:, jj * 128:(jj + 1) * 128], xin[s][:, c * 128:(c + 1) * 128], ident_f[:], True, True,
                           [xin_b[s]] + CONST, [bq[bk][jj]])
                    ev += 1
                    if ev % 2 == 0:
                        ACOPY(hT[:, c, c0:c0 + n], banks[bk][:, 0:n], bqs(bk, 0, nb), [hT_b[ti]])
                    else:
                        TCOPY(hT[:, c, c0:c0 + n], banks[bk][:, 0:n], bqs(bk, 0, nb), [hT_b[ti]])
                if kind == "M":
                    for l in range(2):
                        for pc in range(2):
                            bk = 2 + (l * 2 + pc) % 2
                            for jj, s in enumerate(slots):
                                MM(banks[bk][:, jj * 128:(jj + 1) * 128], pin[s][:, l, pc * 128:(pc + 1) * 128], ident_f[:], True, True,
                                   [pin_b[s]] + CONST, [bq[bk][jj]])
                            ACOPY(pT[:, l, pc, c0:c0 + n], banks[bk][:, 0:n], bqs(bk, 0, nb), [pT_b[ti]])
            else:
                s = cnt % NX
                cnt += 1
                P.dma("sp", [(xin[s][0:NS, :], xs)], writes=[xin_b[s]], dbuf=xin_b[s])
                P.dma("sp", [(pin[s][0:NS, :, :], psm.rearrange("l t f -> t l f"))], writes=[pin_b[s]], dbuf=pin_b[s])
                for c in range(8):
                    MM(banks[0][:, c * NS:(c + 1) * NS], xin[s][0:NS, c * 128:(c + 1) * 128], ident_f[0:NS, 0:NS], True, True,
                       [xin_b[s]] + CONST, [bq[0][0]])
                ACOPY(hT[:, :, c0:c0 + NS], banks[0][:, 0:8 * NS].rearrange("p (c n) -> p c n", c=8), [bq[0][0]], [hT_b[ti]])
                for l in range(2):
                    for pc in range(2):
                        q = l * 2 + pc
                        MM(banks[1][:, q * NS:(q + 1) * NS], pin[s][0:NS, l, pc * 128:(pc + 1) * 128], ident_f[0:NS, 0:NS], True, True,
                           [pin_b[s]] + CONST, [bq[1][0]])
                ACOPY(pT[:, :, :, c0:c0 + NS], banks[1][:, 0:4 * NS].rearrange("p (l c n) -> p l c n", l=2, c=2), [bq[1][0]], [pT_b[ti]])

    def rms_stats(srcs, n, rt, reads, invd):
        sq, sq_b, lnv, rstd, r_b = rt
        ssb = 7
        for c, s_ap in enumerate(srcs):
            k = c % 2
            ACT(sq[k][:, 0:n], s_ap, AF.Square, reads, [sq_b[k]])
            MM(banks[ssb][:, 0:n], ones_b[:], sq[k][:, 0:n], c == 0, c == len(srcs) - 1, [sq_b[k]] + CONST, bqs(ssb))
        ACT(lnv[:, 0:n], banks[ssb][:, 0:n], AF.Ln, bqs(ssb) + CONST, [r_b], bias=epsc[:, 0:1], scale=invd)
        ACT(rstd[:, 0:n], lnv[:, 0:n], AF.Exp, [r_b], [r_b], scale=-0.5)
        return rstd

    def alloc_rms(ph, tag):
        sq = [P.sbuf("sq%s%d" % (tag, i), [128, 512], BF16, ph) for i in range(2)]
        sq_b = [Buf("sq") for _ in range(2)]
        lnv = P.sbuf("lnv" + tag, [128, 512], F32, ph)
        rstd = P.sbuf("rstd" + tag, [128, 512], F32, ph)
        return (sq, sq_b, lnv, rstd, Buf("rstd"))

    def stage_prenorm(sb, ph, gbase, dst, dst_b):
        rt = alloc_rms(ph, "pn%d" % gbase)
        for ti, t in enumerate(sb_tiles(sb)):
            c0, n = t["c0"], t["n"]
            rstd = rms_stats([hT[:, c, c0:c0 + n] for c in range(8)], n, rt, [hT_b[ti]], 1.0 / D)
            for c in range(8):
                STT(dst[:, c, c0:c0 + n], hT[:, c, c0:c0 + n], gcol(gbase, c), rstd[:, 0:n], ALU.mult, ALU.mult,
                    [hT_b[ti], rt[4]] + CONST, [dst_b[ti]])

    def stage_l0(sb, ph):
        kind = sb["kind"]
        full = kind == "M"
        tiles = sb_tiles(sb)
        nblk = sb["nblk"]
        has_s = sb["ns"] > 0
        wsl = [[P.sbuf("w0_%d_%d" % (s, k), [128, 8, 128], BF16, ph) for k in range(4)] for s in range(2)]
        wsl_b = [[Buf("w0") for k in range(4)] for s in range(2)]
        qd = [P.sbuf("qd%d" % s, [128, SBW], BF16, ph) for s in range(2)] if full else None
        kd = [P.sbuf("kd%d" % s, [128, SBW], BF16, ph) for s in range(2)]
        zs = [P.sbuf("zs%d" % s, [128, SBW], BF16, ph) for s in range(2)] if full else None
        Vtm = [P.sbuf("Vtm%d" % s, [128, 9, 128], BF16, ph) for s in range(2)]
        kdtm = [P.sbuf("kdtm%d" % s, [128, 9, 128], BF16, ph) for s in range(2)]
        hd = [P.sbuf("hd%d" % s, [128, 16], F32, ph) for s in range(2)]
        dec = [P.sbuf("dec%d" % s, [128, 16], F32, ph) for s in range(2)]
        negr = [P.sbuf("negr%d" % s, [128, 16], F32, ph) for s in range(2)]
        rr = [P.sbuf("rr%d" % s, [128, 16], F32, ph) for s in range(2)]
        hp_b = [[Buf("hp%d_%d" % (s, i)) for i in range(MAXT)] for s in range(2)]
        sc_b = [[Buf("sc%d_%d" % (s, i)) for i in range(MAXT)] for s in range(2)]
        tq = [P.sbuf("tq%d" % s, [128, 512], F32, ph) for s in range(2)] if full else None
        tsg = [P.sbuf("tsg%d" % s, [128, 512], F32, ph) for s in range(2)]
        tk = [P.sbuf("tk%d" % s, [128, 512], F32, ph) for s in range(2)]
        tb = [P.sbuf("tb%d" % s, [128, 512], F32, ph) for s in range(2)]
        tE1 = [P.sbuf("tE1%d" % s, [128, 512], BF16, ph) for s in range(2)] if full else None
        tE2 = [P.sbuf("tE2%d" % s, [128, 512], BF16, ph) for s in range(2)]
        tq_b = [Buf("tq") for _ in range(2)]
        tsg_b = [Buf("tsg") for _ in range(2)]
        tk_b = [Buf("tk") for _ in range(2)]
        tb_b = [Buf("tb") for _ in range(2)]
        tE1_b = [Buf("tE1") for _ in range(2)]
        tE2_b = [Buf("tE2") for _ in range(2)]
        ATs_b = [Buf("ATs") for _ in range(2)]
        Sp_b = [Buf("Sp") for _ in range(2)]
        Sd = [P.sbuf("Sd%d" % s, [128, 128], F32, ph) for s in range(2)]
        Sd_b = [Buf("Sd") for _ in range(2)]
        if full:
            ATs = [P.sbuf("ATs%d" % s, [128, 128], BF16, ph) for s in range(2)]
            Sp = [P.sbuf("Sp%d" % s, [128, 128], BF16, ph) for s in range(2)]
            osq = [P.sbuf("osq0", [128, 512], BF16, ph)] * 2
            osq_b = [Buf("osq")] * 2
            lnv0 = P.sbuf("lnv0", [128, 512], F32, ph)
            rstd0 = P.sbuf("rstd0", [128, 512], F32, ph)
            r0_b = Buf("r0")
            t1 = [P.sbuf("t1_0", [128, 512], F32, ph)] * 2
            t1_b = [Buf("t1")] * 2
            for s in range(2):
                MEMSET(ATs[s][:], 0.0, [ATs_b[s]])
        if has_s:
            qss = [P.sbuf("qss%d" % s, [128, NS], F32, ph) for s in range(2)]
            fss = [P.sbuf("fss%d" % s, [128, NS], F32, ph) for s in range(2)]
            kstm = [P.sbuf("kstm%d" % s, [NS, 128], BF16, ph) for s in range(2)]
            vstm = [P.sbuf("vstm%d" % s, [NS, 128], F32, ph) for s in range(2)]
            vblk = [P.sbuf("vblk%d" % s, [NS, NS, 128], BF16, ph) for s in range(2)]
            ss_b = [Buf("ss") for _ in range(2)]
            NSIN = 3
            sin = [P.sbuf("sin%d" % s, [128, 4, 128], F32, ph) for s in range(NSIN)]
            sin_b = [Buf("sin") for _ in range(NSIN)]
            sin_ctr = [0]
        print("  l0 scratch remaining:", nc.sbuf_bytes_remaining)

        def load_w(h):
            s = h % 2
            cols = [h * 128, 2048 + h * 128, 4096 + h * 128, 6144 + h * 128]
            for k in range(4):
                if not full and k in (0, 3):
                    continue
                P.dma("pool", [(wsl[s][k][:], w_in_a[:, cols[k]:cols[k] + 128].rearrange("(kc p) n -> p kc n", p=128))],
                      writes=[wsl_b[s][k]], dbuf=wsl_b[s][k])

        def head_proj(h):
            s = h % 2
            wq, wf, wi, wz = wsl[s]
            wq_b, wf_b, wi_b, wz_b = wsl_b[s]
            lnomc = lnomlv[:, h:h + 1]

            def stageA(ti):
                t = tiles[ti]
                c0, n = t["c0"], t["n"]
                p2 = ti % 2
                xr = [xnT_b[ti]]
                hpb = hp_b[s][ti]
                nb = len(t["blks"])
                for kc in range(8):
                    MM(banks[1][:, 0:n], wf[:, kc, :], xnT[:, kc, c0:c0 + n], kc == 0, kc == 7, [wf_b] + xr, bqs(1))
                yield
                ACT(tsg[p2][:, 0:n], banks[1][:, 0:n], AF.Exp, bqs(1), [tsg_b[p2]])
                ACT(tsg[p2][:, 0:n], tsg[p2][:, 0:n], AF.Ln, [tsg_b[p2]] + CONST, [tsg_b[p2]], bias=onec[:, 0:1], scale=1.0)
                ACT(tk[p2][:, 0:n], tsg[p2][:, 0:n], AF.Exp, [tsg_b[p2]] + CONST, [tk_b[p2]], bias=lnomc, scale=-1.0)
                if not t["samp"]:
                    ACT(tsg[p2][:, 0:n], tk[p2][:, 0:n], AF.Ln, [tk_b[p2]] + CONST, [tsg_b[p2]], bias=onec[:, 0:1], scale=-1.0)
                    j0 = t["blks"][0]
                    for jj, j in enumerate(t["blks"]):
                        for kc in range(8):
                            MM(banks[3][:, jj * 128:(jj + 1) * 128], xnT[:, kc, j * 128:(j + 1) * 128], wi[:, kc, :], kc == 0, kc == 7,
                               [wi_b] + xr, [bq[3][jj]])
                        if jj % 2 == 1 and jj + 1 < nb:
                            yield
                    TCOPY(Vtm[s][:, j0:j0 + nb, :], banks[3][:, 0:n].rearrange("p (j t) -> p j t", t=128), bqs(3, 0, nb), [hpb])
                else:
                    for kc in range(8):
                        MM(banks[3][0:NS, 0:128], xnT[:, kc, c0:c0 + NS], wi[:, kc, :], kc == 0, kc == 7, [wi_b] + xr, [bq[3][0]])
                yield
                if full:
                    for kc in range(8):
                        MM(banks[0][:, 0:n], wq[:, kc, :], xnT[:, kc, c0:c0 + n], kc == 0, kc == 7, [wq_b] + xr, bqs(0))
                    for kc in range(8):
                        MM(banks[2][:, 0:n], wz[:, kc, :], xnT[:, kc, c0:c0 + n], kc == 0, kc == 7, [wz_b] + xr, bqs(2))
                    yield
                    if not t["samp"]:
                        ACT(tq[p2][:, 0:n], banks[0][:, 0:n], AF.Silu, bqs(0), [tq_b[p2]])
                    else:
                        ACT(qss[s][:, 0:n], banks[0][:, 0:n], AF.Silu, bqs(0), [ss_b[s]])
                    ACT(zs[s][:, c0:c0 + n], banks[2][:, 0:n], AF.Silu, bqs(2), [hpb])
                if t["samp"]:
                    TS(fss[s][:, 0:n], tk[p2][:, 0:n], -1.0, 1.0, ALU.mult, ALU.add, [tk_b[p2]], [ss_b[s]])
                    TCOPY(kd[s][:, c0:c0 + n], tk[p2][:, 0:n], [tk_b[p2]], [hpb])
                    MM(banks[4][0:NS, 0:128], kd[s][:, c0:c0 + n], ident_b[:], True, True, [hpb] + CONST, [bq[4][0]])
                    ACOPY(kstm[s][:], banks[4][0:NS, 0:128], [bq[4][0]], [ss_b[s]])
                    ACOPY(vstm[s][:], banks[3][0:NS, 0:128], [bq[3][0]], [ss_b[s]])
                    for sp_ in range(NS):
                        TS(vblk[s][:, sp_, :], vstm[s][:], ident_f[0:NS, sp_:sp_ + 1], None, ALU.mult, None, [ss_b[s]] + CONST, [ss_b[s]])
                yield

            def stageB(ti):
                t = tiles[ti]
                c0, n = t["c0"], t["n"]
                p2 = ti % 2
                hpb = hp_b[s][ti]
                scb = sc_b[s][ti]
                nb = len(t["blks"])
                j0 = t["blks"][0]
                for jj in range(nb):
                    sl = slice(jj * 128, (jj + 1) * 128)
                    SCAN(tb[p2][:, sl], ones_f[:], tsg[p2][:, sl], [tsg_b[p2]] + CONST, [tb_b[p2]])
                tb3 = tb[p2][:, 0:n].rearrange("p (j t) -> p j t", t=128)
                blast = tb3[:, :, 127:128].rearrange("p j o -> p (j o)")
                TS(rr[s][:, j0:j0 + nb], blast, 0.5, None, ALU.mult, None, [tb_b[p2]], [scb])
                TT(tb3, tb3, rr[s][:, j0:j0 + nb].rearrange("p (j o) -> p j o", o=1).to_broadcast([128, nb, 128]), ALU.subtract,
                   [tb_b[p2], scb], [tb_b[p2]])
                yield
                ACT(hd[s][:, j0:j0 + nb], rr[s][:, j0:j0 + nb], AF.Exp, [scb], [scb])
                ACT(dec[s][:, j0:j0 + nb], rr[s][:, j0:j0 + nb], AF.Exp, [scb], [scb], scale=2.0)
                if full:
                    ACT(tE1[p2][:, 0:n], tb[p2][:, 0:n], AF.Exp, [tb_b[p2]], [tE1_b[p2]])
                ACT(tE2[p2][:, 0:n], tb[p2][:, 0:n], AF.Exp, [tb_b[p2]], [tE2_b[p2]], scale=-1.0)
                yield
                TT(kd[s][:, c0:c0 + n], tk[p2][:, 0:n], tE2[p2][:, 0:n], ALU.mult, [tk_b[p2], tE2_b[p2]], [hpb])
                if full:
                    TT(qd[s][:, c0:c0 + n], tq[p2][:, 0:n], tE1[p2][:, 0:n], ALU.mult, [tq_b[p2], tE1_b[p2]], [hpb])
                for jj, j in enumerate(t["blks"]):
                    MM(banks[4][:, jj * 128:(jj + 1) * 128], kd[s][:, j * 128:(j + 1) * 128], ident_b[:], True, True, [hpb] + CONST, [bq[4][jj]])
                TCOPY(kdtm[s][:, j0:j0 + nb, :], banks[4][:, 0:n].rearrange("p (j t) -> p j t", t=128), bqs(4, 0, nb), [hpb])
                yield

            nt = len(tiles)
            for r in range(nt + 1):
                gens = []
                if r >= 1 and not tiles[r - 1]["samp"]:
                    gens.append(stageB(r - 1))
                if r < nt:
                    gens.append(stageA(r))
                while gens:
                    for g in list(gens):
                        try:
                            next(g)
                            yield
                        except StopIteration:
                            gens.remove(g)

        def dphase(h, s, ti, c0, n):
            k = ti % 2
            oc = t1[0]
            TCOPY(oc[:, 0:n], banks[6][:, 0:n], bqs(6), [t1_b[0]])
            TT(osq[k][:, 0:n], oc[:, 0:n], oc[:, 0:n], ALU.mult, [t1_b[0]], [osq_b[k]])
            MM(banks[6][:, 0:n], ones_b[:], osq[k][:, 0:n], True, True, [osq_b[k]] + CONST, bqs(6))
            ACT(lnv0[:, 0:n], banks[6][:, 0:n], AF.Ln, bqs(6) + CONST, [r0_b], bias=epsc[:, 0:1], scale=1.0 / 128)
            ACT(rstd0[:, 0:n], lnv0[:, 0:n], AF.Exp, [r0_b], [r0_b], scale=-0.5)
            TT(oc[:, 0:n], oc[:, 0:n], rstd0[:, 0:n], ALU.mult, [t1_b[0], r0_b], [t1_b[0]])
            STT(ogT[:, h, c0:c0 + n], oc[:, 0:n], gcol(V_ON, h), zs[s][:, c0:c0 + n], ALU.mult, ALU.mult,
                [t1_b[0], hp_b[s][ti]] + CONST, [og_b[h][ti]])

        def head_scan(h):
            s = h % 2
            Sh = Sst[:, h, :]
            pending = []

            def flush():
                for fn in pending:
                    fn()
                del pending[:]

            blocks = [(ti, jj, j) for ti, t in enumerate(tiles) if not t["samp"] for jj, j in enumerate(t["blks"])]
            slots_ = {}
            NPF = 2

            def load_sin(g4):
                g = sin_ctr[0] % NSIN
                sin_ctr[0] += 1
                slots_[g4] = g
                P.dma("sp", [(sin[g][:], st_in[g4 * 4:g4 * 4 + 4, h].rearrange("s k v -> k s v"))], writes=[sin_b[g]], dbuf=sin_b[g])

            if has_s:
                for g4 in range(NPF):
                    load_sin(g4)

            def pe_front(bi):
                ti, jj, j = blocks[bi]
                a = j % 2
                blk = slice(j * 128, (j + 1) * 128)
                MM(banks[7][:, a * 128:(a + 1) * 128], kdtm[s][:, j, :], Vtm[s][:, j, :], True, True, [hp_b[s][ti]], [bq[7][a]])
                if full:
                    MM(banks[5][:, a * 128:(a + 1) * 128], kd[s][:, blk], qd[s][:, blk], True, True, [hp_b[s][ti]], [bq[5][a]])

            pe_front(0)
            for bi, (ti, jj, j) in enumerate(blocks):
                t = tiles[ti]
                c0, n = t["c0"], t["n"]
                nb = len(t["blks"])
                hpr = [hp_b[s][ti]]
                scr = [sc_b[s][ti]]
                a = j % 2
                blk = slice(j * 128, (j + 1) * 128)
                Uq = banks[7][:, a * 128:(a + 1) * 128]
                Aq = banks[5][:, a * 128:(a + 1) * 128]
                if bi + 1 < len(blocks):
                    pe_front(bi + 1)
                if full:
                    TS(Sp[a][:], Sh, hd[s][:, j:j + 1], None, ALU.mult, None, [S_b[h]] + scr, [Sp_b[a]])
                TS(Sd[a][:], Uq, hd[s][:, j:j + 1], None, ALU.mult, None, [bq[7][a]] + scr, [Sd_b[a]])
                if full:
                    CPRED(ATs[a][:], mtri[:], Aq, [bq[5][a], ATs_b[a]] + CONST, [ATs_b[a]])
                STT(Sh, Sh, dec[s][:, j:j + 1], Sd[a][:], ALU.mult, ALU.add, [S_b[h], Sd_b[a]] + scr, [S_b[h]])
                flush()
                if full:
                    def cons(a=a, j=j, jj=jj, blk=blk, hpr=hpr):
                        oq = banks[6][:, jj * 128:(jj + 1) * 128]
                        MM(oq, Sp[a][:], qd[s][:, blk], True, False, [Sp_b[a]] + hpr, [bq[6][jj]])
                        MM(oq, Vtm[s][:, j, :], ATs[a][:], False, True, [ATs_b[a]] + hpr, [bq[6][jj]])
                    pending.append(cons)
                    if jj == nb - 1:
                        pending.append(lambda ti=ti, c0=c0, n=n: dphase(h, s, ti, c0, n))
                yield
            flush()
            yield
            if has_s:
                P.dma("sp", [(stp_d[h], Sh)], reads=[S_b[h]], dbuf=stp_cb, is_out=True)
                ti = len(tiles) - 1
                c0 = tiles[ti]["c0"]
                for g4 in range(4):
                    s0 = g4 * 4
                    bk = 4 + g4 % 2
                    g = slots_[g4]
                    MM(banks[bk][:, :], kstm[s][:], vblk[s][:, s0:s0 + 4, :], True, True, [ss_b[s]], bqs(bk))
                    for si in range(4):
                        sidx = s0 + si
                        STT(sin[g][:, si, :], sin[g][:, si, :], fss[s][:, sidx:sidx + 1], banks[bk][:, si * 128:(si + 1) * 128], ALU.mult, ALU.add,
                            [sin_b[g], ss_b[s], bq[bk][si]], [sin_b[g]])
                        MM(banks[6][:, sidx:sidx + 1], sin[g][:, si, :], qss[s][:, sidx:sidx + 1], True, True, [sin_b[g], ss_b[s]], [bq[6][0]])
                    P.dma("sp", [(sts_d[s0:s0 + 4, h].rearrange("s k v -> k s v"), sin[g][:])], reads=[sin_b[g]], dbuf=sin_b[g], is_out=True)
                    if g4 + NPF < 4:
                        load_sin(g4 + NPF)
                    yield
                dphase(h, s, ti, c0, NS)
                yield

        def drive(gens):
            RATIO = int(os.environ.get("KB_RATIO", "2"))
            gens = [[g, (RATIO if i == 0 else 1)] for i, g in enumerate(gens) if g is not None]
            while gens:
                for it in list(gens):
                    for _ in range(it[1]):
                        try:
                            next(it[0])
                        except StopIteration:
                            gens.remove(it)
                            break

        import os
        NH = int(os.environ.get("KB_NH", "16"))
        load_w(0)
        prev_scan = None
        for h in range(NH):
            if h + 1 < NH:
                load_w(h + 1)
            drive([head_proj(h), prev_scan])
            prev_scan = head_scan(h)
        drive([prev_scan])

    def stage_l0u(sb, ph):
        kind = sb["kind"]
        full = kind == "M"
        tiles = sb_tiles(sb)
        has_s = sb["ns"] > 0
        NU = 3
        wsl = [[P.sbuf("w0_%d_%d" % (s, k), [128, 8, 128], BF16, ph) for k in range(4)] for s in range(NU)]
        wsl_b = [[Buf("w0") for k in range(4)] for s in range(NU)]
        qd = [P.sbuf("qd%d" % s, [128, 512], BF16, ph) for s in range(NU)] if full else None
        kd = [P.sbuf("kd%d" % s, [128, 512], BF16, ph) for s in range(NU)]
        zs = [P.sbuf("zs%d" % s, [128, 512], BF16, ph) for s in range(NU)] if full else None
        Vtm = [P.sbuf("Vtm%d" % s, [128, 4, 128], BF16, ph) for s in range(NU)]
        kdtm = [P.sbuf("kdtm%d" % s, [128, 4, 128], BF16, ph) for s in range(NU)]
        hd = [P.sbuf("hd%d" % s, [128, 4], F32, ph) for s in range(NU)]
        dec = [P.sbuf("dec%d" % s, [128, 4], F32, ph) for s in range(NU)]
        rr = [P.sbuf("rr%d" % s, [128, 4], F32, ph) for s in range(NU)]
        hp_b = [Buf("hp%d" % s) for s in range(NU)]
        sc_b = [Buf("sc%d" % s) for s in range(NU)]
        tq = [P.sbuf("tq%d" % s, [128, 512], F32, ph) for s in range(2)] if full else None
        tsg = [P.sbuf("tsg%d" % s, [128, 512], F32, ph) for s in range(2)]
        tk = [P.sbuf("tk%d" % s, [128, 512], F32, ph) for s in range(2)]
        tb = [P.sbuf("tb%d" % s, [128, 512], F32, ph) for s in range(2)]
        tE1 = [P.sbuf("tE1%d" % s, [128, 512], BF16, ph) for s in range(2)] if full else None
        tE2 = [P.sbuf("tE2%d" % s, [128, 512], BF16, ph) for s in range(2)]
        tq_b = [Buf("tq") for _ in range(2)]
        tsg_b = [Buf("tsg") for _ in range(2)]
        tk_b = [Buf("tk") for _ in range(2)]
        tb_b = [Buf("tb") for _ in range(2)]
        tE1_b = [Buf("tE1") for _ in range(2)]
        tE2_b = [Buf("tE2") for _ in range(2)]
        ATs_b = [Buf("ATs") for _ in range(2)]
        Sp_b = [Buf("Sp") for _ in range(2)]
        Sd = [P.sbuf("Sd%d" % s, [128, 128], F32, ph) for s in range(2)]
        Sd_b = [Buf("Sd") for _ in range(2)]
        if full:
            ATs = [P.sbuf("ATs%d" % s, [128, 128], BF16, ph) for s in range(2)]
            Sp = [P.sbuf("Sp%d" % s, [128, 128], BF16, ph) for s in range(2)]
            osq = P.sbuf("osq0", [128, 512], BF16, ph)
            osq_b = Buf("osq")
            lnv0 = P.sbuf("lnv0", [128, 512], F32, ph)
            rstd0 = P.sbuf("rstd0", [128, 512], F32, ph)
            r0_b = Buf("r0")
            oc = P.sbuf("oc", [128, 512], F32, ph)
            oc_b = Buf("oc")
            for s in range(2):
                MEMSET(ATs[s][:], 0.0, [ATs_b[s]])
        if has_s:
            qss = [P.sbuf("qss%d" % s, [128, NS], F32, ph) for s in range(NU)]
            fss = [P.sbuf("fss%d" % s, [128, NS], F32, ph) for s in range(NU)]
            kstm = [P.sbuf("kstm%d" % s, [NS, 128], BF16, ph) for s in range(NU)]
            vstm = [P.sbuf("vstm%d" % s, [NS, 128], F32, ph) for s in range(NU)]
            ss_b = [Buf("ss") for _ in range(NU)]
            vblk = P.sbuf("vblk", [NS, NS, 128], BF16, ph)
            vblk_b = Buf("vblk")
            NSIN = 3
            sin = [P.sbuf("sin%d" % s, [128, 4, 128], F32, ph) for s in range(NSIN)]
            sin_b = [Buf("sin") for _ in range(NSIN)]
            sin_ctr = [0]
        if not full:
            ones512 = P.sbuf("ones512", [128, 512], F32, ph)
            ones512_b = Buf("ones512")
            MEMSET(ones512[:], 1.0, [ones512_b])
        print("  l0u scratch remaining:", nc.sbuf_bytes_remaining)

        units = [(ti, h) for ti in range(len(tiles)) for h in range(16)]
        NUN = len(units)

        def load_w(ui):
            ti, h = units[ui]
            s = ui % NU
            cols = [h * 128, 2048 + h * 128, 4096 + h * 128, 6144 + h * 128]
            for k in range(4):
                if not full and k in (0, 3):
                    continue
                P.dma("pool", [(wsl[s][k][:], w_in_a[:, cols[k]:cols[k] + 128].rearrange("(kc p) n -> p kc n", p=128))],
                      writes=[wsl_b[s][k]], dbuf=wsl_b[s][k])

        sin_slots = {}

        def load_sin(h, g4):
            g = sin_ctr[0] % NSIN
            sin_ctr[0] += 1
            sin_slots[(h, g4)] = g
            P.dma("sp", [(sin[g][:], st_in[g4 * 4:g4 * 4 + 4, h].rearrange("s k v -> k s v"))], writes=[sin_b[g]], dbuf=sin_b[g])

        def stageA(ui):
            ti, h = units[ui]
            s = ui % NU
            p2 = ui % 2
            t = tiles[ti]
            c0, n = t["c0"], t["n"]
            nb = len(t["blks"])
            wq, wf, wi, wz = wsl[s]
            wq_b, wf_b, wi_b, wz_b = wsl_b[s]
            lnomc = lnomlv[:, h:h + 1]
            xr = [xnT_b[ti]]
            hpb = hp_b[s]
            for kc in range(8):
                MM(banks[1][:, 0:n], wf[:, kc, :], xnT[:, kc, c0:c0 + n], kc == 0, kc == 7, [wf_b] + xr, bqs(1))
            yield
            ACT(tsg[p2][:, 0:n], banks[1][:, 0:n], AF.Exp, bqs(1), [tsg_b[p2]])
            ACT(tsg[p2][:, 0:n], tsg[p2][:, 0:n], AF.Ln, [tsg_b[p2]] + CONST, [tsg_b[p2]], bias=onec[:, 0:1], scale=1.0)
            ACT(tk[p2][:, 0:n], tsg[p2][:, 0:n], AF.Exp, [tsg_b[p2]] + CONST, [tk_b[p2]], bias=lnomc, scale=-1.0)
            if not t["samp"]:
                ACT(tsg[p2][:, 0:n], tk[p2][:, 0:n], AF.Ln, [tk_b[p2]] + CONST, [tsg_b[p2]], bias=onec[:, 0:1], scale=-1.0)
                for jj, j in enumerate(t["blks"]):
                    for kc in range(8):
                        MM(banks[3][:, jj * 128:(jj + 1) * 128], xnT[:, kc, j * 128:(j + 1) * 128], wi[:, kc, :], kc == 0, kc == 7,
                           [wi_b] + xr, [bq[3][jj]])
                    if jj % 2 == 1 and jj + 1 < nb:
                        yield
                TCOPY(Vtm[s][:, 0:nb, :], banks[3][:, 0:n].rearrange("p (j t) -> p j t", t=128), bqs(3, 0, nb), [hpb])
            else:
                for kc in range(8):
                    MM(banks[3][0:NS, 0:128], xnT[:, kc, c0:c0 + NS], wi[:, kc, :], kc == 0, kc == 7, [wi_b] + xr, [bq[3][0]])
            yield
            if full:
                for kc in range(8):
                    MM(banks[0][:, 0:n], wq[:, kc, :], xnT[:, kc, c0:c0 + n], kc == 0, kc == 7, [wq_b] + xr, bqs(0))
                for kc in range(8):
                    MM(banks[2][:, 0:n], wz[:, kc, :], xnT[:, kc, c0:c0 + n], kc == 0, kc == 7, [wz_b] + xr, bqs(2))
                yield
                if not t["samp"]:
                    ACT(tq[p2][:, 0:n], banks[0][:, 0:n], AF.Silu, bqs(0), [tq_b[p2]])
                else:
                    ACT(qss[s][:, 0:n], banks[0][:, 0:n], AF.Silu, bqs(0), [ss_b[s]])
                ACT(zs[s][:, 0:n], banks[2][:, 0:n], AF.Silu, bqs(2), [hpb])
            if t["samp"]:
                TS(fss[s][:, 0:n], tk[p2][:, 0:n], -1.0, 1.0, ALU.mult, ALU.add, [tk_b[p2]], [ss_b[s]])
                TCOPY(kd[s][:, 0:n], tk[p2][:, 0:n], [tk_b[p2]], [hpb])
                MM(banks[4][0:NS, 0:128], kd[s][:, 0:n], ident_b[:], True, True, [hpb] + CONST, [bq[4][0]])
                ACOPY(kstm[s][:], banks[4][0:NS, 0:128], [bq[4][0]], [ss_b[s]])
                ACOPY(vstm[s][:], banks[3][0:NS, 0:128], [bq[3][0]], [ss_b[s]])
            yield

        def stageB(ui):
            ti, h = units[ui]
            s = ui % NU
            p2 = ui % 2
            t = tiles[ti]
            if t["samp"]:
                return
            n = t["n"]
            nb = len(t["blks"])
            hpb = hp_b[s]
            scb = sc_b[s]
            if not full:
                SCAN(tb[p2][:, 0:n], ones512[:, 0:n], tsg[p2][:, 0:n], [tsg_b[p2], ones512_b], [tb_b[p2]])
                yield
                bend = tb[p2][:, n - 1:n]
                ACT(dec[s][:, 0:1], bend, AF.Exp, [tb_b[p2]], [scb])
                ACT(tE2[p2][:, 0:n], tb[p2][:, 0:n], AF.Exp, [tb_b[p2]], [tE2_b[p2]], bias=bend, scale=-1.0)
                yield
                TT(kd[s][:, 0:n], tk[p2][:, 0:n], tE2[p2][:, 0:n], ALU.mult, [tk_b[p2], tE2_b[p2]], [hpb])
                for jj in range(nb):
                    MM(banks[4][:, jj * 128:(jj + 1) * 128], kd[s][:, jj * 128:(jj + 1) * 128], ident_b[:], True, True, [hpb] + CONST, [bq[4][jj]])
                TCOPY(kdtm[s][:, 0:nb, :], banks[4][:, 0:n].rearrange("p (j t) -> p j t", t=128), bqs(4, 0, nb), [hpb])
                yield
                return
            for jj in range(nb):
                sl = slice(jj * 128, (jj + 1) * 128)
                SCAN(tb[p2][:, sl], ones_f[:], tsg[p2][:, sl], [tsg_b[p2]] + CONST, [tb_b[p2]])
            tb3 = tb[p2][:, 0:n].rearrange("p (j t) -> p j t", t=128)
            blast = tb3[:, :, 127:128].rearrange("p j o -> p (j o)")
            TS(rr[s][:, 0:nb], blast, 0.5, None, ALU.mult, None, [tb_b[p2]], [scb])
            TT(tb3, tb3, rr[s][:, 0:nb].rearrange("p (j o) -> p j o", o=1).to_broadcast([128, nb, 128]), ALU.subtract,
               [tb_b[p2], scb], [tb_b[p2]])
            yield
            ACT(hd[s][:, 0:nb], rr[s][:, 0:nb], AF.Exp, [scb], [scb])
            ACT(dec[s][:, 0:nb], rr[s][:, 0:nb], AF.Exp, [scb], [scb], scale=2.0)
            if full:
                ACT(tE1[p2][:, 0:n], tb[p2][:, 0:n], AF.Exp, [tb_b[p2]], [tE1_b[p2]])
            ACT(tE2[p2][:, 0:n], tb[p2][:, 0:n], AF.Exp, [tb_b[p2]], [tE2_b[p2]], scale=-1.0)
            yield
            TT(kd[s][:, 0:n], tk[p2][:, 0:n], tE2[p2][:, 0:n], ALU.mult, [tk_b[p2], tE2_b[p2]], [hpb])
            if full:
                TT(qd[s][:, 0:n], tq[p2][:, 0:n], tE1[p2][:, 0:n], ALU.mult, [tq_b[p2], tE1_b[p2]], [hpb])
            for jj in range(nb):
                MM(banks[4][:, jj * 128:(jj + 1) * 128], kd[s][:, jj * 128:(jj + 1) * 128], ident_b[:], True, True, [hpb] + CONST, [bq[4][jj]])
            TCOPY(kdtm[s][:, 0:nb, :], banks[4][:, 0:n].rearrange("p (j t) -> p j t", t=128), bqs(4, 0, nb), [hpb])
            yield

        def dphase(h, s, c0, n):
            ACT(oc[:, 0:n], banks[6][:, 0:n], AF.Identity, bqs(6), [oc_b])
            ACT(osq[:, 0:n], oc[:, 0:n], AF.Square, [oc_b], [osq_b])
            MM(banks[6][:, 0:n], ones_b[:], osq[:, 0:n], True, True, [osq_b] + CONST, bqs(6))
            ACT(lnv0[:, 0:n], banks[6][:, 0:n], AF.Ln, bqs(6) + CONST, [r0_b], bias=epsc[:, 0:1], scale=1.0 / 128)
            ACT(rstd0[:, 0:n], lnv0[:, 0:n], AF.Exp, [r0_b], [r0_b], scale=-0.5)
            TT(oc[:, 0:n], oc[:, 0:n], rstd0[:, 0:n], ALU.mult, [oc_b, r0_b], [oc_b])
            STT(ogT[:, h, c0:c0 + n], oc[:, 0:n], gcol(V_ON, h), zs[s][:, 0:n], ALU.mult, ALU.mult,
                [oc_b, hp_b[s]] + CONST, [og_b[h][0], og_b[h][1], og_b[h][2]])

        def scan(ui):
            ti, h = units[ui]
            s = ui % NU
            t = tiles[ti]
            c0, n = t["c0"], t["n"]
            nb = len(t["blks"])
            Sh = Sst[:, h, :]
            hpr = [hp_b[s]]
            scr = [sc_b[s]]
            if not t["samp"] and not full:
                for jj in range(nb):
                    MM(banks[7][:, 0:128], kdtm[s][:, jj, :], Vtm[s][:, jj, :], jj == 0, jj == nb - 1, hpr, [bq[7][0]])
                yield
                STT(Sh, Sh, dec[s][:, 0:1], banks[7][:, 0:128], ALU.mult, ALU.add, [S_b[h], bq[7][0]] + scr, [S_b[h]])
                yield
                return
            if not t["samp"]:
                def pe_front(jj):
                    a = jj % 2
                    blk = slice(jj * 128, (jj + 1) * 128)
                    MM(banks[7][:, a * 128:(a + 1) * 128], kdtm[s][:, jj, :], Vtm[s][:, jj, :], True, True, hpr, [bq[7][a]])
                    if full:
                        MM(banks[5][:, a * 128:(a + 1) * 128], kd[s][:, blk], qd[s][:, blk], True, True, hpr, [bq[5][a]])
                pend = []
                pe_front(0)
                for jj in range(nb):
                    a = jj % 2
                    blk = slice(jj * 128, (jj + 1) * 128)
                    Uq = banks[7][:, a * 128:(a + 1) * 128]
                    Aq = banks[5][:, a * 128:(a + 1) * 128]
                    if jj + 1 < nb:
                        pe_front(jj + 1)
                    if full:
                        ACT(Sp[a][:], Sh, AF.Identity, [S_b[h]] + scr, [Sp_b[a]], scale=hd[s][:, jj:jj + 1])
                    ACT(Sd[a][:], Uq, AF.Identity, [bq[7][a]] + scr, [Sd_b[a]], scale=hd[s][:, jj:jj + 1])
                    if full:
                        CPRED(ATs[a][:], mtri[:], Aq, [bq[5][a], ATs_b[a]] + CONST, [ATs_b[a]])
                    STT(Sh, Sh, dec[s][:, jj:jj + 1], Sd[a][:], ALU.mult, ALU.add, [S_b[h], Sd_b[a]] + scr, [S_b[h]])
                    for fn in pend:
                        fn()
                    del pend[:]
                    if full:
                        def cons(a=a, jj=jj, blk=blk):
                            oq = banks[6][:, jj * 128:(jj + 1) * 128]
                            MM(oq, Sp[a][:], qd[s][:, blk], True, False, [Sp_b[a]] + hpr, [bq[6][jj]])
                            MM(oq, Vtm[s][:, jj, :], ATs[a][:], False, True, [ATs_b[a]] + hpr, [bq[6][jj]])
                        pend.append(cons)
                    yield
                for fn in pend:
                    fn()
                if full:
                    dphase(h, s, c0, n)
                yield
                return
            P.dma("sp", [(stp_d[h], Sh)], reads=[S_b[h]], dbuf=stp_cb, is_out=True)
            for g4 in range(2):
                load_sin(h, g4)
            for sp_ in range(NS):
                TS(vblk[:, sp_, :], vstm[s][:], ident_f[0:NS, sp_:sp_ + 1], None, ALU.mult, None, [ss_b[s]] + CONST, [vblk_b])
            yield
            for g4 in range(4):
                s0 = g4 * 4
                bk = 5 if g4 % 2 == 0 else 7
                g = sin_slots[(h, g4)]
                MM(banks[bk][:, :], kstm[s][:], vblk[:, s0:s0 + 4, :], True, True, [ss_b[s], vblk_b], bqs(bk))
                for si in range(4):
                    sidx = s0 + si
                    STT(sin[g][:, si, :], sin[g][:, si, :], fss[s][:, sidx:sidx + 1], banks[bk][:, si * 128:(si + 1) * 128], ALU.mult, ALU.add,
                        [sin_b[g], ss_b[s], bq[bk][si]], [sin_b[g]])
                    MM(banks[6][:, sidx:sidx + 1], sin[g][:, si, :], qss[s][:, sidx:sidx + 1], True, True, [sin_b[g], ss_b[s]], [bq[6][0]])
                P.dma("sp", [(sts_d[s0:s0 + 4, h].rearrange("s k v -> k s v"), sin[g][:])], reads=[sin_b[g]], dbuf=sin_b[g], is_out=True)
                if g4 + 2 < 4:
                    load_sin(h, g4 + 2)
                yield
            dphase(h, s, c0, NS)
            yield

        load_w(0)
        if NUN > 1:
            load_w(1)
        for idx in range(NUN + 2):
            if idx + 2 < NUN:
                load_w(idx + 2)
            gens = []
            if 0 <= idx - 1 < NUN:
                gens.append(stageB(idx - 1))
            if idx < NUN:
                gens.append(stageA(idx))
            if 0 <= idx - 2 < NUN:
                gens.append(scan(idx - 2))
            while gens:
                for g in list(gens):
                    try:
                        next(g)
                    except StopIteration:
                        gens.remove(g)

    def stage_out(sb, ph, l, w_out):
        tiles = sb_tiles(sb)
        mix = P.sbuf("mix", [128, 8, SBW], F32, ph)
        mix_b = [[Buf("mix") for _ in range(MAXT)] for _ in range(8)]
        wo = [P.sbuf("wo%d" % i, [128, 16, 128], BF16, ph) for i in range(2)]
        wo_b = [Buf("wo") for _ in range(2)]
        rt = alloc_rms(ph, "po")
        tmp = [P.sbuf("tmpo%d" % i, [128, 512], F32, ph) for i in range(2)]
        tmp_b = [Buf("tmpo") for _ in range(2)]
        wg = [P.sbuf("wg%d" % i, [128, 8, 128], BF16, ph) for i in range(2)]
        wg_b = [Buf("wg") for _ in range(2)]
        wp = [P.sbuf("wp%d" % i, [128, 2, 128], BF16, ph) for i in range(2)]
        sgt = [P.sbuf("sgt%d" % i, [128, 512], F32, ph) for i in range(2)]
        sgt_b = [Buf("sgt") for _ in range(2)]
        print("  out scratch remaining:", nc.sbuf_bytes_remaining)

        def ldo(dc):
            P.dma("pool", [(wo[dc % 2][:], w_out[:, dc * 128:(dc + 1) * 128].rearrange("(cc p) n -> p cc n", p=128))],
                  writes=[wo_b[dc % 2]], dbuf=wo_b[dc % 2])
        ldo(0)
        k = 0
        for dc in range(8):
            if dc + 1 < 8:
                ldo(dc + 1)
            for ti, t in enumerate(tiles):
                c0, n = t["c0"], t["n"]
                bk = k % 3
                k += 1
                for cc in range(16):
                    MM(banks[bk][:, 0:n], wo[dc % 2][:, cc, :], ogT[:, cc, c0:c0 + n], cc == 0, cc == 15, [wo_b[dc % 2], og_b[cc][ti]], bqs(bk))
                ACOPY(mix[:, dc, c0:c0 + n], banks[bk][:, 0:n], bqs(bk), [mix_b[dc][ti]])
        gb = V_POST0 if l == 0 else V_POST1
        for ti, t in enumerate(tiles):
            c0, n = t["c0"], t["n"]
            rstd = rms_stats([mix[:, c, c0:c0 + n] for c in range(8)], n, rt, [mix_b[c][ti] for c in range(8)], 1.0 / D)
            for c in range(8):
                a = c % 2
                STT(tmp[a][:, 0:n], mix[:, c, c0:c0 + n], gcol(gb, c), rstd[:, 0:n], ALU.mult, ALU.mult, [mix_b[c][ti], rt[4]] + CONST, [tmp_b[a]])
                TT(hT[:, c, c0:c0 + n], hT[:, c, c0:c0 + n], tmp[a][:, 0:n], ALU.add, [tmp_b[a], hT_b[ti]], [hT_b[ti]])
                ACOPY(xnT[:, c, c0:c0 + n], hT[:, c, c0:c0 + n], [hT_b[ti]], [xnT_b[ti]])

        def ldg(dc):
            P.dma("pool", [(wg[dc % 2][:], w_pg[l][:, dc * 128:(dc + 1) * 128].rearrange("(kc p) n -> p kc n", p=128)),
                           (wp[dc % 2][:], w_pe[l][:, dc * 128:(dc + 1) * 128].rearrange("(kc p) n -> p kc n", p=128))],
                  writes=[wg_b[dc % 2]], dbuf=wg_b[dc % 2])
        ldg(0)
        k = 0
        for dc in range(8):
            if dc + 1 < 8:
                ldg(dc + 1)
            for ti, t in enumerate(tiles):
                c0, n = t["c0"], t["n"]
                a = k % 2
                bg = a
                bp = 2 + a
                k += 1
                for kc in range(8):
                    MM(banks[bg][:, 0:n], wg[dc % 2][:, kc, :], xnT[:, kc, c0:c0 + n], kc == 0, kc == 7, [wg_b[dc % 2], xnT_b[ti]], bqs(bg))
                for pc in range(2):
                    MM(banks[bp][:, 0:n], wp[dc % 2][:, pc, :], pT[:, l, pc, c0:c0 + n], pc == 0, pc == 1, [wg_b[dc % 2], pT_b[ti]], bqs(bp))
                ACT(sgt[a][:, 0:n], banks[bg][:, 0:n], AF.Sigmoid, bqs(bg), [sgt_b[a]])
                TT(tmp[a][:, 0:n], banks[bp][:, 0:n], sgt[a][:, 0:n], ALU.mult, bqs(bp) + [sgt_b[a]], [tmp_b[a]])
                TT(hT[:, dc, c0:c0 + n], hT[:, dc, c0:c0 + n], tmp[a][:, 0:n], ALU.add, [tmp_b[a], hT_b[ti]], [hT_b[ti]])

    def stage_l1(sb, sbi, ph):
        tiles = sb_tiles(sb)
        nblk = sb["nblk"]
        has_s = sb["ns"] > 0
        first = sb["t0"] == 0
        NT = nblk * 128 + sb["ns"]
        kT = P.sbuf("kT", [128, 4, 128 + SBW], BF16, ph)
        Vt = P.sbuf("Vt", [128, 10, 256], BF16, ph)
        rC = P.sbuf("rC", [128, SBW], BF16, ph)
        rS = P.sbuf("rS", [128, SBW], BF16, ph)
        kT_b = [Buf("kT%d" % i) for i in range(MAXT + 1)]
        Vt_b = [Buf("Vt%d" % i) for i in range(10)]
        rope_b = Buf("rope")
        rf = P.sbuf("rf", [128, 512], F32, ph)
        rb = P.sbuf("rb", [128, 512], BF16, ph)
        rt1 = P.sbuf("rt1", [128, 512], F32, ph)
        rt2 = P.sbuf("rt2", [128, 512], F32, ph)
        rf_b, rb_b, rt1_b, rt2_b = Buf("rf"), Buf("rb"), Buf("rt1"), Buf("rt2")
        if has_s:
            kfl = P.sbuf("kfl", [128, 4, 128], F32, ph)
            ksf = P.sbuf("ksf", [128, 4, NS], F32, ph)
            kfl_b = Buf("kfl")
            qsa = P.sbuf("qsa", [128, 16, NS], BF16, ph)
            zsa = P.sbuf("zsa", [128, 16, NS], BF16, ph)
            qsa_b = Buf("qsa")
            vsb = P.sbuf("vsb", [NS, 256], BF16, ph)
            ostg = P.sbuf("ostg", [128, 768], F32, ph)
            ostg_b = Buf("ostg")
        g0 = sb["t0"]
        pairs = [(rC[:, 0:nblk * 128], ropec_d[:, g0:g0 + nblk * 128]), (rS[:, 0:nblk * 128], ropes_d[:, g0:g0 + nblk * 128])]
        if has_s:
            pairs += [(rC[:, nblk * 128:NT], ropec_d[:, NMAIN:NMAIN + NS]), (rS[:, nblk * 128:NT], ropes_d[:, NMAIN:NMAIN + NS])]
        P.dma("pool", pairs, writes=[rope_b], dbuf=rope_b)
        ACOPY(kT[:, :, 0:128], kT_halo[:], [halo_b], [kT_b[0]])
        ACOPY(Vt[:, 0, :], V_halo[:], [halo_b], [Vt_b[0]])

        def rope(src_ps, src_bank, dst, c0, n, reads_extra, writes, f32copy=None):
            ACOPY(rf[:, 0:n], src_ps, [bankb[src_bank]], [rf_b])
            ACOPY(rb[:, 0:n], src_ps, [bankb[src_bank]], [rb_b])
            MM(banks[3][:, 0:n], prot[:], rb[:, 0:n], True, True, [rb_b] + CONST, [bankb[3]])
            TT(rt1[:, 0:n], rf[:, 0:n], rC[:, c0:c0 + n], ALU.mult, [rf_b, rope_b], [rt1_b])
            TT(rt2[:, 0:n], banks[3][:, 0:n], rS[:, c0:c0 + n], ALU.mult, [bankb[3], rope_b], [rt2_b])
            TT(dst, rt1[:, 0:n], rt2[:, 0:n], ALU.add, [rt1_b, rt2_b] + reads_extra, writes)
            if f32copy is not None:
                o_ap, lo, hi, wr = f32copy
                TT(o_ap, rt1[:, lo:hi], rt2[:, lo:hi], ALU.add, [rt1_b, rt2_b], wr)

        pa = ExitStack()
        kvn = P.sbuf("kvn", [128, 8, 512], BF16, pa)
        kvn_b = Buf("kvn")
        wk = P.sbuf("wk", [128, 8, 4, 128], BF16, pa)
        wv = P.sbuf("wv", [128, 8, 256], BF16, pa)
        wkv_b = Buf("wkv")
        rt = alloc_rms(pa, "l1")
        vlast = P.sbuf("vlast", [128, 256], F32, pa)
        vlast_b = Buf("vlast")
        print("  l1a scratch remaining:", nc.sbuf_bytes_remaining)
        wkpairs = []
        for kh_ in range(4):
            src_ = w_kv[:, kh_ * 64:(kh_ + 1) * 64].rearrange("(kc p) d -> p kc d", p=128)
            wkpairs += [(wk[:, :, kh_, 0:64], src_), (wk[:, :, kh_, 64:128], src_)]
        wkpairs.append((wv[:], w_kv[:, 256:512].rearrange("(kc p) n -> p kc n", p=128)))
        P.dma("pool", wkpairs, writes=[wkv_b], dbuf=wkv_b)
        for ti, t in enumerate(tiles):
            c0, n = t["c0"], t["n"]
            rstd = rms_stats([hT[:, c, c0:c0 + n] for c in range(8)], n, rt, [hT_b[ti]], 1.0 / D)
            for c in range(8):
                STT(xnT[:, c, c0:c0 + n], hT[:, c, c0:c0 + n], gcol(V_PRE1, c), rstd[:, 0:n], ALU.mult, ALU.mult,
                    [hT_b[ti], rt[4]] + CONST, [xnT_b[ti]])
                STT(kvn[:, c, 0:n], hT[:, c, c0:c0 + n], gcol(V_KV, c), rstd[:, 0:n], ALU.mult, ALU.mult,
                    [hT_b[ti], rt[4]] + CONST, [kvn_b])
            for kh in range(4):
                bk = kh % 2
                for kc in range(8):
                    MM(banks[bk][:, 0:n], wk[:, kc, kh, :], kvn[:, kc, 0:n], kc == 0, kc == 7, [wkv_b, kvn_b], [bankb[bk]])
                f32c = None
                if has_s and t["samp"]:
                    f32c = (ksf[:, kh, :], 0, NS, [kfl_b])
                elif has_s and (nblk - 1) in t["blks"]:
                    lo = (nblk - 1) * 128 - c0
                    f32c = (kfl[:, kh, :], lo, lo + 128, [kfl_b])
                rope(banks[bk][:, 0:n], bk, kT[:, kh, 128 + c0:128 + c0 + n], c0, n, [], [kT_b[1 + ti]], f32c)
            if not t["samp"]:
                for jj, j in enumerate(t["blks"]):
                    bk = 4 + jj % 2
                    for kc in range(8):
                        MM(banks[bk][:, 0:256], kvn[:, kc, jj * 128:(jj + 1) * 128], wv[:, kc, :], kc == 0, kc == 7, [wkv_b, kvn_b], [bankb[bk]])
                    ACOPY(Vt[:, 1 + j, :], banks[bk][:, 0:256], [bankb[bk]], [Vt_b[1 + j]])
                    if has_s and j == nblk - 1:
                        ACOPY(vlast[:], banks[bk][:, 0:256], [bankb[bk]], [vlast_b])
                        P.dma("sp", [(vp_d, vlast[:])], reads=[vlast_b], dbuf=vlast_b, is_out=True)
            else:
                for kc in range(8):
                    MM(banks[4][0:NS, 0:256], kvn[:, kc, 0:NS], wv[:, kc, :], kc == 0, kc == 7, [wkv_b, kvn_b], [bankb[4]])
                ACOPY(ostg[0:NS, 512:768], banks[4][0:NS, 0:256], [bankb[4]], [ostg_b])
        if has_s:
            for kh in range(4):
                MM(banks[5][:, kh * 128:(kh + 1) * 128], kfl[:, kh, :], ident_f[:], True, True, [kfl_b] + CONST, [bankb[5]])
            kp_st = P.sbuf("kp_st", [128, 256], F32, pa)
            kp_b = Buf("kp_st")
            ACOPY(kp_st[:].rearrange("p (kh d) -> p kh d", kh=4), banks[5][:, :].rearrange("p (kh e) -> p kh e", kh=4)[:, :, 0:64], [bankb[5]], [kp_b])
            P.dma("sp", [(kp_d, kp_st[:])], reads=[kp_b], dbuf=kp_b, is_out=True)
            for kh in range(4):
                MM(banks[6][0:NS, kh * 128:(kh + 1) * 128], ksf[:, kh, :], ident_f[:], True, True, [kfl_b] + CONST, [bankb[6]])
            ACOPY(ostg[0:NS, 0:256].rearrange("p (kh d) -> p kh d", kh=4), banks[6][0:NS, :].rearrange("p (kh e) -> p kh e", kh=4)[:, :, 0:64], [bankb[6]], [ostg_b])
            ksd_b = Buf("ksd")
            P.dma("sp", [(ks_d, ostg[0:NS, 0:256]), (vs_d, ostg[0:NS, 512:768])], reads=[ostg_b], writes=[ksd_b], dbuf=ostg_b, is_out=True)
        P.barrier()
        pa.close()
        if os.environ.get("KB_L1", "") == "a":
            return

        wq = P.sbuf("wq", [128, 8, 512], BF16, ph)
        wz = P.sbuf("wz", [128, 8, 512], BF16, ph)
        wq_b, wz_b = Buf("wq"), Buf("wz")
        qg = P.sbuf("qg", [128, 4, SBW], BF16, ph)
        zg = P.sbuf("zg", [128, 4, SBW], BF16, ph)
        qg_b = [Buf("qg%d" % i) for i in range(MAXT)]
        PT = [P.sbuf("PT%d" % i, [128, 2, 512], BF16, ph) for i in range(4)]
        PT_b = [Buf("PT") for _ in range(4)]
        rden = P.sbuf("rden", [128, 512], F32, ph)
        rden_b = Buf("rden")
        at = P.sbuf("at", [128, 512], F32, ph)
        at_b = Buf("at")
        if has_s:
            ckd = [P.sbuf("ckd%d" % i, [128, 4, 2, 64], BF16, ph) for i in range(2)]
            cvt = [P.sbuf("cvt%d" % i, [128, 256], BF16, ph) for i in range(2)]
            cc_b = [Buf("cc") for _ in range(2)]
            KcT = P.sbuf("KcT", [128, 512], BF16, ph)
            KcT_b = Buf("KcT")
            PTs = P.sbuf("PTs", [128, 32], BF16, ph)
            PTs_b = Buf("PTs")
        print("  l1b scratch remaining:", nc.sbuf_bytes_remaining)

        def ldq(kh):
            P.dma("pool", [(wq[:], w_in_b[:, kh * 512:(kh + 1) * 512].rearrange("(kc p) n -> p kc n", p=128))], writes=[wq_b], dbuf=wq_b)
            P.dma("pool", [(wz[:], w_in_b[:, 2048 + kh * 512:2048 + (kh + 1) * 512].rearrange("(kc p) n -> p kc n", p=128))], writes=[wz_b], dbuf=wz_b)

        ldq(0)
        blkcount = 0
        for kh in range(4):
            for ti, t in enumerate(tiles):
                c0, n = t["c0"], t["n"]
                for i in range(4):
                    bk = i % 2
                    for kc in range(8):
                        MM(banks[bk][:, 0:n], wq[:, kc, i * 128:(i + 1) * 128], xnT[:, kc, c0:c0 + n], kc == 0, kc == 7, [wq_b, xnT_b[ti]], [bankb[bk]])
                    f32c = None
                    rope(banks[bk][:, 0:n], bk, qg[:, i, c0:c0 + n], c0, n, [], [qg_b[ti]], None)
                    bz = 4 + i % 2
                    for kc in range(8):
                        MM(banks[bz][:, 0:n], wz[:, kc, i * 128:(i + 1) * 128], xnT[:, kc, c0:c0 + n], kc == 0, kc == 7, [wz_b, xnT_b[ti]], [bankb[bz]])
                    ACT(zg[:, i, c0:c0 + n], banks[bz][:, 0:n], AF.Silu, [bankb[bz]], [qg_b[ti]])
                if t["samp"]:
                    ACOPY(qsa[:, kh * 4:(kh + 1) * 4, :], qg[:, :, c0:c0 + NS], [qg_b[ti]], [qsa_b])
                    ACOPY(zsa[:, kh * 4:(kh + 1) * 4, :], zg[:, :, c0:c0 + NS], [qg_b[ti]], [qsa_b])
            if kh + 1 < 4:
                ldq(kh + 1)
            steps = [(ti, j) for ti, t in enumerate(tiles) for j in t["blks"] if not (first and j == 0)]

            def emit_scores(idx):
                ti, j = steps[idx]
                st_ = idx % 2
                pm_ = maskf4 if (first and j == 1) else maskp4
                prev_cols = slice(j * 128, (j + 1) * 128)
                cur_cols = slice((j + 1) * 128, (j + 2) * 128)
                blk = slice(j * 128, (j + 1) * 128)
                kprev_b = kT_b[0] if j == 0 else kT_b[1 + (j - 1) // 4]
                kcur_b = kT_b[1 + j // 4]
                for par in range(2):
                    pr = slice(par * 64, (par + 1) * 64)
                    b0, b1 = 2 * par, 2 * par + 1
                    pt = PT[st_ * 2 + par]
                    ptb = PT_b[st_ * 2 + par]
                    MM(banks[b0][:, :], kT[pr, kh, prev_cols], qg[pr, :, blk], True, True, [kprev_b, qg_b[ti]], [bankb[b0]])
                    MM(banks[b1][:, :], kT[pr, kh, cur_cols], qg[pr, :, blk], True, True, [kcur_b, qg_b[ti]], [bankb[b1]])
                    ACT(pt[:, 0, :], banks[b0][:, :], AF.Exp, [bankb[b0]], [ptb], scale=0.125)
                    ACT(pt[:, 1, :], banks[b1][:, :], AF.Exp, [bankb[b1]], [ptb], scale=0.125)
                    TT(pt[:, 0, :], pt[:, 0, :], pm_[:], ALU.mult, [ptb] + CONST, [ptb])
                    TT(pt[:, 1, :], pt[:, 1, :], maskc4[:], ALU.mult, [ptb] + CONST, [ptb])

            def emit_pv(idx):
                ti, j = steps[idx]
                st_ = idx % 2
                blk = slice(j * 128, (j + 1) * 128)
                bo = 4 + 2 * (idx % 2)
                bd = bo + 1
                for par in range(2):
                    pr = slice(par * 64, (par + 1) * 64)
                    pt = PT[st_ * 2 + par]
                    ptb = PT_b[st_ * 2 + par]
                    tp = (0, par * 64)
                    MM(banks[bo][pr, :], Vt[:, j, kh * 64:(kh + 1) * 64], pt[:, 0, :], True, False, [Vt_b[j], ptb], [bankb[bo]], tile_position=tp)
                    MM(banks[bo][pr, :], Vt[:, 1 + j, kh * 64:(kh + 1) * 64], pt[:, 1, :], False, True, [Vt_b[1 + j], ptb], [bankb[bo]], tile_position=tp)
                    MM(banks[bd][pr, :], ones_b[:, 0:64], pt[:, 0, :], True, False, [ptb] + CONST, [bankb[bd]], tile_position=tp)
                    MM(banks[bd][pr, :], ones_b[:, 0:64], pt[:, 1, :], False, False, [ptb] + CONST, [bankb[bd]], tile_position=tp)
                    MM(banks[bd][pr, :], ones_b[0:1, 0:64], sinkrow[0:1, kh, par, :, :].rearrange("p i q -> p (i q)"), False, True, CONST, [bankb[bd]], tile_position=tp)
                ACT(rden[:], banks[bd][:, :], AF.Ln, [bankb[bd]], [rden_b])
                ACT(rden[:], rden[:], AF.Exp, [rden_b], [rden_b], scale=-1.0)
                TT(at[:], banks[bo][:, :], rden[:], ALU.mult, [bankb[bo], rden_b], [at_b])
                TT(ogT[:, kh * 4:(kh + 1) * 4, blk], at[:].rearrange("p (i q) -> p i q", i=4), zg[:, :, blk], ALU.mult,
                   [at_b, qg_b[ti]], [og_b[kh * 4 + i][ti] for i in range(4)])

            for idx in range(len(steps) + 1):
                if idx < len(steps):
                    emit_scores(idx)
                if idx >= 1:
                    emit_pv(idx - 1)
        lastj = nblk - 1
        ACOPY(kT_halo[:], kT[:, :, 128 + lastj * 128:128 + (lastj + 1) * 128], [kT_b[1 + lastj // 4]], [halo_b])
        ACOPY(V_halo[:], Vt[:, 1 + lastj, :], [Vt_b[1 + lastj]], [halo_b])
        if has_s and os.environ.get("KB_L1", "") != "b":
            ti = len(tiles) - 1
            c0 = tiles[ti]["c0"]
            for s_ in range(NS):
                a = s_ % 2
                ksrc = ck[s_].rearrange("j (kh d) -> j kh d", kh=4)
                P.dma("pool", [(ckd[a][:, :, 0, :], ksrc), (ckd[a][:, :, 1, :], ksrc), (cvt[a][:], cv[s_])],
                      writes=[cc_b[a]], dbuf=cc_b[a])
                MM(banks[5][0:1, 0:256], ident_f[0:NS, s_:s_ + 1], ostg[0:NS, 0:256], True, True, [ostg_b] + CONST, [bankb[5]])
                MM(banks[5][0:1, 256:512], ident_f[0:NS, s_:s_ + 1], ostg[0:NS, 512:768], True, True, [ostg_b] + CONST, [bankb[5]])
                ACOPY(ckd[a][0:1, :, 0, :], banks[5][0:1, 0:256].rearrange("p (kh d) -> p kh d", kh=4), [bankb[5]], [cc_b[a]])
                ACOPY(ckd[a][0:1, :, 1, :], banks[5][0:1, 0:256].rearrange("p (kh d) -> p kh d", kh=4), [bankb[5]], [cc_b[a]])
                ACOPY(cvt[a][0:1, :], banks[5][0:1, 256:512], [bankb[5]], [cc_b[a]])
                for kh in range(4):
                    MM(banks[0][:, kh * 128:(kh + 1) * 128], ckd[a][:, kh, :, :].rearrange("p c d -> p (c d)"), ident_b[:], True, True, [cc_b[a]] + CONST, [bankb[0]])
                ACOPY(KcT[:], banks[0][:, :], [bankb[0]], [KcT_b])
                for par in range(2):
                    pr = slice(par * 64, (par + 1) * 64)
                    qb_ = 1 if par == 0 else 4
                    for kh in range(4):
                        MM(banks[qb_][:, kh * 4:(kh + 1) * 4], KcT[pr, kh * 128:(kh + 1) * 128], qsa[pr, kh * 4:(kh + 1) * 4, s_], True, True, [KcT_b, qsa_b], [bankb[qb_]])
                    ACT(PTs[:, par * 16:(par + 1) * 16], banks[qb_][:, 0:16], AF.Exp, [bankb[qb_]], [PTs_b], scale=0.125)
                for kh in range(4):
                    for par in range(2):
                        pr = slice(par * 64, (par + 1) * 64)
                        col = par * 16 + kh * 4
                        tp = (0, par * 64)
                        MM(banks[2][pr, kh * 4:(kh + 1) * 4], cvt[a][:, kh * 64:(kh + 1) * 64], PTs[:, col:col + 4], True, True, [cc_b[a], PTs_b], [bankb[2]], tile_position=tp)
                        MM(banks[3][pr, kh * 4:(kh + 1) * 4], ones_b[:, 0:64], PTs[:, col:col + 4], True, False, [PTs_b] + CONST, [bankb[3]], tile_position=tp)
                        MM(banks[3][pr, kh * 4:(kh + 1) * 4], ones_b[0:1, 0:64], sinkrow[0:1, kh, par, :, 0:1].rearrange("p i q -> p (i q)"), False, True, CONST, [bankb[3]], tile_position=tp)
                P.op("dve", (lambda o, i_: (lambda e: e.reciprocal(o, i_)))(rden[:, 0:16], banks[3][:, 0:16]), [bankb[3]], [rden_b])
                TT(at[:, 0:16], banks[2][:, 0:16], rden[:, 0:16], ALU.mult, [bankb[2], rden_b], [at_b])
                TT(ogT[:, :, c0 + s_], at[:, 0:16], zsa[:, :, s_], ALU.mult, [at_b, qsa_b], [og_b[h_][ti] for h_ in range(16)])

    def stage_y(sb, ph):
        tiles = sb_tiles(sb)
        first = sb["t0"] == 0
        yst = [P.sbuf("yst%d" % i, [128, D], F32, ph) for i in range(3)]
        yst_b = [Buf("yst") for _ in range(3)]
        k = 0
        for ti, t in enumerate(tiles):
            c0 = t["c0"]
            if t["samp"]:
                a = k % 3
                k += 1
                for half in range(2):
                    bk = half
                    for cc in range(4):
                        c = half * 4 + cc
                        MM(banks[bk][0:NS, cc * 128:(cc + 1) * 128], hT[:, c, c0:c0 + NS], ident_f[:], True, True, [hT_b[ti]] + CONST, [bankb[bk]])
                    ACOPY(yst[a][0:NS, half * 512:(half + 1) * 512], banks[bk][0:NS, :], [bankb[bk]], [yst_b[a]])
                P.dma("sp", [(y_d[2048:2048 + NS, :], yst[a][0:NS, :])], reads=[yst_b[a]], dbuf=yst_b[a], is_out=True)
                continue
            for j in t["blks"]:
                if first and j == 0:
                    continue
                a = k % 3
                k += 1
                for half in range(2):
                    bk = (2 * k + half) % 4
                    for cc in range(4):
                        c = half * 4 + cc
                        MM(banks[bk][:, cc * 128:(cc + 1) * 128], hT[:, c, j * 128:(j + 1) * 128], ident_f[:], True, True, [hT_b[ti]] + CONST, [bankb[bk]])
                    if half == 0:
                        ACOPY(yst[a][:, 0:512], banks[bk][:, :], [bankb[bk]], [yst_b[a]])
                    else:
                        TCOPY(yst[a][:, 512:1024], banks[bk][:, :], [bankb[bk]], [yst_b[a]])
                row = (j - 1) * 128 if first else (8 + j) * 128
                P.dma("sp", [(y_d[row:row + 128, :], yst[a][:])], reads=[yst_b[a]], dbuf=yst_b[a], is_out=True)

    def dump_h(sb):
        col0 = sb["t0"]
        for ti, t in enumerate(sb_tiles(sb)):
            c0, n = t["c0"], t["n"]
            g0 = (NMAIN if t["samp"] else col0 + c0)
            P.dma("sp", [(dbg_d[:, :, g0:g0 + n], hT[:, :, c0:c0 + n])], reads=[hT_b[ti]], dbuf=hT_b[ti], is_out=True)

    import os
    sel = os.environ.get("KB_SBS")
    for sbi, sb in enumerate(SBS):
        if sel is not None and str(sbi) not in sel.split(","):
            continue
        if sb["kind"] == "P" and stop_after in ("L0nopre", "LOAD"):
            continue
        ph = ExitStack()
        stage_load(sb, ph)
        stage_prenorm(sb, ph, V_PRE0, xnT, xnT_b)
        P.barrier()
        ph.close()
        if stop_after == "LOAD":
            dump_h(sb)
            P.barrier()
            continue
        ph = ExitStack()
        if sb["kind"] == "P":
            stage_l0u(sb, ph)
        else:
            stage_l0(sb, ph)
        P.barrier()
        ph.close()
        if sb["kind"] == "P":
            continue
        if os.environ.get("KB_OUT", "1") == "1":
            ph = ExitStack()
            stage_out(sb, ph, 0, w_out_a)
            P.barrier()
            ph.close()
        if stop_after in ("L0", "L0nopre"):
            dump_h(sb)
            P.barrier()
            continue
        ph = ExitStack()
        stage_l1(sb, sbi, ph)
        P.barrier()
        ph.close()
        if os.environ.get("KB_OUT1", "1") == "1":
            ph = ExitStack()
            stage_out(sb, ph, 1, w_out_b)
            P.barrier()
            ph.close()
        if stop_after == "L1":
            dump_h(sb)
            P.barrier()
        if os.environ.get("KB_Y", "1") == "1":
            ph = ExitStack()
            stage_y(sb, ph)
            P.barrier()
            ph.close()

    P.emit()
    print("stats:", P.stats)
    return nc


def _consts(half):
    ident = np.eye(128, dtype=np.float32)
    s = np.arange(128)[:, None]
    t = np.arange(128)[None, :]
    mtri = (t >= s).astype(np.uint32)
    maskc = (s <= t).astype(np.float32)
    maskp = (s > t).astype(np.float32)
    maskf = maskp.copy() if half == 1 else np.zeros((128, 128), np.float32)
    prot = np.zeros((128, 128), np.float32)
    for base in (0, 64):
        for i in range(8):
            prot[base + i + 8, base + i] = 1.0
            prot[base + i, base + i + 8] = 1.0
    pos = np.concatenate([half * 2048 - 128 + np.arange(NMAIN), np.full(NS, PAST_LEN)]).astype(np.float32)
    inv = (ROPE_THETA ** (-np.arange(0, 16, 2, dtype=np.float32) / 16)).astype(np.float32)
    ang = pos[None, :] * inv[:, None]
    cos = np.cos(ang).astype(np.float32)
    sin = np.sin(ang).astype(np.float32)
    ropec = np.ones((128, TTOT), np.float32)
    ropes = np.zeros((128, TTOT), np.float32)
    for base in (0, 64):
        ropec[base:base + 8] = cos
        ropec[base + 8:base + 16] = cos
        ropes[base:base + 8] = -sin
        ropes[base + 8:base + 16] = sin
    return dict(ident=ident, mtri=mtri, maskc=maskc, maskp=maskp, maskf=maskf, prot=prot, ropec=ropec, ropes=ropes)


def _col(v):
    v = np.asarray(v, np.float32).reshape(-1, 128)
    return np.ascontiguousarray(v.T)


def make_in_maps(inp):
    f = lambda a: np.ascontiguousarray(np.asarray(a, dtype=np.float32))
    xpr, xsm = f(inp["x_prompt"]), f(inp["x_sample"])
    ppr, psa = f(inp["p_prompt"]), f(inp["p_sample"])
    st, ck, cv = f(inp["state_hgrn"]), f(inp["cache_k"]), f(inp["cache_v"])
    vecs = np.concatenate([
        _col(inp["pre_norm_g"][0]), _col(inp["pre_norm_g"][1]), _col(inp["post_norm_g"][0]), _col(inp["post_norm_g"][1]),
        _col(inp["kv_norm_g"]), _col(inp["onorm_a"][0]), _col(inp["lb_logits"][0]), _col(inp["lb_logits"][1])], axis=1)
    assert vecs.shape == (128, NVEC)
    shared = dict(
        w_in_a=f(inp["w_in_a"][0]), w_out_a=f(inp["w_out_a"][0]), w_kv=f(inp["w_kv"]), w_in_b=f(inp["w_in_b"][0]),
        w_out_b=f(inp["w_out_b"][0]), w_pe=f(inp["w_pe"]), w_pg=f(inp["w_pg"]), vecs=np.ascontiguousarray(vecs),
        sinks=f(inp["sinks"]).reshape(1, 32))
    maps = []
    for c in range(8):
        b, half = c // 2, c % 2
        xm = np.zeros((NMAIN, D), np.float32)
        xp = np.zeros((NPRE, D), np.float32)
        pm = np.zeros((2, NMAIN, 256), np.float32)
        if half == 0:
            xm[128:] = xpr[b, 0:2048]
            pm[:, 128:] = ppr[:, b, 0:2048]
        else:
            xm[:] = xpr[b, 1920:4096]
            pm[:] = ppr[:, b, 1920:4096]
            xp[:] = xpr[b, 0:1920]
        m = dict(shared)
        m.update(_consts(half))
        m.update(xm=xm, xp=xp, xs=np.ascontiguousarray(xsm[c * NS:(c + 1) * NS, 0]), pm=pm,
                 psm=np.ascontiguousarray(psa[:, c * NS:(c + 1) * NS, 0]),
                 st_in=np.ascontiguousarray(st[0, c * NS:(c + 1) * NS]),
                 ck=np.ascontiguousarray(ck[c * NS:(c + 1) * NS].reshape(NS, 128, 256)),
                 cv=np.ascontiguousarray(cv[c * NS:(c + 1) * NS].reshape(NS, 128, 256)))
        maps.append(m)
    return maps


def assemble(results):
    y_p = np.zeros((4, 4096, D), np.float32)
    y_s = np.zeros((128, 1, D), np.float32)
    st_p = np.zeros((1, 4, 16, 128, 128), np.float32)
    st_s = np.zeros((1, 128, 16, 128, 128), np.float32)
    k_p = np.zeros((4, 128, 4, 64), np.float32)
    v_p = np.zeros((4, 128, 4, 64), np.float32)
    k_s = np.zeros((128, 1, 4, 64), np.float32)
    v_s = np.zeros((128, 1, 4, 64), np.float32)
    for c in range(8):
        r = results[c]
        b, half = c // 2, c % 2
        y_p[b, half * 2048:(half + 1) * 2048] = r["y"][0:2048]
        y_s[c * NS:(c + 1) * NS, 0] = r["y"][2048:2048 + NS]
        st_s[0, c * NS:(c + 1) * NS] = r["st_s"]
        k_s[c * NS:(c + 1) * NS, 0] = r["ks"].reshape(NS, 4, 64)
        v_s[c * NS:(c + 1) * NS, 0] = r["vs"].reshape(NS, 4, 64)
        if half == 1:
            st_p[0, b] = r["st_p"]
            k_p[b] = r["kp"].reshape(128, 4, 64)
            v_p[b] = r["vp"].reshape(128, 4, 64)
    return (y_p, y_s, st_p, st_s, k_p, v_p, k_s, v_s)


def kernel(**inputs):
    nc = build_program()
    in_maps = make_in_maps(inputs)
    res = run_bass_kernel_spmd(nc, in_maps, core_ids=list(range(8)))
    return assemble(res.results)
```

```python
import numpy as np
from contextlib import ExitStack
import concourse.bass as bass
import concourse.mybir as mybir
from concourse.bass_utils import run_bass_kernel_spmd

F32 = mybir.dt.float32
BF16 = mybir.dt.bfloat16
U32 = mybir.dt.uint32
ALU = mybir.AluOpType
AF = mybir.ActivationFunctionType

COMPUTE = ("pe", "act", "dve")
QUEUES = ("sp", "pool")
ALLENG = COMPUTE + QUEUES


class Buf:
    __slots__ = ("name", "last_w", "readers", "sem", "keep", "excl")

    def __init__(self, name="", keep=False, excl=False):
        self.excl = excl
        self.name = name
        self.last_w = None
        self.readers = {}
        self.sem = None
        self.keep = keep


class Carrier:
    __slots__ = ("cnt", "handle", "q")

    def __init__(self):
        self.cnt = 0
        self.handle = None
        self.q = None


class Op:
    __slots__ = ("eng", "fn", "deps", "signal", "sigval", "is_dma", "carrier", "dval")

    def __init__(self, eng, fn, is_dma=False):
        self.eng = eng
        self.fn = fn
        self.deps = []
        self.signal = False
        self.sigval = 0
        self.is_dma = is_dma
        self.carrier = None
        self.dval = 0


class Prog:
    def __init__(self, nc):
        self.nc = nc
        self.ops = []
        self.es = ExitStack()
        self.carriers = []
        self.free_carriers = {"sp": [], "pool": []}
        self.active_bufs = []
        self.out_ops = []
        self.last = {e: None for e in ALLENG}
        self.bar = None
        self.bar_pending = set()
        self.dma_since_bar = []
        self.nalloc = 0
        self.nuniq = 0

    def sbuf(self, name, shape, dtype, es=None):
        self.nalloc += 1
        return (es or self.es).enter_context(self.nc.sbuf_tensor("s%d_%s" % (self.nalloc, name), list(shape), dtype))

    def psum(self, name, shape, dtype=F32):
        return self.es.enter_context(self.nc.psum_tensor("p_" + name, list(shape), dtype))

    def _adddep(self, op, d):
        if d is op or d is None:
            return
        for x in op.deps:
            if x is d:
                return
        op.deps.append(d)
        d.signal = True

    def _deps(self, op, reads, writes):
        ex = [b for b in reads if b.excl and b not in writes]
        if ex:
            reads = [b for b in reads if not b.excl]
            writes = list(writes) + ex
        for b in reads:
            d = b.last_w
            if d is not None:
                if (not d.is_dma) and (not op.is_dma) and d.eng == op.eng and op.eng == "pe":
                    pass
                else:
                    self._adddep(op, d)
        for b in writes:
            cands = [b.last_w] + list(b.readers.values())
            for d in cands:
                if d is None:
                    continue
                if (not d.is_dma) and (not op.is_dma) and d.eng == op.eng:
                    continue
                self._adddep(op, d)
        if op.eng in self.bar_pending:
            self.bar_pending.discard(op.eng)
            for d in self.bar:
                if d.is_dma or d.eng != op.eng:
                    self._adddep(op, d)
        for b in reads:
            if op.is_dma:
                self.nuniq += 1
                b.readers["dma%d" % self.nuniq] = op
            else:
                b.readers[op.eng] = op
        for b in writes:
            b.last_w = op
            b.readers = {}
        self.last[op.eng] = op

    def op(self, eng, fn, reads=(), writes=()):
        o = Op(eng, fn)
        self._deps(o, reads, writes)
        self.ops.append(o)
        return o

    def dma(self, q, pairs, reads=(), writes=(), dbuf=None, is_out=False):
        if dbuf.sem is None:
            if self.free_carriers[q]:
                dbuf.sem = self.free_carriers[q].pop()
            else:
                dbuf.sem = Carrier()
                dbuf.sem.q = q
                self.carriers.append(dbuf.sem)
            self.active_bufs.append(dbuf)
        c = dbuf.sem
        assert c.q == q, "a DMA buffer must stay on one queue type"
        o = Op(q, pairs, is_dma=True)
        c.cnt += 16 * len(pairs)
        o.carrier = c
        o.dval = c.cnt
        o.signal = True
        self._deps(o, reads, writes)
        self.ops.append(o)
        self.dma_since_bar.append(o)
        if is_out:
            self.out_ops.append(o)
        return o

    def barrier(self):
        ops = [o for o in self.last.values() if o is not None and not o.is_dma]
        latest = {}
        for o in self.dma_since_bar:
            latest[id(o.carrier)] = o
        ops += list(latest.values())
        self.dma_since_bar = []
        self.bar = ops
        self.bar_pending = set(ALLENG)
        keep = []
        for b in self.active_bufs:
            if b.keep:
                keep.append(b)
            else:
                self.free_carriers[b.sem.q].append(b.sem)
                b.sem = None
        self.active_bufs = keep

    def emit(self):
        nc = self.nc
        es = self.es
        sems = {}
        for e in COMPUTE:
            sems[e] = es.enter_context(nc.semaphore("sem_" + e))
        for i, c in enumerate(self.carriers):
            c.handle = es.enter_context(nc.semaphore("dsem%d" % i))
        cnt = {e: 0 for e in COMPUTE}
        for o in self.ops:
            if o.is_dma:
                continue
            if o.signal:
                cnt[o.eng] += 1
                o.sigval = cnt[o.eng]
        by_eng = {e: [] for e in ALLENG}
        for o in self.ops:
            by_eng[o.eng].append(o)
        self.stats = {e: len(v) for e, v in by_eng.items()}
        self.stats["sig"] = dict(cnt)
        self.stats["dma_sems"] = len(self.carriers)
        final_waits = {}
        for o in self.out_ops:
            k = id(o.carrier)
            if k not in final_waits or final_waits[k][1] < o.dval:
                final_waits[k] = (o.carrier.handle, o.dval)

        def stream(ename, e):
            known = {}
            nwait = 0
            for o in by_eng[ename]:
                need = {}
                for d in o.deps:
                    if d.is_dma:
                        s, v = d.carrier.handle, d.dval
                    else:
                        s, v = sems[d.eng], d.sigval
                    k = id(s)
                    if k not in need or need[k][1] < v:
                        need[k] = (s, v)
                for k, (s, v) in need.items():
                    if known.get(k, 0) >= v:
                        continue
                    known[k] = v
                    e.wait_ge(s, v)
                    nwait += 1
                if o.is_dma:
                    for (out_ap, in_ap) in o.fn:
                        e.dma_start(out=out_ap, in_=in_ap).then_inc(o.carrier.handle, 16)
                else:
                    ins = o.fn(e)
                    if o.signal:
                        ins.then_inc(sems[ename], 1)
            if ename == "sp":
                for s, v in final_waits.values():
                    e.wait_ge(s, v)
            self.stats[ename + "_waits"] = nwait

        with nc.Block() as block:
            @block.tensor
            def _(e):
                stream("pe", e)

            @block.scalar
            def _(e):
                stream("act", e)

            @block.vector
            def _(e):
                stream("dve", e)

            @block.gpsimd
            def _(e):
                stream("pool", e)

            @block.sync
            def _(e):
                stream("sp", e)
        es.close()


D = 1024
NMAIN = 2176
NPRE = 1920
NS = 16
TTOT = NMAIN + NS
SBW = 1152
EPS = 1e-6
PAST_LEN = 16384
ROPE_THETA = 500000.0

V_PRE0, V_PRE1, V_POST0, V_POST1, V_KV, V_ON, V_LB0, V_LB1 = 0, 8, 16, 24, 32, 40, 56, 72
NVEC = 88

SBS = [
    dict(kind="P", t0=0, nblk=8, ns=0),
    dict(kind="P", t0=1024, nblk=7, ns=0),
    dict(kind="M", t0=0, nblk=9, ns=0),
    dict(kind="M", t0=1152, nblk=8, ns=NS),
]


def sb_tiles(sb):
    tiles = []
    nb = sb["nblk"]
    j = 0
    while j < nb:
        k = min(4, nb - j)
        tiles.append(dict(c0=j * 128, n=k * 128, blks=list(range(j, j + k)), samp=False))
        j += k
    if sb["ns"]:
        tiles.append(dict(c0=nb * 128, n=sb["ns"], blks=[], samp=True))
    return tiles


def build_program(stop_after=None):
    nc = bass.Bass("TRN2", target_bir_lowering=False)
    P = Prog(nc)

    def din(name, shape, dt=F32):
        return nc.dram_tensor(name, list(shape), dt, kind="ExternalInput").ap()

    def dout(name, shape, dt=F32):
        return nc.dram_tensor(name, list(shape), dt, kind="ExternalOutput").ap()

    xm = din("xm", [NMAIN, D])
    xp = din("xp", [NPRE, D])
    xs = din("xs", [NS, D])
    pm = din("pm", [2, NMAIN, 256])
    psm = din("psm", [2, NS, 256])
    st_in = din("st_in", [NS, 16, 128, 128])
    ck = din("ck", [NS, 128, 256])
    cv = din("cv", [NS, 128, 256])
    w_in_a = din("w_in_a", [D, 8192])
    w_out_a = din("w_out_a", [2048, D])
    w_kv = din("w_kv", [D, 512])
    w_in_b = din("w_in_b", [D, 4096])
    w_out_b = din("w_out_b", [2048, D])
    w_pe = din("w_pe", [2, 256, D])
    w_pg = din("w_pg", [2, D, D])
    vecs_d = din("vecs", [128, NVEC])
    sinks_d = din("sinks", [1, 32])
    ident_d = din("ident", [128, 128])
    mtri_d = din("mtri", [128, 128], U32)
    maskc_d = din("maskc", [128, 128])
    maskp_d = din("maskp", [128, 128])
    maskf_d = din("maskf", [128, 128])
    prot_d = din("prot", [128, 128])
    ropec_d = din("ropec", [128, TTOT])
    ropes_d = din("ropes", [128, TTOT])

    y_d = dout("y", [2048 + NS, D])
    stp_d = dout("st_p", [16, 128, 128])
    sts_d = dout("st_s", [NS, 16, 128, 128])
    kp_d = dout("kp", [128, 256])
    vp_d = dout("vp", [128, 256])
    ks_d = dout("ks", [NS, 256])
    vs_d = dout("vs", [NS, 256])
    dbg_d = dout("dbg", [128, 8, TTOT]) if stop_after else None

    hT = P.sbuf("hT", [128, 8, SBW], F32)
    xnT = P.sbuf("xnT", [128, 8, SBW], BF16)
    ogT = P.sbuf("ogT", [128, 16, SBW], BF16)
    pT = P.sbuf("pT", [128, 2, 2, SBW], BF16)
    Sst = P.sbuf("Sst", [128, 16, 128], F32)
    ident_f = P.sbuf("ident_f", [128, 128], F32)
    ident_b = P.sbuf("ident_b", [128, 128], BF16)
    ones_b = P.sbuf("ones_b", [128, 128], BF16)
    ones_f = P.sbuf("ones_f", [128, 128], F32)
    epsc = P.sbuf("epsc", [128, 1], F32)
    mtri = P.sbuf("mtri", [128, 128], U32)
    maskc = P.sbuf("maskc", [128, 128], BF16)
    maskp = P.sbuf("maskp", [128, 128], BF16)
    maskf = P.sbuf("maskf", [128, 128], BF16)
    prot = P.sbuf("prot", [128, 128], BF16)
    maskc4 = P.sbuf("maskc4", [128, 512], BF16)
    maskp4 = P.sbuf("maskp4", [128, 512], BF16)
    maskf4 = P.sbuf("maskf4", [128, 512], BF16)
    vecs = P.sbuf("vecs", [128, NVEC], F32)
    lbv = P.sbuf("lbv", [128, 16], F32)
    omlv = P.sbuf("omlv", [128, 16], F32)
    nomlv = P.sbuf("nomlv", [128, 16], F32)
    lnomlv = P.sbuf("lnomlv", [128, 16], F32)
    onec = P.sbuf("onec", [128, 1], F32)
    sinkx = P.sbuf("sinkx", [1, 32], F32)
    sinkrow = P.sbuf("sinkrow", [1, 4, 2, 4, 128], BF16)
    kT_halo = P.sbuf("kT_halo", [128, 4, 128], BF16)
    V_halo = P.sbuf("V_halo", [128, 256], BF16)

    MAXT = 3
    hT_b = [Buf("hT%d" % i, keep=True) for i in range(MAXT)]
    xnT_b = [Buf("xnT%d" % i) for i in range(MAXT)]
    pT_b = [Buf("pT%d" % i) for i in range(MAXT)]
    og_b = [[Buf("og%d_%d" % (h, i)) for i in range(MAXT)] for h in range(16)]
    S_b = [Buf("S%d" % h) for h in range(16)]
    const_b = Buf("const", keep=True)
    halo_b = Buf("halo")
    stp_cb = Buf("stp_carrier", keep=True)

    banks = [P.psum("bank%d" % i, [128, 512]) for i in range(8)]
    bankb = [Buf("bank%d" % i, excl=True) for i in range(8)]
    bq = [[bankb[i]] * 4 for i in range(8)]

    def bqs(i, q0=0, q1=4):
        return [bankb[i]]


    def MM(out, lhsT, rhs, start, stop, reads, writes, **kw):
        P.op("pe", lambda e: e.matmul(out, lhsT, rhs, start=start, stop=stop, **kw), reads, writes)

    def ACT(out, in_, func, reads, writes, bias=None, scale=None):
        kw = {}
        if bias is not None:
            kw["bias"] = bias
        if scale is not None:
            kw["scale"] = scale
        P.op("act", lambda e: e.activation(out, in_, func, **kw), reads, writes)

    def ACOPY(out, in_, reads, writes):
        P.op("act", lambda e: e.copy(out, in_), reads, writes)

    def TCOPY(out, in_, reads, writes):
        P.op("dve", lambda e: e.tensor_copy(out, in_), reads, writes)

    def TT(out, in0, in1, op, reads, writes):
        P.op("dve", lambda e: e.tensor_tensor(out, in0, in1, op), reads, writes)

    def TS(out, in0, s1, s2, op0, op1, reads, writes):
        if s2 is None:
            P.op("dve", lambda e: e.tensor_scalar(out, in0, s1, None, op0), reads, writes)
        else:
            P.op("dve", lambda e: e.tensor_scalar(out, in0, s1, s2, op0, op1), reads, writes)

    def STT(out, in0, sc, in1, op0, op1, reads, writes):
        P.op("dve", lambda e: e.scalar_tensor_tensor(out, in0, sc, in1, op0, op1), reads, writes)

    def SCAN(out, d0, d1, reads, writes):
        P.op("dve", lambda e: e.tensor_tensor_scan(out, d0, d1, 0.0, ALU.mult, ALU.add), reads, writes)

    def CPRED(out, mask, data, reads, writes):
        P.op("dve", lambda e: e.copy_predicated(out, mask, data), reads, writes)

    def MEMSET(ap, val, writes):
        P.op("dve", lambda e: e.memset(ap, val), (), writes)

    print("sbuf remaining after persistent:", nc.sbuf_bytes_remaining)

    P.dma("sp", [(ident_f[:], ident_d)], writes=[const_b], dbuf=const_b)
    P.dma("sp", [(mtri[:], mtri_d)], writes=[const_b], dbuf=const_b)
    P.dma("sp", [(vecs[:], vecs_d)], writes=[const_b], dbuf=const_b)
    P.dma("sp", [(sinkx[:], sinks_d)], writes=[const_b], dbuf=const_b)
    cb2 = Buf("const2", keep=True)
    P.dma("pool", [(ident_b[:], ident_d), (maskc[:], maskc_d), (maskp[:], maskp_d), (maskf[:], maskf_d),
                   (prot[:], prot_d)], writes=[cb2], dbuf=cb2)
    cb3 = Buf("const3")
    MEMSET(ones_b[:], 1.0, [cb3])
    MEMSET(ones_f[:], 1.0, [cb3])
    MEMSET(epsc[:], EPS, [cb3])
    MEMSET(Sst[:], 0.0, S_b)
    MEMSET(ogT[:], 0.0, [b for hb in og_b for b in hb])
    MEMSET(kT_halo[:], 0.0, [halo_b])
    MEMSET(V_halo[:], 0.0, [halo_b])
    TT(lbv[:], vecs[:, V_LB0:V_LB0 + 16], vecs[:, V_LB1:V_LB1 + 16], ALU.subtract, [const_b], [cb3])
    ACT(lbv[:], lbv[:], AF.Sigmoid, [cb3], [cb3])
    TS(omlv[:], lbv[:], -1.0, 1.0, ALU.mult, ALU.add, [cb3], [cb3])
    TS(nomlv[:], lbv[:], 1.0, -1.0, ALU.mult, ALU.add, [cb3], [cb3])
    MEMSET(onec[:], 1.0, [cb3])
    ACT(lnomlv[:], omlv[:], AF.Ln, [cb3], [cb3])
    ACT(sinkx[:], sinkx[:], AF.Exp, [const_b], [cb3])
    sx4 = sinkx[:].rearrange("p (kh i par o) -> p kh par i o", kh=4, i=4, par=2, o=1)
    TCOPY(sinkrow[:], sx4.to_broadcast([1, 4, 2, 4, 128]), [cb3], [cb3])
    for m4, m1 in ((maskc4, maskc), (maskp4, maskp), (maskf4, maskf)):
        for i in range(4):
            TCOPY(m4[:, i * 128:(i + 1) * 128], m1[:], [cb2], [cb3])
    CONST = [const_b, cb2, cb3]

    def gcol(base, c):
        return vecs[:, base + c:base + c + 1]

    def stage_load(sb, ph):
        kind = sb["kind"]
        src = xp if kind == "P" else xm
        tiles = sb_tiles(sb)
        NX = 6
        xin = [P.sbuf("xin%d" % i, [128, D], F32, ph) for i in range(NX)]
        xin_b = [Buf("xin%d" % i) for i in range(NX)]
        pin = [P.sbuf("pin%d" % i, [128, 2, 256], F32, ph) for i in range(NX)] if kind == "M" else None
        pin_b = [Buf("pin%d" % i) for i in range(NX)]
        cnt = 0
        ev = 0
        for ti, t in enumerate(tiles):
            c0, n = t["c0"], t["n"]
            if not t["samp"]:
                slots = []
                for j in t["blks"]:
                    s = cnt % NX
                    cnt += 1
                    r0 = sb["t0"] + j * 128
                    P.dma("sp", [(xin[s][:], src[r0:r0 + 128, :])], writes=[xin_b[s]], dbuf=xin_b[s])
                    if kind == "M":
                        P.dma("sp", [(pin[s][:], pm[:, r0:r0 + 128, :].rearrange("l t f -> t l f"))],
                              writes=[pin_b[s]], dbuf=pin_b[s])
                    slots.append(s)
                nb = len(slots)
                for c in range(8):
                    bk = c % 2
                    for jj, s in enumerate(slots):
                        MM(banks[bk][:, jj * 128:(jj + 1) * 128], xin[s][:, c * 128:(c + 1) * 128], ident_f[:], True, True,
                           [xin_b[s]] + CONST, [bq[bk][jj]])
                    ev += 1
                    if ev % 2 == 0:
                        ACOPY(hT[:, c, c0:c0 + n], banks[bk][:, 0:n], bqs(bk, 0, nb), [hT_b[ti]])
                    else:
                        TCOPY(hT[:, c, c0:c0 + n], banks[bk][:, 0:n], bqs(bk, 0, nb), [hT_b[ti]])
                if kind == "M":
                    for l in range(2):
                        for pc in range(2):
                            bk = 2 + (l * 2 + pc) % 2
                            for jj, s in enumerate(slots):
                                MM(banks[bk][:, jj * 128:(jj + 1) * 128], pin[s][:, l, pc * 128:(pc + 1) * 128], ident_f[:], True, True,
                                   [pin_b[s]] + CONST, [bq[bk][jj]])
                            ACOPY(pT[:, l, pc, c0:c0 + n], banks[bk][:, 0:n], bqs(bk, 0, nb), [pT_b[ti]])
            else:
                s = cnt % NX
                cnt += 1
                P.dma("sp", [(xin[s][0:NS, :], xs)], writes=[xin_b[s]], dbuf=xin_b[s])
                P.dma("sp", [(pin[s][0:NS, :, :], psm.rearrange("l t f -> t l f"))], writes=[pin_b[s]], dbuf=pin_b[s])
                for c in range(8):
                    MM(banks[0][:, c * NS:(c + 1) * NS], xin[s][0:NS, c * 128:(c + 1) * 128], ident_f[0:NS, 0:NS], True, True,
                       [xin_b[s]] + CONST, [bq[0][0]])
                ACOPY(hT[:, :, c0:c0 + NS], banks[0][:, 0:8 * NS].rearrange("p (c n) -> p c n", c=8), [bq[0][0]], [hT_b[ti]])
                for l in range(2):
                    for pc in range(2):
                        q = l * 2 + pc
                        MM(banks[1][:, q * NS:(q + 1) * NS], pin[s][0:NS, l, pc * 128:(pc + 1) * 128], ident_f[0:NS, 0:NS], True, True,
                           [pin_b[s]] + CONST, [bq[1][0]])
                ACOPY(pT[:, :, :, c0:c0 + NS], banks[1][:, 0:4 * NS].rearrange("p (l c n) -> p l c n", l=2, c=2), [bq[1][0]], [pT_b[ti]])

    def rms_stats(srcs, n, rt, reads, invd):
        sq, sq_b, lnv, rstd, r_b = rt
        ssb = 7
        for c, s_ap in enumerate(srcs):
            k = c % 2
            ACT(sq[k][:, 0:n], s_ap, AF.Square, reads, [sq_b[k]])
            MM(banks[ssb][:, 0:n], ones_b[:], sq[k][:, 0:n], c == 0, c == len(srcs) - 1, [sq_b[k]] + CONST, bqs(ssb))
        ACT(lnv[:, 0:n], banks[ssb][:, 0:n], AF.Ln, bqs(ssb) + CONST, [r_b], bias=epsc[:, 0:1], scale=invd)
        ACT(rstd[:, 0:n], lnv[:, 0:n], AF.Exp, [r_b], [r_b], scale=-0.5)
        return rstd

    def alloc_rms(ph, tag):
        sq = [P.sbuf("sq%s%d" % (tag, i), [128, 512], BF16, ph) for i in range(2)]
        sq_b = [Buf("sq") for _ in range(2)]
        lnv = P.sbuf("lnv" + tag, [128, 512], F32, ph)
        rstd = P.sbuf("rstd" + tag, [128, 512], F32, ph)
        return (sq, sq_b, lnv, rstd, Buf("rstd"))

    def stage_prenorm(sb, ph, gbase, dst, dst_b):
        rt = alloc_rms(ph, "pn%d" % gbase)
        for ti, t in enumerate(sb_tiles(sb)):
            c0, n = t["c0"], t["n"]
            rstd = rms_stats([hT[:, c, c0:c0 + n] for c in range(8)], n, rt, [hT_b[ti]], 1.0 / D)
            for c in range(8):
                STT(dst[:, c, c0:c0 + n], hT[:, c, c0:c0 + n], gcol(gbase, c), rstd[:, 0:n], ALU.mult, ALU.mult,
                    [hT_b[ti], rt[4]] + CONST, [dst_b[ti]])

    def stage_l0(sb, ph):
        kind = sb["kind"]
        full = kind == "M"
        tiles = sb_tiles(sb)
        nblk = sb["nblk"]
        has_s = sb["ns"] > 0
        wsl = [[P.sbuf("w0_%d_%d" % (s, k), [128, 8, 128], BF16, ph) for k in range(4)] for s in range(2)]
        wsl_b = [[Buf("w0") for k in range(4)] for s in range(2)]
        qd = [P.sbuf("qd%d" % s, [128, SBW], BF16, ph) for s in range(2)] if full else None
        kd = [P.sbuf("kd%d" % s, [128, SBW], BF16, ph) for s in range(2)]
        zs = [P.sbuf("zs%d" % s, [128, SBW], BF16, ph) for s in range(2)] if full else None
        Vtm = [P.sbuf("Vtm%d" % s, [128, 9, 128], BF16, ph) for s in range(2)]
        kdtm = [P.sbuf("kdtm%d" % s, [128, 9, 128], BF16, ph) for s in range(2)]
        hd = [P.sbuf("hd%d" % s, [128, 16], F32, ph) for s in range(2)]
        dec = [P.sbuf("dec%d" % s, [128, 16], F32, ph) for s in range(2)]
        negr = [P.sbuf("negr%d" % s, [128, 16], F32, ph) for s in range(2)]
        rr = [P.sbuf("rr%d" % s, [128, 16], F32, ph) for s in range(2)]
        hp_b = [[Buf("hp%d_%d" % (s, i)) for i in range(MAXT)] for s in range(2)]
        sc_b = [[Buf("sc%d_%d" % (s, i)) for i in range(MAXT)] for s in range(2)]
        tq = [P.sbuf("tq%d" % s, [128, 512], F32, ph) for s in range(2)] if full else None
        tsg = [P.sbuf("tsg%d" % s, [128, 512], F32, ph) for s in range(2)]
        tk = [P.sbuf("tk%d" % s, [128, 512], F32, ph) for s in range(2)]
        tb = [P.sbuf("tb%d" % s, [128, 512], F32, ph) for s in range(2)]
        tE1 = [P.sbuf("tE1%d" % s, [128, 512], BF16, ph) for s in range(2)] if full else None
        tE2 = [P.sbuf("tE2%d" % s, [128, 512], BF16, ph) for s in range(2)]
        tq_b = [Buf("tq") for _ in range(2)]
        tsg_b = [Buf("tsg") for _ in range(2)]
        tk_b = [Buf("tk") for _ in range(2)]
        tb_b = [Buf("tb") for _ in range(2)]
        tE1_b = [Buf("tE1") for _ in range(2)]
        tE2_b = [Buf("tE2") for _ in range(2)]
        ATs_b = [Buf("ATs") for _ in range(2)]
        Sp_b = [Buf("Sp") for _ in range(2)]
        Sd = [P.sbuf("Sd%d" % s, [128, 128], F32, ph) for s in range(2)]
        Sd_b = [Buf("Sd") for _ in range(2)]
        if full:
            ATs = [P.sbuf("ATs%d" % s, [128, 128], BF16, ph) for s in range(2)]
            Sp = [P.sbuf("Sp%d" % s, [128, 128], BF16, ph) for s in range(2)]
            osq = [P.sbuf("osq0", [128, 512], BF16, ph)] * 2
            osq_b = [Buf("osq")] * 2
            lnv0 = P.sbuf("lnv0", [128, 512], F32, ph)
            rstd0 = P.sbuf("rstd0", [128, 512], F32, ph)
            r0_b = Buf("r0")
            t1 = [P.sbuf("t1_0", [128, 512], F32, ph)] * 2
            t1_b = [Buf("t1")] * 2
            for s in range(2):
                MEMSET(ATs[s][:], 0.0, [ATs_b[s]])
        if has_s:
            qss = [P.sbuf("qss%d" % s, [128, NS], F32, ph) for s in range(2)]
            fss = [P.sbuf("fss%d" % s, [128, NS], F32, ph) for s in range(2)]
            kstm = [P.sbuf("kstm%d" % s, [NS, 128], BF16, ph) for s in range(2)]
            vstm = [P.sbuf("vstm%d" % s, [NS, 128], F32, ph) for s in range(2)]
            vblk = [P.sbuf("vblk%d" % s, [NS, NS, 128], BF16, ph) for s in range(2)]
            ss_b = [Buf("ss") for _ in range(2)]
            NSIN = 3
            sin = [P.sbuf("sin%d" % s, [128, 4, 128], F32, ph) for s in range(NSIN)]
            sin_b = [Buf("sin") for _ in range(NSIN)]
            sin_ctr = [0]
        dmy = P.sbuf("dmy", [128, 2], F32, ph)
        dmy_b = Buf("dmy")
        print("  l0 scratch remaining:", nc.sbuf_bytes_remaining)

        def load_w(h):
            s = h % 2
            cols = [h * 128, 2048 + h * 128, 4096 + h * 128, 6144 + h * 128]
            for k in range(4):
                if not full and k in (0, 3):
                    continue
                P.dma("pool", [(wsl[s][k][:], w_in_a[:, cols[k]:cols[k] + 128].rearrange("(kc p) n -> p kc n", p=128))],
                      writes=[wsl_b[s][k]], dbuf=wsl_b[s][k])

        def head_proj(h):
            s = h % 2
            wq, wf, wi, wz = wsl[s]
            wq_b, wf_b, wi_b, wz_b = wsl_b[s]
            lnomc = lnomlv[:, h:h + 1]

            def stageA(ti):
                t = tiles[ti]
                c0, n = t["c0"], t["n"]
                p2 = ti % 2
                xr = [xnT_b[ti]]
                hpb = hp_b[s][ti]
                nb = len(t["blks"])
                for kc in range(8):
                    MM(banks[1][:, 0:n], wf[:, kc, :], xnT[:, kc, c0:c0 + n], kc == 0, kc == 7, [wf_b] + xr, bqs(1))
                yield
                ACT(tsg[p2][:, 0:n], banks[1][:, 0:n], AF.Exp, bqs(1), [tsg_b[p2]])
                ACT(tsg[p2][:, 0:n], tsg[p2][:, 0:n], AF.Ln, [tsg_b[p2]] + CONST, [tsg_b[p2]], bias=onec[:, 0:1], scale=1.0)
                ACT(tk[p2][:, 0:n], tsg[p2][:, 0:n], AF.Exp, [tsg_b[p2]] + CONST, [tk_b[p2]], bias=lnomc, scale=-1.0)
                if not t["samp"]:
                    ACT(tsg[p2][:, 0:n], tk[p2][:, 0:n], AF.Ln, [tk_b[p2]] + CONST, [tsg_b[p2]], bias=onec[:, 0:1], scale=-1.0)
                    if full:
                        ACT(dmy[:, 0:1], onec[:, 0:1], AF.Silu, CONST, [dmy_b])
                    j0 = t["blks"][0]
                    for jj, j in enumerate(t["blks"]):
                        for kc in range(8):
                            MM(banks[3][:, jj * 128:(jj + 1) * 128], xnT[:, kc, j * 128:(j + 1) * 128], wi[:, kc, :], kc == 0, kc == 7,
                               [wi_b] + xr, [bq[3][jj]])
                        if jj % 2 == 1 and jj + 1 < nb:
                            yield
                    TCOPY(Vtm[s][:, j0:j0 + nb, :], banks[3][:, 0:n].rearrange("p (j t) -> p j t", t=128), bqs(3, 0, nb), [hpb])
                else:
                    for kc in range(8):
                        MM(banks[3][0:NS, 0:128], xnT[:, kc, c0:c0 + NS], wi[:, kc, :], kc == 0, kc == 7, [wi_b] + xr, [bq[3][0]])
                yield
                if full:
                    for kc in range(8):
                        MM(banks[0][:, 0:n], wq[:, kc, :], xnT[:, kc, c0:c0 + n], kc == 0, kc == 7, [wq_b] + xr, bqs(0))
                    for kc in range(8):
                        MM(banks[2][:, 0:n], wz[:, kc, :], xnT[:, kc, c0:c0 + n], kc == 0, kc == 7, [wz_b] + xr, bqs(2))
                    yield
                    if not t["samp"]:
                        ACT(tq[p2][:, 0:n], banks[0][:, 0:n], AF.Silu, bqs(0), [tq_b[p2]])
                    else:
                        ACT(qss[s][:, 0:n], banks[0][:, 0:n], AF.Silu, bqs(0), [ss_b[s]])
                    ACT(zs[s][:, c0:c0 + n], banks[2][:, 0:n], AF.Silu, bqs(2), [hpb])
                    ACT(dmy[:, 1:2], onec[:, 0:1], AF.Exp, CONST, [dmy_b])
                if t["samp"]:
                    TS(fss[s][:, 0:n], tk[p2][:, 0:n], -1.0, 1.0, ALU.mult, ALU.add, [tk_b[p2]], [ss_b[s]])
                    TCOPY(kd[s][:, c0:c0 + n], tk[p2][:, 0:n], [tk_b[p2]], [hpb])
                    MM(banks[4][0:NS, 0:128], kd[s][:, c0:c0 + n], ident_b[:], True, True, [hpb] + CONST, [bq[4][0]])
                    ACOPY(kstm[s][:], banks[4][0:NS, 0:128], [bq[4][0]], [ss_b[s]])
                    ACOPY(vstm[s][:], banks[3][0:NS, 0:128], [bq[3][0]], [ss_b[s]])
                    for sp_ in range(NS):
                        TS(vblk[s][:, sp_, :], vstm[s][:], ident_f[0:NS, sp_:sp_ + 1], None, ALU.mult, None, [ss_b[s]] + CONST, [ss_b[s]])
                yield

            def stageB(ti):
                t = tiles[ti]
                c0, n = t["c0"], t["n"]
                p2 = ti % 2
                hpb = hp_b[s][ti]
                scb = sc_b[s][ti]
                nb = len(t["blks"])
                j0 = t["blks"][0]
                for jj in range(nb):
                    sl = slice(jj * 128, (jj + 1) * 128)
                    SCAN(tb[p2][:, sl], ones_f[:], tsg[p2][:, sl], [tsg_b[p2]] + CONST, [tb_b[p2]])
                tb3 = tb[p2][:, 0:n].rearrange("p (j t) -> p j t", t=128)
                blast = tb3[:, :, 127:128].rearrange("p j o -> p (j o)")
                TS(rr[s][:, j0:j0 + nb], blast, 0.5, None, ALU.mult, None, [tb_b[p2]], [scb])
                TT(tb3, tb3, rr[s][:, j0:j0 + nb].rearrange("p (j o) -> p j o", o=1).to_broadcast([128, nb, 128]), ALU.subtract,
                   [tb_b[p2], scb], [tb_b[p2]])
                yield
                ACT(hd[s][:, j0:j0 + nb], rr[s][:, j0:j0 + nb], AF.Exp, [scb], [scb])
                ACT(dec[s][:, j0:j0 + nb], rr[s][:, j0:j0 + nb], AF.Exp, [scb], [scb], scale=2.0)
                if full:
                    ACT(tE1[p2][:, 0:n], tb[p2][:, 0:n], AF.Exp, [tb_b[p2]], [tE1_b[p2]])
                ACT(tE2[p2][:, 0:n], tb[p2][:, 0:n], AF.Exp, [tb_b[p2]], [tE2_b[p2]], scale=-1.0)
                yield
                TT(kd[s][:, c0:c0 + n], tk[p2][:, 0:n], tE2[p2][:, 0:n], ALU.mult, [tk_b[p2], tE2_b[p2]], [hpb])
                if full:
                    TT(qd[s][:, c0:c0 + n], tq[p2][:, 0:n], tE1[p2][:, 0:n], ALU.mult, [tq_b[p2], tE1_b[p2]], [hpb])
                for jj, j in enumerate(t["blks"]):
                    MM(banks[4][:, jj * 128:(jj + 1) * 128], kd[s][:, j * 128:(j + 1) * 128], ident_b[:], True, True, [hpb] + CONST, [bq[4][jj]])
                TCOPY(kdtm[s][:, j0:j0 + nb, :], banks[4][:, 0:n].rearrange("p (j t) -> p j t", t=128), bqs(4, 0, nb), [hpb])
                yield

            nt = len(tiles)
            for r in range(nt + 1):
                gens = []
                if r >= 1 and not tiles[r - 1]["samp"]:
                    gens.append(stageB(r - 1))
                if r < nt:
                    gens.append(stageA(r))
                while gens:
                    for g in list(gens):
                        try:
                            next(g)
                            yield
                        except StopIteration:
                            gens.remove(g)

        def dphase(h, s, ti, c0, n):
            k = ti % 2
            oc = t1[0]
            TCOPY(oc[:, 0:n], banks[6][:, 0:n], bqs(6), [t1_b[0]])
            TT(osq[k][:, 0:n], oc[:, 0:n], oc[:, 0:n], ALU.mult, [t1_b[0]], [osq_b[k]])
            MM(banks[6][:, 0:n], ones_b[:], osq[k][:, 0:n], True, True, [osq_b[k]] + CONST, bqs(6))
            ACT(lnv0[:, 0:n], banks[6][:, 0:n], AF.Ln, bqs(6) + CONST, [r0_b], bias=epsc[:, 0:1], scale=1.0 / 128)
            ACT(rstd0[:, 0:n], lnv0[:, 0:n], AF.Exp, [r0_b], [r0_b], scale=-0.5)
            TT(oc[:, 0:n], oc[:, 0:n], rstd0[:, 0:n], ALU.mult, [t1_b[0], r0_b], [t1_b[0]])
            STT(ogT[:, h, c0:c0 + n], oc[:, 0:n], gcol(V_ON, h), zs[s][:, c0:c0 + n], ALU.mult, ALU.mult,
                [t1_b[0], hp_b[s][ti]] + CONST, [og_b[h][ti]])

        def head_scan(h):
            s = h % 2
            Sh = Sst[:, h, :]
            pending = []

            def flush():
                for fn in pending:
                    fn()
                del pending[:]

            blocks = [(ti, jj, j) for ti, t in enumerate(tiles) if not t["samp"] for jj, j in enumerate(t["blks"])]
            slots_ = {}
            NPF = 2

            def load_sin(g4):
                g = sin_ctr[0] % NSIN
                sin_ctr[0] += 1
                slots_[g4] = g
                P.dma("sp", [(sin[g][:], st_in[g4 * 4:g4 * 4 + 4, h].rearrange("s k v -> k s v"))], writes=[sin_b[g]], dbuf=sin_b[g])

            if has_s:
                for g4 in range(NPF):
                    load_sin(g4)

            def pe_front(bi):
                ti, jj, j = blocks[bi]
                a = j % 2
                blk = slice(j * 128, (j + 1) * 128)
                MM(banks[7][:, a * 128:(a + 1) * 128], kdtm[s][:, j, :], Vtm[s][:, j, :], True, True, [hp_b[s][ti]], [bq[7][a]])
                if full:
                    MM(banks[5][:, a * 128:(a + 1) * 128], kd[s][:, blk], qd[s][:, blk], True, True, [hp_b[s][ti]], [bq[5][a]])

            pe_front(0)
            for bi, (ti, jj, j) in enumerate(blocks):
                t = tiles[ti]
                c0, n = t["c0"], t["n"]
                nb = len(t["blks"])
                hpr = [hp_b[s][ti]]
                scr = [sc_b[s][ti]]
                a = j % 2
                blk = slice(j * 128, (j + 1) * 128)
                Uq = banks[7][:, a * 128:(a + 1) * 128]
                Aq = banks[5][:, a * 128:(a + 1) * 128]
                if bi + 1 < len(blocks):
                    pe_front(bi + 1)
                if full:
                    TS(Sp[a][:], Sh, hd[s][:, j:j + 1], None, ALU.mult, None, [S_b[h]] + scr, [Sp_b[a]])
                TS(Sd[a][:], Uq, hd[s][:, j:j + 1], None, ALU.mult, None, [bq[7][a]] + scr, [Sd_b[a]])
                if full:
                    CPRED(ATs[a][:], mtri[:], Aq, [bq[5][a], ATs_b[a]] + CONST, [ATs_b[a]])
                STT(Sh, Sh, dec[s][:, j:j + 1], Sd[a][:], ALU.mult, ALU.add, [S_b[h], Sd_b[a]] + scr, [S_b[h]])
                flush()
                if full:
                    def cons(a=a, j=j, jj=jj, blk=blk, hpr=hpr):
                        oq = banks[6][:, jj * 128:(jj + 1) * 128]
                        MM(oq, Sp[a][:], qd[s][:, blk], True, False, [Sp_b[a]] + hpr, [bq[6][jj]])
                        MM(oq, Vtm[s][:, j, :], ATs[a][:], False, True, [ATs_b[a]] + hpr, [bq[6][jj]])
                    pending.append(cons)
                    if jj == nb - 1:
                        pending.append(lambda ti=ti, c0=c0, n=n: dphase(h, s, ti, c0, n))
                yield
            flush()
            yield
            if has_s:
                P.dma("sp", [(stp_d[h], Sh)], reads=[S_b[h]], dbuf=stp_cb, is_out=True)
                ti = len(tiles) - 1
                c0 = tiles[ti]["c0"]
                for g4 in range(4):
                    s0 = g4 * 4
                    bk = 4 + g4 % 2
                    g = slots_[g4]
                    MM(banks[bk][:, :], kstm[s][:], vblk[s][:, s0:s0 + 4, :], True, True, [ss_b[s]], bqs(bk))
                    for si in range(4):
                        sidx = s0 + si
                        STT(sin[g][:, si, :], sin[g][:, si, :], fss[s][:, sidx:sidx + 1], banks[bk][:, si * 128:(si + 1) * 128], ALU.mult, ALU.add,
                            [sin_b[g], ss_b[s], bq[bk][si]], [sin_b[g]])
                        MM(banks[6][:, sidx:sidx + 1], sin[g][:, si, :], qss[s][:, sidx:sidx + 1], True, True, [sin_b[g], ss_b[s]], [bq[6][0]])
                    P.dma("sp", [(sts_d[s0:s0 + 4, h].rearrange("s k v -> k s v"), sin[g][:])], reads=[sin_b[g]], dbuf=sin_b[g], is_out=True)
                    if g4 + NPF < 4:
                        load_sin(g4 + NPF)
                    yield
                dphase(h, s, ti, c0, NS)
                yield

        def drive(gens):
            RATIO = int(os.environ.get("KB_RATIO", "2"))
            gens = [[g, (RATIO if i == 0 else 1)] for i, g in enumerate(gens) if g is not None]
            while gens:
                for it in list(gens):
                    for _ in range(it[1]):
                        try:
                            next(it[0])
                        except StopIteration:
                            gens.remove(it)
                            break

        import os
        NH = int(os.environ.get("KB_NH", "16"))
        load_w(0)
        prev_scan = None
        for h in range(NH):
            if h + 1 < NH:
                load_w(h + 1)
            drive([head_proj(h), prev_scan])
            prev_scan = head_scan(h)
        drive([prev_scan])

    def stage_l0u(sb, ph):
        kind = sb["kind"]
        full = kind == "M"
        tiles = sb_tiles(sb)
        has_s = sb["ns"] > 0
        NU = 3
        wsl = [[P.sbuf("w0_%d_%d" % (s, k), [128, 8, 128], BF16, ph) for k in range(4)] for s in range(NU)]
        wsl_b = [[Buf("w0") for k in range(4)] for s in range(NU)]
        qd = [P.sbuf("qd%d" % s, [128, 512], BF16, ph) for s in range(NU)] if full else None
        kd = [P.sbuf("kd%d" % s, [128, 512], BF16, ph) for s in range(NU)]
        zs = [P.sbuf("zs%d" % s, [128, 512], BF16, ph) for s in range(NU)] if full else None
        Vtm = [P.sbuf("Vtm%d" % s, [128, 4, 128], BF16, ph) for s in range(NU)]
        kdtm = [P.sbuf("kdtm%d" % s, [128, 4, 128], BF16, ph) for s in range(NU)]
        hd = [P.sbuf("hd%d" % s, [128, 4], F32, ph) for s in range(NU)]
        dec = [P.sbuf("dec%d" % s, [128, 4], F32, ph) for s in range(NU)]
        rr = [P.sbuf("rr%d" % s, [128, 4], F32, ph) for s in range(NU)]
        hp_b = [Buf("hp%d" % s) for s in range(NU)]
        sc_b = [Buf("sc%d" % s) for s in range(NU)]
        tq = [P.sbuf("tq%d" % s, [128, 512], F32, ph) for s in range(2)] if full else None
        tsg = [P.sbuf("tsg%d" % s, [128, 512], F32, ph) for s in range(2)]
        tk = [P.sbuf("tk%d" % s, [128, 512], F32, ph) for s in range(2)]
        tb = [P.sbuf("tb%d" % s, [128, 512], F32, ph) for s in range(2)]
        tE1 = [P.sbuf("tE1%d" % s, [128, 512], BF16, ph) for s in range(2)] if full else None
        tE2 = [P.sbuf("tE2%d" % s, [128, 512], BF16, ph) for s in range(2)]
        tq_b = [Buf("tq") for _ in range(2)]
        tsg_b = [Buf("tsg") for _ in range(2)]
        tk_b = [Buf("tk") for _ in range(2)]
        tb_b = [Buf("tb") for _ in range(2)]
        tE1_b = [Buf("tE1") for _ in range(2)]
        tE2_b = [Buf("tE2") for _ in range(2)]
        ATs_b = [Buf("ATs") for _ in range(2)]
        Sp_b = [Buf("Sp") for _ in range(2)]
        Sd = [P.sbuf("Sd%d" % s, [128, 128], F32, ph) for s in range(2)]
        Sd_b = [Buf("Sd") for _ in range(2)]
        if full:
            ATs = [P.sbuf("ATs%d" % s, [128, 128], BF16, ph) for s in range(2)]
            Sp = [P.sbuf("Sp%d" % s, [128, 128], BF16, ph) for s in range(2)]
            osq = P.sbuf("osq0", [128, 512], BF16, ph)
            osq_b = Buf("osq")
            lnv0 = P.sbuf("lnv0", [128, 512], F32, ph)
            rstd0 = P.sbuf("rstd0", [128, 512], F32, ph)
            r0_b = Buf("r0")
            oc = P.sbuf("oc", [128, 512], F32, ph)
            oc_b = Buf("oc")
            for s in range(2):
                MEMSET(ATs[s][:], 0.0, [ATs_b[s]])
        if has_s:
            qss = [P.sbuf("qss%d" % s, [128, NS], F32, ph) for s in range(NU)]
            fss = [P.sbuf("fss%d" % s, [128, NS], F32, ph) for s in range(NU)]
            kstm = [P.sbuf("kstm%d" % s, [NS, 128], BF16, ph) for s in range(NU)]
            vstm = [P.sbuf("vstm%d" % s, [NS, 128], F32, ph) for s in range(NU)]
            ss_b = [Buf("ss") for _ in range(NU)]
            vblk = P.sbuf("vblk", [NS, NS, 128], BF16, ph)
            vblk_b = Buf("vblk")
            NSIN = 3
            sin = [P.sbuf("sin%d" % s, [128, 4, 128], F32, ph) for s in range(NSIN)]
            sin_b = [Buf("sin") for _ in range(NSIN)]
            sin_ctr = [0]
        if not full:
            ones512 = P.sbuf("ones512", [128, 512], F32, ph)
            ones512_b = Buf("ones512")
            MEMSET(ones512[:], 1.0, [ones512_b])
        print("  l0u scratch remaining:", nc.sbuf_bytes_remaining)

        units = [(ti, h) for ti in range(len(tiles)) for h in range(16)]
        NUN = len(units)

        def load_w(ui):
            ti, h = units[ui]
            s = ui % NU
            cols = [h * 128, 2048 + h * 128, 4096 + h * 128, 6144 + h * 128]
            for k in range(4):
                if not full and k in (0, 3):
                    continue
                P.dma("pool", [(wsl[s][k][:], w_in_a[:, cols[k]:cols[k] + 128].rearrange("(kc p) n -> p kc n", p=128))],
                      writes=[wsl_b[s][k]], dbuf=wsl_b[s][k])

        sin_slots = {}

        def load_sin(h, g4):
            g = sin_ctr[0] % NSIN
            sin_ctr[0] += 1
            sin_slots[(h, g4)] = g
            P.dma("sp", [(sin[g][:], st_in[g4 * 4:g4 * 4 + 4, h].rearrange("s k v -> k s v"))], writes=[sin_b[g]], dbuf=sin_b[g])

        def stageA(ui):
            ti, h = units[ui]
            s = ui % NU
            p2 = ui % 2
            t = tiles[ti]
            c0, n = t["c0"], t["n"]
            nb = len(t["blks"])
            wq, wf, wi, wz = wsl[s]
            wq_b, wf_b, wi_b, wz_b = wsl_b[s]
            lnomc = lnomlv[:, h:h + 1]
            xr = [xnT_b[ti]]
            hpb = hp_b[s]
            for kc in range(8):
                MM(banks[1][:, 0:n], wf[:, kc, :], xnT[:, kc, c0:c0 + n], kc == 0, kc == 7, [wf_b] + xr, bqs(1))
            yield
            ACT(tsg[p2][:, 0:n], banks[1][:, 0:n], AF.Exp, bqs(1), [tsg_b[p2]])
            ACT(tsg[p2][:, 0:n], tsg[p2][:, 0:n], AF.Ln, [tsg_b[p2]] + CONST, [tsg_b[p2]], bias=onec[:, 0:1], scale=1.0)
            ACT(tk[p2][:, 0:n], tsg[p2][:, 0:n], AF.Exp, [tsg_b[p2]] + CONST, [tk_b[p2]], bias=lnomc, scale=-1.0)
            if not t["samp"]:
                ACT(tsg[p2][:, 0:n], tk[p2][:, 0:n], AF.Ln, [tk_b[p2]] + CONST, [tsg_b[p2]], bias=onec[:, 0:1], scale=-1.0)
                for jj, j in enumerate(t["blks"]):
                    for kc in range(8):
                        MM(banks[3][:, jj * 128:(jj + 1) * 128], xnT[:, kc, j * 128:(j + 1) * 128], wi[:, kc, :], kc == 0, kc == 7,
                           [wi_b] + xr, [bq[3][jj]])
                    if jj % 2 == 1 and jj + 1 < nb:
                        yield
                TCOPY(Vtm[s][:, 0:nb, :], banks[3][:, 0:n].rearrange("p (j t) -> p j t", t=128), bqs(3, 0, nb), [hpb])
            else:
                for kc in range(8):
                    MM(banks[3][0:NS, 0:128], xnT[:, kc, c0:c0 + NS], wi[:, kc, :], kc == 0, kc == 7, [wi_b] + xr, [bq[3][0]])
            yield
            if full:
                for kc in range(8):
                    MM(banks[0][:, 0:n], wq[:, kc, :], xnT[:, kc, c0:c0 + n], kc == 0, kc == 7, [wq_b] + xr, bqs(0))
                for kc in range(8):
                    MM(banks[2][:, 0:n], wz[:, kc, :], xnT[:, kc, c0:c0 + n], kc == 0, kc == 7, [wz_b] + xr, bqs(2))
                yield
                if not t["samp"]:
                    ACT(tq[p2][:, 0:n], banks[0][:, 0:n], AF.Silu, bqs(0), [tq_b[p2]])
                else:
                    ACT(qss[s][:, 0:n], banks[0][:, 0:n], AF.Silu, bqs(0), [ss_b[s]])
                ACT(zs[s][:, 0:n], banks[2][:, 0:n], AF.Silu, bqs(2), [hpb])
            if t["samp"]:
                TS(fss[s][:, 0:n], tk[p2][:, 0:n], -1.0, 1.0, ALU.mult, ALU.add, [tk_b[p2]], [ss_b[s]])
                TCOPY(kd[s][:, 0:n], tk[p2][:, 0:n], [tk_b[p2]], [hpb])
                MM(banks[4][0:NS, 0:128], kd[s][:, 0:n], ident_b[:], True, True, [hpb] + CONST, [bq[4][0]])
                ACOPY(kstm[s][:], banks[4][0:NS, 0:128], [bq[4][0]], [ss_b[s]])
                ACOPY(vstm[s][:], banks[3][0:NS, 0:128], [bq[3][0]], [ss_b[s]])
            yield

        def stageB(ui):
            ti, h = units[ui]
            s = ui % NU
            p2 = ui % 2
            t = tiles[ti]
            if t["samp"]:
                return
            n = t["n"]
            nb = len(t["blks"])
            hpb = hp_b[s]
            scb = sc_b[s]
            if not full:
                SCAN(tb[p2][:, 0:n], ones512[:, 0:n], tsg[p2][:, 0:n], [tsg_b[p2], ones512_b], [tb_b[p2]])
                yield
                bend = tb[p2][:, n - 1:n]
                ACT(dec[s][:, 0:1], bend, AF.Exp, [tb_b[p2]], [scb])
                ACT(tE2[p2][:, 0:n], tb[p2][:, 0:n], AF.Exp, [tb_b[p2]], [tE2_b[p2]], bias=bend, scale=-1.0)
                yield
                TT(kd[s][:, 0:n], tk[p2][:, 0:n], tE2[p2][:, 0:n], ALU.mult, [tk_b[p2], tE2_b[p2]], [hpb])
                for jj in range(nb):
                    MM(banks[4][:, jj * 128:(jj + 1) * 128], kd[s][:, jj * 128:(jj + 1) * 128], ident_b[:], True, True, [hpb] + CONST, [bq[4][jj]])
                TCOPY(kdtm[s][:, 0:nb, :], banks[4][:, 0:n].rearrange("p (j t) -> p j t", t=128), bqs(4, 0, nb), [hpb])
                yield
                return
            for jj in range(nb):
                sl = slice(jj * 128, (jj + 1) * 128)
                SCAN(tb[p2][:, sl], ones_f[:], tsg[p2][:, sl], [tsg_b[p2]] + CONST, [tb_b[p2]])
            tb3 = tb[p2][:, 0:n].rearrange("p (j t) -> p j t", t=128)
            blast = tb3[:, :, 127:128].rearrange("p j o -> p (j o)")
            TS(rr[s][:, 0:nb], blast, 0.5, None, ALU.mult, None, [tb_b[p2]], [scb])
            TT(tb3, tb3, rr[s][:, 0:nb].rearrange("p (j o) -> p j o", o=1).to_broadcast([128, nb, 128]), ALU.subtract,
               [tb_b[p2], scb], [tb_b[p2]])
            yield
            ACT(hd[s][:, 0:nb], rr[s][:, 0:nb], AF.Exp, [scb], [scb])
            ACT(dec[s][:, 0:nb], rr[s][:, 0:nb], AF.Exp, [scb], [scb], scale=2.0)
            if full:
                ACT(tE1[p2][:, 0:n], tb[p2][:, 0:n], AF.Exp, [tb_b[p2]], [tE1_b[p2]])
            ACT(tE2[p2][:, 0:n], tb[p2][:, 0:n], AF.Exp, [tb_b[p2]], [tE2_b[p2]], scale=-1.0)
            yield
            TT(kd[s][:, 0:n], tk[p2][:, 0:n], tE2[p2][:, 0:n], ALU.mult, [tk_b[p2], tE2_b[p2]], [hpb])
            if full:
                TT(qd[s][:, 0:n], tq[p2][:, 0:n], tE1[p2][:, 0:n], ALU.mult, [tq_b[p2], tE1_b[p2]], [hpb])
            for jj in range(nb):
                MM(banks[4][:, jj * 128:(jj + 1) * 128], kd[s][:, jj * 128:(jj + 1) * 128], ident_b[:], True, True, [hpb] + CONST, [bq[4][jj]])
            TCOPY(kdtm[s][:, 0:nb, :], banks[4][:, 0:n].rearrange("p (j t) -> p j t", t=128), bqs(4, 0, nb), [hpb])
            yield

        def dphase(h, s, c0, n):
            ACT(oc[:, 0:n], banks[6][:, 0:n], AF.Identity, bqs(6), [oc_b])
            ACT(osq[:, 0:n], oc[:, 0:n], AF.Square, [oc_b], [osq_b])
            MM(banks[6][:, 0:n], ones_b[:], osq[:, 0:n], True, True, [osq_b] + CONST, bqs(6))
            ACT(lnv0[:, 0:n], banks[6][:, 0:n], AF.Ln, bqs(6) + CONST, [r0_b], bias=epsc[:, 0:1], scale=1.0 / 128)
            ACT(rstd0[:, 0:n], lnv0[:, 0:n], AF.Exp, [r0_b], [r0_b], scale=-0.5)
            TT(oc[:, 0:n], oc[:, 0:n], rstd0[:, 0:n], ALU.mult, [oc_b, r0_b], [oc_b])
            STT(ogT[:, h, c0:c0 + n], oc[:, 0:n], gcol(V_ON, h), zs[s][:, 0:n], ALU.mult, ALU.mult,
                [oc_b, hp_b[s]] + CONST, [og_b[h][0], og_b[h][1], og_b[h][2]])

        def scan(ui):
            ti, h = units[ui]
            s = ui % NU
            t = tiles[ti]
            c0, n = t["c0"], t["n"]
            nb = len(t["blks"])
            Sh = Sst[:, h, :]
            hpr = [hp_b[s]]
            scr = [sc_b[s]]
            if not t["samp"] and not full:
                for jj in range(nb):
                    MM(banks[7][:, 0:128], kdtm[s][:, jj, :], Vtm[s][:, jj, :], jj == 0, jj == nb - 1, hpr, [bq[7][0]])
                yield
                STT(Sh, Sh, dec[s][:, 0:1], banks[7][:, 0:128], ALU.mult, ALU.add, [S_b[h], bq[7][0]] + scr, [S_b[h]])
                yield
                return
            if not t["samp"]:
                def pe_front(jj):
                    a = jj % 2
                    blk = slice(jj * 128, (jj + 1) * 128)
                    MM(banks[7][:, a * 128:(a + 1) * 128], kdtm[s][:, jj, :], Vtm[s][:, jj, :], True, True, hpr, [bq[7][a]])
                    if full:
                        MM(banks[5][:, a * 128:(a + 1) * 128], kd[s][:, blk], qd[s][:, blk], True, True, hpr, [bq[5][a]])
                pend = []
                pe_front(0)
                for jj in range(nb):
                    a = jj % 2
                    blk = slice(jj * 128, (jj + 1) * 128)
                    Uq = banks[7][:, a * 128:(a + 1) * 128]
                    Aq = banks[5][:, a * 128:(a + 1) * 128]
                    if jj + 1 < nb:
                        pe_front(jj + 1)
                    if full:
                        ACT(Sp[a][:], Sh, AF.Identity, [S_b[h]] + scr, [Sp_b[a]], scale=hd[s][:, jj:jj + 1])
                    ACT(Sd[a][:], Uq, AF.Identity, [bq[7][a]] + scr, [Sd_b[a]], scale=hd[s][:, jj:jj + 1])
                    if full:
                        CPRED(ATs[a][:], mtri[:], Aq, [bq[5][a], ATs_b[a]] + CONST, [ATs_b[a]])
                    STT(Sh, Sh, dec[s][:, jj:jj + 1], Sd[a][:], ALU.mult, ALU.add, [S_b[h], Sd_b[a]] + scr, [S_b[h]])
                    for fn in pend:
                        fn()
                    del pend[:]
                    if full:
                        def cons(a=a, jj=jj, blk=blk):
                            oq = banks[6][:, jj * 128:(jj + 1) * 128]
                            MM(oq, Sp[a][:], qd[s][:, blk], True, False, [Sp_b[a]] + hpr, [bq[6][jj]])
                            MM(oq, Vtm[s][:, jj, :], ATs[a][:], False, True, [ATs_b[a]] + hpr, [bq[6][jj]])
                        pend.append(cons)
                    yield
                for fn in pend:
                    fn()
                if full:
                    dphase(h, s, c0, n)
                yield
                return
            P.dma("sp", [(stp_d[h], Sh)], reads=[S_b[h]], dbuf=stp_cb, is_out=True)
            for g4 in range(2):
                load_sin(h, g4)
            for sp_ in range(NS):
                TS(vblk[:, sp_, :], vstm[s][:], ident_f[0:NS, sp_:sp_ + 1], None, ALU.mult, None, [ss_b[s]] + CONST, [vblk_b])
            yield
            for g4 in range(4):
                s0 = g4 * 4
                bk = 5 if g4 % 2 == 0 else 7
                g = sin_slots[(h, g4)]
                MM(banks[bk][:, :], kstm[s][:], vblk[:, s0:s0 + 4, :], True, True, [ss_b[s], vblk_b], bqs(bk))
                for si in range(4):
                    sidx = s0 + si
                    STT(sin[g][:, si, :], sin[g][:, si, :], fss[s][:, sidx:sidx + 1], banks[bk][:, si * 128:(si + 1) * 128], ALU.mult, ALU.add,
                        [sin_b[g], ss_b[s], bq[bk][si]], [sin_b[g]])
                    MM(banks[6][:, sidx:sidx + 1], sin[g][:, si, :], qss[s][:, sidx:sidx + 1], True, True, [sin_b[g], ss_b[s]], [bq[6][0]])
                P.dma("sp", [(sts_d[s0:s0 + 4, h].rearrange("s k v -> k s v"), sin[g][:])], reads=[sin_b[g]], dbuf=sin_b[g], is_out=True)
                if g4 + 2 < 4:
                    load_sin(h, g4 + 2)
                yield
            dphase(h, s, c0, NS)
            yield

        load_w(0)
        if NUN > 1:
            load_w(1)
        for idx in range(NUN + 2):
            if idx + 2 < NUN:
                load_w(idx + 2)
            gens = []
            if 0 <= idx - 1 < NUN:
                gens.append(stageB(idx - 1))
            if idx < NUN:
                gens.append(stageA(idx))
            if 0 <= idx - 2 < NUN:
                gens.append(scan(idx - 2))
            while gens:
                for g in list(gens):
                    try:
                        next(g)
                    except StopIteration:
                        gens.remove(g)

    def stage_out(sb, ph, l, w_out):
        tiles = sb_tiles(sb)
        mix = P.sbuf("mix", [128, 8, SBW], F32, ph)
        mix_b = [[Buf("mix") for _ in range(MAXT)] for _ in range(8)]
        wo = [P.sbuf("wo%d" % i, [128, 16, 128], BF16, ph) for i in range(2)]
        wo_b = [Buf("wo") for _ in range(2)]
        rt = alloc_rms(ph, "po")
        tmp = [P.sbuf("tmpo%d" % i, [128, 512], F32, ph) for i in range(2)]
        tmp_b = [Buf("tmpo") for _ in range(2)]
        wg = [P.sbuf("wg%d" % i, [128, 8, 128], BF16, ph) for i in range(2)]
        wg_b = [Buf("wg") for _ in range(2)]
        wp = [P.sbuf("wp%d" % i, [128, 2, 128], BF16, ph) for i in range(2)]
        sgt = [P.sbuf("sgt%d" % i, [128, 512], F32, ph) for i in range(2)]
        sgt_b = [Buf("sgt") for _ in range(2)]
        print("  out scratch remaining:", nc.sbuf_bytes_remaining)

        def ldo(dc):
            P.dma("pool", [(wo[dc % 2][:], w_out[:, dc * 128:(dc + 1) * 128].rearrange("(cc p) n -> p cc n", p=128))],
                  writes=[wo_b[dc % 2]], dbuf=wo_b[dc % 2])
        ldo(0)
        k = 0
        for dc in range(8):
            if dc + 1 < 8:
                ldo(dc + 1)
            for ti, t in enumerate(tiles):
                c0, n = t["c0"], t["n"]
                bk = k % 3
                k += 1
                for cc in range(16):
                    MM(banks[bk][:, 0:n], wo[dc % 2][:, cc, :], ogT[:, cc, c0:c0 + n], cc == 0, cc == 15, [wo_b[dc % 2], og_b[cc][ti]], bqs(bk))
                ACOPY(mix[:, dc, c0:c0 + n], banks[bk][:, 0:n], bqs(bk), [mix_b[dc][ti]])
        gb = V_POST0 if l == 0 else V_POST1
        for ti, t in enumerate(tiles):
            c0, n = t["c0"], t["n"]
            rstd = rms_stats([mix[:, c, c0:c0 + n] for c in range(8)], n, rt, [mix_b[c][ti] for c in range(8)], 1.0 / D)
            for c in range(8):
                a = c % 2
                STT(tmp[a][:, 0:n], mix[:, c, c0:c0 + n], gcol(gb, c), rstd[:, 0:n], ALU.mult, ALU.mult, [mix_b[c][ti], rt[4]] + CONST, [tmp_b[a]])
                TT(hT[:, c, c0:c0 + n], hT[:, c, c0:c0 + n], tmp[a][:, 0:n], ALU.add, [tmp_b[a], hT_b[ti]], [hT_b[ti]])
                ACOPY(xnT[:, c, c0:c0 + n], hT[:, c, c0:c0 + n], [hT_b[ti]], [xnT_b[ti]])

        def ldg(dc):
            P.dma("pool", [(wg[dc % 2][:], w_pg[l][:, dc * 128:(dc + 1) * 128].rearrange("(kc p) n -> p kc n", p=128)),
                           (wp[dc % 2][:], w_pe[l][:, dc * 128:(dc + 1) * 128].rearrange("(kc p) n -> p kc n", p=128))],
                  writes=[wg_b[dc % 2]], dbuf=wg_b[dc % 2])
        ldg(0)
        k = 0
        for dc in range(8):
            if dc + 1 < 8:
                ldg(dc + 1)
            for ti, t in enumerate(tiles):
                c0, n = t["c0"], t["n"]
                a = k % 2
                bg = a
                bp = 2 + a
                k += 1
                for kc in range(8):
                    MM(banks[bg][:, 0:n], wg[dc % 2][:, kc, :], xnT[:, kc, c0:c0 + n], kc == 0, kc == 7, [wg_b[dc % 2], xnT_b[ti]], bqs(bg))
                for pc in range(2):
                    MM(banks[bp][:, 0:n], wp[dc % 2][:, pc, :], pT[:, l, pc, c0:c0 + n], pc == 0, pc == 1, [wg_b[dc % 2], pT_b[ti]], bqs(bp))
                ACT(sgt[a][:, 0:n], banks[bg][:, 0:n], AF.Sigmoid, bqs(bg), [sgt_b[a]])
                TT(tmp[a][:, 0:n], banks[bp][:, 0:n], sgt[a][:, 0:n], ALU.mult, bqs(bp) + [sgt_b[a]], [tmp_b[a]])
                TT(hT[:, dc, c0:c0 + n], hT[:, dc, c0:c0 + n], tmp[a][:, 0:n], ALU.add, [tmp_b[a], hT_b[ti]], [hT_b[ti]])

    def stage_l1(sb, sbi, ph):
        tiles = sb_tiles(sb)
        nblk = sb["nblk"]
        has_s = sb["ns"] > 0
        first = sb["t0"] == 0
        NT = nblk * 128 + sb["ns"]
        kT = P.sbuf("kT", [128, 4, 128 + SBW], BF16, ph)
        Vt = P.sbuf("Vt", [128, 10, 256], BF16, ph)
        rC = P.sbuf("rC", [128, SBW], BF16, ph)
        rS = P.sbuf("rS", [128, SBW], BF16, ph)
        kT_b = [Buf("kT%d" % i) for i in range(MAXT + 1)]
        Vt_b = [Buf("Vt%d" % i) for i in range(10)]
        rope_b = Buf("rope")
        rf = P.sbuf("rf", [128, 512], F32, ph)
        rb = P.sbuf("rb", [128, 512], BF16, ph)
        rt1 = P.sbuf("rt1", [128, 512], F32, ph)
        rt2 = P.sbuf("rt2", [128, 512], F32, ph)
        rf_b, rb_b, rt1_b, rt2_b = Buf("rf"), Buf("rb"), Buf("rt1"), Buf("rt2")
        if has_s:
            kfl = P.sbuf("kfl", [128, 4, 128], F32, ph)
            ksf = P.sbuf("ksf", [128, 4, NS], F32, ph)
            kfl_b = Buf("kfl")
            qsa = P.sbuf("qsa", [128, 16, NS], BF16, ph)
            zsa = P.sbuf("zsa", [128, 16, NS], BF16, ph)
            qsa_b = Buf("qsa")
            vsb = P.sbuf("vsb", [NS, 256], BF16, ph)
            ostg = P.sbuf("ostg", [128, 768], F32, ph)
            ostg_b = Buf("ostg")
        g0 = sb["t0"]
        pairs = [(rC[:, 0:nblk * 128], ropec_d[:, g0:g0 + nblk * 128]), (rS[:, 0:nblk * 128], ropes_d[:, g0:g0 + nblk * 128])]
        if has_s:
            pairs += [(rC[:, nblk * 128:NT], ropec_d[:, NMAIN:NMAIN + NS]), (rS[:, nblk * 128:NT], ropes_d[:, NMAIN:NMAIN + NS])]
        P.dma("pool", pairs, writes=[rope_b], dbuf=rope_b)
        ACOPY(kT[:, :, 0:128], kT_halo[:], [halo_b], [kT_b[0]])
        ACOPY(Vt[:, 0, :], V_halo[:], [halo_b], [Vt_b[0]])

        def rope(src_ps, src_bank, dst, c0, n, reads_extra, writes, f32copy=None):
            ACOPY(rf[:, 0:n], src_ps, [bankb[src_bank]], [rf_b])
            ACOPY(rb[:, 0:n], src_ps, [bankb[src_bank]], [rb_b])
            MM(banks[3][:, 0:n], prot[:], rb[:, 0:n], True, True, [rb_b] + CONST, [bankb[3]])
            TT(rt1[:, 0:n], rf[:, 0:n], rC[:, c0:c0 + n], ALU.mult, [rf_b, rope_b], [rt1_b])
            TT(rt2[:, 0:n], banks[3][:, 0:n], rS[:, c0:c0 + n], ALU.mult, [bankb[3], rope_b], [rt2_b])
            TT(dst, rt1[:, 0:n], rt2[:, 0:n], ALU.add, [rt1_b, rt2_b] + reads_extra, writes)
            if f32copy is not None:
                o_ap, lo, hi, wr = f32copy
                TT(o_ap, rt1[:, lo:hi], rt2[:, lo:hi], ALU.add, [rt1_b, rt2_b], wr)

        pa = ExitStack()
        kvn = P.sbuf("kvn", [128, 8, 512], BF16, pa)
        kvn_b = Buf("kvn")
        wk = P.sbuf("wk", [128, 8, 4, 128], BF16, pa)
        wv = P.sbuf("wv", [128, 8, 256], BF16, pa)
        wkv_b = Buf("wkv")
        rt = alloc_rms(pa, "l1")
        vlast = P.sbuf("vlast", [128, 256], F32, pa)
        vlast_b = Buf("vlast")
        print("  l1a scratch remaining:", nc.sbuf_bytes_remaining)
        wkpairs = []
        for kh_ in range(4):
            src_ = w_kv[:, kh_ * 64:(kh_ + 1) * 64].rearrange("(kc p) d -> p kc d", p=128)
            wkpairs += [(wk[:, :, kh_, 0:64], src_), (wk[:, :, kh_, 64:128], src_)]
        wkpairs.append((wv[:], w_kv[:, 256:512].rearrange("(kc p) n -> p kc n", p=128)))
        P.dma("pool", wkpairs, writes=[wkv_b], dbuf=wkv_b)
        for ti, t in enumerate(tiles):
            c0, n = t["c0"], t["n"]
            rstd = rms_stats([hT[:, c, c0:c0 + n] for c in range(8)], n, rt, [hT_b[ti]], 1.0 / D)
            for c in range(8):
                STT(xnT[:, c, c0:c0 + n], hT[:, c, c0:c0 + n], gcol(V_PRE1, c), rstd[:, 0:n], ALU.mult, ALU.mult,
                    [hT_b[ti], rt[4]] + CONST, [xnT_b[ti]])
                STT(kvn[:, c, 0:n], hT[:, c, c0:c0 + n], gcol(V_KV, c), rstd[:, 0:n], ALU.mult, ALU.mult,
                    [hT_b[ti], rt[4]] + CONST, [kvn_b])
            for kh in range(4):
                bk = kh % 2
                for kc in range(8):
                    MM(banks[bk][:, 0:n], wk[:, kc, kh, :], kvn[:, kc, 0:n], kc == 0, kc == 7, [wkv_b, kvn_b], [bankb[bk]])
                f32c = None
                if has_s and t["samp"]:
                    f32c = (ksf[:, kh, :], 0, NS, [kfl_b])
                elif has_s and (nblk - 1) in t["blks"]:
                    lo = (nblk - 1) * 128 - c0
                    f32c = (kfl[:, kh, :], lo, lo + 128, [kfl_b])
                rope(banks[bk][:, 0:n], bk, kT[:, kh, 128 + c0:128 + c0 + n], c0, n, [], [kT_b[1 + ti]], f32c)
            if not t["samp"]:
                for jj, j in enumerate(t["blks"]):
                    bk = 4 + jj % 2
                    for kc in range(8):
                        MM(banks[bk][:, 0:256], kvn[:, kc, jj * 128:(jj + 1) * 128], wv[:, kc, :], kc == 0, kc == 7, [wkv_b, kvn_b], [bankb[bk]])
                    ACOPY(Vt[:, 1 + j, :], banks[bk][:, 0:256], [bankb[bk]], [Vt_b[1 + j]])
                    if has_s and j == nblk - 1:
                        ACOPY(vlast[:], banks[bk][:, 0:256], [bankb[bk]], [vlast_b])
                        P.dma("sp", [(vp_d, vlast[:])], reads=[vlast_b], dbuf=vlast_b, is_out=True)
            else:
                for kc in range(8):
                    MM(banks[4][0:NS, 0:256], kvn[:, kc, 0:NS], wv[:, kc, :], kc == 0, kc == 7, [wkv_b, kvn_b], [bankb[4]])
                ACOPY(ostg[0:NS, 512:768], banks[4][0:NS, 0:256], [bankb[4]], [ostg_b])
        if has_s:
            for kh in range(4):
                MM(banks[5][:, kh * 128:(kh + 1) * 128], kfl[:, kh, :], ident_f[:], True, True, [kfl_b] + CONST, [bankb[5]])
            kp_st = P.sbuf("kp_st", [128, 256], F32, pa)
            kp_b = Buf("kp_st")
            ACOPY(kp_st[:].rearrange("p (kh d) -> p kh d", kh=4), banks[5][:, :].rearrange("p (kh e) -> p kh e", kh=4)[:, :, 0:64], [bankb[5]], [kp_b])
            P.dma("sp", [(kp_d, kp_st[:])], reads=[kp_b], dbuf=kp_b, is_out=True)
            for kh in range(4):
                MM(banks[6][0:NS, kh * 128:(kh + 1) * 128], ksf[:, kh, :], ident_f[:], True, True, [kfl_b] + CONST, [bankb[6]])
            ACOPY(ostg[0:NS, 0:256].rearrange("p (kh d) -> p kh d", kh=4), banks[6][0:NS, :].rearrange("p (kh e) -> p kh e", kh=4)[:, :, 0:64], [bankb[6]], [ostg_b])
            ksd_b = Buf("ksd")
            P.dma("sp", [(ks_d, ostg[0:NS, 0:256]), (vs_d, ostg[0:NS, 512:768])], reads=[ostg_b], writes=[ksd_b], dbuf=ostg_b, is_out=True)
        P.barrier()
        pa.close()
        if os.environ.get("KB_L1", "") == "a":
            return

        wq = P.sbuf("wq", [128, 8, 512], BF16, ph)
        wz = P.sbuf("wz", [128, 8, 512], BF16, ph)
        wq_b, wz_b = Buf("wq"), Buf("wz")
        qg = P.sbuf("qg", [128, 4, SBW], BF16, ph)
        zg = P.sbuf("zg", [128, 4, SBW], BF16, ph)
        qg_b = [Buf("qg%d" % i) for i in range(MAXT)]
        PT = [P.sbuf("PT%d" % i, [128, 2, 512], BF16, ph) for i in range(4)]
        PT_b = [Buf("PT") for _ in range(4)]
        rden = P.sbuf("rden", [128, 512], F32, ph)
        rden_b = Buf("rden")
        at = P.sbuf("at", [128, 512], F32, ph)
        at_b = Buf("at")
        if has_s:
            ckd = [P.sbuf("ckd%d" % i, [128, 4, 2, 64], BF16, ph) for i in range(2)]
            cvt = [P.sbuf("cvt%d" % i, [128, 256], BF16, ph) for i in range(2)]
            cc_b = [Buf("cc") for _ in range(2)]
            KcT = P.sbuf("KcT", [128, 512], BF16, ph)
            KcT_b = Buf("KcT")
            PTs = P.sbuf("PTs", [128, 32], BF16, ph)
            PTs_b = Buf("PTs")
        print("  l1b scratch remaining:", nc.sbuf_bytes_remaining)

        def ldq(kh):
            P.dma("pool", [(wq[:], w_in_b[:, kh * 512:(kh + 1) * 512].rearrange("(kc p) n -> p kc n", p=128))], writes=[wq_b], dbuf=wq_b)
            P.dma("pool", [(wz[:], w_in_b[:, 2048 + kh * 512:2048 + (kh + 1) * 512].rearrange("(kc p) n -> p kc n", p=128))], writes=[wz_b], dbuf=wz_b)

        ldq(0)
        blkcount = 0
        for kh in range(4):
            for ti, t in enumerate(tiles):
                c0, n = t["c0"], t["n"]
                for i in range(4):
                    bk = i % 2
                    for kc in range(8):
                        MM(banks[bk][:, 0:n], wq[:, kc, i * 128:(i + 1) * 128], xnT[:, kc, c0:c0 + n], kc == 0, kc == 7, [wq_b, xnT_b[ti]], [bankb[bk]])
                    f32c = None
                    rope(banks[bk][:, 0:n], bk, qg[:, i, c0:c0 + n], c0, n, [], [qg_b[ti]], None)
                    bz = 4 + i % 2
                    for kc in range(8):
                        MM(banks[bz][:, 0:n], wz[:, kc, i * 128:(i + 1) * 128], xnT[:, kc, c0:c0 + n], kc == 0, kc == 7, [wz_b, xnT_b[ti]], [bankb[bz]])
                    ACT(zg[:, i, c0:c0 + n], banks[bz][:, 0:n], AF.Silu, [bankb[bz]], [qg_b[ti]])
                if t["samp"]:
                    ACOPY(qsa[:, kh * 4:(kh + 1) * 4, :], qg[:, :, c0:c0 + NS], [qg_b[ti]], [qsa_b])
                    ACOPY(zsa[:, kh * 4:(kh + 1) * 4, :], zg[:, :, c0:c0 + NS], [qg_b[ti]], [qsa_b])
            if kh + 1 < 4:
                ldq(kh + 1)
            steps = [(ti, j) for ti, t in enumerate(tiles) for j in t["blks"] if not (first and j == 0)]

            def emit_scores(idx):
                ti, j = steps[idx]
                st_ = idx % 2
                pm_ = maskf4 if (first and j == 1) else maskp4
                prev_cols = slice(j * 128, (j + 1) * 128)
                cur_cols = slice((j + 1) * 128, (j + 2) * 128)
                blk = slice(j * 128, (j + 1) * 128)
                kprev_b = kT_b[0] if j == 0 else kT_b[1 + (j - 1) // 4]
                kcur_b = kT_b[1 + j // 4]
                for par in range(2):
                    pr = slice(par * 64, (par + 1) * 64)
                    b0, b1 = 2 * par, 2 * par + 1
                    pt = PT[st_ * 2 + par]
                    ptb = PT_b[st_ * 2 + par]
                    MM(banks[b0][:, :], kT[pr, kh, prev_cols], qg[pr, :, blk], True, True, [kprev_b, qg_b[ti]], [bankb[b0]])
                    MM(banks[b1][:, :], kT[pr, kh, cur_cols], qg[pr, :, blk], True, True, [kcur_b, qg_b[ti]], [bankb[b1]])
                    ACT(pt[:, 0, :], banks[b0][:, :], AF.Exp, [bankb[b0]], [ptb], scale=0.125)
                    ACT(pt[:, 1, :], banks[b1][:, :], AF.Exp, [bankb[b1]], [ptb], scale=0.125)
                    TT(pt[:, 0, :], pt[:, 0, :], pm_[:], ALU.mult, [ptb] + CONST, [ptb])
                    TT(pt[:, 1, :], pt[:, 1, :], maskc4[:], ALU.mult, [ptb] + CONST, [ptb])

            def emit_pv(idx):
                ti, j = steps[idx]
                st_ = idx % 2
                blk = slice(j * 128, (j + 1) * 128)
                bo = 4 + 2 * (idx % 2)
                bd = bo + 1
                for par in range(2):
                    pr = slice(par * 64, (par + 1) * 64)
                    pt = PT[st_ * 2 + par]
                    ptb = PT_b[st_ * 2 + par]
                    tp = (0, par * 64)
                    MM(banks[bo][pr, :], Vt[:, j, kh * 64:(kh + 1) * 64], pt[:, 0, :], True, False, [Vt_b[j], ptb], [bankb[bo]], tile_position=tp)
                    MM(banks[bo][pr, :], Vt[:, 1 + j, kh * 64:(kh + 1) * 64], pt[:, 1, :], False, True, [Vt_b[1 + j], ptb], [bankb[bo]], tile_position=tp)
                    MM(banks[bd][pr, :], ones_b[:, 0:64], pt[:, 0, :], True, False, [ptb] + CONST, [bankb[bd]], tile_position=tp)
                    MM(banks[bd][pr, :], ones_b[:, 0:64], pt[:, 1, :], False, False, [ptb] + CONST, [bankb[bd]], tile_position=tp)
                    MM(banks[bd][pr, :], ones_b[0:1, 0:64], sinkrow[0:1, kh, par, :, :].rearrange("p i q -> p (i q)"), False, True, CONST, [bankb[bd]], tile_position=tp)
                ACT(rden[:], banks[bd][:, :], AF.Ln, [bankb[bd]], [rden_b])
                ACT(rden[:], rden[:], AF.Exp, [rden_b], [rden_b], scale=-1.0)
                TT(at[:], banks[bo][:, :], rden[:], ALU.mult, [bankb[bo], rden_b], [at_b])
                TT(ogT[:, kh * 4:(kh + 1) * 4, blk], at[:].rearrange("p (i q) -> p i q", i=4), zg[:, :, blk], ALU.mult,
                   [at_b, qg_b[ti]], [og_b[kh * 4 + i][ti] for i in range(4)])

            for idx in range(len(steps) + 1):
                if idx < len(steps):
                    emit_scores(idx)
                if idx >= 1:
                    emit_pv(idx - 1)
        lastj = nblk - 1
        ACOPY(kT_halo[:], kT[:, :, 128 + lastj * 128:128 + (lastj + 1) * 128], [kT_b[1 + lastj // 4]], [halo_b])
        ACOPY(V_halo[:], Vt[:, 1 + lastj, :], [Vt_b[1 + lastj]], [halo_b])
        if has_s and os.environ.get("KB_L1", "") != "b":
            ti = len(tiles) - 1
            c0 = tiles[ti]["c0"]
            for s_ in range(NS):
                a = s_ % 2
                ksrc = ck[s_].rearrange("j (kh d) -> j kh d", kh=4)
                P.dma("pool", [(ckd[a][:, :, 0, :], ksrc), (ckd[a][:, :, 1, :], ksrc), (cvt[a][:], cv[s_])],
                      writes=[cc_b[a]], dbuf=cc_b[a])
                MM(banks[5][0:1, 0:256], ident_f[0:NS, s_:s_ + 1], ostg[0:NS, 0:256], True, True, [ostg_b] + CONST, [bankb[5]])
                MM(banks[5][0:1, 256:512], ident_f[0:NS, s_:s_ + 1], ostg[0:NS, 512:768], True, True, [ostg_b] + CONST, [bankb[5]])
                ACOPY(ckd[a][0:1, :, 0, :], banks[5][0:1, 0:256].rearrange("p (kh d) -> p kh d", kh=4), [bankb[5]], [cc_b[a]])
                ACOPY(ckd[a][0:1, :, 1, :], banks[5][0:1, 0:256].rearrange("p (kh d) -> p kh d", kh=4), [bankb[5]], [cc_b[a]])
                ACOPY(cvt[a][0:1, :], banks[5][0:1, 256:512], [bankb[5]], [cc_b[a]])
                for kh in range(4):
                    MM(banks[0][:, kh * 128:(kh + 1) * 128], ckd[a][:, kh, :, :].rearrange("p c d -> p (c d)"), ident_b[:], True, True, [cc_b[a]] + CONST, [bankb[0]])
                ACOPY(KcT[:], banks[0][:, :], [bankb[0]], [KcT_b])
                for par in range(2):
                    pr = slice(par * 64, (par + 1) * 64)
                    qb_ = 1 if par == 0 else 4
                    for kh in range(4):
                        MM(banks[qb_][:, kh * 4:(kh + 1) * 4], KcT[pr, kh * 128:(kh + 1) * 128], qsa[pr, kh * 4:(kh + 1) * 4, s_], True, True, [KcT_b, qsa_b], [bankb[qb_]])
                    ACT(PTs[:, par * 16:(par + 1) * 16], banks[qb_][:, 0:16], AF.Exp, [bankb[qb_]], [PTs_b], scale=0.125)
                for kh in range(4):
                    for par in range(2):
                        pr = slice(par * 64, (par + 1) * 64)
                        col = par * 16 + kh * 4
                        tp = (0, par * 64)
                        MM(banks[2][pr, kh * 4:(kh + 1) * 4], cvt[a][:, kh * 64:(kh + 1) * 64], PTs[:, col:col + 4], True, True, [cc_b[a], PTs_b], [bankb[2]], tile_position=tp)
                        MM(banks[3][pr, kh * 4:(kh + 1) * 4], ones_b[:, 0:64], PTs[:, col:col + 4], True, False, [PTs_b] + CONST, [bankb[3]], tile_position=tp)
                        MM(banks[3][pr, kh * 4:(kh + 1) * 4], ones_b[0:1, 0:64], sinkrow[0:1, kh, par, :, 0:1].rearrange("p i q -> p (i q)"), False, True, CONST, [bankb[3]], tile_position=tp)
                P.op("dve", (lambda o, i_: (lambda e: e.reciprocal(o, i_)))(rden[:, 0:16], banks[3][:, 0:16]), [bankb[3]], [rden_b])
                TT(at[:, 0:16], banks[2][:, 0:16], rden[:, 0:16], ALU.mult, [bankb[2], rden_b], [at_b])
                TT(ogT[:, :, c0 + s_], at[:, 0:16], zsa[:, :, s_], ALU.mult, [at_b, qsa_b], [og_b[h_][ti] for h_ in range(16)])

    def stage_y(sb, ph):
        tiles = sb_tiles(sb)
        first = sb["t0"] == 0
        yst = [P.sbuf("yst%d" % i, [128, D], F32, ph) for i in range(3)]
        yst_b = [Buf("yst") for _ in range(3)]
        k = 0
        for ti, t in enumerate(tiles):
            c0 = t["c0"]
            if t["samp"]:
                a = k % 3
                k += 1
                for half in range(2):
                    bk = half
                    for cc in range(4):
                        c = half * 4 + cc
                        MM(banks[bk][0:NS, cc * 128:(cc + 1) * 128], hT[:, c, c0:c0 + NS], ident_f[:], True, True, [hT_b[ti]] + CONST, [bankb[bk]])
                    ACOPY(yst[a][0:NS, half * 512:(half + 1) * 512], banks[bk][0:NS, :], [bankb[bk]], [yst_b[a]])
                P.dma("sp", [(y_d[2048:2048 + NS, :], yst[a][0:NS, :])], reads=[yst_b[a]], dbuf=yst_b[a], is_out=True)
                continue
            for j in t["blks"]:
                if first and j == 0:
                    continue
                a = k % 3
                k += 1
                for half in range(2):
                    bk = (2 * k + half) % 4
                    for cc in range(4):
                        c = half * 4 + cc
                        MM(banks[bk][:, cc * 128:(cc + 1) * 128], hT[:, c, j * 128:(j + 1) * 128], ident_f[:], True, True, [hT_b[ti]] + CONST, [bankb[bk]])
                    if half == 0:
                        ACOPY(yst[a][:, 0:512], banks[bk][:, :], [bankb[bk]], [yst_b[a]])
                    else:
                        TCOPY(yst[a][:, 512:1024], banks[bk][:, :], [bankb[bk]], [yst_b[a]])
                row = (j - 1) * 128 if first else (8 + j) * 128
                P.dma("sp", [(y_d[row:row + 128, :], yst[a][:])], reads=[yst_b[a]], dbuf=yst_b[a], is_out=True)

    def dump_h(sb):
        col0 = sb["t0"]
        for ti, t in enumerate(sb_tiles(sb)):
            c0, n = t["c0"], t["n"]
            g0 = (NMAIN if t["samp"] else col0 + c0)
            P.dma("sp", [(dbg_d[:, :, g0:g0 + n], hT[:, :, c0:c0 + n])], reads=[hT_b[ti]], dbuf=hT_b[ti], is_out=True)

    import os
    sel = os.environ.get("KB_SBS")
    for sbi, sb in enumerate(SBS):
        if sel is not None and str(sbi) not in sel.split(","):
            continue
        if sb["kind"] == "P" and stop_after in ("L0nopre", "LOAD"):
            continue
        ph = ExitStack()
        stage_load(sb, ph)
        stage_prenorm(sb, ph, V_PRE0, xnT, xnT_b)
        P.barrier()
        ph.close()
        if stop_after == "LOAD":
            dump_h(sb)
            P.barrier()
            continue
        ph = ExitStack()
        if sb["kind"] == "P":
            stage_l0u(sb, ph)
        else:
            stage_l0(sb, ph)
        P.barrier()
        ph.close()
        if sb["kind"] == "P":
            continue
        if os.environ.get("KB_OUT", "1") == "1":
            ph = ExitStack()
            stage_out(sb, ph, 0, w_out_a)
            P.barrier()
            ph.close()
        if stop_after in ("L0", "L0nopre"):
            dump_h(sb)
            P.barrier()
            continue
        ph = ExitStack()
        stage_l1(sb, sbi, ph)
        P.barrier()
        ph.close()
        if os.environ.get("KB_OUT1", "1") == "1":
            ph = ExitStack()
            stage_out(sb, ph, 1, w_out_b)
            P.barrier()
            ph.close()
        if stop_after == "L1":
            dump_h(sb)
            P.barrier()
        if os.environ.get("KB_Y", "1") == "1":
            ph = ExitStack()
            stage_y(sb, ph)
            P.barrier()
            ph.close()

    P.emit()
    print("stats:", P.stats)
    return nc


def _consts(half):
    ident = np.eye(128, dtype=np.float32)
    s = np.arange(128)[:, None]
    t = np.arange(128)[None, :]
    mtri = (t >= s).astype(np.uint32)
    maskc = (s <= t).astype(np.float32)
    maskp = (s > t).astype(np.float32)
    maskf = maskp.copy() if half == 1 else np.zeros((128, 128), np.float32)
    prot = np.zeros((128, 128), np.float32)
    for base in (0, 64):
        for i in range(8):
            prot[base + i + 8, base + i] = 1.0
            prot[base + i, base + i + 8] = 1.0
    pos = np.concatenate([half * 2048 - 128 + np.arange(NMAIN), np.full(NS, PAST_LEN)]).astype(np.float32)
    inv = (ROPE_THETA ** (-np.arange(0, 16, 2, dtype=np.float32) / 16)).astype(np.float32)
    ang = pos[None, :] * inv[:, None]
    cos = np.cos(ang).astype(np.float32)
    sin = np.sin(ang).astype(np.float32)
    ropec = np.ones((128, TTOT), np.float32)
    ropes = np.zeros((128, TTOT), np.float32)
    for base in (0, 64):
        ropec[base:base + 8] = cos
        ropec[base + 8:base + 16] = cos
        ropes[base:base + 8] = -sin
        ropes[base + 8:base + 16] = sin
    return dict(ident=ident, mtri=mtri, maskc=maskc, maskp=maskp, maskf=maskf, prot=prot, ropec=ropec, ropes=ropes)


def _col(v):
    v = np.asarray(v, np.float32).reshape(-1, 128)
    return np.ascontiguousarray(v.T)


def make_in_maps(inp):
    f = lambda a: np.ascontiguousarray(np.asarray(a, dtype=np.float32))
    xpr, xsm = f(inp["x_prompt"]), f(inp["x_sample"])
    ppr, psa = f(inp["p_prompt"]), f(inp["p_sample"])
    st, ck, cv = f(inp["state_hgrn"]), f(inp["cache_k"]), f(inp["cache_v"])
    vecs = np.concatenate([
        _col(inp["pre_norm_g"][0]), _col(inp["pre_norm_g"][1]), _col(inp["post_norm_g"][0]), _col(inp["post_norm_g"][1]),
        _col(inp["kv_norm_g"]), _col(inp["onorm_a"][0]), _col(inp["lb_logits"][0]), _col(inp["lb_logits"][1])], axis=1)
    assert vecs.shape == (128, NVEC)
    shared = dict(
        w_in_a=f(inp["w_in_a"][0]), w_out_a=f(inp["w_out_a"][0]), w_kv=f(inp["w_kv"]), w_in_b=f(inp["w_in_b"][0]),
        w_out_b=f(inp["w_out_b"][0]), w_pe=f(inp["w_pe"]), w_pg=f(inp["w_pg"]), vecs=np.ascontiguousarray(vecs),
        sinks=f(inp["sinks"]).reshape(1, 32))
    maps = []
    for c in range(8):
        b, half = c // 2, c % 2
        xm = np.zeros((NMAIN, D), np.float32)
        xp = np.zeros((NPRE, D), np.float32)
        pm = np.zeros((2, NMAIN, 256), np.float32)
        if half == 0:
            xm[128:] = xpr[b, 0:2048]
            pm[:, 128:] = ppr[:, b, 0:2048]
        else:
            xm[:] = xpr[b, 1920:4096]
            pm[:] = ppr[:, b, 1920:4096]
            xp[:] = xpr[b, 0:1920]
        m = dict(shared)
        m.update(_consts(half))
        m.update(xm=xm, xp=xp, xs=np.ascontiguousarray(xsm[c * NS:(c + 1) * NS, 0]), pm=pm,
                 psm=np.ascontiguousarray(psa[:, c * NS:(c + 1) * NS, 0]),
                 st_in=np.ascontiguousarray(st[0, c * NS:(c + 1) * NS]),
                 ck=np.ascontiguousarray(ck[c * NS:(c + 1) * NS].reshape(NS, 128, 256)),
                 cv=np.ascontiguousarray(cv[c * NS:(c + 1) * NS].reshape(NS, 128, 256)))
        maps.append(m)
    return maps


def assemble(results):
    y_p = np.zeros((4, 4096, D), np.float32)
    y_s = np.zeros((128, 1, D), np.float32)
    st_p = np.zeros((1, 4, 16, 128, 128), np.float32)
    st_s = np.zeros((1, 128, 16, 128, 128), np.float32)
    k_p = np.zeros((4, 128, 4, 64), np.float32)
    v_p = np.zeros((4, 128, 4, 64), np.float32)
    k_s = np.zeros((128, 1, 4, 64), np.float32)
    v_s = np.zeros((128, 1, 4, 64), np.float32)
    for c in range(8):
        r = results[c]
        b, half = c // 2, c % 2
        y_p[b, half * 2048:(half + 1) * 2048] = r["y"][0:2048]
        y_s[c * NS:(c + 1) * NS, 0] = r["y"][2048:2048 + NS]
        st_s[0, c * NS:(c + 1) * NS] = r["st_s"]
        k_s[c * NS:(c + 1) * NS, 0] = r["ks"].reshape(NS, 4, 64)
        v_s[c * NS:(c + 1) * NS, 0] = r["vs"].reshape(NS, 4, 64)
        if half == 1:
            st_p[0, b] = r["st_p"]
            k_p[b] = r["kp"].reshape(128, 4, 64)
            v_p[b] = r["vp"].reshape(128, 4, 64)
    return (y_p, y_s, st_p, st_s, k_p, v_p, k_s, v_s)


def kernel(**inputs):
    nc = build_program()
    in_maps = make_in_maps(inputs)
    res = run_bass_kernel_spmd(nc, in_maps, core_ids=list(range(8)))
    return assemble(res.results)
```

```python
import numpy as np
from contextlib import ExitStack
import concourse.bass as bass
import concourse.mybir as mybir
from concourse.bass_utils import run_bass_kernel_spmd

F32 = mybir.dt.float32
BF16 = mybir.dt.bfloat16
U32 = mybir.dt.uint32
ALU = mybir.AluOpType
AF = mybir.ActivationFunctionType

COMPUTE = ("pe", "act", "dve")
QUEUES = ("sp", "pool")
ALLENG = COMPUTE + QUEUES


class Buf:
    __slots__ = ("name", "last_w", "readers", "sem", "keep", "excl")

    def __init__(self, name="", keep=False, excl=False):
        self.excl = excl
        self.name = name
        self.last_w = None
        self.readers = {}
        self.sem = None
        self.keep = keep


class Carrier:
    __slots__ = ("cnt", "handle", "q")

    def __init__(self):
        self.cnt = 0
        self.handle = None
        self.q = None


class Op:
    __slots__ = ("eng", "fn", "deps", "signal", "sigval", "is_dma", "carrier", "dval")

    def __init__(self, eng, fn, is_dma=False):
        self.eng = eng
        self.fn = fn
        self.deps = []
        self.signal = False
        self.sigval = 0
        self.is_dma = is_dma
        self.carrier = None
        self.dval = 0


class Prog:
    def __init__(self, nc):
        self.nc = nc
        self.ops = []
        self.es = ExitStack()
        self.carriers = []
        self.free_carriers = {"sp": [], "pool": []}
        self.active_bufs = []
        self.out_ops = []
        self.last = {e: None for e in ALLENG}
        self.bar = None
        self.bar_pending = set()
        self.dma_since_bar = []
        self.nalloc = 0
        self.nuniq = 0

    def sbuf(self, name, shape, dtype, es=None):
        self.nalloc += 1
        return (es or self.es).enter_context(self.nc.sbuf_tensor("s%d_%s" % (self.nalloc, name), list(shape), dtype))

    def psum(self, name, shape, dtype=F32):
        return self.es.enter_context(self.nc.psum_tensor("p_" + name, list(shape), dtype))

    def _adddep(self, op, d):
        if d is op or d is None:
            return
        for x in op.deps:
            if x is d:
                return
        op.deps.append(d)
        d.signal = True

    def _deps(self, op, reads, writes):
        ex = [b for b in reads if b.excl and b not in writes]
        if ex:
            reads = [b for b in reads if not b.excl]
            writes = list(writes) + ex
        for b in reads:
            d = b.last_w
            if d is not None:
                if (not d.is_dma) and (not op.is_dma) and d.eng == op.eng and op.eng == "pe":
                    pass
                else:
                    self._adddep(op, d)
        for b in writes:
            cands = [b.last_w] + list(b.readers.values())
            for d in cands:
                if d is None:
                    continue
                if (not d.is_dma) and (not op.is_dma) and d.eng == op.eng:
                    continue
                self._adddep(op, d)
        if op.eng in self.bar_pending:
            self.bar_pending.discard(op.eng)
            for d in self.bar:
                if d.is_dma or d.eng != op.eng:
                    self._adddep(op, d)
        for b in reads:
            if op.is_dma:
                self.nuniq += 1
                b.readers["dma%d" % self.nuniq] = op
            else:
                b.readers[op.eng] = op
        for b in writes:
            b.last_w = op
            b.readers = {}
        self.last[op.eng] = op

    def op(self, eng, fn, reads=(), writes=()):
        o = Op(eng, fn)
        self._deps(o, reads, writes)
        self.ops.append(o)
        return o

    def dma(self, q, pairs, reads=(), writes=(), dbuf=None, is_out=False):
        if dbuf.sem is None:
            if self.free_carriers[q]:
                dbuf.sem = self.free_carriers[q].pop()
            else:
                dbuf.sem = Carrier()
                dbuf.sem.q = q
                self.carriers.append(dbuf.sem)
            self.active_bufs.append(dbuf)
        c = dbuf.sem
        assert c.q == q, "a DMA buffer must stay on one queue type"
        o = Op(q, pairs, is_dma=True)
        c.cnt += 16 * len(pairs)
        o.carrier = c
        o.dval = c.cnt
        o.signal = True
        self._deps(o, reads, writes)
        self.ops.append(o)
        self.dma_since_bar.append(o)
        if is_out:
            self.out_ops.append(o)
        return o

    def barrier(self):
        ops = [o for o in self.last.values() if o is not None and not o.is_dma]
        latest = {}
        for o in self.dma_since_bar:
            latest[id(o.carrier)] = o
        ops += list(latest.values())
        self.dma_since_bar = []
        self.bar = ops
        self.bar_pending = set(ALLENG)
        keep = []
        for b in self.active_bufs:
            if b.keep:
                keep.append(b)
            else:
                self.free_carriers[b.sem.q].append(b.sem)
                b.sem = None
        self.active_bufs = keep

    def emit(self):
        nc = self.nc
        es = self.es
        sems = {}
        for e in COMPUTE:
            sems[e] = es.enter_context(nc.semaphore("sem_" + e))
        for i, c in enumerate(self.carriers):
            c.handle = es.enter_context(nc.semaphore("dsem%d" % i))
        cnt = {e: 0 for e in COMPUTE}
        for o in self.ops:
            if o.is_dma:
                continue
            if o.signal:
                cnt[o.eng] += 1
                o.sigval = cnt[o.eng]
        by_eng = {e: [] for e in ALLENG}
        for o in self.ops:
            by_eng[o.eng].append(o)
        self.stats = {e: len(v) for e, v in by_eng.items()}
        self.stats["sig"] = dict(cnt)
        self.stats["dma_sems"] = len(self.carriers)
        final_waits = {}
        for o in self.out_ops:
            k = id(o.carrier)
            if k not in final_waits or final_waits[k][1] < o.dval:
                final_waits[k] = (o.carrier.handle, o.dval)

        def stream(ename, e):
            known = {}
            nwait = 0
            for o in by_eng[ename]:
                need = {}
                for d in o.deps:
                    if d.is_dma:
                        s, v = d.carrier.handle, d.dval
                    else:
                        s, v = sems[d.eng], d.sigval
                    k = id(s)
                    if k not in need or need[k][1] < v:
                        need[k] = (s, v)
                for k, (s, v) in need.items():
                    if known.get(k, 0) >= v:
                        continue
                    known[k] = v
                    e.wait_ge(s, v)
                    nwait += 1
                if o.is_dma:
                    for (out_ap, in_ap) in o.fn:
                        e.dma_start(out=out_ap, in_=in_ap).then_inc(o.carrier.handle, 16)
                else:
                    ins = o.fn(e)
                    if o.signal:
                        ins.then_inc(sems[ename], 1)
            if ename == "sp":
                for s, v in final_waits.values():
                    e.wait_ge(s, v)
            self.stats[ename + "_waits"] = nwait

        with nc.Block() as block:
            @block.tensor
            def _(e):
                stream("pe", e)

            @block.scalar
            def _(e):
                stream("act", e)

            @block.vector
            def _(e):
                stream("dve", e)

            @block.gpsimd
            def _(e):
                stream("pool", e)

            @block.sync
            def _(e):
                stream("sp", e)
        es.close()


D = 1024
NMAIN = 2176
NPRE = 1920
NS = 16
TTOT = NMAIN + NS
SBW = 1152
EPS = 1e-6
PAST_LEN = 16384
ROPE_THETA = 500000.0

V_PRE0, V_PRE1, V_POST0, V_POST1, V_KV, V_ON, V_LB0, V_LB1 = 0, 8, 16, 24, 32, 40, 56, 72
NVEC = 88

SBS = [
    dict(kind="P", t0=0, nblk=8, ns=0),
    dict(kind="P", t0=1024, nblk=7, ns=0),
    dict(kind="M", t0=0, nblk=9, ns=0),
    dict(kind="M", t0=1152, nblk=8, ns=NS),
]


def sb_tiles(sb):
    tiles = []
    nb = sb["nblk"]
    j = 0
    while j < nb:
        k = min(4, nb - j)
        tiles.append(dict(c0=j * 128, n=k * 128, blks=list(range(j, j + k)), samp=False))
        j += k
    if sb["ns"]:
        tiles.append(dict(c0=nb * 128, n=sb["ns"], blks=[], samp=True))
    return tiles


def build_program(stop_after=None):
    nc = bass.Bass("TRN2", target_bir_lowering=False)
    P = Prog(nc)

    def din(name, shape, dt=F32):
        return nc.dram_tensor(name, list(shape), dt, kind="ExternalInput").ap()

    def dout(name, shape, dt=F32):
        return nc.dram_tensor(name, list(shape), dt, kind="ExternalOutput").ap()

    xm = din("xm", [NMAIN, D])
    xp = din("xp", [NPRE, D])
    xs = din("xs", [NS, D])
    pm = din("pm", [2, NMAIN, 256])
    psm = din("psm", [2, NS, 256])
    st_in = din("st_in", [NS, 16, 128, 128])
    ck = din("ck", [NS, 128, 256])
    cv = din("cv", [NS, 128, 256])
    w_in_a = din("w_in_a", [D, 8192])
    w_out_a = din("w_out_a", [2048, D])
    w_kv = din("w_kv", [D, 512])
    w_in_b = din("w_in_b", [D, 4096])
    w_out_b = din("w_out_b", [2048, D])
    w_pe = din("w_pe", [2, 256, D])
    w_pg = din("w_pg", [2, D, D])
    vecs_d = din("vecs", [128, NVEC])
    sinks_d = din("sinks", [1, 32])
    ident_d = din("ident", [128, 128])
    mtri_d = din("mtri", [128, 128], U32)
    maskc_d = din("maskc", [128, 128])
    maskp_d = din("maskp", [128, 128])
    maskf_d = din("maskf", [128, 128])
    prot_d = din("prot", [128, 128])
    ropec_d = din("ropec", [128, TTOT])
    ropes_d = din("ropes", [128, TTOT])

    y_d = dout("y", [2048 + NS, D])
    stp_d = dout("st_p", [16, 128, 128])
    sts_d = dout("st_s", [NS, 16, 128, 128])
    kp_d = dout("kp", [128, 256])
    vp_d = dout("vp", [128, 256])
    ks_d = dout("ks", [NS, 256])
    vs_d = dout("vs", [NS, 256])
    dbg_d = dout("dbg", [128, 8, TTOT]) if stop_after else None

    hT = P.sbuf("hT", [128, 8, SBW], F32)
    xnT = P.sbuf("xnT", [128, 8, SBW], BF16)
    ogT = P.sbuf("ogT", [128, 16, SBW], BF16)
    pT = P.sbuf("pT", [128, 2, 2, SBW], BF16)
    Sst = P.sbuf("Sst", [128, 16, 128], F32)
    ident_f = P.sbuf("ident_f", [128, 128], F32)
    ident_b = P.sbuf("ident_b", [128, 128], BF16)
    ones_b = P.sbuf("ones_b", [128, 128], BF16)
    ones_f = P.sbuf("ones_f", [128, 128], F32)
    epsc = P.sbuf("epsc", [128, 1], F32)
    mtri = P.sbuf("mtri", [128, 128], U32)
    maskc = P.sbuf("maskc", [128, 128], BF16)
    maskp = P.sbuf("maskp", [128, 128], BF16)
    maskf = P.sbuf("maskf", [128, 128], BF16)
    prot = P.sbuf("prot", [128, 128], BF16)
    maskc4 = P.sbuf("maskc4", [128, 512], BF16)
    maskp4 = P.sbuf("maskp4", [128, 512], BF16)
    maskf4 = P.sbuf("maskf4", [128, 512], BF16)
    vecs = P.sbuf("vecs", [128, NVEC], F32)
    lbv = P.sbuf("lbv", [128, 16], F32)
    omlv = P.sbuf("omlv", [128, 16], F32)
    nomlv = P.sbuf("nomlv", [128, 16], F32)
    lnomlv = P.sbuf("lnomlv", [128, 16], F32)
    onec = P.sbuf("onec", [128, 1], F32)
    sinkx = P.sbuf("sinkx", [1, 32], F32)
    sinkrow = P.sbuf("sinkrow", [1, 4, 2, 4, 128], BF16)
    kT_halo = P.sbuf("kT_halo", [128, 4, 128], BF16)
    V_halo = P.sbuf("V_halo", [128, 256], BF16)

    MAXT = 3
    hT_b = [Buf("hT%d" % i, keep=True) for i in range(MAXT)]
    xnT_b = [Buf("xnT%d" % i) for i in range(MAXT)]
    pT_b = [Buf("pT%d" % i) for i in range(MAXT)]
    og_b = [[Buf("og%d_%d" % (h, i)) for i in range(MAXT)] for h in range(16)]
    S_b = [Buf("S%d" % h) for h in range(16)]
    const_b = Buf("const", keep=True)
    halo_b = Buf("halo")
    stp_cb = Buf("stp_carrier", keep=True)

    banks = [P.psum("bank%d" % i, [128, 512]) for i in range(8)]
    bankb = [Buf("bank%d" % i, excl=True) for i in range(8)]
    bq = [[bankb[i]] * 4 for i in range(8)]

    def bqs(i, q0=0, q1=4):
        return [bankb[i]]


    def MM(out, lhsT, rhs, start, stop, reads, writes, **kw):
        P.op("pe", lambda e: e.matmul(out, lhsT, rhs, start=start, stop=stop, **kw), reads, writes)

    def ACT(out, in_, func, reads, writes, bias=None, scale=None):
        kw = {}
        if bias is not None:
            kw["bias"] = bias
        if scale is not None:
            kw["scale"] = scale
        P.op("act", lambda e: e.activation(out, in_, func, **kw), reads, writes)

    def ACOPY(out, in_, reads, writes):
        P.op("act", lambda e: e.copy(out, in_), reads, writes)

    def TCOPY(out, in_, reads, writes):
        P.op("dve", lambda e: e.tensor_copy(out, in_), reads, writes)

    def TT(out, in0, in1, op, reads, writes):
        P.op("dve", lambda e: e.tensor_tensor(out, in0, in1, op), reads, writes)

    def TS(out, in0, s1, s2, op0, op1, reads, writes):
        if s2 is None:
            P.op("dve", lambda e: e.tensor_scalar(out, in0, s1, None, op0), reads, writes)
        else:
            P.op("dve", lambda e: e.tensor_scalar(out, in0, s1, s2, op0, op1), reads, writes)

    def STT(out, in0, sc, in1, op0, op1, reads, writes):
        P.op("dve", lambda e: e.scalar_tensor_tensor(out, in0, sc, in1, op0, op1), reads, writes)

    def SCAN(out, d0, d1, reads, writes):
        P.op("dve", lambda e: e.tensor_tensor_scan(out, d0, d1, 0.0, ALU.mult, ALU.add), reads, writes)

    def CPRED(out, mask, data, reads, writes):
        P.op("dve", lambda e: e.copy_predicated(out, mask, data), reads, writes)

    def MEMSET(ap, val, writes):
        P.op("dve", lambda e: e.memset(ap, val), (), writes)

    print("sbuf remaining after persistent:", nc.sbuf_bytes_remaining)

    P.dma("sp", [(ident_f[:], ident_d)], writes=[const_b], dbuf=const_b)
    P.dma("sp", [(mtri[:], mtri_d)], writes=[const_b], dbuf=const_b)
    P.dma("sp", [(vecs[:], vecs_d)], writes=[const_b], dbuf=const_b)
    P.dma("sp", [(sinkx[:], sinks_d)], writes=[const_b], dbuf=const_b)
    cb2 = Buf("const2", keep=True)
    P.dma("pool", [(ident_b[:], ident_d), (maskc[:], maskc_d), (maskp[:], maskp_d), (maskf[:], maskf_d),
                   (prot[:], prot_d)], writes=[cb2], dbuf=cb2)
    cb3 = Buf("const3")
    MEMSET(ones_b[:], 1.0, [cb3])
    MEMSET(ones_f[:], 1.0, [cb3])
    MEMSET(epsc[:], EPS, [cb3])
    MEMSET(Sst[:], 0.0, S_b)
    MEMSET(ogT[:], 0.0, [b for hb in og_b for b in hb])
    MEMSET(kT_halo[:], 0.0, [halo_b])
    MEMSET(V_halo[:], 0.0, [halo_b])
    TT(lbv[:], vecs[:, V_LB0:V_LB0 + 16], vecs[:, V_LB1:V_LB1 + 16], ALU.subtract, [const_b], [cb3])
    ACT(lbv[:], lbv[:], AF.Sigmoid, [cb3], [cb3])
    TS(omlv[:], lbv[:], -1.0, 1.0, ALU.mult, ALU.add, [cb3], [cb3])
    TS(nomlv[:], lbv[:], 1.0, -1.0, ALU.mult, ALU.add, [cb3], [cb3])
    MEMSET(onec[:], 1.0, [cb3])
    ACT(lnomlv[:], omlv[:], AF.Ln, [cb3], [cb3])
    ACT(sinkx[:], sinkx[:], AF.Exp, [const_b], [cb3])
    sx4 = sinkx[:].rearrange("p (kh i par o) -> p kh par i o", kh=4, i=4, par=2, o=1)
    TCOPY(sinkrow[:], sx4.to_broadcast([1, 4, 2, 4, 128]), [cb3], [cb3])
    for m4, m1 in ((maskc4, maskc), (maskp4, maskp), (maskf4, maskf)):
        for i in range(4):
            TCOPY(m4[:, i * 128:(i + 1) * 128], m1[:], [cb2], [cb3])
    CONST = [const_b, cb2, cb3]

    def gcol(base, c):
        return vecs[:, base + c:base + c + 1]

    def stage_load(sb, ph):
        kind = sb["kind"]
        src = xp if kind == "P" else xm
        tiles = sb_tiles(sb)
        NX = 6
        xin = [P.sbuf("xin%d" % i, [128, D], F32, ph) for i in range(NX)]
        xin_b = [Buf("xin%d" % i) for i in range(NX)]
        pin = [P.sbuf("pin%d" % i, [128, 2, 256], F32, ph) for i in range(NX)] if kind == "M" else None
        pin_b = [Buf("pin%d" % i) for i in range(NX)]
        cnt = 0
        ev = 0
        for ti, t in enumerate(tiles):
            c0, n = t["c0"], t["n"]
            if not t["samp"]:
                slots = []
                for j in t["blks"]:
                    s = cnt % NX
                    cnt += 1
                    r0 = sb["t0"] + j * 128
                    P.dma("sp", [(xin[s][:], src[r0:r0 + 128, :])], writes=[xin_b[s]], dbuf=xin_b[s])
                    if kind == "M":
                        P.dma("sp", [(pin[s][:], pm[:, r0:r0 + 128, :].rearrange("l t f -> t l f"))],
                              writes=[pin_b[s]], dbuf=pin_b[s])
                    slots.append(s)
                nb = len(slots)
                for c in range(8):
                    bk = c % 2
                    for jj, s in enumerate(slots):
                        MM(banks[bk][:, jj * 128:(jj + 1) * 128], xin[s][:, c * 128:(c + 1) * 128], ident_f[:], True, True,
                           [xin_b[s]] + CONST, [bq[bk][jj]])
                    ev += 1
                    if ev % 2 == 0:
                        ACOPY(hT[:, c, c0:c0 + n], banks[bk][:, 0:n], bqs(bk, 0, nb), [hT_b[ti]])
                    else:
                        TCOPY(hT[:, c, c0:c0 + n], banks[bk][:, 0:n], bqs(bk, 0, nb), [hT_b[ti]])
                if kind == "M":
                    for l in range(2):
                        for pc in range(2):
                            bk = 2 + (l * 2 + pc) % 2
                            for jj, s in enumerate(slots):
                                MM(banks[bk][:, jj * 128:(jj + 1) * 128], pin[s][:, l, pc * 128:(pc + 1) * 128], ident_f[:], True, True,
                                   [pin_b[s]] + CONST, [bq[bk][jj]])
                            ACOPY(pT[:, l, pc, c0:c0 + n], banks[bk][:, 0:n], bqs(bk, 0, nb), [pT_b[ti]])
            else:
                s = cnt % NX
                cnt += 1
                P.dma("sp", [(xin[s][0:NS, :], xs)], writes=[xin_b[s]], dbuf=xin_b[s])
                P.dma("sp", [(pin[s][0:NS, :, :], psm.rearrange("l t f -> t l f"))], writes=[pin_b[s]], dbuf=pin_b[s])
                for c in range(8):
                    MM(banks[0][:, c * NS:(c + 1) * NS], xin[s][0:NS, c * 128:(c + 1) * 128], ident_f[0:NS, 0:NS], True, True,
                       [xin_b[s]] + CONST, [bq[0][0]])
                ACOPY(hT[:, :, c0:c0 + NS], banks[0][:, 0:8 * NS].rearrange("p (c n) -> p c n", c=8), [bq[0][0]], [hT_b[ti]])
                for l in range(2):
                    for pc in range(2):
                        q = l * 2 + pc
                        MM(banks[1][:, q * NS:(q + 1) * NS], pin[s][0:NS, l, pc * 128:(pc + 1) * 128], ident_f[0:NS, 0:NS], True, True,
                           [pin_b[s]] + CONST, [bq[1][0]])
                ACOPY(pT[:, :, :, c0:c0 + NS], banks[1][:, 0:4 * NS].rearrange("p (l c n) -> p l c n", l=2, c=2), [bq[1][0]], [pT_b[ti]])

    def rms_stats(srcs, n, rt, reads, invd):
        sq, sq_b, lnv, rstd, r_b = rt
        ssb = 7
        for c, s_ap in enumerate(srcs):
            k = c % 2
            ACT(sq[k][:, 0:n], s_ap, AF.Square, reads, [sq_b[k]])
            MM(banks[ssb][:, 0:n], ones_b[:], sq[k][:, 0:n], c == 0, c == len(srcs) - 1, [sq_b[k]] + CONST, bqs(ssb))
        ACT(lnv[:, 0:n], banks[ssb][:, 0:n], AF.Ln, bqs(ssb) + CONST, [r_b], bias=epsc[:, 0:1], scale=invd)
        ACT(rstd[:, 0:n], lnv[:, 0:n], AF.Exp, [r_b], [r_b], scale=-0.5)
        return rstd

    def alloc_rms(ph, tag):
        sq = [P.sbuf("sq%s%d" % (tag, i), [128, 512], BF16, ph) for i in range(2)]
        sq_b = [Buf("sq") for _ in range(2)]
        lnv = P.sbuf("lnv" + tag, [128, 512], F32, ph)
        rstd = P.sbuf("rstd" + tag, [128, 512], F32, ph)
        return (sq, sq_b, lnv, rstd, Buf("rstd"))

    def stage_prenorm(sb, ph, gbase, dst, dst_b):
        rt = alloc_rms(ph, "pn%d" % gbase)
        for ti, t in enumerate(sb_tiles(sb)):
            c0, n = t["c0"], t["n"]
            rstd = rms_stats([hT[:, c, c0:c0 + n] for c in range(8)], n, rt, [hT_b[ti]], 1.0 / D)
            for c in range(8):
                STT(dst[:, c, c0:c0 + n], hT[:, c, c0:c0 + n], gcol(gbase, c), rstd[:, 0:n], ALU.mult, ALU.mult,
                    [hT_b[ti], rt[4]] + CONST, [dst_b[ti]])

    def stage_l0(sb, ph):
        kind = sb["kind"]
        full = kind == "M"
        tiles = sb_tiles(sb)
        nblk = sb["nblk"]
        has_s = sb["ns"] > 0
        wsl = [[P.sbuf("w0_%d_%d" % (s, k), [128, 8, 128], BF16, ph) for k in range(4)] for s in range(2)]
        wsl_b = [[Buf("w0") for k in range(4)] for s in range(2)]
        qd = [P.sbuf("qd%d" % s, [128, SBW], BF16, ph) for s in range(2)] if full else None
        kd = [P.sbuf("kd%d" % s, [128, SBW], BF16, ph) for s in range(2)]
        zs = [P.sbuf("zs%d" % s, [128, SBW], BF16, ph) for s in range(2)] if full else None
        Vtm = [P.sbuf("Vtm%d" % s, [128, 9, 128], BF16, ph) for s in range(2)]
        kdtm = [P.sbuf("kdtm%d" % s, [128, 9, 128], BF16, ph) for s in range(2)]
        hd = [P.sbuf("hd%d" % s, [128, 16], F32, ph) for s in range(2)]
        dec = [P.sbuf("dec%d" % s, [128, 16], F32, ph) for s in range(2)]
        negr = [P.sbuf("negr%d" % s, [128, 16], F32, ph) for s in range(2)]
        rr = [P.sbuf("rr%d" % s, [128, 16], F32, ph) for s in range(2)]
        hp_b = [[Buf("hp%d_%d" % (s, i)) for i in range(MAXT)] for s in range(2)]
        sc_b = [[Buf("sc%d_%d" % (s, i)) for i in range(MAXT)] for s in range(2)]
        tq = [P.sbuf("tq%d" % s, [128, 512], F32, ph) for s in range(2)] if full else None
        tsg = [P.sbuf("tsg%d" % s, [128, 512], F32, ph) for s in range(2)]
        tk = [P.sbuf("tk%d" % s, [128, 512], F32, ph) for s in range(2)]
        tb = [P.sbuf("tb%d" % s, [128, 512], F32, ph) for s in range(2)]
        tE1 = [P.sbuf("tE1%d" % s, [128, 512], BF16, ph) for s in range(2)] if full else None
        tE2 = [P.sbuf("tE2%d" % s, [128, 512], BF16, ph) for s in range(2)]
        tq_b = [Buf("tq") for _ in range(2)]
        tsg_b = [Buf("tsg") for _ in range(2)]
        tk_b = [Buf("tk") for _ in range(2)]
        tb_b = [Buf("tb") for _ in range(2)]
        tE1_b = [Buf("tE1") for _ in range(2)]
        tE2_b = [Buf("tE2") for _ in range(2)]
        ATs_b = [Buf("ATs") for _ in range(2)]
        Sp_b = [Buf("Sp") for _ in range(2)]
        Sd = [P.sbuf("Sd%d" % s, [128, 128], F32, ph) for s in range(2)]
        Sd_b = [Buf("Sd") for _ in range(2)]
        if full:
            ATs = [P.sbuf("ATs%d" % s, [128, 128], BF16, ph) for s in range(2)]
            Sp = [P.sbuf("Sp%d" % s, [128, 128], BF16, ph) for s in range(2)]
            osq = [P.sbuf("osq0", [128, 512], BF16, ph)] * 2
            osq_b = [Buf("osq")] * 2
            lnv0 = P.sbuf("lnv0", [128, 512], F32, ph)
            rstd0 = P.sbuf("rstd0", [128, 512], F32, ph)
            r0_b = Buf("r0")
            t1 = [P.sbuf("t1_0", [128, 512], F32, ph)] * 2
            t1_b = [Buf("t1")] * 2
            for s in range(2):
                MEMSET(ATs[s][:], 0.0, [ATs_b[s]])
        if has_s:
            qss = [P.sbuf("qss%d" % s, [128, NS], F32, ph) for s in range(2)]
            fss = [P.sbuf("fss%d" % s, [128, NS], F32, ph) for s in range(2)]
            kstm = [P.sbuf("kstm%d" % s, [NS, 128], BF16, ph) for s in range(2)]
            vstm = [P.sbuf("vstm%d" % s, [NS, 128], F32, ph) for s in range(2)]
            vblk = [P.sbuf("vblk%d" % s, [NS, NS, 128], BF16, ph) for s in range(2)]
            ss_b = [Buf("ss") for _ in range(2)]
            NSIN = 3
            sin = [P.sbuf("sin%d" % s, [128, 4, 128], F32, ph) for s in range(NSIN)]
            sin_b = [Buf("sin") for _ in range(NSIN)]
            sin_ctr = [0]
        dmy = P.sbuf("dmy", [128, 2], F32, ph)
        dmy_b = Buf("dmy")
        print("  l0 scratch remaining:", nc.sbuf_bytes_remaining)

        def load_w(h):
            s = h % 2
            cols = [h * 128, 2048 + h * 128, 4096 + h * 128, 6144 + h * 128]
            for k in range(4):
                if not full and k in (0, 3):
                    continue
                P.dma("pool", [(wsl[s][k][:], w_in_a[:, cols[k]:cols[k] + 128].rearrange("(kc p) n -> p kc n", p=128))],
                      writes=[wsl_b[s][k]], dbuf=wsl_b[s][k])

        def head_proj(h):
            s = h % 2
            wq, wf, wi, wz = wsl[s]
            wq_b, wf_b, wi_b, wz_b = wsl_b[s]
            lnomc = lnomlv[:, h:h + 1]

            def stageA(ti):
                t = tiles[ti]
                c0, n = t["c0"], t["n"]
                p2 = ti % 2
                xr = [xnT_b[ti]]
                hpb = hp_b[s][ti]
                nb = len(t["blks"])
                for kc in range(8):
                    MM(banks[1][:, 0:n], wf[:, kc, :], xnT[:, kc, c0:c0 + n], kc == 0, kc == 7, [wf_b] + xr, bqs(1))
                yield
                ACT(tsg[p2][:, 0:n], banks[1][:, 0:n], AF.Exp, bqs(1), [tsg_b[p2]])
                ACT(tsg[p2][:, 0:n], tsg[p2][:, 0:n], AF.Ln, [tsg_b[p2]] + CONST, [tsg_b[p2]], bias=onec[:, 0:1], scale=1.0)
                ACT(tk[p2][:, 0:n], tsg[p2][:, 0:n], AF.Exp, [tsg_b[p2]] + CONST, [tk_b[p2]], bias=lnomc, scale=-1.0)
                if not t["samp"]:
                    ACT(tsg[p2][:, 0:n], tk[p2][:, 0:n], AF.Ln, [tk_b[p2]] + CONST, [tsg_b[p2]], bias=onec[:, 0:1], scale=-1.0)
                    if full:
                        ACT(dmy[:, 0:1], onec[:, 0:1], AF.Silu, CONST, [dmy_b])
                    j0 = t["blks"][0]
                    for jj, j in enumerate(t["blks"]):
                        for kc in range(8):
                            MM(banks[3][:, jj * 128:(jj + 1) * 128], xnT[:, kc, j * 128:(j + 1) * 128], wi[:, kc, :], kc == 0, kc == 7,
                               [wi_b] + xr, [bq[3][jj]])
                        if jj % 2 == 1 and jj + 1 < nb:
                            yield
                    TCOPY(Vtm[s][:, j0:j0 + nb, :], banks[3][:, 0:n].rearrange("p (j t) -> p j t", t=128), bqs(3, 0, nb), [hpb])
                else:
                    for kc in range(8):
                        MM(banks[3][0:NS, 0:128], xnT[:, kc, c0:c0 + NS], wi[:, kc, :], kc == 0, kc == 7, [wi_b] + xr, [bq[3][0]])
                yield
                if full:
                    for kc in range(8):
                        MM(banks[0][:, 0:n], wq[:, kc, :], xnT[:, kc, c0:c0 + n], kc == 0, kc == 7, [wq_b] + xr, bqs(0))
                    for kc in range(8):
                        MM(banks[2][:, 0:n], wz[:, kc, :], xnT[:, kc, c0:c0 + n], kc == 0, kc == 7, [wz_b] + xr, bqs(2))
                    yield
                    if not t["samp"]:
                        ACT(tq[p2][:, 0:n], banks[0][:, 0:n], AF.Silu, bqs(0), [tq_b[p2]])
                    else:
                        ACT(qss[s][:, 0:n], banks[0][:, 0:n], AF.Silu, bqs(0), [ss_b[s]])
                    ACT(zs[s][:, c0:c0 + n], banks[2][:, 0:n], AF.Silu, bqs(2), [hpb])
                    ACT(dmy[:, 1:2], onec[:, 0:1], AF.Exp, CONST, [dmy_b])
                if t["samp"]:
                    TS(fss[s][:, 0:n], tk[p2][:, 0:n], -1.0, 1.0, ALU.mult, ALU.add, [tk_b[p2]], [ss_b[s]])
                    TCOPY(kd[s][:, c0:c0 + n], tk[p2][:, 0:n], [tk_b[p2]], [hpb])
                    MM(banks[4][0:NS, 0:128], kd[s][:, c0:c0 + n], ident_b[:], True, True, [hpb] + CONST, [bq[4][0]])
                    ACOPY(kstm[s][:], banks[4][0:NS, 0:128], [bq[4][0]], [ss_b[s]])
                    ACOPY(vstm[s][:], banks[3][0:NS, 0:128], [bq[3][0]], [ss_b[s]])
                    for sp_ in range(NS):
                        TS(vblk[s][:, sp_, :], vstm[s][:], ident_f[0:NS, sp_:sp_ + 1], None, ALU.mult, None, [ss_b[s]] + CONST, [ss_b[s]])
                yield

            def stageB(ti):
                t = tiles[ti]
                c0, n = t["c0"], t["n"]
                p2 = ti % 2
                hpb = hp_b[s][ti]
                scb = sc_b[s][ti]
                nb = len(t["blks"])
                j0 = t["blks"][0]
                for jj in range(nb):
                    sl = slice(jj * 128, (jj + 1) * 128)
                    SCAN(tb[p2][:, sl], ones_f[:], tsg[p2][:, sl], [tsg_b[p2]] + CONST, [tb_b[p2]])
                tb3 = tb[p2][:, 0:n].rearrange("p (j t) -> p j t", t=128)
                blast = tb3[:, :, 127:128].rearrange("p j o -> p (j o)")
                TS(rr[s][:, j0:j0 + nb], blast, 0.5, None, ALU.mult, None, [tb_b[p2]], [scb])
                TT(tb3, tb3, rr[s][:, j0:j0 + nb].rearrange("p (j o) -> p j o", o=1).to_broadcast([128, nb, 128]), ALU.subtract,
                   [tb_b[p2], scb], [tb_b[p2]])
                yield
                ACT(hd[s][:, j0:j0 + nb], rr[s][:, j0:j0 + nb], AF.Exp, [scb], [scb])
                ACT(dec[s][:, j0:j0 + nb], rr[s][:, j0:j0 + nb], AF.Exp, [scb], [scb], scale=2.0)
                if full:
                    ACT(tE1[p2][:, 0:n], tb[p2][:, 0:n], AF.Exp, [tb_b[p2]], [tE1_b[p2]])
                ACT(tE2[p2][:, 0:n], tb[p2][:, 0:n], AF.Exp, [tb_b[p2]], [tE2_b[p2]], scale=-1.0)
                yield
                TT(kd[s][:, c0:c0 + n], tk[p2][:, 0:n], tE2[p2][:, 0:n], ALU.mult, [tk_b[p2], tE2_b[p2]], [hpb])
                if full:
                    TT(qd[s][:, c0:c0 + n], tq[p2][:, 0:n], tE1[p2][:, 0:n], ALU.mult, [tq_b[p2], tE1_b[p2]], [hpb])
                for jj, j in enumerate(t["blks"]):
                    MM(banks[4][:, jj * 128:(jj + 1) * 128], kd[s][:, j * 128:(j + 1) * 128], ident_b[:], True, True, [hpb] + CONST, [bq[4][jj]])
                TCOPY(kdtm[s][:, j0:j0 + nb, :], banks[4][:, 0:n].rearrange("p (j t) -> p j t", t=128), bqs(4, 0, nb), [hpb])
                yield

            nt = len(tiles)
            for r in range(nt + 1):
                gens = []
                if r >= 1 and not tiles[r - 1]["samp"]:
                    gens.append(stageB(r - 1))
                if r < nt:
                    gens.append(stageA(r))
                while gens:
                    for g in list(gens):
                        try:
                            next(g)
                            yield
                        except StopIteration:
                            gens.remove(g)

        def dphase(h, s, ti, c0, n):
            k = ti % 2
            oc = t1[0]
            TCOPY(oc[:, 0:n], banks[6][:, 0:n], bqs(6), [t1_b[0]])
            TT(osq[k][:, 0:n], oc[:, 0:n], oc[:, 0:n], ALU.mult, [t1_b[0]], [osq_b[k]])
            MM(banks[6][:, 0:n], ones_b[:], osq[k][:, 0:n], True, True, [osq_b[k]] + CONST, bqs(6))
            ACT(lnv0[:, 0:n], banks[6][:, 0:n], AF.Ln, bqs(6) + CONST, [r0_b], bias=epsc[:, 0:1], scale=1.0 / 128)
            ACT(rstd0[:, 0:n], lnv0[:, 0:n], AF.Exp, [r0_b], [r0_b], scale=-0.5)
            TT(oc[:, 0:n], oc[:, 0:n], rstd0[:, 0:n], ALU.mult, [t1_b[0], r0_b], [t1_b[0]])
            STT(ogT[:, h, c0:c0 + n], oc[:, 0:n], gcol(V_ON, h), zs[s][:, c0:c0 + n], ALU.mult, ALU.mult,
                [t1_b[0], hp_b[s][ti]] + CONST, [og_b[h][ti]])

        def head_scan(h):
            s = h % 2
            Sh = Sst[:, h, :]
            pending = []

            def flush():
                for fn in pending:
                    fn()
                del pending[:]

            blocks = [(ti, jj, j) for ti, t in enumerate(tiles) if not t["samp"] for jj, j in enumerate(t["blks"])]
            slots_ = {}
            NPF = 2

            def load_sin(g4):
                g = sin_ctr[0] % NSIN
                sin_ctr[0] += 1
                slots_[g4] = g
                P.dma("sp", [(sin[g][:], st_in[g4 * 4:g4 * 4 + 4, h].rearrange("s k v -> k s v"))], writes=[sin_b[g]], dbuf=sin_b[g])

            if has_s:
                for g4 in range(NPF):
                    load_sin(g4)

            def pe_front(bi):
                ti, jj, j = blocks[bi]
                a = j % 2
                blk = slice(j * 128, (j + 1) * 128)
                MM(banks[7][:, a * 128:(a + 1) * 128], kdtm[s][:, j, :], Vtm[s][:, j, :], True, True, [hp_b[s][ti]], [bq[7][a]])
                if full:
                    MM(banks[5][:, a * 128:(a + 1) * 128], kd[s][:, blk], qd[s][:, blk], True, True, [hp_b[s][ti]], [bq[5][a]])

            pe_front(0)
            for bi, (ti, jj, j) in enumerate(blocks):
                t = tiles[ti]
                c0, n = t["c0"], t["n"]
                nb = len(t["blks"])
                hpr = [hp_b[s][ti]]
                scr = [sc_b[s][ti]]
                a = j % 2
                blk = slice(j * 128, (j + 1) * 128)
                Uq = banks[7][:, a * 128:(a + 1) * 128]
                Aq = banks[5][:, a * 128:(a + 1) * 128]
                if bi + 1 < len(blocks):
                    pe_front(bi + 1)
                if full:
                    TS(Sp[a][:], Sh, hd[s][:, j:j + 1], None, ALU.mult, None, [S_b[h]] + scr, [Sp_b[a]])
                TS(Sd[a][:], Uq, hd[s][:, j:j + 1], None, ALU.mult, None, [bq[7][a]] + scr, [Sd_b[a]])
                if full:
                    CPRED(ATs[a][:], mtri[:], Aq, [bq[5][a], ATs_b[a]] + CONST, [ATs_b[a]])
                STT(Sh, Sh, dec[s][:, j:j + 1], Sd[a][:], ALU.mult, ALU.add, [S_b[h], Sd_b[a]] + scr, [S_b[h]])
                flush()
                if full:
                    def cons(a=a, j=j, jj=jj, blk=blk, hpr=hpr):
                        oq = banks[6][:, jj * 128:(jj + 1) * 128]
                        MM(oq, Sp[a][:], qd[s][:, blk], True, False, [Sp_b[a]] + hpr, [bq[6][jj]])
                        MM(oq, Vtm[s][:, j, :], ATs[a][:], False, True, [ATs_b[a]] + hpr, [bq[6][jj]])
                    pending.append(cons)
                    if jj == nb - 1:
                        pending.append(lambda ti=ti, c0=c0, n=n: dphase(h, s, ti, c0, n))
                yield
            flush()
            yield
            if has_s:
                P.dma("sp", [(stp_d[h], Sh)], reads=[S_b[h]], dbuf=stp_cb, is_out=True)
                ti = len(tiles) - 1
                c0 = tiles[ti]["c0"]
                for g4 in range(4):
                    s0 = g4 * 4
                    bk = 4 + g4 % 2
                    g = slots_[g4]
                    MM(banks[bk][:, :], kstm[s][:], vblk[s][:, s0:s0 + 4, :], True, True, [ss_b[s]], bqs(bk))
                    for si in range(4):
                        sidx = s0 + si
                        STT(sin[g][:, si, :], sin[g][:, si, :], fss[s][:, sidx:sidx + 1], banks[bk][:, si * 128:(si + 1) * 128], ALU.mult, ALU.add,
                            [sin_b[g], ss_b[s], bq[bk][si]], [sin_b[g]])
                        MM(banks[6][:, sidx:sidx + 1], sin[g][:, si, :], qss[s][:, sidx:sidx + 1], True, True, [sin_b[g], ss_b[s]], [bq[6][0]])
                    P.dma("sp", [(sts_d[s0:s0 + 4, h].rearrange("s k v -> k s v"), sin[g][:])], reads=[sin_b[g]], dbuf=sin_b[g], is_out=True)
                    if g4 + NPF < 4:
                        load_sin(g4 + NPF)
                    yield
                dphase(h, s, ti, c0, NS)
                yield

        def drive(gens):
            RATIO = int(os.environ.get("KB_RATIO", "2"))
            gens = [[g, (RATIO if i == 0 else 1)] for i, g in enumerate(gens) if g is not None]
            while gens:
                for it in list(gens):
                    for _ in range(it[1]):
                        try:
                            next(it[0])
                        except StopIteration:
                            gens.remove(it)
                            break

        import os
        NH = int(os.environ.get("KB_NH", "16"))
        load_w(0)
        prev_scan = None
        for h in range(NH):
            if h + 1 < NH:
                load_w(h + 1)
            drive([head_proj(h), prev_scan])
            prev_scan = head_scan(h)
        drive([prev_scan])

    def stage_l0u(sb, ph):
        kind = sb["kind"]
        full = kind == "M"
        tiles = sb_tiles(sb)
        has_s = sb["ns"] > 0
        NU = 3
        wsl = [[P.sbuf("w0_%d_%d" % (s, k), [128, 8, 128], BF16, ph) for k in range(4)] for s in range(NU)]
        wsl_b = [[Buf("w0") for k in range(4)] for s in range(NU)]
        qd = [P.sbuf("qd%d" % s, [128, 512], BF16, ph) for s in range(NU)] if full else None
        kd = [P.sbuf("kd%d" % s, [128, 512], BF16, ph) for s in range(NU)]
        zs = [P.sbuf("zs%d" % s, [128, 512], BF16, ph) for s in range(NU)] if full else None
        Vtm = [P.sbuf("Vtm%d" % s, [128, 4, 128], BF16, ph) for s in range(NU)]
        kdtm = [P.sbuf("kdtm%d" % s, [128, 4, 128], BF16, ph) for s in range(NU)]
        hd = [P.sbuf("hd%d" % s, [128, 4], F32, ph) for s in range(NU)]
        dec = [P.sbuf("dec%d" % s, [128, 4], F32, ph) for s in range(NU)]
        rr = [P.sbuf("rr%d" % s, [128, 4], F32, ph) for s in range(NU)]
        hp_b = [Buf("hp%d" % s) for s in range(NU)]
        sc_b = [Buf("sc%d" % s) for s in range(NU)]
        tq = [P.sbuf("tq%d" % s, [128, 512], F32, ph) for s in range(2)] if full else None
        tsg = [P.sbuf("tsg%d" % s, [128, 512], F32, ph) for s in range(2)]
        tk = [P.sbuf("tk%d" % s, [128, 512], F32, ph) for s in range(2)]
        tb = [P.sbuf("tb%d" % s, [128, 512], F32, ph) for s in range(2)]
        tE1 = [P.sbuf("tE1%d" % s, [128, 512], BF16, ph) for s in range(2)] if full else None
        tE2 = [P.sbuf("tE2%d" % s, [128, 512], BF16, ph) for s in range(2)]
        tq_b = [Buf("tq") for _ in range(2)]
        tsg_b = [Buf("tsg") for _ in range(2)]
        tk_b = [Buf("tk") for _ in range(2)]
        tb_b = [Buf("tb") for _ in range(2)]
        tE1_b = [Buf("tE1") for _ in range(2)]
        tE2_b = [Buf("tE2") for _ in range(2)]
        ATs_b = [Buf("ATs") for _ in range(2)]
        Sp_b = [Buf("Sp") for _ in range(2)]
        Sd = [P.sbuf("Sd%d" % s, [128, 128], F32, ph) for s in range(2)]
        Sd_b = [Buf("Sd") for _ in range(2)]
        if full:
            ATs = [P.sbuf("ATs%d" % s, [128, 128], BF16, ph) for s in range(2)]
            Sp = [P.sbuf("Sp%d" % s, [128, 128], BF16, ph) for s in range(2)]
            osq = P.sbuf("osq0", [128, 512], BF16, ph)
            osq_b = Buf("osq")
            lnv0 = P.sbuf("lnv0", [128, 512], F32, ph)
            rstd0 = P.sbuf("rstd0", [128, 512], F32, ph)
            r0_b = Buf("r0")
            oc = P.sbuf("oc", [128, 512], F32, ph)
            oc_b = Buf("oc")
            for s in range(2):
                MEMSET(ATs[s][:], 0.0, [ATs_b[s]])
        if has_s:
            qss = [P.sbuf("qss%d" % s, [128, NS], F32, ph) for s in range(NU)]
            fss = [P.sbuf("fss%d" % s, [128, NS], F32, ph) for s in range(NU)]
            kstm = [P.sbuf("kstm%d" % s, [NS, 128], BF16, ph) for s in range(NU)]
            vstm = [P.sbuf("vstm%d" % s, [NS, 128], F32, ph) for s in range(NU)]
            ss_b = [Buf("ss") for _ in range(NU)]
            vblk = P.sbuf("vblk", [NS, NS, 128], BF16, ph)
            vblk_b = Buf("vblk")
            NSIN = 3
            sin = [P.sbuf("sin%d" % s, [128, 4, 128], F32, ph) for s in range(NSIN)]
            sin_b = [Buf("sin") for _ in range(NSIN)]
            sin_ctr = [0]
        if not full:
            ones512 = P.sbuf("ones512", [128, 512], F32, ph)
            ones512_b = Buf("ones512")
            MEMSET(ones512[:], 1.0, [ones512_b])
        print("  l0u scratch remaining:", nc.sbuf_bytes_remaining)

        units = [(ti, h) for ti in range(len(tiles)) for h in range(16)]
        NUN = len(units)

        def load_w(ui):
            ti, h = units[ui]
            s = ui % NU
            cols = [h * 128, 2048 + h * 128, 4096 + h * 128, 6144 + h * 128]
            for k in range(4):
                if not full and k in (0, 3):
                    continue
                P.dma("pool", [(wsl[s][k][:], w_in_a[:, cols[k]:cols[k] + 128].rearrange("(kc p) n -> p kc n", p=128))],
                      writes=[wsl_b[s][k]], dbuf=wsl_b[s][k])

        sin_slots = {}

        def load_sin(h, g4):
            g = sin_ctr[0] % NSIN
            sin_ctr[0] += 1
            sin_slots[(h, g4)] = g
            P.dma("sp", [(sin[g][:], st_in[g4 * 4:g4 * 4 + 4, h].rearrange("s k v -> k s v"))], writes=[sin_b[g]], dbuf=sin_b[g])

        def stageA(ui):
            ti, h = units[ui]
            s = ui % NU
            p2 = ui % 2
            t = tiles[ti]
            c0, n = t["c0"], t["n"]
            nb = len(t["blks"])
            wq, wf, wi, wz = wsl[s]
            wq_b, wf_b, wi_b, wz_b = wsl_b[s]
            lnomc = lnomlv[:, h:h + 1]
            xr = [xnT_b[ti]]
            hpb = hp_b[s]
            for kc in range(8):
                MM(banks[1][:, 0:n], wf[:, kc, :], xnT[:, kc, c0:c0 + n], kc == 0, kc == 7, [wf_b] + xr, bqs(1))
            yield
            ACT(tsg[p2][:, 0:n], banks[1][:, 0:n], AF.Exp, bqs(1), [tsg_b[p2]])
            ACT(tsg[p2][:, 0:n], tsg[p2][:, 0:n], AF.Ln, [tsg_b[p2]] + CONST, [tsg_b[p2]], bias=onec[:, 0:1], scale=1.0)
            ACT(tk[p2][:, 0:n], tsg[p2][:, 0:n], AF.Exp, [tsg_b[p2]] + CONST, [tk_b[p2]], bias=lnomc, scale=-1.0)
            if not t["samp"]:
                ACT(tsg[p2][:, 0:n], tk[p2][:, 0:n], AF.Ln, [tk_b[p2]] + CONST, [tsg_b[p2]], bias=onec[:, 0:1], scale=-1.0)
                for jj, j in enumerate(t["blks"]):
                    for kc in range(8):
                        MM(banks[3][:, jj * 128:(jj + 1) * 128], xnT[:, kc, j * 128:(j + 1) * 128], wi[:, kc, :], kc == 0, kc == 7,
                           [wi_b] + xr, [bq[3][jj]])
                    if jj % 2 == 1 and jj + 1 < nb:
                        yield
                TCOPY(Vtm[s][:, 0:nb, :], banks[3][:, 0:n].rearrange("p (j t) -> p j t", t=128), bqs(3, 0, nb), [hpb])
            else:
                for kc in range(8):
                    MM(banks[3][0:NS, 0:128], xnT[:, kc, c0:c0 + NS], wi[:, kc, :], kc == 0, kc == 7, [wi_b] + xr, [bq[3][0]])
            yield
            if full:
                for kc in range(8):
                    MM(banks[0][:, 0:n], wq[:, kc, :], xnT[:, kc, c0:c0 + n], kc == 0, kc == 7, [wq_b] + xr, bqs(0))
                for kc in range(8):
                    MM(banks[2][:, 0:n], wz[:, kc, :], xnT[:, kc, c0:c0 + n], kc == 0, kc == 7, [wz_b] + xr, bqs(2))
                yield
                if not t["samp"]:
                    ACT(tq[p2][:, 0:n], banks[0][:, 0:n], AF.Silu, bqs(0), [tq_b[p2]])
                else:
                    ACT(qss[s][:, 0:n], banks[0][:, 0:n], AF.Silu, bqs(0), [ss_b[s]])
                ACT(zs[s][:, 0:n], banks[2][:, 0:n], AF.Silu, bqs(2), [hpb])
            if t["samp"]:
                TS(fss[s][:, 0:n], tk[p2][:, 0:n], -1.0, 1.0, ALU.mult, ALU.add, [tk_b[p2]], [ss_b[s]])
                TCOPY(kd[s][:, 0:n], tk[p2][:, 0:n], [tk_b[p2]], [hpb])
                MM(banks[4][0:NS, 0:128], kd[s][:, 0:n], ident_b[:], True, True, [hpb] + CONST, [bq[4][0]])
                ACOPY(kstm[s][:], banks[4][0:NS, 0:128], [bq[4][0]], [ss_b[s]])
                ACOPY(vstm[s][:], banks[3][0:NS, 0:128], [bq[3][0]], [ss_b[s]])
            yield

        def stageB(ui):
            ti, h = units[ui]
            s = ui % NU
            p2 = ui % 2
            t = tiles[ti]
            if t["samp"]:
                return
            n = t["n"]
            nb = len(t["blks"])
            hpb = hp_b[s]
            scb = sc_b[s]
            if not full:
                SCAN(tb[p2][:, 0:n], ones512[:, 0:n], tsg[p2][:, 0:n], [tsg_b[p2], ones512_b], [tb_b[p2]])
                yield
                bend = tb[p2][:, n - 1:n]
                ACT(dec[s][:, 0:1], bend, AF.Exp, [tb_b[p2]], [scb])
                ACT(tE2[p2][:, 0:n], tb[p2][:, 0:n], AF.Exp, [tb_b[p2]], [tE2_b[p2]], bias=bend, scale=-1.0)
                yield
                TT(kd[s][:, 0:n], tk[p2][:, 0:n], tE2[p2][:, 0:n], ALU.mult, [tk_b[p2], tE2_b[p2]], [hpb])
                for jj in range(nb):
                    MM(banks[4][:, jj * 128:(jj + 1) * 128], kd[s][:, jj * 128:(jj + 1) * 128], ident_b[:], True, True, [hpb] + CONST, [bq[4][jj]])
                TCOPY(kdtm[s][:, 0:nb, :], banks[4][:, 0:n].rearrange("p (j t) -> p j t", t=128), bqs(4, 0, nb), [hpb])
                yield
                return
            for jj in range(nb):
                sl = slice(jj * 128, (jj + 1) * 128)
                SCAN(tb[p2][:, sl], ones_f[:], tsg[p2][:, sl], [tsg_b[p2]] + CONST, [tb_b[p2]])
            tb3 = tb[p2][:, 0:n].rearrange("p (j t) -> p j t", t=128)
            blast = tb3[:, :, 127:128].rearrange("p j o -> p (j o)")
            TS(rr[s][:, 0:nb], blast, 0.5, None, ALU.mult, None, [tb_b[p2]], [scb])
            TT(tb3, tb3, rr[s][:, 0:nb].rearrange("p (j o) -> p j o", o=1).to_broadcast([128, nb, 128]), ALU.subtract,
               [tb_b[p2], scb], [tb_b[p2]])
            yield
            ACT(hd[s][:, 0:nb], rr[s][:, 0:nb], AF.Exp, [scb], [scb])
            ACT(dec[s][:, 0:nb], rr[s][:, 0:nb], AF.Exp, [scb], [scb], scale=2.0)
            if full:
                ACT(tE1[p2][:, 0:n], tb[p2][:, 0:n], AF.Exp, [tb_b[p2]], [tE1_b[p2]])
            ACT(tE2[p2][:, 0:n], tb[p2][:, 0:n], AF.Exp, [tb_b[p2]], [tE2_b[p2]], scale=-1.0)
            yield
            TT(kd[s][:, 0:n], tk[p2][:, 0:n], tE2[p2][:, 0:n], ALU.mult, [tk_b[p2], tE2_b[p2]], [hpb])
            if full:
                TT(qd[s][:, 0:n], tq[p2][:, 0:n], tE1[p2][:, 0:n], ALU.mult, [tq_b[p2], tE1_b[p2]], [hpb])
            for jj in range(nb):
                MM(banks[4][:, jj * 128:(jj + 1) * 128], kd[s][:, jj * 128:(jj + 1) * 128], ident_b[:], True, True, [hpb] + CONST, [bq[4][jj]])
            TCOPY(kdtm[s][:, 0:nb, :], banks[4][:, 0:n].rearrange("p (j t) -> p j t", t=128), bqs(4, 0, nb), [hpb])
            yield

        def dphase(h, s, c0, n):
            ACT(oc[:, 0:n], banks[6][:, 0:n], AF.Identity, bqs(6), [oc_b])
            ACT(osq[:, 0:n], oc[:, 0:n], AF.Square, [oc_b], [osq_b])
            MM(banks[6][:, 0:n], ones_b[:], osq[:, 0:n], True, True, [osq_b] + CONST, bqs(6))
            ACT(lnv0[:, 0:n], banks[6][:, 0:n], AF.Ln, bqs(6) + CONST, [r0_b], bias=epsc[:, 0:1], scale=1.0 / 128)
            ACT(rstd0[:, 0:n], lnv0[:, 0:n], AF.Exp, [r0_b], [r0_b], scale=-0.5)
            TT(oc[:, 0:n], oc[:, 0:n], rstd0[:, 0:n], ALU.mult, [oc_b, r0_b], [oc_b])
            STT(ogT[:, h, c0:c0 + n], oc[:, 0:n], gcol(V_ON, h), zs[s][:, 0:n], ALU.mult, ALU.mult,
                [oc_b, hp_b[s]] + CONST, [og_b[h][0], og_b[h][1], og_b[h][2]])

        def scan(ui):
            ti, h = units[ui]
            s = ui % NU
            t = tiles[ti]
            c0, n = t["c0"], t["n"]
            nb = len(t["blks"])
            Sh = Sst[:, h, :]
            hpr = [hp_b[s]]
            scr = [sc_b[s]]
            if not t["samp"] and not full:
                for jj in range(nb):
                    MM(banks[7][:, 0:128], kdtm[s][:, jj, :], Vtm[s][:, jj, :], jj == 0, jj == nb - 1, hpr, [bq[7][0]])
                yield
                STT(Sh, Sh, dec[s][:, 0:1], banks[7][:, 0:128], ALU.mult, ALU.add, [S_b[h], bq[7][0]] + scr, [S_b[h]])
                yield
                return
            if not t["samp"]:
                def pe_front(jj):
                    a = jj % 2
                    blk = slice(jj * 128, (jj + 1) * 128)
                    MM(banks[7][:, a * 128:(a + 1) * 128], kdtm[s][:, jj, :], Vtm[s][:, jj, :], True, True, hpr, [bq[7][a]])
                    if full:
                        MM(banks[5][:, a * 128:(a + 1) * 128], kd[s][:, blk], qd[s][:, blk], True, True, hpr, [bq[5][a]])
                pend = []
                pe_front(0)
                for jj in range(nb):
                    a = jj % 2
                    blk = slice(jj * 128, (jj + 1) * 128)
                    Uq = banks[7][:, a * 128:(a + 1) * 128]
                    Aq = banks[5][:, a * 128:(a + 1) * 128]
                    if jj + 1 < nb:
                        pe_front(jj + 1)
                    if full:
                        ACT(Sp[a][:], Sh, AF.Identity, [S_b[h]] + scr, [Sp_b[a]], scale=hd[s][:, jj:jj + 1])
                    ACT(Sd[a][:], Uq, AF.Identity, [bq[7][a]] + scr, [Sd_b[a]], scale=hd[s][:, jj:jj + 1])
                    if full:
                        CPRED(ATs[a][:], mtri[:], Aq, [bq[5][a], ATs_b[a]] + CONST, [ATs_b[a]])
                    STT(Sh, Sh, dec[s][:, jj:jj + 1], Sd[a][:], ALU.mult, ALU.add, [S_b[h], Sd_b[a]] + scr, [S_b[h]])
                    for fn in pend:
                        fn()
                    del pend[:]
                    if full:
                        def cons(a=a, jj=jj, blk=blk):
                            oq = banks[6][:, jj * 128:(jj + 1) * 128]
                            MM(oq, Sp[a][:], qd[s][:, blk], True, False, [Sp_b[a]] + hpr, [bq[6][jj]])
                            MM(oq, Vtm[s][:, jj, :], ATs[a][:], False, True, [ATs_b[a]] + hpr, [bq[6][jj]])
                        pend.append(cons)
                    yield
                for fn in pend:
                    fn()
                if full:
                    dphase(h, s, c0, n)
                yield
                return
            P.dma("sp", [(stp_d[h], Sh)], reads=[S_b[h]], dbuf=stp_cb, is_out=True)
            for g4 in range(2):
                load_sin(h, g4)
            for sp_ in range(NS):
                TS(vblk[:, sp_, :], vstm[s][:], ident_f[0:NS, sp_:sp_ + 1], None, ALU.mult, None, [ss_b[s]] + CONST, [vblk_b])
            yield
            for g4 in range(4):
                s0 = g4 * 4
                bk = 5 if g4 % 2 == 0 else 7
                g = sin_slots[(h, g4)]
                MM(banks[bk][:, :], kstm[s][:], vblk[:, s0:s0 + 4, :], True, True, [ss_b[s], vblk_b], bqs(bk))
                for si in range(4):
                    sidx = s0 + si
                    STT(sin[g][:, si, :], sin[g][:, si, :], fss[s][:, sidx:sidx + 1], banks[bk][:, si * 128:(si + 1) * 128], ALU.mult, ALU.add,
                        [sin_b[g], ss_b[s], bq[bk][si]], [sin_b[g]])
                    MM(banks[6][:, sidx:sidx + 1], sin[g][:, si, :], qss[s][:, sidx:sidx + 1], True, True, [sin_b[g], ss_b[s]], [bq[6][0]])
                P.dma("sp", [(sts_d[s0:s0 + 4, h].rearrange("s k v -> k s v"), sin[g][:])], reads=[sin_b[g]], dbuf=sin_b[g], is_out=True)
                if g4 + 2 < 4:
                    load_sin(h, g4 + 2)
                yield
            dphase(h, s, c0, NS)
            yield

        load_w(0)
        if NUN > 1:
            load_w(1)
        for idx in range(NUN + 2):
            if idx + 2 < NUN:
                load_w(idx + 2)
            gens = []
            if 0 <= idx - 1 < NUN:
                gens.append(stageB(idx - 1))
            if idx < NUN:
                gens.append(stageA(idx))
            if 0 <= idx - 2 < NUN:
                gens.append(scan(idx - 2))
            while gens:
                for g in list(gens):
                    try:
                        next(g)
                    except StopIteration:
                        gens.remove(g)

    def stage_out(sb, ph, l, w_out):
        tiles = sb_tiles(sb)
        mix = P.sbuf("mix", [128, 8, SBW], F32, ph)
        mix_b = [[Buf("mix") for _ in range(MAXT)] for _ in range(8)]
        wo = [P.sbuf("wo%d" % i, [128, 16, 128], BF16, ph) for i in range(2)]
        wo_b = [Buf("wo") for _ in range(2)]
        rt = alloc_rms(ph, "po")
        tmp = [P.sbuf("tmpo%d" % i, [128, 512], F32, ph) for i in range(2)]
        tmp_b = [Buf("tmpo") for _ in range(2)]
        wg = [P.sbuf("wg%d" % i, [128, 8, 128], BF16, ph) for i in range(2)]
        wg_b = [Buf("wg") for _ in range(2)]
        wp = [P.sbuf("wp%d" % i, [128, 2, 128], BF16, ph) for i in range(2)]
        sgt = [P.sbuf("sgt%d" % i, [128, 512], F32, ph) for i in range(2)]
        sgt_b = [Buf("sgt") for _ in range(2)]
        print("  out scratch remaining:", nc.sbuf_bytes_remaining)

        def ldo(dc):
            P.dma("pool", [(wo[dc % 2][:], w_out[:, dc * 128:(dc + 1) * 128].rearrange("(cc p) n -> p cc n", p=128))],
                  writes=[wo_b[dc % 2]], dbuf=wo_b[dc % 2])
        ldo(0)
        k = 0
        for dc in range(8):
            if dc + 1 < 8:
                ldo(dc + 1)
            for ti, t in enumerate(tiles):
                c0, n = t["c0"], t["n"]
                bk = k % 3
                k += 1
                for cc in range(16):
                    MM(banks[bk][:, 0:n], wo[dc % 2][:, cc, :], ogT[:, cc, c0:c0 + n], cc == 0, cc == 15, [wo_b[dc % 2], og_b[cc][ti]], bqs(bk))
                ACOPY(mix[:, dc, c0:c0 + n], banks[bk][:, 0:n], bqs(bk), [mix_b[dc][ti]])
        gb = V_POST0 if l == 0 else V_POST1
        for ti, t in enumerate(tiles):
            c0, n = t["c0"], t["n"]
            rstd = rms_stats([mix[:, c, c0:c0 + n] for c in range(8)], n, rt, [mix_b[c][ti] for c in range(8)], 1.0 / D)
            for c in range(8):
                a = c % 2
                STT(tmp[a][:, 0:n], mix[:, c, c0:c0 + n], gcol(gb, c), rstd[:, 0:n], ALU.mult, ALU.mult, [mix_b[c][ti], rt[4]] + CONST, [tmp_b[a]])
                TT(hT[:, c, c0:c0 + n], hT[:, c, c0:c0 + n], tmp[a][:, 0:n], ALU.add, [tmp_b[a], hT_b[ti]], [hT_b[ti]])
                ACOPY(xnT[:, c, c0:c0 + n], hT[:, c, c0:c0 + n], [hT_b[ti]], [xnT_b[ti]])

        def ldg(dc):
            P.dma("pool", [(wg[dc % 2][:], w_pg[l][:, dc * 128:(dc + 1) * 128].rearrange("(kc p) n -> p kc n", p=128)),
                           (wp[dc % 2][:], w_pe[l][:, dc * 128:(dc + 1) * 128].rearrange("(kc p) n -> p kc n", p=128))],
                  writes=[wg_b[dc % 2]], dbuf=wg_b[dc % 2])
        ldg(0)
        k = 0
        for dc in range(8):
            if dc + 1 < 8:
                ldg(dc + 1)
            for ti, t in enumerate(tiles):
                c0, n = t["c0"], t["n"]
                a = k % 2
                bg = a
                bp = 2 + a
                k += 1
                for kc in range(8):
                    MM(banks[bg][:, 0:n], wg[dc % 2][:, kc, :], xnT[:, kc, c0:c0 + n], kc == 0, kc == 7, [wg_b[dc % 2], xnT_b[ti]], bqs(bg))
                for pc in range(2):
                    MM(banks[bp][:, 0:n], wp[dc % 2][:, pc, :], pT[:, l, pc, c0:c0 + n], pc == 0, pc == 1, [wg_b[dc % 2], pT_b[ti]], bqs(bp))
                ACT(sgt[a][:, 0:n], banks[bg][:, 0:n], AF.Sigmoid, bqs(bg), [sgt_b[a]])
                TT(tmp[a][:, 0:n], banks[bp][:, 0:n], sgt[a][:, 0:n], ALU.mult, bqs(bp) + [sgt_b[a]], [tmp_b[a]])
                TT(hT[:, dc, c0:c0 + n], hT[:, dc, c0:c0 + n], tmp[a][:, 0:n], ALU.add, [tmp_b[a], hT_b[ti]], [hT_b[ti]])

    def stage_l1(sb, sbi, ph):
        tiles = sb_tiles(sb)
        nblk = sb["nblk"]
        has_s = sb["ns"] > 0
        first = sb["t0"] == 0
        NT = nblk * 128 + sb["ns"]
        kT = P.sbuf("kT", [128, 4, 128 + SBW], BF16, ph)
        Vt = P.sbuf("Vt", [128, 10, 256], BF16, ph)
        rC = P.sbuf("rC", [128, SBW], BF16, ph)
        rS = P.sbuf("rS", [128, SBW], BF16, ph)
        kT_b = [Buf("kT%d" % i) for i in range(MAXT + 1)]
        Vt_b = [Buf("Vt%d" % i) for i in range(10)]
        rope_b = Buf("rope")
        rf = P.sbuf("rf", [128, 512], F32, ph)
        rb = P.sbuf("rb", [128, 512], BF16, ph)
        rt1 = P.sbuf("rt1", [128, 512], F32, ph)
        rt2 = P.sbuf("rt2", [128, 512], F32, ph)
        rf_b, rb_b, rt1_b, rt2_b = Buf("rf"), Buf("rb"), Buf("rt1"), Buf("rt2")
        if has_s:
            kfl = P.sbuf("kfl", [128, 4, 128], F32, ph)
            ksf = P.sbuf("ksf", [128, 4, NS], F32, ph)
            kfl_b = Buf("kfl")
            qsa = P.sbuf("qsa", [128, 16, NS], BF16, ph)
            zsa = P.sbuf("zsa", [128, 16, NS], BF16, ph)
            qsa_b = Buf("qsa")
            vsb = P.sbuf("vsb", [NS, 256], BF16, ph)
            ostg = P.sbuf("ostg", [128, 768], F32, ph)
            ostg_b = Buf("ostg")
        g0 = sb["t0"]
        pairs = [(rC[:, 0:nblk * 128], ropec_d[:, g0:g0 + nblk * 128]), (rS[:, 0:nblk * 128], ropes_d[:, g0:g0 + nblk * 128])]
        if has_s:
            pairs += [(rC[:, nblk * 128:NT], ropec_d[:, NMAIN:NMAIN + NS]), (rS[:, nblk * 128:NT], ropes_d[:, NMAIN:NMAIN + NS])]
        P.dma("pool", pairs, writes=[rope_b], dbuf=rope_b)
        ACOPY(kT[:, :, 0:128], kT_halo[:], [halo_b], [kT_b[0]])
        ACOPY(Vt[:, 0, :], V_halo[:], [halo_b], [Vt_b[0]])

        def rope(src_ps, src_bank, dst, c0, n, reads_extra, writes, f32copy=None):
            ACOPY(rf[:, 0:n], src_ps, [bankb[src_bank]], [rf_b])
            ACOPY(rb[:, 0:n], src_ps, [bankb[src_bank]], [rb_b])
            MM(banks[3][:, 0:n], prot[:], rb[:, 0:n], True, True, [rb_b] + CONST, [bankb[3]])
            TT(rt1[:, 0:n], rf[:, 0:n], rC[:, c0:c0 + n], ALU.mult, [rf_b, rope_b], [rt1_b])
            TT(rt2[:, 0:n], banks[3][:, 0:n], rS[:, c0:c0 + n], ALU.mult, [bankb[3], rope_b], [rt2_b])
            TT(dst, rt1[:, 0:n], rt2[:, 0:n], ALU.add, [rt1_b, rt2_b] + reads_extra, writes)
            if f32copy is not None:
                o_ap, lo, hi, wr = f32copy
                TT(o_ap, rt1[:, lo:hi], rt2[:, lo:hi], ALU.add, [rt1_b, rt2_b], wr)

        pa = ExitStack()
        kvn = P.sbuf("kvn", [128, 8, 512], BF16, pa)
        kvn_b = Buf("kvn")
        wk = P.sbuf("wk", [128, 8, 4, 128], BF16, pa)
        wv = P.sbuf("wv", [128, 8, 256], BF16, pa)
        wkv_b = Buf("wkv")
        rt = alloc_rms(pa, "l1")
        vlast = P.sbuf("vlast", [128, 256], F32, pa)
        vlast_b = Buf("vlast")
        print("  l1a scratch remaining:", nc.sbuf_bytes_remaining)
        wkpairs = []
        for kh_ in range(4):
            src_ = w_kv[:, kh_ * 64:(kh_ + 1) * 64].rearrange("(kc p) d -> p kc d", p=128)
            wkpairs += [(wk[:, :, kh_, 0:64], src_), (wk[:, :, kh_, 64:128], src_)]
        wkpairs.append((wv[:], w_kv[:, 256:512].rearrange("(kc p) n -> p kc n", p=128)))
        P.dma("pool", wkpairs, writes=[wkv_b], dbuf=wkv_b)
        for ti, t in enumerate(tiles):
            c0, n = t["c0"], t["n"]
            rstd = rms_stats([hT[:, c, c0:c0 + n] for c in range(8)], n, rt, [hT_b[ti]], 1.0 / D)
            for c in range(8):
                STT(xnT[:, c, c0:c0 + n], hT[:, c, c0:c0 + n], gcol(V_PRE1, c), rstd[:, 0:n], ALU.mult, ALU.mult,
                    [hT_b[ti], rt[4]] + CONST, [xnT_b[ti]])
                STT(kvn[:, c, 0:n], hT[:, c, c0:c0 + n], gcol(V_KV, c), rstd[:, 0:n], ALU.mult, ALU.mult,
                    [hT_b[ti], rt[4]] + CONST, [kvn_b])
            for kh in range(4):
                bk = kh % 2
                for kc in range(8):
                    MM(banks[bk][:, 0:n], wk[:, kc, kh, :], kvn[:, kc, 0:n], kc == 0, kc == 7, [wkv_b, kvn_b], [bankb[bk]])
                f32c = None
                if has_s and t["samp"]:
                    f32c = (ksf[:, kh, :], 0, NS, [kfl_b])
                elif has_s and (nblk - 1) in t["blks"]:
                    lo = (nblk - 1) * 128 - c0
                    f32c = (kfl[:, kh, :], lo, lo + 128, [kfl_b])
                rope(banks[bk][:, 0:n], bk, kT[:, kh, 128 + c0:128 + c0 + n], c0, n, [], [kT_b[1 + ti]], f32c)
            if not t["samp"]:
                for jj, j in enumerate(t["blks"]):
                    bk = 4 + jj % 2
                    for kc in range(8):
                        MM(banks[bk][:, 0:256], kvn[:, kc, jj * 128:(jj + 1) * 128], wv[:, kc, :], kc == 0, kc == 7, [wkv_b, kvn_b], [bankb[bk]])
                    ACOPY(Vt[:, 1 + j, :], banks[bk][:, 0:256], [bankb[bk]], [Vt_b[1 + j]])
                    if has_s and j == nblk - 1:
                        ACOPY(vlast[:], banks[bk][:, 0:256], [bankb[bk]], [vlast_b])
                        P.dma("sp", [(vp_d, vlast[:])], reads=[vlast_b], dbuf=vlast_b, is_out=True)
            else:
                for kc in range(8):
                    MM(banks[4][0:NS, 0:256], kvn[:, kc, 0:NS], wv[:, kc, :], kc == 0, kc == 7, [wkv_b, kvn_b], [bankb[4]])
                ACOPY(ostg[0:NS, 512:768], banks[4][0:NS, 0:256], [bankb[4]], [ostg_b])
        if has_s:
            for kh in range(4):
                MM(banks[5][:, kh * 128:(kh + 1) * 128], kfl[:, kh, :], ident_f[:], True, True, [kfl_b] + CONST, [bankb[5]])
            kp_st = P.sbuf("kp_st", [128, 256], F32, pa)
            kp_b = Buf("kp_st")
            ACOPY(kp_st[:].rearrange("p (kh d) -> p kh d", kh=4), banks[5][:, :].rearrange("p (kh e) -> p kh e", kh=4)[:, :, 0:64], [bankb[5]], [kp_b])
            P.dma("sp", [(kp_d, kp_st[:])], reads=[kp_b], dbuf=kp_b, is_out=True)
            for kh in range(4):
                MM(banks[6][0:NS, kh * 128:(kh + 1) * 128], ksf[:, kh, :], ident_f[:], True, True, [kfl_b] + CONST, [bankb[6]])
            ACOPY(ostg[0:NS, 0:256].rearrange("p (kh d) -> p kh d", kh=4), banks[6][0:NS, :].rearrange("p (kh e) -> p kh e", kh=4)[:, :, 0:64], [bankb[6]], [ostg_b])
            ksd_b = Buf("ksd")
            P.dma("sp", [(ks_d, ostg[0:NS, 0:256]), (vs_d, ostg[0:NS, 512:768])], reads=[ostg_b], writes=[ksd_b], dbuf=ostg_b, is_out=True)
        P.barrier()
        pa.close()
        if os.environ.get("KB_L1", "") == "a":
            return

        wq = P.sbuf("wq", [128, 8, 512], BF16, ph)
        wz = P.sbuf("wz", [128, 8, 512], BF16, ph)
        wq_b, wz_b = Buf("wq"), Buf("wz")
        qg = P.sbuf("qg", [128, 4, SBW], BF16, ph)
        zg = P.sbuf("zg", [128, 4, SBW], BF16, ph)
        qg_b = [Buf("qg%d" % i) for i in range(MAXT)]
        PT = [P.sbuf("PT%d" % i, [128, 2, 512], BF16, ph) for i in range(4)]
        PT_b = [Buf("PT") for _ in range(4)]
        rden = P.sbuf("rden", [128, 512], F32, ph)
        rden_b = Buf("rden")
        at = P.sbuf("at", [128, 512], F32, ph)
        at_b = Buf("at")
        if has_s:
            ckd = [P.sbuf("ckd%d" % i, [128, 4, 2, 64], BF16, ph) for i in range(2)]
            cvt = [P.sbuf("cvt%d" % i, [128, 256], BF16, ph) for i in range(2)]
            cc_b = [Buf("cc") for _ in range(2)]
            KcT = P.sbuf("KcT", [128, 512], BF16, ph)
            KcT_b = Buf("KcT")
            PTs = P.sbuf("PTs", [128, 32], BF16, ph)
            PTs_b = Buf("PTs")
        print("  l1b scratch remaining:", nc.sbuf_bytes_remaining)

        def ldq(kh):
            P.dma("pool", [(wq[:], w_in_b[:, kh * 512:(kh + 1) * 512].rearrange("(kc p) n -> p kc n", p=128))], writes=[wq_b], dbuf=wq_b)
            P.dma("pool", [(wz[:], w_in_b[:, 2048 + kh * 512:2048 + (kh + 1) * 512].rearrange("(kc p) n -> p kc n", p=128))], writes=[wz_b], dbuf=wz_b)

        ldq(0)
        blkcount = 0
        for kh in range(4):
            for ti, t in enumerate(tiles):
                c0, n = t["c0"], t["n"]
                for i in range(4):
                    bk = i % 2
                    for kc in range(8):
                        MM(banks[bk][:, 0:n], wq[:, kc, i * 128:(i + 1) * 128], xnT[:, kc, c0:c0 + n], kc == 0, kc == 7, [wq_b, xnT_b[ti]], [bankb[bk]])
                    f32c = None
                    rope(banks[bk][:, 0:n], bk, qg[:, i, c0:c0 + n], c0, n, [], [qg_b[ti]], None)
                    bz = 4 + i % 2
                    for kc in range(8):
                        MM(banks[bz][:, 0:n], wz[:, kc, i * 128:(i + 1) * 128], xnT[:, kc, c0:c0 + n], kc == 0, kc == 7, [wz_b, xnT_b[ti]], [bankb[bz]])
                    ACT(zg[:, i, c0:c0 + n], banks[bz][:, 0:n], AF.Silu, [bankb[bz]], [qg_b[ti]])
                if t["samp"]:
                    ACOPY(qsa[:, kh * 4:(kh + 1) * 4, :], qg[:, :, c0:c0 + NS], [qg_b[ti]], [qsa_b])
                    ACOPY(zsa[:, kh * 4:(kh + 1) * 4, :], zg[:, :, c0:c0 + NS], [qg_b[ti]], [qsa_b])
            if kh + 1 < 4:
                ldq(kh + 1)
            steps = [(ti, j) for ti, t in enumerate(tiles) for j in t["blks"] if not (first and j == 0)]

            def emit_scores(idx):
                ti, j = steps[idx]
                st_ = idx % 2
                pm_ = maskf4 if (first and j == 1) else maskp4
                prev_cols = slice(j * 128, (j + 1) * 128)
                cur_cols = slice((j + 1) * 128, (j + 2) * 128)
                blk = slice(j * 128, (j + 1) * 128)
                kprev_b = kT_b[0] if j == 0 else kT_b[1 + (j - 1) // 4]
                kcur_b = kT_b[1 + j // 4]
                for par in range(2):
                    pr = slice(par * 64, (par + 1) * 64)
                    b0, b1 = 2 * par, 2 * par + 1
                    pt = PT[st_ * 2 + par]
                    ptb = PT_b[st_ * 2 + par]
                    MM(banks[b0][:, :], kT[pr, kh, prev_cols], qg[pr, :, blk], True, True, [kprev_b, qg_b[ti]], [bankb[b0]])
                    MM(banks[b1][:, :], kT[pr, kh, cur_cols], qg[pr, :, blk], True, True, [kcur_b, qg_b[ti]], [bankb[b1]])
                    ACT(pt[:, 0, :], banks[b0][:, :], AF.Exp, [bankb[b0]], [ptb], scale=0.125)
                    ACT(pt[:, 1, :], banks[b1][:, :], AF.Exp, [bankb[b1]], [ptb], scale=0.125)
                    TT(pt[:, 0, :], pt[:, 0, :], pm_[:], ALU.mult, [ptb] + CONST, [ptb])
                    TT(pt[:, 1, :], pt[:, 1, :], maskc4[:], ALU.mult, [ptb] + CONST, [ptb])

            def emit_pv(idx):
                ti, j = steps[idx]
                st_ = idx % 2
                blk = slice(j * 128, (j + 1) * 128)
                bo = 4 + 2 * (idx % 2)
                bd = bo + 1
                for par in range(2):
                    pr = slice(par * 64, (par + 1) * 64)
                    pt = PT[st_ * 2 + par]
                    ptb = PT_b[st_ * 2 + par]
                    tp = (0, par * 64)
                    MM(banks[bo][pr, :], Vt[:, j, kh * 64:(kh + 1) * 64], pt[:, 0, :], True, False, [Vt_b[j], ptb], [bankb[bo]], tile_position=tp)
                    MM(banks[bo][pr, :], Vt[:, 1 + j, kh * 64:(kh + 1) * 64], pt[:, 1, :], False, True, [Vt_b[1 + j], ptb], [bankb[bo]], tile_position=tp)
                    MM(banks[bd][pr, :], ones_b[:, 0:64], pt[:, 0, :], True, False, [ptb] + CONST, [bankb[bd]], tile_position=tp)
                    MM(banks[bd][pr, :], ones_b[:, 0:64], pt[:, 1, :], False, False, [ptb] + CONST, [bankb[bd]], tile_position=tp)
                    MM(banks[bd][pr, :], ones_b[0:1, 0:64], sinkrow[0:1, kh, par, :, :].rearrange("p i q -> p (i q)"), False, True, CONST, [bankb[bd]], tile_position=tp)
                ACT(rden[:], banks[bd][:, :], AF.Ln, [bankb[bd]], [rden_b])
                ACT(rden[:], rden[:], AF.Exp, [rden_b], [rden_b], scale=-1.0)
                TT(at[:], banks[bo][:, :], rden[:], ALU.mult, [bankb[bo], rden_b], [at_b])
                TT(ogT[:, kh * 4:(kh + 1) * 4, blk], at[:].rearrange("p (i q) -> p i q", i=4), zg[:, :, blk], ALU.mult,
                   [at_b, qg_b[ti]], [og_b[kh * 4 + i][ti] for i in range(4)])

            for idx in range(len(steps) + 1):
                if idx < len(steps):
                    emit_scores(idx)
                if idx >= 1:
                    emit_pv(idx - 1)
        lastj = nblk - 1
        ACOPY(kT_halo[:], kT[:, :, 128 + lastj * 128:128 + (lastj + 1) * 128], [kT_b[1 + lastj // 4]], [halo_b])
        ACOPY(V_halo[:], Vt[:, 1 + lastj, :], [Vt_b[1 + lastj]], [halo_b])
        if has_s and os.environ.get("KB_L1", "") != "b":
            ti = len(tiles) - 1
            c0 = tiles[ti]["c0"]
            for s_ in range(NS):
                a = s_ % 2
                ksrc = ck[s_].rearrange("j (kh d) -> j kh d", kh=4)
                P.dma("pool", [(ckd[a][:, :, 0, :], ksrc), (ckd[a][:, :, 1, :], ksrc), (cvt[a][:], cv[s_])],
                      writes=[cc_b[a]], dbuf=cc_b[a])
                MM(banks[5][0:1, 0:256], ident_f[0:NS, s_:s_ + 1], ostg[0:NS, 0:256], True, True, [ostg_b] + CONST, [bankb[5]])
                MM(banks[5][0:1, 256:512], ident_f[0:NS, s_:s_ + 1], ostg[0:NS, 512:768], True, True, [ostg_b] + CONST, [bankb[5]])
                ACOPY(ckd[a][0:1, :, 0, :], banks[5][0:1, 0:256].rearrange("p (kh d) -> p kh d", kh=4), [bankb[5]], [cc_b[a]])
                ACOPY(ckd[a][0:1, :, 1, :], banks[5][0:1, 0:256].rearrange("p (kh d) -> p kh d", kh=4), [bankb[5]], [cc_b[a]])
                ACOPY(cvt[a][0:1, :], banks[5][0:1, 256:512], [bankb[5]], [cc_b[a]])
                for kh in range(4):
                    MM(banks[0][:, kh * 128:(kh + 1) * 128], ckd[a][:, kh, :, :].rearrange("p c d -> p (c d)"), ident_b[:], True, True, [cc_b[a]] + CONST, [bankb[0]])
                ACOPY(KcT[:], banks[0][:, :], [bankb[0]], [KcT_b])
                for par in range(2):
                    pr = slice(par * 64, (par + 1) * 64)
                    qb_ = 1 if par == 0 else 4
                    for kh in range(4):
                        MM(banks[qb_][:, kh * 4:(kh + 1) * 4], KcT[pr, kh * 128:(kh + 1) * 128], qsa[pr, kh * 4:(kh + 1) * 4, s_], True, True, [KcT_b, qsa_b], [bankb[qb_]])
                    ACT(PTs[:, par * 16:(par + 1) * 16], banks[qb_][:, 0:16], AF.Exp, [bankb[qb_]], [PTs_b], scale=0.125)
                for kh in range(4):
                    for par in range(2):
                        pr = slice(par * 64, (par + 1) * 64)
                        col = par * 16 + kh * 4
                        tp = (0, par * 64)
                        MM(banks[2][pr, kh * 4:(kh + 1) * 4], cvt[a][:, kh * 64:(kh + 1) * 64], PTs[:, col:col + 4], True, True, [cc_b[a], PTs_b], [bankb[2]], tile_position=tp)
                        MM(banks[3][pr, kh * 4:(kh + 1) * 4], ones_b[:, 0:64], PTs[:, col:col + 4], True, False, [PTs_b] + CONST, [bankb[3]], tile_position=tp)
                        MM(banks[3][pr, kh * 4:(kh + 1) * 4], ones_b[0:1, 0:64], sinkrow[0:1, kh, par, :, 0:1].rearrange("p i q -> p (i q)"), False, True, CONST, [bankb[3]], tile_position=tp)
                P.op("dve", (lambda o, i_: (lambda e: e.reciprocal(o, i_)))(rden[:, 0:16], banks[3][:, 0:16]), [bankb[3]], [rden_b])
                TT(at[:, 0:16], banks[2][:, 0:16], rden[:, 0:16], ALU.mult, [bankb[2], rden_b], [at_b])
                TT(ogT[:, :, c0 + s_], at[:, 0:16], zsa[:, :, s_], ALU.mult, [at_b, qsa_b], [og_b[h_][ti] for h_ in range(16)])

    def stage_y(sb, ph):
        tiles = sb_tiles(sb)
        first = sb["t0"] == 0
        yst = [P.sbuf("yst%d" % i, [128, D], F32, ph) for i in range(3)]
        yst_b = [Buf("yst") for _ in range(3)]
        k = 0
        for ti, t in enumerate(tiles):
            c0 = t["c0"]
            if t["samp"]:
                a = k % 3
                k += 1
                for half in range(2):
                    bk = half
                    for cc in range(4):
                        c = half * 4 + cc
                        MM(banks[bk][0:NS, cc * 128:(cc + 1) * 128], hT[:, c, c0:c0 + NS], ident_f[:], True, True, [hT_b[ti]] + CONST, [bankb[bk]])
                    ACOPY(yst[a][0:NS, half * 512:(half + 1) * 512], banks[bk][0:NS, :], [bankb[bk]], [yst_b[a]])
                P.dma("sp", [(y_d[2048:2048 + NS, :], yst[a][0:NS, :])], reads=[yst_b[a]], dbuf=yst_b[a], is_out=True)
                continue
            for j in t["blks"]:
                if first and j == 0:
                    continue
                a = k % 3
                k += 1
                for half in range(2):
                    bk = (2 * k + half) % 4
                    for cc in range(4):
                        c = half * 4 + cc
                        MM(banks[bk][:, cc * 128:(cc + 1) * 128], hT[:, c, j * 128:(j + 1) * 128], ident_f[:], True, True, [hT_b[ti]] + CONST, [bankb[bk]])
                    if half == 0:
                        ACOPY(yst[a][:, 0:512], banks[bk][:, :], [bankb[bk]], [yst_b[a]])
                    else:
                        TCOPY(yst[a][:, 512:1024], banks[bk][:, :], [bankb[bk]], [yst_b[a]])
                row = (j - 1) * 128 if first else (8 + j) * 128
                P.dma("sp", [(y_d[row:row + 128, :], yst[a][:])], reads=[yst_b[a]], dbuf=yst_b[a], is_out=True)

    def dump_h(sb):
        col0 = sb["t0"]
        for ti, t in enumerate(sb_tiles(sb)):
            c0, n = t["c0"], t["n"]
            g0 = (NMAIN if t["samp"] else col0 + c0)
            P.dma("sp", [(dbg_d[:, :, g0:g0 + n], hT[:, :, c0:c0 + n])], reads=[hT_b[ti]], dbuf=hT_b[ti], is_out=True)

    import os
    sel = os.environ.get("KB_SBS")
    for sbi, sb in enumerate(SBS):
        if sel is not None and str(sbi) not in sel.split(","):
            continue
        if sb["kind"] == "P" and stop_after in ("L0nopre", "LOAD"):
            continue
        ph = ExitStack()
        stage_load(sb, ph)
        stage_prenorm(sb, ph, V_PRE0, xnT, xnT_b)
        if sb["kind"] == "P":
            stage_l0u(sb, ph)
            P.barrier()
            ph.close()
            continue
        P.barrier()
        ph.close()
        if stop_after == "LOAD":
            dump_h(sb)
            P.barrier()
            continue
        ph = ExitStack()
        stage_l0(sb, ph)
        P.barrier()
        ph.close()
        if sb["kind"] == "P":
            continue
        if os.environ.get("KB_OUT", "1") == "1":
            ph = ExitStack()
            stage_out(sb, ph, 0, w_out_a)
            P.barrier()
            ph.close()
        if stop_after in ("L0", "L0nopre"):
            dump_h(sb)
            P.barrier()
            continue
        ph = ExitStack()
        stage_l1(sb, sbi, ph)
        P.barrier()
        ph.close()
        if os.environ.get("KB_OUT1", "1") == "1":
            ph = ExitStack()
            stage_out(sb, ph, 1, w_out_b)
            P.barrier()
            ph.close()
        if stop_after == "L1":
            dump_h(sb)
            P.barrier()
        if os.environ.get("KB_Y", "1") == "1":
            ph = ExitStack()
            stage_y(sb, ph)
            P.barrier()
            ph.close()

    P.emit()
    print("stats:", P.stats)
    return nc


def _consts(half):
    ident = np.eye(128, dtype=np.float32)
    s = np.arange(128)[:, None]
    t = np.arange(128)[None, :]
    mtri = (t >= s).astype(np.uint32)
    maskc = (s <= t).astype(np.float32)
    maskp = (s > t).astype(np.float32)
    maskf = maskp.copy() if half == 1 else np.zeros((128, 128), np.float32)
    prot = np.zeros((128, 128), np.float32)
    for base in (0, 64):
        for i in range(8):
            prot[base + i + 8, base + i] = 1.0
            prot[base + i, base + i + 8] = 1.0
    pos = np.concatenate([half * 2048 - 128 + np.arange(NMAIN), np.full(NS, PAST_LEN)]).astype(np.float32)
    inv = (ROPE_THETA ** (-np.arange(0, 16, 2, dtype=np.float32) / 16)).astype(np.float32)
    ang = pos[None, :] * inv[:, None]
    cos = np.cos(ang).astype(np.float32)
    sin = np.sin(ang).astype(np.float32)
    ropec = np.ones((128, TTOT), np.float32)
    ropes = np.zeros((128, TTOT), np.float32)
    for base in (0, 64):
        ropec[base:base + 8] = cos
        ropec[base + 8:base + 16] = cos
        ropes[base:base + 8] = -sin
        ropes[base + 8:base + 16] = sin
    return dict(ident=ident, mtri=mtri, maskc=maskc, maskp=maskp, maskf=maskf, prot=prot, ropec=ropec, ropes=ropes)


def _col(v):
    v = np.asarray(v, np.float32).reshape(-1, 128)
    return np.ascontiguousarray(v.T)


def make_in_maps(inp):
    f = lambda a: np.ascontiguousarray(np.asarray(a, dtype=np.float32))
    xpr, xsm = f(inp["x_prompt"]), f(inp["x_sample"])
    ppr, psa = f(inp["p_prompt"]), f(inp["p_sample"])
    st, ck, cv = f(inp["state_hgrn"]), f(inp["cache_k"]), f(inp["cache_v"])
    vecs = np.concatenate([
        _col(inp["pre_norm_g"][0]), _col(inp["pre_norm_g"][1]), _col(inp["post_norm_g"][0]), _col(inp["post_norm_g"][1]),
        _col(inp["kv_norm_g"]), _col(inp["onorm_a"][0]), _col(inp["lb_logits"][0]), _col(inp["lb_logits"][1])], axis=1)
    assert vecs.shape == (128, NVEC)
    shared = dict(
        w_in_a=f(inp["w_in_a"][0]), w_out_a=f(inp["w_out_a"][0]), w_kv=f(inp["w_kv"]), w_in_b=f(inp["w_in_b"][0]),
        w_out_b=f(inp["w_out_b"][0]), w_pe=f(inp["w_pe"]), w_pg=f(inp["w_pg"]), vecs=np.ascontiguousarray(vecs),
        sinks=f(inp["sinks"]).reshape(1, 32))
    maps = []
    for c in range(8):
        b, half = c // 2, c % 2
        xm = np.zeros((NMAIN, D), np.float32)
        xp = np.zeros((NPRE, D), np.float32)
        pm = np.zeros((2, NMAIN, 256), np.float32)
        if half == 0:
            xm[128:] = xpr[b, 0:2048]
            pm[:, 128:] = ppr[:, b, 0:2048]
        else:
            xm[:] = xpr[b, 1920:4096]
            pm[:] = ppr[:, b, 1920:4096]
            xp[:] = xpr[b, 0:1920]
        m = dict(shared)
        m.update(_consts(half))
        m.update(xm=xm, xp=xp, xs=np.ascontiguousarray(xsm[c * NS:(c + 1) * NS, 0]), pm=pm,
                 psm=np.ascontiguousarray(psa[:, c * NS:(c + 1) * NS, 0]),
                 st_in=np.ascontiguousarray(st[0, c * NS:(c + 1) * NS]),
                 ck=np.ascontiguousarray(ck[c * NS:(c + 1) * NS].reshape(NS, 128, 256)),
                 cv=np.ascontiguousarray(cv[c * NS:(c + 1) * NS].reshape(NS, 128, 256)))
        maps.append(m)
    return maps


def assemble(results):
    y_p = np.zeros((4, 4096, D), np.float32)
    y_s = np.zeros((128, 1, D), np.float32)
    st_p = np.zeros((1, 4, 16, 128, 128), np.float32)
    st_s = np.zeros((1, 128, 16, 128, 128), np.float32)
    k_p = np.zeros((4, 128, 4, 64), np.float32)
    v_p = np.zeros((4, 128, 4, 64), np.float32)
    k_s = np.zeros((128, 1, 4, 64), np.float32)
    v_s = np.zeros((128, 1, 4, 64), np.float32)
    for c in range(8):
        r = results[c]
        b, half = c // 2, c % 2
        y_p[b, half * 2048:(half + 1) * 2048] = r["y"][0:2048]
        y_s[c * NS:(c + 1) * NS, 0] = r["y"][2048:2048 + NS]
        st_s[0, c * NS:(c + 1) * NS] = r["st_s"]
        k_s[c * NS:(c + 1) * NS, 0] = r["ks"].reshape(NS, 4, 64)
        v_s[c * NS:(c + 1) * NS, 0] = r["vs"].reshape(NS, 4, 64)
        if half == 1:
            st_p[0, b] = r["st_p"]
            k_p[b] = r["kp"].reshape(128, 4, 64)
            v_p[b] = r["vp"].reshape(128, 4, 64)
    return (y_p, y_s, st_p, st_s, k_p, v_p, k_s, v_s)


def kernel(**inputs):
    nc = build_program()
    in_maps = make_in_maps(inputs)
    res = run_bass_kernel_spmd(nc, in_maps, core_ids=list(range(8)))
    return assemble(res.results)
```
